# Optimizing a Trainium2 kernel written in Bass

```python
import math
import jax, jax.numpy as jnp
from jax import lax
import numpy as np

D_MODEL = 1024
BATCH = 4
SEQ = 8192
DEPTH = 2

HEAD_DIM = 64
A_GROUPS = ((128, 1), (512, 4), (2048, 16))
A_N_GROUPS = 3
A_HEADS_PER_GROUP = 8
A_HEADS = A_N_GROUPS * A_HEADS_PER_GROUP
A_QKV_WIDTH = A_HEADS * HEAD_DIM
A_WIDTH = A_HEADS_PER_GROUP * HEAD_DIM
A_Q_BLOCK = 64
B_Q_HEADS = 8
B_KV_HEADS = 2
B_WIDTH = B_Q_HEADS * HEAD_DIM
B_KV_WIDTH = B_KV_HEADS * HEAD_DIM
B_Q_BLOCK = 128
GRID_W = 64
ROPE_THETA = 10000.0
REL_BUCKETS = 32
REL_MAX_DISTANCE = 1024
LN_EPS = 1e-5
QK_EPS = 1e-6
NEG_INF = -1e30
DEEPNORM_ALPHA = float((2 * DEPTH) ** 0.25)
DEEPNORM_BETA = float((8 * DEPTH) ** -0.25)
SPLITS = (A_QKV_WIDTH, A_QKV_WIDTH, A_QKV_WIDTH, A_WIDTH,
          B_WIDTH, B_KV_WIDTH, B_KV_WIDTH, B_WIDTH, 2 * D_MODEL)
SPLIT_POINTS = tuple(int(v) for v in np.cumsum(SPLITS)[:-1])
IN_WIDTH = int(sum(SPLITS))

kernel_name = "hybrid_dilated_gqa_gated_encoder"


def t5_bucket(rel):
    half = REL_BUCKETS // 2
    max_exact = half // 2
    ret = jnp.where(rel > 0, half, 0)
    a = jnp.abs(rel)
    af = jnp.maximum(a, 1).astype(jnp.float32)
    large = max_exact + (jnp.log(af / max_exact) / math.log(REL_MAX_DISTANCE / max_exact)
                         * (half - max_exact)).astype(jnp.int32)
    large = jnp.minimum(large, half - 1)
    return ret + jnp.where(a < max_exact, a, large)


def dilated_window_attention(q, k, v, table_g, window, dilation):
    B, S, H, E = q.shape
    d = dilation
    R = window // (2 * d)
    L = S // d
    Qb = math.gcd(L, A_Q_BLOCK)
    nblk = L // Qb
    Kw = Qb + 2 * R

    def phases(a):
        return a.reshape(B, L, d, H, E).transpose(0, 2, 3, 1, 4)

    qp = phases(q).reshape(B, d, H, nblk, Qb, E)
    pad = ((0, 0), (0, 0), (0, 0), (R, R), (0, 0))
    idx = jnp.arange(nblk)[:, None] * Qb + jnp.arange(Kw)[None, :]
    kb = jnp.pad(phases(k), pad)[:, :, :, idx]
    vb = jnp.pad(phases(v), pad)[:, :, :, idx]
    rel = jnp.arange(Kw)[None, :] - R - jnp.arange(Qb)[:, None]
    bias = table_g[t5_bucket(rel * d)].transpose(2, 0, 1).astype(jnp.float32)
    key_pos = idx - R
    valid = (jnp.abs(rel) <= R)[None] & ((key_pos >= 0) & (key_pos < L))[:, None, :]
    logits = jnp.einsum('bdhnqe,bdhnke->bdhnqk', qp, kb,
                        preferred_element_type=jnp.float32) * (E ** -0.5)
    logits = jnp.where(valid, logits + bias[None, None, :, None], NEG_INF)
    m = jnp.max(logits, axis=-1, keepdims=True)
    p = jnp.exp(logits - m)
    s = jnp.sum(p, axis=-1, keepdims=True)
    o = jnp.einsum('bdhnqk,bdhnke->bdhnqe', p, vb.astype(jnp.float32)) / s
    lse = (m + jnp.log(s))[..., 0]
    o = o.reshape(B, d, H, L, E).transpose(0, 3, 1, 2, 4).reshape(B, S, H, E)
    lse = lse.reshape(B, d, H, L).transpose(0, 3, 1, 2).reshape(B, S, H)
    return o, lse


def mixer_a(q, k, v, rel_table):
    B, S, _ = q.shape
    shp = (B, S, A_N_GROUPS, A_HEADS_PER_GROUP, HEAD_DIM)
    q, k, v = q.reshape(shp), k.reshape(shp), v.reshape(shp)
    outs, lses = [], []
    for g, (window, dil) in enumerate(A_GROUPS):
        table_g = rel_table[:, g * A_HEADS_PER_GROUP:(g + 1) * A_HEADS_PER_GROUP]
        o_g, l_g = dilated_window_attention(q[:, :, g], k[:, :, g], v[:, :, g], table_g, window, dil)
        outs.append(o_g)
        lses.append(l_g)
    w = jax.nn.softmax(jnp.stack(lses, axis=0), axis=0)
    o = jnp.sum(w[..., None] * jnp.stack(outs, axis=0), axis=0)
    return o.reshape(B, S, A_WIDTH).astype(q.dtype)


def rms_head(x, g):
    xf = x.astype(jnp.float32)
    y = xf * lax.rsqrt(jnp.mean(xf * xf, axis=-1, keepdims=True) + QK_EPS) * g.astype(jnp.float32)
    return y.astype(x.dtype)


def axial_angles(S):
    rows = S // GRID_W
    row_ids = jnp.repeat(jnp.arange(rows), GRID_W).astype(jnp.float32)
    col_ids = jnp.tile(jnp.arange(GRID_W), rows).astype(jnp.float32)
    half = HEAD_DIM // 2
    inv = ROPE_THETA ** (-jnp.arange(0, half, 2, dtype=jnp.float32) / half)
    return row_ids[:, None] * inv[None], col_ids[:, None] * inv[None]


def rotate_half_rope(x, ang):
    n = ang.shape[-1]
    x1, x2 = x[..., :n], x[..., n:]
    cos = jnp.cos(ang)[None, :, None, :]
    sin = jnp.sin(ang)[None, :, None, :]
    return jnp.concatenate([x1 * cos - x2 * sin, x2 * cos + x1 * sin], axis=-1)


def axial_rope(x, ang_row, ang_col):
    xf = x.astype(jnp.float32)
    half = HEAD_DIM // 2
    y = jnp.concatenate([rotate_half_rope(xf[..., :half], ang_row),
                         rotate_half_rope(xf[..., half:], ang_col)], axis=-1)
    return y.astype(x.dtype)


def mixer_b(q, k, v, q_norm_g, k_norm_g):
    B, S, _ = q.shape
    q = q.reshape(B, S, B_Q_HEADS, HEAD_DIM)
    k = k.reshape(B, S, B_KV_HEADS, HEAD_DIM)
    v = v.reshape(B, S, B_KV_HEADS, HEAD_DIM)
    ang_row, ang_col = axial_angles(S)
    q = axial_rope(rms_head(q, q_norm_g), ang_row, ang_col)
    k = axial_rope(rms_head(k, k_norm_g), ang_row, ang_col)
    G = B_Q_HEADS // B_KV_HEADS
    nq = S // B_Q_BLOCK
    qb = q.reshape(B, nq, B_Q_BLOCK, B_KV_HEADS, G, HEAD_DIM).transpose(1, 0, 2, 3, 4, 5)

    def one_block(qblk):
        logits = jnp.einsum('bqkge,bske->bkgqs', qblk, k,
                            preferred_element_type=jnp.float32) * (HEAD_DIM ** -0.5)
        p = jax.nn.softmax(logits, axis=-1).astype(v.dtype)
        return jnp.einsum('bkgqs,bske->bqkge', p, v)

    o = lax.map(one_block, qb)
    return o.transpose(1, 0, 2, 3, 4, 5).reshape(B, S, B_WIDTH)


def layer_norm(h, g, b):
    hf = h.astype(jnp.float32)
    mu = jnp.mean(hf, axis=-1, keepdims=True)
    var = jnp.mean(jnp.square(hf - mu), axis=-1, keepdims=True)
    y = (hf - mu) * lax.rsqrt(var + LN_EPS) * g.astype(jnp.float32) + b.astype(jnp.float32)
    return y.astype(h.dtype)


def hybrid_layer(x, c, rel_table, ln_g, ln_b, w_ada, b_ada, w_in, b_gate,
                 q_norm_g, k_norm_g, w_pa, w_pb, w_o):
    mod = jax.nn.silu(c) @ w_ada + b_ada
    shift, scale, gate = jnp.split(mod, 3, axis=-1)
    u = x * (1.0 + scale[:, None, :]) + shift[:, None, :]
    proj = u @ w_in
    aq, ak, av, az, bq, bk, bv, bz, gl = jnp.split(proj, SPLIT_POINTS, axis=-1)
    y_a = mixer_a(aq, ak, av, rel_table) * jax.nn.silu(az)
    y_b = mixer_b(bq, bk, bv, q_norm_g, k_norm_g) * jax.nn.silu(bz)
    g_a, g_b = jnp.split(jax.nn.sigmoid(gl + b_gate), 2, axis=-1)
    merged = g_a * (y_a @ w_pa) + g_b * (y_b @ w_pb)
    out = merged @ w_o
    return layer_norm(DEEPNORM_ALPHA * x + gate[:, None, :] * out, ln_g, ln_b)


def setup_inputs(seed: int = 0) -> dict:
    key = jax.random.key(seed)
    ks = jax.random.split(key, 16)
    f32 = jnp.float32
    D = D_MODEL
    x = jax.random.normal(ks[0], (BATCH, SEQ, D), f32)
    c = jax.random.normal(ks[1], (BATCH, D), f32)
    rel_table = 0.2 * jax.random.normal(ks[2], (REL_BUCKETS, A_HEADS), f32)
    ln_g = 1.0 + 0.02 * jax.random.normal(ks[3], (DEPTH, D), f32)
    ln_b = 0.02 * jax.random.normal(ks[4], (DEPTH, D), f32)
    w_ada = jax.random.normal(ks[5], (DEPTH, D, 3 * D), f32) * (D ** -0.5)
    b_ada = 0.02 * jax.random.normal(ks[6], (DEPTH, 3 * D), f32)
    col_scale = np.concatenate([
        np.full((n,), DEEPNORM_BETA if i in (2, 6) else 1.0, np.float32)
        for i, n in enumerate(SPLITS)])
    w_in = jax.random.normal(ks[7], (DEPTH, D, IN_WIDTH), f32) * (D ** -0.5) * jnp.asarray(col_scale)
    b_gate = 0.02 * jax.random.normal(ks[8], (DEPTH, 2 * D), f32)
    q_norm_g = 1.0 + 0.02 * jax.random.normal(ks[9], (DEPTH, HEAD_DIM), f32)
    k_norm_g = 1.0 + 0.02 * jax.random.normal(ks[10], (DEPTH, HEAD_DIM), f32)
    w_pa = jax.random.normal(ks[11], (DEPTH, A_WIDTH, D), f32) * (A_WIDTH ** -0.5) * DEEPNORM_BETA
    w_pb = jax.random.normal(ks[12], (DEPTH, B_WIDTH, D), f32) * (B_WIDTH ** -0.5) * DEEPNORM_BETA
    w_o = jax.random.normal(ks[13], (DEPTH, D, D), f32) * (D ** -0.5) * DEEPNORM_BETA
    return {"x": x, "c": c, "rel_table": rel_table, "ln_g": ln_g, "ln_b": ln_b,
            "w_ada": w_ada, "b_ada": b_ada, "w_in": w_in, "b_gate": b_gate,
            "q_norm_g": q_norm_g, "k_norm_g": k_norm_g, "w_pa": w_pa, "w_pb": w_pb, "w_o": w_o}


def reference(x, c, rel_table, ln_g, ln_b, w_ada, b_ada, w_in, b_gate,
              q_norm_g, k_norm_g, w_pa, w_pb, w_o):
    for l in range(DEPTH):
        x = hybrid_layer(x, c, rel_table, ln_g[l], ln_b[l], w_ada[l], b_ada[l], w_in[l], b_gate[l],
                         q_norm_g[l], k_norm_g[l], w_pa[l], w_pb[l], w_o[l])
    return x
```

```python
import math
from contextlib import ExitStack

import numpy as np
import concourse.bass as bass
import concourse.mybir as mybir
from concourse.bass_utils import run_bass_kernel_spmd

F32 = mybir.dt.float32
BF16 = mybir.dt.bfloat16
AF = mybir.ActivationFunctionType
ALU = mybir.AluOpType

D = 1024
T = 4096
S = 8192
DEPTH = 2
GROUPS = ((128, 1), (512, 4), (2048, 16))
ALPHA = float((2 * DEPTH) ** 0.25)
LN_EPS = 1e-5
QK_EPS = 1e-6
C_AQ, C_AK, C_AV, C_AZ, C_BQ, C_BK, C_BV, C_BZ, C_GL = 0, 1536, 3072, 4608, 5120, 5632, 5760, 5888, 6400
XCOLS = 4096


def _xlayout():
    units = []
    sec = {}

    def add_unit(items):
        u = len(units)
        r = 0
        for k, n in items:
            sec[k] = (u, r, n)
            r += n
        units.append(r)

    add_unit([("KB", 128)])
    add_unit([("VB", 130)])
    for g, (_, d) in enumerate(GROUPS):
        nk = 8 * d
        nv = -(-(d * 64 * 520) // XCOLS)
        items = [(("AKF", g), nk), (("AKL", g), nk), (("AVF", g), nv), (("AVL", g), nv)]
        if nk + nk + nv + nv <= 130:
            add_unit(items)
        else:
            for it in items:
                add_unit([it])
    return units, sec


XUNITS, XSEC = _xlayout()


class Prog:
    CE = ("pe", "act", "dve", "pool")
    ALLE = ("pe", "act", "dve", "pool", "sp")
    NDS = 8

    def __init__(self, nc, es):
        self.nc = nc
        self.sem = {}
        for e in self.CE:
            self.sem[("c", e)] = es.enter_context(nc.semaphore(f"c_{e}"))
        for q in ("sp", "act", "pool"):
            for i in range(self.NDS):
                self.sem[("d", q, i)] = es.enter_context(nc.semaphore(f"d_{q}{i}"))
        self.sem[("cc",)] = es.enter_context(nc.semaphore("ccsem"))
        self.cnt = {k: 0 for k in self.sem}
        self.dnext = {q: 0 for q in ("sp", "act", "pool")}
        self.ops = {e: [] for e in self.ALLE}
        self.waited = {e: {} for e in self.ALLE}
        self.lastw = {}
        self.readers = {}
        self.nops = 0

    def _deps(self, reads, writes):
        deps = set()
        for k in reads:
            if k in self.lastw:
                deps.add(self.lastw[k])
        for k in writes:
            if k in self.lastw:
                deps.add(self.lastw[k])
            deps.update(self.readers.get(k, ()))
        return deps

    def _commit(self, ticket, reads, writes):
        for k in reads:
            self.readers.setdefault(k, []).append(ticket)
        for k in writes:
            self.lastw[k] = ticket
            self.readers[k] = []

    def _waits(self, eng, deps):
        w = []
        for (sk, val) in sorted(deps, key=lambda t: (str(t[0]), t[1])):
            if eng == "pe" and sk == ("c", "pe"):
                continue
            if self.waited[eng].get(sk, 0) >= val:
                continue
            self.waited[eng][sk] = val
            w.append((self.sem[sk], val))
        return w

    def op(self, eng, fn, reads=(), writes=()):
        deps = self._deps(reads, writes)
        w = self._waits(eng, deps)
        sk = ("c", eng)
        self.cnt[sk] += 1
        t = (sk, self.cnt[sk])
        self.ops[eng].append((w, fn, self.sem[sk], 1))
        self._commit(t, reads, writes)
        self.nops += 1
        return t

    def dma(self, q, out, in_, reads=(), writes=()):
        deps = self._deps(reads, writes)
        slot = self.dnext[q] % self.NDS
        self.dnext[q] += 1
        sk = ("d", q, slot)
        if self.cnt[sk] > 0:
            deps.add((sk, self.cnt[sk]))
        w = self._waits(q, deps)
        self.cnt[sk] += 16
        t = (sk, self.cnt[sk])
        self.ops[q].append((w, lambda e: e.dma_start(out=out, in_=in_), self.sem[sk], 16))
        self._commit(t, reads, writes)
        self.nops += 1
        return t

    def collective(self, fn, reads=(), writes=()):
        deps = self._deps(reads, writes)
        w = self._waits("pool", deps)
        sk = ("cc",)
        self.cnt[sk] += 1
        t = (sk, self.cnt[sk])
        self.ops["pool"].append((w, fn, self.sem[sk], 1))
        self._commit(t, reads, writes)
        return t

    def barrier(self):
        tickets = set((sk, v) for sk, v in self.cnt.items() if v > 0)
        for e in self.ALLE:
            w = self._waits(e, tickets)
            if w:
                self.ops[e].append((w, None, None, 0))
        self.lastw = {}
        self.readers = {}

    def flush(self):
        nc = self.nc
        ops = self.ops

        def replay(lst, e):
            for (w, fn, sem, inc) in lst:
                for (s, v) in w:
                    e.wait_ge(s, v)
                if fn is not None:
                    ins = fn(e)
                    ins.then_inc(sem, inc)

        with nc.Block() as block:
            @block.tensor
            def _(e):
                replay(ops["pe"], e)

            @block.scalar
            def _(e):
                replay(ops["act"], e)

            @block.vector
            def _(e):
                replay(ops["dve"], e)

            @block.gpsimd
            def _(e):
                replay(ops["pool"], e)

            @block.sync
            def _(e):
                replay(ops["sp"], e)
        self.ops = {e: [] for e in self.ALLE}

    def mm(self, out, lhsT, rhs, start=True, stop=True, reads=(), writes=()):
        return self.op("pe", lambda e: e.matmul(out, lhsT, rhs, start=start, stop=stop), reads, writes)

    def tr(self, out, in_, ident, reads=(), writes=()):
        return self.op("pe", lambda e: e.transpose(out, in_, ident), reads, writes)

    def act(self, out, in_, func, bias=None, scale=None, reads=(), writes=(), eng="act"):
        kw = {}
        if bias is not None:
            kw["bias"] = bias
        if scale is not None:
            kw["scale"] = scale
        return self.op(eng, lambda e: e.activation(out, in_, func, **kw), reads, writes)

    def tt(self, eng, out, in0, in1, op, reads=(), writes=()):
        return self.op(eng, lambda e: e.tensor_tensor(out, in0, in1, op), reads, writes)

    def ts(self, eng, out, in0, s1, s2, op0, op1=None, reads=(), writes=()):
        if op1 is None:
            return self.op(eng, lambda e: e.tensor_scalar(out, in0, s1, None, op0), reads, writes)
        return self.op(eng, lambda e: e.tensor_scalar(out, in0, s1, s2, op0, op1), reads, writes)

    def stt(self, out, in0, scalar, in1, op0, op1, reads=(), writes=()):
        return self.op("dve", lambda e: e.scalar_tensor_tensor(out, in0, scalar, in1, op0, op1), reads, writes)

    def copy(self, eng, out, in_, reads=(), writes=()):
        if eng == "act":
            return self.op("act", lambda e: e.activation(out, in_, AF.Copy), reads, writes)
        return self.op(eng, lambda e: e.tensor_copy(out, in_), reads, writes)

    def memset(self, eng, ap, val, reads=(), writes=()):
        return self.op(eng, lambda e: e.memset(ap, val), reads, writes)

    def recip(self, out, in_, reads=(), writes=()):
        return self.op("dve", lambda e: e.reciprocal(out, in_), reads, writes)


class TPool:
    def __init__(self, tiles, name):
        self.tiles = tiles
        self.name = name
        self.i = 0

    def next(self):
        j = self.i % len(self.tiles)
        self.i += 1
        return self.tiles[j], (self.name, j)


def _build(debug=None):
    nc = bass.Bass("TRN2", target_bir_lowering=False)
    dbg_kind = "ExternalOutput" if debug else "Internal"

    def din(name, shape, dt=F32):
        return nc.dram_tensor(name, list(shape), dt, kind="ExternalInput").ap()

    def dscr(name, shape, dt=BF16, dbg=False):
        isdbg = bool(debug) and dbg and name in debug.get("outs", ())
        return nc.dram_tensor(name, list(shape), dt, kind=("ExternalOutput" if isdbg else "Internal")).ap()

    x_in = din("x", [T, D])
    cT_in = din("cT", [128, 8])
    ln_g_in = din("ln_g", [DEPTH, 1, D])
    ln_b_in = din("ln_b", [DEPTH, 1, D])
    w_ada_in = din("w_ada", [DEPTH, D, 3 * D])
    b_ada_in = din("b_ada", [DEPTH, 1, 3 * D])
    w_in_in = din("w_in", [DEPTH, D, 8448])
    bgT_in = din("bgT", [DEPTH, 128, 16])
    qkg_in = din("qkg", [DEPTH, 128, 2])
    w_pa_in = din("w_pa", [DEPTH, 512, D])
    w_pb_in = din("w_pb", [DEPTH, 512, D])
    w_o_in = din("w_o", [DEPTH, D, D])
    cmat_in = din("cmat", [3, 128, 128])
    cstab_in = din("cstab", [2, 128, T])
    abias_in = din("abias", [128, 24 * 256])
    tri_in = din("tri", [2, 128, 256])
    msk_in = din("msk", [128, 2])
    y_out = nc.dram_tensor("y", [T, D], F32, kind="ExternalOutput").ap()

    aqT = dscr("aqT", [3, 512, T], dbg=True)
    akT = [dscr(f"akT{g}", [512, d, T // d + 128], dbg=True) for g, (_, d) in enumerate(GROUPS)]
    avp = [dscr(f"avp{g}", [d, T // d + 128, 520], dbg=True) for g, (_, d) in enumerate(GROUPS)]
    sazT = dscr("sazT", [512, T], dbg=True)
    sbzT = dscr("sbzT", [512, T], dbg=True)
    qbT = dscr("qbT", [512, T], dbg=True)
    gT = dscr("gT", [2048, T], dbg=True)
    yaT = dscr("yaT", [512, T], dbg=True)
    ybT = dscr("ybT", [512, T], dbg=True)
    x1 = dscr("x1", [T, D], F32, dbg=True)
    Bdram = dscr("Bdram", [128, 24 * 256], BF16)
    xsrc_u = [dscr(f"xsrc{u}", [n, XCOLS]) for u, n in enumerate(XUNITS)]
    xdst_u = [dscr(f"xdst{u}", [2 * n, XCOLS]) for u, n in enumerate(XUNITS)]

    def xs(key):
        u, r0, n = XSEC[key]
        return xsrc_u[u][r0:r0 + n, :]

    def xd(key, rank):
        u, r0, n = XSEC[key]
        return xdst_u[u][rank * XUNITS[u] + r0:rank * XUNITS[u] + r0 + n, :]
    dbg_names = ["aqT", "akT0", "akT1", "akT2", "avp0", "avp1", "avp2", "sazT", "sbzT", "qbT", "gT",
                 "yaT", "ybT", "x1"]

    with ExitStack() as top:
        P = Prog(nc, top)

        def sb(es, name, shape, dt):
            return es.enter_context(nc.sbuf_tensor("s_" + name, list(shape), dt))

        PSALL = top.enter_context(nc.psum_tensor("psall", [128, 8, 512], F32))
        PS = [PSALL[:, i, :] for i in range(8)]

        class PSPool:
            def __init__(self, idx):
                self.idx = idx
                self.i = 0

            def next(self):
                b = self.idx[self.i % len(self.idx)]
                self.i += 1
                return PS[b], ("ps", b)

        def pspool(idx, name):
            return PSPool(idx)

        cmat = sb(top, "cmat", [128, 3, 128], F32)
        ones_f = sb(top, "ones_f", [128, 128], F32)
        msk = sb(top, "msk", [128, 2], F32)
        P.dma("sp", cmat[:, :, :], cmat_in.rearrange("c p n -> p c n"), writes=["cmat"])
        P.dma("sp", msk[:, :], msk_in, writes=["msk"])
        P.memset("pool", ones_f[:, :], 1.0, writes=["ones"])
        epsc = sb(top, "epsc", [128, 2], F32)
        P.memset("pool", epsc[:, 0:1], QK_EPS, writes=["epsc"])
        P.memset("pool", epsc[:, 1:2], LN_EPS, writes=["epsc"])
        ident = cmat[:, 0, :]
        bones = cmat[:, 1, :]
        rrot = cmat[:, 2, :]

        identb = sb(top, "identb", [128, 128], BF16)
        P.copy("dve", identb[:, :], ident, reads=["cmat"], writes=["identb"])
        with ExitStack() as es:
            ab = sb(es, "ab", [128, 24 * 256], F32)
            abb = sb(es, "abb", [128, 24 * 256], BF16)
            tri = sb(es, "tri", [128, 2, 256], F32)
            P.dma("sp", ab[:, :], abias_in, writes=["ab"])
            P.dma("sp", tri[:, :, :], tri_in.rearrange("c p n -> p c n"), writes=["tri"])
            for gh in range(24):
                sl = slice(gh * 256, (gh + 1) * 256)
                P.stt(ab[:, sl], ab[:, sl], 8.0, tri[:, 0, :], ALU.mult, ALU.mult, reads=["ab", "tri"], writes=[("ab", gh)])
                P.tt("dve", abb[:, sl], ab[:, sl], tri[:, 1, :], ALU.add, reads=[("ab", gh), "tri"], writes=[("abb", gh)])
            P.dma("sp", Bdram, abb[:, :], reads=[("abb", gh) for gh in range(24)])
            P.barrier()
            P.flush()

        for l in range(DEPTH):
            if debug and l > debug.get("layers", DEPTH) - 1:
                break
            xsrc_l = x_in if l == 0 else x1
            ydst_l = x1 if l < DEPTH - 1 else y_out
            with ExitStack() as lay:
                shiftT = sb(lay, f"shiftT{l}", [128, 8], F32)
                sc1T = sb(lay, f"sc1T{l}", [128, 8], F32)
                gate_b = sb(lay, f"gate_b{l}", [128, D], F32)
                lng_b = sb(lay, f"lng_b{l}", [128, D], F32)
                lnb_b = sb(lay, f"lnb_b{l}", [128, D], F32)
                bgT = sb(lay, f"bgT{l}", [128, 16], F32)
                qkg = sb(lay, f"qkg{l}", [128, 2], F32)

                with ExitStack() as es:
                    cT = sb(es, f"cT{l}", [128, 8], F32)
                    silc = sb(es, f"silc{l}", [128, 8], F32)
                    rows = sb(es, f"rows{l}", [1, 5 * D], F32)
                    brow = sb(es, f"brow{l}", [1, 3 * D], F32)
                    wst = [sb(es, f"wada{l}_{i}", [128, 8, 512], F32) for i in range(2)]
                    wp = TPool(wst, "wada")
                    psp = pspool([0, 1], "ps")
                    P.dma("sp", cT[:, :], cT_in, writes=["cT"])
                    P.dma("sp", brow[:, :], b_ada_in[l], writes=["brow"])
                    P.dma("sp", rows[:, 3 * D:4 * D], ln_g_in[l], writes=["rows_ln"])
                    P.dma("sp", rows[:, 4 * D:5 * D], ln_b_in[l], writes=["rows_ln"])
                    P.dma("sp", bgT[:, :], bgT_in[l], writes=["bgT"])
                    P.dma("sp", qkg[:, :], qkg_in[l], writes=["qkg"])
                    P.act(silc[:, :], cT[:, :], AF.Silu, reads=["cT"], writes=["silc"])
                    for n in range(6):
                        wt, kw = wp.next()
                        P.dma("sp", wt[:, :, :],
                              w_ada_in[l, :, n * 512:(n + 1) * 512].rearrange("(kc p) n -> p kc n", p=128),
                              writes=[kw])
                        ps, kp = psp.next()
                        for kc in range(8):
                            P.mm(ps[0:1, :], silc[:, kc:kc + 1], wt[:, kc, :], start=(kc == 0), stop=(kc == 7),
                                 reads=[kw, "silc"], writes=[kp])
                        P.tt("dve", rows[0:1, n * 512:(n + 1) * 512], ps[0:1, :], brow[0:1, n * 512:(n + 1) * 512],
                             ALU.add, reads=[kp, "brow"], writes=[("rows", n)])
                    ps, kp = psp.next()
                    for j in range(16):
                        P.mm(ps[:, j:j + 1], rows[0:1, j * 128:(j + 1) * 128], ones_f[0:1, 0:1],
                             reads=[("rows", j // 4), "ones"], writes=[kp])
                    P.copy("dve", shiftT[:, :], ps[:, 0:8], reads=[kp], writes=["mod"])
                    P.ts("dve", sc1T[:, :], ps[:, 8:16], 1.0, None, ALU.add, reads=[kp], writes=["mod2"])
                    for (dst, c0, rk) in ((gate_b, 2 * D, [("rows", 4), ("rows", 5)]), (lng_b, 3 * D, ["rows_ln"]),
                                          (lnb_b, 4 * D, ["rows_ln"])):
                        for nh in range(2):
                            ps, kp = psp.next()
                            P.mm(ps[:, :], ones_f[0:1, :], rows[0:1, c0 + nh * 512:c0 + (nh + 1) * 512],
                                 reads=rk + ["ones"], writes=[kp])
                            P.copy("dve", dst[:, nh * 512:(nh + 1) * 512], ps[:, :], reads=[kp], writes=["bc"])
                    P.barrier()
                    P.flush()

                with ExitStack() as es:
                  if not (debug and debug.get("skipP1")):
                        uT = sb(es, f"uT{l}", [128, 8, T], BF16)
                        with ExitStack() as es2:
                            xp = TPool([sb(es2, f"xt{l}_{i}", [128, 4, D], F32) for i in range(2)], "xt")
                            psp = pspool([0, 1, 2, 3], "ps")
                            for tb in range(8):
                                xt, kx = xp.next()
                                P.dma("sp", xt[:, :, :],
                                      xsrc_l[tb * 512:(tb + 1) * 512, :].rearrange("(t p) d -> p t d", p=128), writes=[kx])
                                for kc in range(8):
                                    ps, kp = psp.next()
                                    for t in range(4):
                                        P.tr(ps[:, t * 128:(t + 1) * 128], xt[:, t, kc * 128:(kc + 1) * 128], ident,
                                             reads=[kx], writes=[kp])
                                    P.act(uT[:, kc, tb * 512:(tb + 1) * 512], ps[:, :], AF.Identity,
                                          bias=shiftT[:, kc:kc + 1], scale=sc1T[:, kc:kc + 1], reads=[kp])
                            P.barrier()
                            P.flush()

                        uTp = sb(es, f"uTp{l}", [128, 8, T], BF16)
                        wstp = TPool([sb(es, f"wst{l}_{i}", [128, 8, 256], F32) for i in range(2)], "wst")
                        wbfp = TPool([sb(es, f"wbf{l}_{i}", [128, 8, 512], BF16) for i in range(2)], "wbf")
                        stgp = TPool([sb(es, f"stg{l}_{i}", [128, 512], BF16) for i in range(4)], "stg")
                        vstp = TPool([sb(es, f"vst{l}_{i}", [128, 8, 65], BF16) for i in range(3)], "vst")
                        f32p = TPool([sb(es, f"f32t{l}_{i}", [128, 512], F32) for i in range(7)], "f32t")
                        csp = TPool([sb(es, f"cst{l}_{i}", [128, 2, 512], F32) for i in range(2)], "cst")
                        psA = pspool([0, 1, 2, 3, 4], "ps")
                        psB = pspool([5, 6, 7], "ps")
                        for vt in vstp.tiles:
                            P.memset("pool", vt[:, :, :], 1.0)
                        P.barrier()
                        evac_rr = [0]

                        def evac_copy(out, in_, reads, writes):
                            evac_rr[0] += 1
                            if evac_rr[0] % 2:
                                return P.copy("act", out, in_, reads=reads, writes=writes)
                            return P.copy("dve", out, in_, reads=reads, writes=writes)

                        def load_strip(c0, W):
                            wb, kb = wbfp.next()
                            for w0 in range(0, W, 256):
                                wt, kw = wstp.next()
                                P.dma("sp", wt[:, :, :],
                                      w_in_in[l, :, c0 + w0:c0 + w0 + 256].rearrange("(kc p) n -> p kc n", p=128),
                                      writes=[kw])
                                P.copy("pool", wb[:, :, w0:w0 + 256], wt[:, :, :], reads=[kw], writes=[(kb, w0)])
                            return wb, [(kb, w0) for w0 in range(0, W, 256)]

                        def permute_uT(d):
                            engs = ("dve", "pool", "act")
                            for kc in range(8):
                                P.copy(engs[kc % 3], uTp[:, kc, :].rearrange("k (r p) -> k r p", r=d),
                                       uT[:, kc, :].rearrange("k (p r) -> k r p", r=d), writes=[("uTp", kc)])

                        def rhs_perm(g, kc, pb):
                            src = uT if g == 0 else uTp
                            return src[:, kc, pb * 512:(pb + 1) * 512]

                        def lhs_perm(g, kc, it):
                            src = uT if g == 0 else uTp
                            return src[:, kc, it * 128:(it + 1) * 128]

                        def fm_chunk(wb, kb, col0, rhs_fn, extra=()):
                            ps, kp = psA.next()
                            for kc in range(8):
                                rd_ = list(kb) + [e_ for e_ in extra if e_[1] == kc]
                                P.mm(ps[:, :], wb[:, kc, col0:col0 + 128], rhs_fn(kc), start=(kc == 0), stop=(kc == 7),
                                     reads=rd_, writes=[kp])
                            return ps, kp

                        def do_qk(which, g, wb, kb):
                            d = GROUPS[g][1]
                            ex_ = [("uTp", kc) for kc in range(8)] if g > 0 else []
                            for fcl in range(4):
                                for pb in range(8):
                                    ps, kp = fm_chunk(wb, kb, fcl * 128, lambda kc: rhs_perm(g, kc, pb), ex_)
                                    stg, ks = stgp.next()
                                    evac_copy(stg[:, :], ps[:, :], [kp], [ks])
                                    rows_ = slice(fcl * 128, (fcl + 1) * 128)
                                    if which == "q":
                                        P.dma("sp", aqT[g, rows_, pb * 512:(pb + 1) * 512], stg[:, :], reads=[ks])
                                    elif g == 0:
                                        P.dma("sp", akT[0][rows_, 0, 64 + pb * 512:64 + (pb + 1) * 512], stg[:, :],
                                              reads=[ks])
                                    elif g == 1:
                                        r, p0 = pb // 2, (pb % 2) * 512
                                        P.dma("sp", akT[1][rows_, r, 64 + p0:64 + p0 + 512], stg[:, :], reads=[ks])
                                    else:
                                        P.dma("sp", akT[2][rows_, 2 * pb:2 * pb + 2, 64:64 + 256],
                                              stg[:, :].rearrange("p (r c) -> p r c", r=2), reads=[ks])

                        def do_v(g, wb, kb):
                            d = GROUPS[g][1]
                            per = (T // d) // 128
                            for it in range(32):
                                ps, kp = psA.next()
                                for kc in range(8):
                                    rd_ = list(kb) + ([("uTp", kc)] if g > 0 else [])
                                    P.mm(ps[:, :], lhs_perm(g, kc, it), wb[:, kc, 0:512], start=(kc == 0), stop=(kc == 7),
                                         reads=rd_, writes=[kp])
                                vt, kv = vstp.next()
                                evac_copy(vt[:, :, 0:64], ps[:, :].rearrange("p (h e) -> p h e", h=8), [kp], [kv])
                                r, p0 = it // per, (it % per) * 128
                                P.dma("sp", avp[g][r, 64 + p0:64 + p0 + 128, :],
                                      vt[:, :, :].rearrange("p h c -> p (h c)"), reads=[kv])

                        def do_gate(dst, s_, wb, kb):
                            for fcl in range(4):
                                for tb in range(8):
                                    ps, kp = fm_chunk(wb, kb, fcl * 128, lambda kc: uT[:, kc, tb * 512:(tb + 1) * 512])
                                    stg, ks = stgp.next()
                                    j = s_ * 4 + fcl
                                    if dst is gT:
                                        P.act(stg[:, :], ps[:, :], AF.Sigmoid, bias=bgT[:, j:j + 1], reads=[kp],
                                              writes=[ks])
                                    else:
                                        P.act(stg[:, :], ps[:, :], AF.Silu, reads=[kp], writes=[ks])
                                    P.dma("sp", dst[j * 128:(j + 1) * 128, tb * 512:(tb + 1) * 512], stg[:, :],
                                          reads=[ks])

                        jobs = []
                        for g in range(3):
                            if g > 0:
                                jobs.append((None, 0, (lambda g_: (lambda wb, kb: permute_uT(GROUPS[g_][1])))(g)))
                            jobs.append((C_AQ + g * 512, 512, (lambda g_: (lambda wb, kb: do_qk("q", g_, wb, kb)))(g)))
                            jobs.append((C_AK + g * 512, 512, (lambda g_: (lambda wb, kb: do_qk("k", g_, wb, kb)))(g)))
                            jobs.append((C_AV + g * 512, 512, (lambda g_: (lambda wb, kb: do_v(g_, wb, kb)))(g)))
                        jobs.append((C_AZ, 512, lambda wb, kb: do_gate(sazT, 0, wb, kb)))
                        jobs.append((C_BZ, 512, lambda wb, kb: do_gate(sbzT, 0, wb, kb)))
                        for s_ in range(4):
                            jobs.append((C_GL + s_ * 512, 512, (lambda s2: (lambda wb, kb: do_gate(gT, s2, wb, kb)))(s_)))
                        strips = [j for j in jobs if j[0] is not None]
                        loaded = {}
                        nxt = [0]

                        def prefetch():
                            if nxt[0] < len(strips):
                                c0_, W_, _ = strips[nxt[0]]
                                loaded[nxt[0]] = load_strip(c0_, W_)
                                nxt[0] += 1

                        prefetch()
                        si = 0
                        for (c0_, W_, fn_) in jobs:
                            if c0_ is None:
                                fn_(None, None)
                                continue
                            wb, kb = loaded.pop(si)
                            si += 1
                            prefetch()
                            fn_(wb, kb)
                        xs_kb = xs("KB")
                        xs_vb = xs("VB").rearrange("r c -> (r c)").rearrange("(t f) -> t f", f=130)
                        wbq, kbq = load_strip(C_BQ, 512)
                        wbk, kbk = load_strip(C_BK, 256)
                        for fcl in range(5):
                            wb, kb, col0, gcol = (wbq, kbq, fcl * 128, 0) if fcl < 4 else (wbk, kbk, 0, 1)
                            for tb in range(8):
                                tsl = slice(tb * 512, (tb + 1) * 512)
                                cst, kc_ = csp.next()
                                P.dma("sp", cst[:, :, :], cstab_in[:, :, tsl].rearrange("c p t -> p c t"), writes=[kc_])
                                ps, kp = fm_chunk(wb, kb, col0, lambda kc: uT[:, kc, tsl])
                                sq, k1 = f32p.next()
                                P.act(sq[:, :], ps[:, :], AF.Square, reads=[kp], writes=[k1])
                                ps2, kp2 = psB.next()
                                P.mm(ps2[:, :], bones, sq[:, :], reads=[k1], writes=[kp2])
                                srt, k2a = f32p.next()
                                P.act(srt[:, :], ps2[:, :], AF.Sqrt, bias=epsc[:, 0:1], reads=[kp2], writes=[k2a])
                                rstd, k2 = f32p.next()
                                P.recip(rstd[:, :], srt[:, :], reads=[k2a], writes=[k2])
                                xn, k3 = f32p.next()
                                P.stt(xn[:, :], ps[:, :], qkg[:, gcol:gcol + 1], rstd[:, :], ALU.mult, ALU.mult,
                                      reads=[kp, k2], writes=[k3])
                                ps3, kp3 = psB.next()
                                P.mm(ps3[:, :], rrot, xn[:, :], reads=[k3], writes=[kp3])
                                ta, k4 = f32p.next()
                                P.tt("pool", ta[:, :], xn[:, :], cst[:, 0, :], ALU.mult, reads=[k3, kc_], writes=[k4])
                                tb_, k5 = f32p.next()
                                P.tt("dve", tb_[:, :], ps3[:, :], cst[:, 1, :], ALU.mult, reads=[kp3, kc_], writes=[k5])
                                stg, ks = stgp.next()
                                P.tt("pool", stg[:, :], ta[:, :], tb_[:, :], ALU.add, reads=[k4, k5], writes=[ks])
                                if fcl < 4:
                                    P.dma("sp", qbT[fcl * 128:(fcl + 1) * 128, tsl], stg[:, :], reads=[ks])
                                else:
                                    P.dma("sp", xs_kb[:, tsl], stg[:, :], reads=[ks])
                        for it in range(32):
                            ps, kp = psA.next()
                            for kc in range(8):
                                P.mm(ps[:, 0:128], uT[:, kc, it * 128:(it + 1) * 128], wbk[:, kc, 128:256],
                                     start=(kc == 0), stop=(kc == 7), reads=kbk, writes=[kp])
                            vt, kv = vstp.next()
                            evac_copy(vt[:, 0:2, 0:64], ps[:, 0:128].rearrange("p (h e) -> p h e", h=2), [kp], [kv])
                            P.dma("sp", xs_vb[it * 128:(it + 1) * 128, :],
                                  vt[:, 0:2, :].rearrange("p h c -> p (h c)"), reads=[kv])
                        P.barrier()
                        P.flush()

                if debug and debug.get("upto") == "P1":
                    break

                with ExitStack() as es:
                    vh = TPool([sb(es, f"vh{l}_{i}", [64, 16, 520], BF16) for i in range(2)], "vh")
                    for g, (_, d) in enumerate(GROUPS):
                        L = T // d
                        for (kk, c0) in (("AKF", 64), ("AKL", L)):
                            sec = xs((kk, g)).rearrange("r c -> (r c)").rearrange(
                                "(f r c) -> f r c", r=d, c=64)
                            for fb in range(4):
                                P.dma("sp", sec[fb * 128:(fb + 1) * 128], akT[g][fb * 128:(fb + 1) * 128, :, c0:c0 + 64])
                        for (kk, c0) in (("AVF", 64), ("AVL", L)):
                            nel = d * 64 * 520
                            sec = xs((kk, g)).rearrange("r c -> (r c)")[0:nel].rearrange(
                                "(r p f) -> r p f", p=64, f=520)
                            P.dma("sp", sec, avp[g][:, c0:c0 + 64, :])
                    P.barrier()
                    for u in range(len(XUNITS)):
                        P.collective((lambda a, b: (lambda e: e.collective_compute(
                            "AllGather", ALU.bypass, replica_groups=[[0, 1], [2, 3], [4, 5], [6, 7]],
                            ins=[a], outs=[b])))(xsrc_u[u], xdst_u[u]))
                    P.barrier()
                    for g, (_, d) in enumerate(GROUPS):
                        L = T // d
                        for (kk, rb, c0) in (("AKL", 0, 0), ("AKF", 1, 64 + L)):
                            sec = xd((kk, g), rb).rearrange("r c -> (r c)").rearrange(
                                "(f r c) -> f r c", r=d, c=64)
                            for fb in range(4):
                                P.dma("sp", akT[g][fb * 128:(fb + 1) * 128, :, c0:c0 + 64], sec[fb * 128:(fb + 1) * 128])
                        for (kk, rb, c0, mc) in (("AVL", 0, 0, 0), ("AVF", 1, 64 + L, 1)):
                            nel = d * 64 * 520
                            sec = xd((kk, g), rb).rearrange("r c -> (r c)")[0:nel].rearrange(
                                "(r p f) -> p r f", p=64, f=520)
                            t_, kt = vh.next()
                            P.dma("sp", t_[:, 0:d, :], sec, writes=[kt])
                            P.ts("dve", t_[:, 0:d, :], t_[:, 0:d, :], msk[0:64, mc:mc + 1], None, ALU.mult,
                                 reads=[kt, "msk"], writes=[kt])
                            P.dma("sp", avp[g][:, c0:c0 + 64, :].rearrange("r p f -> p r f"), t_[:, 0:d, :], reads=[kt])
                    P.barrier()
                    P.flush()

                if debug and debug.get("upto") == "X":
                    break

                with ExitStack() as es:
                    B8 = sb(es, f"B8{l}", [128, 24, 256], BF16)
                    acc = sb(es, f"acc{l}", [65, 4, T], F32)
                    vaug = sb(es, f"vaug{l}", [128, 48, 4, 65], BF16)
                    ktp = TPool([sb(es, f"kt{l}_{i}", [64, 6144], BF16) for i in range(2)], "kt")
                    qtp = TPool([sb(es, f"qt{l}_{i}", [64, T], BF16) for i in range(2)], "qt")
                    ptp = TPool([sb(es, f"pt{l}_{i}", [128, 256], BF16) for i in range(5)], "pt")
                    rdp = TPool([sb(es, f"rd{l}_{i}", [65, 512], F32) for i in range(2)], "rd")
                    nmp = TPool([sb(es, f"nm{l}_{i}", [64, 512], F32) for i in range(2)], "nm")
                    szp = TPool([sb(es, f"sz{l}_{i}", [64, 512], BF16) for i in range(2)], "sz")
                    ysp = TPool([sb(es, f"ys{l}_{i}", [64, 512], BF16) for i in range(2)], "ys")
                    psS = pspool([0, 1, 2, 3], "ps")
                    psO = pspool([4, 5, 6], "ps")
                    psN = pspool([7], "ps")
                    P.dma("sp", B8[:, :, :], Bdram.rearrange("p (g c) -> p g c", c=256), writes=["E"])
                    for hh in range(2):
                        for g, (_, d) in enumerate(GROUPS):
                            L = T // d
                            nch = L // 128 + 1
                            vsrc = avp[g].rearrange("r (m p) (h c) -> p (r m) h c", p=128, c=65)
                            for c0 in range(0, d * nch, 12):
                                c1 = min(d * nch, c0 + 12)
                                P.dma("sp", vaug[:, c0:c1, :, :], vsrc[:, c0:c1, hh * 4:(hh + 1) * 4, :],
                                      writes=["vaug"])
                            for hl in range(4):
                                h = hh * 4 + hl
                                kt, kk = ktp.next()
                                P.dma("sp", kt[:, 0:d * (L + 128)],
                                      akT[g][h * 64:(h + 1) * 64, :, :].rearrange("p r c -> p (r c)"), writes=[kk])
                                qt, kq = qtp.next()
                                P.dma("sp", qt[:, :], aqT[g, h * 64:(h + 1) * 64, :], writes=[kq])
                                blocks = [(r, n) for r in range(d) for n in range(L // 128)]
                                st = {}

                                def stageA(i):
                                    r, n = blocks[i]
                                    ps, kp = psS.next()
                                    P.mm(ps[:, 0:256], identb[:, :], B8[:, g * 8 + h, :], start=True, stop=False,
                                         reads=["E", "identb"], writes=[kp])
                                    for c in range(2):
                                        k0 = r * (L + 128) + 128 * (n + c)
                                        P.mm(ps[:, c * 128:(c + 1) * 128], kt[:, k0:k0 + 128],
                                             qt[:, r * L + 128 * n:r * L + 128 * (n + 1)], start=False, stop=True,
                                             reads=[kk, kq], writes=[kp])
                                    pt, kpt = ptp.next()
                                    P.act(pt[:, :], ps[:, 0:256], AF.Exp, scale=0.125, reads=[kp], writes=[kpt])
                                    st[i] = (pt, kpt)

                                def stageB(i):
                                    r, n = blocks[i]
                                    pt, kpt = st.pop(i)
                                    po, ko = psO.next()
                                    for c in range(2):
                                        ch = r * nch + n + c
                                        P.mm(po[0:65, 0:128], vaug[:, ch, hl, :], pt[:, c * 128:(c + 1) * 128],
                                             start=(c == 0), stop=(c == 1), reads=[kpt, "vaug"], writes=[ko])
                                    t0 = r + d * 128 * n
                                    av_ = acc[:, hl, t0:t0 + d * 127 + 1:d]
                                    if g == 0:
                                        P.copy("act", av_, po[0:65, 0:128], reads=[ko], writes=[("acc", hl)])
                                    else:
                                        P.tt("dve", av_, av_, po[0:65, 0:128], ALU.add, reads=[ko, ("acc", hl)],
                                             writes=[("acc", hl)])

                                nb = len(blocks)
                                LK = 3
                                for i in range(nb + LK):
                                    if i < nb:
                                        stageA(i)
                                    if i >= LK:
                                        stageB(i - LK)
                        for hl in range(4):
                            h = hh * 4 + hl
                            for tb in range(8):
                                tsl = slice(tb * 512, (tb + 1) * 512)
                                sz, ksz = szp.next()
                                P.dma("sp", sz[:, :], sazT[h * 64:(h + 1) * 64, tsl], writes=[ksz])
                                rd, krd = rdp.next()
                                P.recip(rd[64:65, :], acc[64:65, hl, tsl], reads=[("acc", hl)], writes=[krd])
                                pb_, kpb = psN.next()
                                P.mm(pb_[0:64, :], ones_f[64:65, 0:64], rd[64:65, :], reads=[krd, "ones"], writes=[kpb])
                                nm, knm = nmp.next()
                                P.tt("dve", nm[:, :], acc[0:64, hl, tsl], pb_[0:64, :], ALU.mult,
                                     reads=[("acc", hl), kpb], writes=[knm])
                                ys, kys = ysp.next()
                                P.tt("pool", ys[:, :], nm[:, :], sz[:, :], ALU.mult, reads=[knm, ksz], writes=[kys])
                                P.dma("pool", yaT[h * 64:(h + 1) * 64, tsl], ys[:, :], reads=[kys])
                    P.barrier()
                    P.flush()

                if debug and debug.get("upto") == "P2":
                    break

                with ExitStack() as es:
                    kTd = sb(es, f"kTd{l}", [128, 2, S], BF16)
                    vb = sb(es, f"vb{l}", [128, 64, 130], BF16)
                    qT = sb(es, f"qT{l}", [128, 4, T], BF16)
                    ptp = TPool([sb(es, f"pB{l}_{i}", [128, 1024], BF16) for i in range(3)], "pB")
                    evp = TPool([sb(es, f"evB{l}_{i}", [65, 512], F32) for i in range(3)], "evB")
                    rdp = TPool([sb(es, f"rdB{l}_{i}", [65, 512], F32) for i in range(2)], "rdB")
                    nm2 = TPool([sb(es, f"nmC{l}_{i}", [64, 512], F32) for i in range(2)], "nmC")
                    szp = TPool([sb(es, f"szB{l}_{i}", [64, 512], BF16) for i in range(3)], "szB")
                    ysp = TPool([sb(es, f"ysB{l}_{i}", [64, 512], BF16) for i in range(3)], "ysB")
                    psA2 = [(PSALL[:, 0:2, :].rearrange("p a b -> p (a b)"), [("ps", 0), ("ps", 1)]),
                            (PSALL[:, 2:4, :].rearrange("p a b -> p (a b)"), [("ps", 2), ("ps", 3)])]
                    psO = pspool([4, 5, 6], "ps")
                    psN = pspool([7], "ps")
                    for rk in range(2):
                        kb_ = xd("KB", rk)
                        for kvh in range(2):
                            for half in range(2):
                                P.dma("sp", kTd[half * 64:(half + 1) * 64, kvh, rk * T:(rk + 1) * T],
                                      kb_[kvh * 64:(kvh + 1) * 64, :], writes=[("kTd", rk, kvh, half)])
                        vsec = xd("VB", rk).rearrange("r c -> (r c)").rearrange("(k p f) -> p k f", p=128, f=130)
                        for k0 in range(0, 32, 8):
                            P.dma("sp", vb[:, rk * 32 + k0:rk * 32 + k0 + 8, :], vsec[:, k0:k0 + 8, :],
                                  writes=[("vb", rk, k0)])
                    for c_ in range(4):
                        P.dma("sp", qT[:, c_, :], qbT[c_ * 128:(c_ + 1) * 128, :], writes=[("qT", c_)])
                    steps = [(qb, hp, kc) for qb in range(8) for hp in range(4) for kc in range(64)]
                    st = {}
                    acc_o = {}

                    def stageA(i):
                        qb, hp, kc = steps[i]
                        kvh = hp // 2
                        rk = kc // 32
                        psa, keys = psA2[i % 2]
                        for hh_ in range(2):
                            pr = hh_ * 64
                            P.mm(psa[:, hh_ * 512:(hh_ + 1) * 512], kTd[pr:pr + 64, kvh, kc * 128:(kc + 1) * 128],
                                 qT[pr:pr + 64, hp, qb * 512:(qb + 1) * 512],
                                 reads=[("kTd", rk, kvh, hh_), ("qT", hp)], writes=keys)
                        pt, kpt = ptp.next()
                        P.act(pt[:, :], psa, AF.Exp, scale=0.125, reads=keys, writes=[kpt])
                        st[i] = (pt, kpt)

                    def stageB(i):
                        qb, hp, kc = steps[i]
                        kvh = hp // 2
                        pt, kpt = st.pop(i)
                        if kc == 0:
                            acc_o[(qb, hp)] = [psO.next(), psO.next()]
                        for hh_ in range(2):
                            po, ko = acc_o[(qb, hp)][hh_]
                            P.mm(po[0:65, :], vb[:, kc, kvh * 65:(kvh + 1) * 65], pt[:, hh_ * 512:(hh_ + 1) * 512],
                                 start=(kc == 0), stop=(kc == 63),
                                 reads=[kpt, ("vb", kc // 32, ((kc % 32) // 8) * 8)], writes=[ko])
                        if kc == 63:
                            tsl = slice(qb * 512, (qb + 1) * 512)
                            for hh_ in range(2):
                                h = 2 * hp + hh_
                                po, ko = acc_o[(qb, hp)][hh_]
                                ev, kev = evp.next()
                                P.copy("dve", ev[:, :], po[0:65, :], reads=[ko], writes=[kev])
                                sz, ksz = szp.next()
                                P.dma("sp", sz[:, :], sbzT[h * 64:(h + 1) * 64, tsl], writes=[ksz])
                                rd, krd = rdp.next()
                                P.recip(rd[64:65, :], ev[64:65, :], reads=[kev], writes=[krd])
                                pb_, kpb = psN.next()
                                P.mm(pb_[0:64, :], ones_f[64:65, 0:64], rd[64:65, :], reads=[krd, "ones"], writes=[kpb])
                                n2, kn2 = nm2.next()
                                P.tt("dve", n2[:, :], ev[0:64, :], pb_[0:64, :], ALU.mult, reads=[kev, kpb], writes=[kn2])
                                ys, kys = ysp.next()
                                P.tt("pool", ys[:, :], n2[:, :], sz[:, :], ALU.mult, reads=[kn2, ksz], writes=[kys])
                                P.dma("pool", ybT[h * 64:(h + 1) * 64, tsl], ys[:, :], reads=[kys])

                    LOOK = 1
                    ns = len(steps)
                    for i in range(ns + LOOK):
                        if i < ns:
                            stageA(i)
                        if i >= LOOK:
                            stageB(i - LOOK)
                    P.barrier()
                    P.flush()

                if debug and debug.get("upto") == "P3":
                    break

                with ExitStack() as es:
                    wpa = sb(es, f"wpa{l}", [128, 4, D], BF16)
                    wpb = sb(es, f"wpb{l}", [128, 4, D], BF16)
                    wo = sb(es, f"wo{l}", [128, 8, D], BF16)
                    wf = TPool([sb(es, f"wf{l}_{i}", [128, 8, 256], F32) for i in range(2)], "wf")
                    for (wdst, wsrc, nk_) in ((wpa, w_pa_in, 4), (wpb, w_pb_in, 4), (wo, w_o_in, 8)):
                        for nh in range(4):
                            wt, kw = wf.next()
                            P.dma("sp", wt[:, 0:nk_, :],
                                  wsrc[l, :, nh * 256:(nh + 1) * 256].rearrange("(k p) n -> p k n", p=128), writes=[kw])
                            P.copy("pool", wdst[:, :, nh * 256:(nh + 1) * 256], wt[:, 0:nk_, :], reads=[kw],
                                   writes=["wP4"])
                    yap = TPool([sb(es, f"ya{l}_{i}", [128, 4, 512], BF16) for i in range(2)], "ya")
                    ybp = TPool([sb(es, f"yb{l}_{i}", [128, 4, 512], BF16) for i in range(2)], "yb")
                    gp = TPool([sb(es, f"gg{l}_{i}", [128, 16, 512], BF16) for i in range(2)], "gg")
                    mtp = TPool([sb(es, f"mT{l}_{i}", [128, 8, 512], BF16) for i in range(2)], "mT")
                    t1p = TPool([sb(es, f"t1{l}_{i}", [128, 512], F32) for i in range(2)], "t1")
                    t2p = TPool([sb(es, f"t2{l}_{i}", [128, 512], F32) for i in range(2)], "t2")
                    xrp = TPool([sb(es, f"xr{l}_{i}", [128, D], F32) for i in range(3)], "xr")
                    hp = TPool([sb(es, f"hh{l}_{i}", [128, D], F32) for i in range(2)], "hh")
                    op_ = TPool([sb(es, f"oo{l}_{i}", [128, D], F32) for i in range(2)], "oo")
                    stp = TPool([sb(es, f"bst{l}_{i}", [128, 16], F32) for i in range(2)], "bst")
                    psA = pspool([0, 1, 2, 3], "ps")
                    psB = pspool([4, 5, 6, 7], "ps")
                    for tb in range(8):
                        tsl = slice(tb * 512, (tb + 1) * 512)
                        ya, kya = yap.next()
                        yb, kyb = ybp.next()
                        gg, kgg = gp.next()
                        P.dma("sp", ya[:, :, :], yaT[:, tsl].rearrange("(h p) t -> p h t", p=128), writes=[kya])
                        P.dma("sp", yb[:, :, :], ybT[:, tsl].rearrange("(h p) t -> p h t", p=128), writes=[kyb])
                        P.dma("sp", gg[:, :, :], gT[:, tsl].rearrange("(j p) t -> p j t", p=128), writes=[kgg])
                        mT, kmT = mtp.next()
                        for m in range(8):
                            pa, kpa = psA.next()
                            for h in range(4):
                                P.mm(pa[:, :], wpa[:, h, m * 128:(m + 1) * 128], ya[:, h, :], start=(h == 0),
                                     stop=(h == 3), reads=["wP4", kya], writes=[kpa])
                            pb_, kpb = psA.next()
                            for h in range(4):
                                P.mm(pb_[:, :], wpb[:, h, m * 128:(m + 1) * 128], yb[:, h, :], start=(h == 0),
                                     stop=(h == 3), reads=["wP4", kyb], writes=[kpb])
                            t1, k1 = t1p.next()
                            P.tt("dve", t1[:, :], pa[:, :], gg[:, m, :], ALU.mult, reads=[kpa, kgg], writes=[k1])
                            t2, k2 = t2p.next()
                            P.tt("dve", t2[:, :], pb_[:, :], gg[:, 8 + m, :], ALU.mult, reads=[kpb, kgg], writes=[k2])
                            P.tt("pool", mT[:, m, :], t1[:, :], t2[:, :], ALU.add, reads=[k1, k2], writes=[(kmT, m)])
                        for tt_ in range(4):
                            r0 = tb * 512 + tt_ * 128
                            xr, kxr = xrp.next()
                            P.dma("sp", xr[:, :], xsrc_l[r0:r0 + 128, :], writes=[kxr])
                            hb, khb = hp.next()
                            for nh in range(2):
                                po, ko = psB.next()
                                for kc in range(8):
                                    P.mm(po[:, :], mT[:, kc, tt_ * 128:(tt_ + 1) * 128], wo[:, kc, nh * 512:(nh + 1) * 512],
                                         start=(kc == 0), stop=(kc == 7), reads=["wP4"] + [(kmT, m) for m in range(8)],
                                         writes=[ko])
                                P.tt("dve", hb[:, nh * 512:(nh + 1) * 512], po[:, :], gate_b[:, nh * 512:(nh + 1) * 512],
                                     ALU.mult, reads=[ko], writes=[(khb, nh)])
                            P.stt(hb[:, :], xr[:, :], ALPHA, hb[:, :], ALU.mult, ALU.add,
                                  reads=[kxr, (khb, 0), (khb, 1)], writes=[(khb, 0), (khb, 1)])
                            bs, kbs = stp.next()
                            for nh in range(2):
                                P.op("dve", (lambda o, i_: (lambda e: e.bn_stats(o, i_)))(
                                    bs[:, nh * 6:(nh + 1) * 6], hb[:, nh * 512:(nh + 1) * 512]),
                                    reads=[(khb, 0), (khb, 1)], writes=[(kbs, nh)])
                            P.op("dve", (lambda o, i_: (lambda e: e.bn_aggr(o, i_)))(bs[:, 12:14], bs[:, 0:12]),
                                 reads=[(kbs, 0), (kbs, 1)], writes=[(kbs, 2)])
                            P.act(bs[:, 15:16], bs[:, 13:14], AF.Sqrt, bias=epsc[:, 1:2], reads=[(kbs, 2)],
                                  writes=[(kbs, 4)])
                            P.recip(bs[:, 14:15], bs[:, 15:16], reads=[(kbs, 4)], writes=[(kbs, 3)])
                            ob, kob = op_.next()
                            P.ts("dve", ob[:, :], hb[:, :], bs[:, 12:13], bs[:, 14:15], ALU.subtract, ALU.mult,
                                 reads=[(khb, 0), (khb, 1), (kbs, 2), (kbs, 3)], writes=[kob])
                            P.tt("pool", ob[:, :], ob[:, :], lng_b[:, :], ALU.mult, reads=[kob], writes=[kob])
                            P.tt("pool", ob[:, :], ob[:, :], lnb_b[:, :], ALU.add, reads=[kob], writes=[kob])
                            P.dma("pool", ydst_l[r0:r0 + 128, :], ob[:, :], reads=[kob])
                    P.barrier()
                    P.flush()
        P.barrier()
        P.flush()
    return nc, dbg_names


def _t5_bucket_np(rel):
    half, max_exact = 16, 8
    ret = np.where(rel > 0, half, 0)
    a = np.abs(rel)
    af = np.maximum(a, 1).astype(np.float32)
    large = max_exact + (np.log(af / np.float32(max_exact)) / np.float32(math.log(1024 / max_exact))
                         * np.float32(half - max_exact)).astype(np.int32)
    large = np.minimum(large, half - 1)
    return ret + np.where(a < max_exact, a, large)


def _host_consts(rel_table):
    ident = np.eye(128, dtype=np.float32)
    bones = np.zeros((128, 128), np.float32)
    bones[:64, :64] = 1.0 / 64
    bones[64:, 64:] = 1.0 / 64
    rrot = np.zeros((128, 128), np.float32)
    for base in range(0, 128, 32):
        for e in range(16):
            rrot[base + e + 16, base + e] = -1.0
            rrot[base + e, base + e + 16] = 1.0
    cmat = np.stack([ident, bones, rrot])
    i = np.arange(128)[:, None]
    j = np.arange(128)[None, :]
    rel0 = i - 64 - j
    rel1 = i + 64 - j
    tri0 = np.concatenate([(i >= j), (i <= j)], axis=1).astype(np.float32)
    tri = np.stack([tri0, (tri0 - 1.0) * 30000.0]).astype(np.float32)
    ab = np.zeros((128, 24, 256), np.float32)
    for g, (_, d) in enumerate(GROUPS):
        for c, rel in enumerate((rel0, rel1)):
            bk = _t5_bucket_np(np.clip(rel, -64, 64) * d)
            ab[:, g * 8:(g + 1) * 8, c * 128:(c + 1) * 128] = rel_table[bk][:, :, g * 8:(g + 1) * 8].transpose(0, 2, 1)
    return cmat, tri, ab.reshape(128, 24 * 256)


def _cs_tables(half):
    t = np.arange(half * T, (half + 1) * T)
    row = (t // 64).astype(np.float32)
    col = (t % 64).astype(np.float32)
    inv = (np.float32(10000.0) ** (-np.arange(0, 32, 2, dtype=np.float32) / np.float32(32))).astype(np.float32)
    ar = (row[:, None] * inv[None]).astype(np.float32)
    ac = (col[:, None] * inv[None]).astype(np.float32)
    ang = np.zeros((128, T), np.float32)
    for p in range(128):
        e = p % 64
        ang[p] = ar[:, e % 16] if e < 32 else ac[:, (e - 32) % 16]
    return np.stack([np.cos(ang), np.sin(ang)]).astype(np.float32)


def _in_maps(inputs):
    f = lambda a: np.ascontiguousarray(np.asarray(a, dtype=np.float32))
    x = f(inputs["x"]); c = f(inputs["c"])
    cmat, tri, ab = _host_consts(f(inputs["rel_table"]))
    bgT = np.ascontiguousarray(f(inputs["b_gate"]).reshape(DEPTH, 16, 128).transpose(0, 2, 1))
    qg = f(inputs["q_norm_g"]); kg = f(inputs["k_norm_g"])
    qkg = np.ascontiguousarray(np.stack([np.tile(qg, (1, 2)), np.tile(kg, (1, 2))], axis=-1))
    shared = {
        "ln_g": f(inputs["ln_g"]).reshape(DEPTH, 1, D), "ln_b": f(inputs["ln_b"]).reshape(DEPTH, 1, D),
        "w_ada": f(inputs["w_ada"]), "b_ada": f(inputs["b_ada"]).reshape(DEPTH, 1, 3 * D),
        "w_in": f(inputs["w_in"]), "bgT": bgT, "qkg": qkg, "w_pa": f(inputs["w_pa"]), "w_pb": f(inputs["w_pb"]),
        "w_o": f(inputs["w_o"]), "cmat": cmat, "abias": ab, "tri": tri,
    }
    cs = [_cs_tables(0), _cs_tables(1)]
    maps = []
    for core in range(8):
        b, hf = core // 2, core % 2
        m = dict(shared)
        m["x"] = np.ascontiguousarray(x[b, hf * T:(hf + 1) * T])
        m["cT"] = np.ascontiguousarray(c[b].reshape(8, 128).T)
        m["cstab"] = cs[hf]
        mk = np.ones((128, 2), np.float32)
        mk[:, 0] = 0.0 if hf == 0 else 1.0
        mk[:, 1] = 1.0 if hf == 0 else 0.0
        m["msk"] = mk
        maps.append(m)
    return maps


def kernel(**inputs):
    nc, _ = _build()
    maps = _in_maps(inputs)
    res = run_bass_kernel_spmd(nc, maps, core_ids=list(range(8)))
    out = np.empty((4, S, D), np.float32)
    for core in range(8):
        b, hf = core // 2, core % 2
        out[b, hf * T:(hf + 1) * T] = res.results[core]["y"]
    return out
```

```python
import math
from contextlib import ExitStack

import numpy as np
import concourse.bass as bass
import concourse.mybir as mybir
from concourse.bass_utils import run_bass_kernel_spmd

F32 = mybir.dt.float32
BF16 = mybir.dt.bfloat16
AF = mybir.ActivationFunctionType
ALU = mybir.AluOpType

D = 1024
T = 4096
S = 8192
DEPTH = 2
GROUPS = ((128, 1), (512, 4), (2048, 16))
ALPHA = float((2 * DEPTH) ** 0.25)
LN_EPS = 1e-5
QK_EPS = 1e-6
C_AQ, C_AK, C_AV, C_AZ, C_BQ, C_BK, C_BV, C_BZ, C_GL = 0, 1536, 3072, 4608, 5120, 5632, 5760, 5888, 6400
XCOLS = 4096


def _xlayout():
    units = []
    sec = {}

    def add_unit(items):
        u = len(units)
        r = 0
        for k, n in items:
            sec[k] = (u, r, n)
            r += n
        units.append(r)

    add_unit([("KB", 128)])
    add_unit([("VB", 130)])
    for g, (_, d) in enumerate(GROUPS):
        nk = 8 * d
        nv = -(-(d * 64 * 520) // XCOLS)
        items = [(("AKF", g), nk), (("AKL", g), nk), (("AVF", g), nv), (("AVL", g), nv)]
        if nk + nk + nv + nv <= 130:
            add_unit(items)
        else:
            for it in items:
                add_unit([it])
    return units, sec


XUNITS, XSEC = _xlayout()


class Prog:
    CE = ("pe", "act", "dve", "pool")
    ALLE = ("pe", "act", "dve", "pool", "sp")
    NDS = 8

    def __init__(self, nc, es):
        self.nc = nc
        self.sem = {}
        for e in self.CE:
            self.sem[("c", e)] = es.enter_context(nc.semaphore(f"c_{e}"))
        for q in ("sp", "act", "pool"):
            for i in range(self.NDS):
                self.sem[("d", q, i)] = es.enter_context(nc.semaphore(f"d_{q}{i}"))
        self.sem[("cc",)] = es.enter_context(nc.semaphore("ccsem"))
        self.cnt = {k: 0 for k in self.sem}
        self.dnext = {q: 0 for q in ("sp", "act", "pool")}
        self.ops = {e: [] for e in self.ALLE}
        self.waited = {e: {} for e in self.ALLE}
        self.lastw = {}
        self.readers = {}
        self.nops = 0

    def _deps(self, reads, writes):
        deps = set()
        for k in reads:
            if k in self.lastw:
                deps.add(self.lastw[k])
        for k in writes:
            if k in self.lastw:
                deps.add(self.lastw[k])
            deps.update(self.readers.get(k, ()))
        return deps

    def _commit(self, ticket, reads, writes):
        for k in reads:
            self.readers.setdefault(k, []).append(ticket)
        for k in writes:
            self.lastw[k] = ticket
            self.readers[k] = []

    def _waits(self, eng, deps):
        w = []
        for (sk, val) in sorted(deps, key=lambda t: (str(t[0]), t[1])):
            if eng == "pe" and sk == ("c", "pe"):
                continue
            if self.waited[eng].get(sk, 0) >= val:
                continue
            self.waited[eng][sk] = val
            w.append((self.sem[sk], val))
        return w

    def op(self, eng, fn, reads=(), writes=()):
        deps = self._deps(reads, writes)
        w = self._waits(eng, deps)
        sk = ("c", eng)
        self.cnt[sk] += 1
        t = (sk, self.cnt[sk])
        self.ops[eng].append((w, fn, self.sem[sk], 1))
        self._commit(t, reads, writes)
        self.nops += 1
        return t

    def dma(self, q, out, in_, reads=(), writes=()):
        deps = self._deps(reads, writes)
        slot = self.dnext[q] % self.NDS
        self.dnext[q] += 1
        sk = ("d", q, slot)
        if self.cnt[sk] > 0:
            deps.add((sk, self.cnt[sk]))
        w = self._waits(q, deps)
        self.cnt[sk] += 16
        t = (sk, self.cnt[sk])
        self.ops[q].append((w, lambda e: e.dma_start(out=out, in_=in_), self.sem[sk], 16))
        self._commit(t, reads, writes)
        self.nops += 1
        return t

    def collective(self, fn, reads=(), writes=()):
        deps = self._deps(reads, writes)
        w = self._waits("pool", deps)
        sk = ("cc",)
        self.cnt[sk] += 1
        t = (sk, self.cnt[sk])
        self.ops["pool"].append((w, fn, self.sem[sk], 1))
        self._commit(t, reads, writes)
        return t

    def barrier(self):
        tickets = set((sk, v) for sk, v in self.cnt.items() if v > 0)
        for e in self.ALLE:
            w = self._waits(e, tickets)
            if w:
                self.ops[e].append((w, None, None, 0))
        self.lastw = {}
        self.readers = {}

    def flush(self):
        nc = self.nc
        ops = self.ops

        def replay(lst, e):
            for (w, fn, sem, inc) in lst:
                for (s, v) in w:
                    e.wait_ge(s, v)
                if fn is not None:
                    ins = fn(e)
                    ins.then_inc(sem, inc)

        with nc.Block() as block:
            @block.tensor
            def _(e):
                replay(ops["pe"], e)

            @block.scalar
            def _(e):
                replay(ops["act"], e)

            @block.vector
            def _(e):
                replay(ops["dve"], e)

            @block.gpsimd
            def _(e):
                replay(ops["pool"], e)

            @block.sync
            def _(e):
                replay(ops["sp"], e)
        self.ops = {e: [] for e in self.ALLE}

    def mm(self, out, lhsT, rhs, start=True, stop=True, reads=(), writes=()):
        return self.op("pe", lambda e: e.matmul(out, lhsT, rhs, start=start, stop=stop), reads, writes)

    def tr(self, out, in_, ident, reads=(), writes=()):
        return self.op("pe", lambda e: e.transpose(out, in_, ident), reads, writes)

    def act(self, out, in_, func, bias=None, scale=None, reads=(), writes=(), eng="act"):
        kw = {}
        if bias is not None:
            kw["bias"] = bias
        if scale is not None:
            kw["scale"] = scale
        return self.op(eng, lambda e: e.activation(out, in_, func, **kw), reads, writes)

    def tt(self, eng, out, in0, in1, op, reads=(), writes=()):
        return self.op(eng, lambda e: e.tensor_tensor(out, in0, in1, op), reads, writes)

    def ts(self, eng, out, in0, s1, s2, op0, op1=None, reads=(), writes=()):
        if op1 is None:
            return self.op(eng, lambda e: e.tensor_scalar(out, in0, s1, None, op0), reads, writes)
        return self.op(eng, lambda e: e.tensor_scalar(out, in0, s1, s2, op0, op1), reads, writes)

    def stt(self, out, in0, scalar, in1, op0, op1, reads=(), writes=()):
        return self.op("dve", lambda e: e.scalar_tensor_tensor(out, in0, scalar, in1, op0, op1), reads, writes)

    def copy(self, eng, out, in_, reads=(), writes=()):
        if eng == "act":
            return self.op("act", lambda e: e.activation(out, in_, AF.Copy), reads, writes)
        return self.op(eng, lambda e: e.tensor_copy(out, in_), reads, writes)

    def memset(self, eng, ap, val, reads=(), writes=()):
        return self.op(eng, lambda e: e.memset(ap, val), reads, writes)

    def recip(self, out, in_, reads=(), writes=()):
        return self.op("dve", lambda e: e.reciprocal(out, in_), reads, writes)


class TPool:
    def __init__(self, tiles, name):
        self.tiles = tiles
        self.name = name
        self.i = 0

    def next(self):
        j = self.i % len(self.tiles)
        self.i += 1
        return self.tiles[j], (self.name, j)


def _build(debug=None):
    nc = bass.Bass("TRN2", target_bir_lowering=False)
    dbg_kind = "ExternalOutput" if debug else "Internal"

    def din(name, shape, dt=F32):
        return nc.dram_tensor(name, list(shape), dt, kind="ExternalInput").ap()

    def dscr(name, shape, dt=BF16, dbg=False):
        isdbg = bool(debug) and dbg and name in debug.get("outs", ())
        return nc.dram_tensor(name, list(shape), dt, kind=("ExternalOutput" if isdbg else "Internal")).ap()

    x_in = din("x", [T, D])
    cT_in = din("cT", [128, 8])
    ln_g_in = din("ln_g", [DEPTH, 1, D])
    ln_b_in = din("ln_b", [DEPTH, 1, D])
    w_ada_in = din("w_ada", [DEPTH, D, 3 * D])
    b_ada_in = din("b_ada", [DEPTH, 1, 3 * D])
    w_in_in = din("w_in", [DEPTH, D, 8448])
    bgT_in = din("bgT", [DEPTH, 128, 16])
    qkg_in = din("qkg", [DEPTH, 128, 2])
    w_pa_in = din("w_pa", [DEPTH, 512, D])
    w_pb_in = din("w_pb", [DEPTH, 512, D])
    w_o_in = din("w_o", [DEPTH, D, D])
    cmat_in = din("cmat", [3, 128, 128])
    cstab_in = din("cstab", [2, 128, T])
    abias_in = din("abias", [128, 24 * 256])
    tri_in = din("tri", [2, 128, 256])
    msk_in = din("msk", [128, 2])
    y_out = nc.dram_tensor("y", [T, D], F32, kind="ExternalOutput").ap()

    aqT = dscr("aqT", [3, 512, T], dbg=True)
    akT = [dscr(f"akT{g}", [512, d, T // d + 128], dbg=True) for g, (_, d) in enumerate(GROUPS)]
    avp = [dscr(f"avp{g}", [d, T // d + 128, 520], dbg=True) for g, (_, d) in enumerate(GROUPS)]
    sazT = dscr("sazT", [512, T], dbg=True)
    sbzT = dscr("sbzT", [512, T], dbg=True)
    qbT = dscr("qbT", [512, T], dbg=True)
    gT = dscr("gT", [2048, T], dbg=True)
    yaT = dscr("yaT", [512, T], dbg=True)
    ybT = dscr("ybT", [512, T], dbg=True)
    x1 = dscr("x1", [T, D], F32, dbg=True)
    Edram = dscr("Edram", [128, 24 * 256], F32)
    xsrc_u = [dscr(f"xsrc{u}", [n, XCOLS]) for u, n in enumerate(XUNITS)]
    xdst_u = [dscr(f"xdst{u}", [2 * n, XCOLS]) for u, n in enumerate(XUNITS)]

    def xs(key):
        u, r0, n = XSEC[key]
        return xsrc_u[u][r0:r0 + n, :]

    def xd(key, rank):
        u, r0, n = XSEC[key]
        return xdst_u[u][rank * XUNITS[u] + r0:rank * XUNITS[u] + r0 + n, :]
    dbg_names = ["aqT", "akT0", "akT1", "akT2", "avp0", "avp1", "avp2", "sazT", "sbzT", "qbT", "gT",
                 "yaT", "ybT", "x1"]

    with ExitStack() as top:
        P = Prog(nc, top)

        def sb(es, name, shape, dt):
            return es.enter_context(nc.sbuf_tensor("s_" + name, list(shape), dt))

        PSALL = top.enter_context(nc.psum_tensor("psall", [128, 8, 512], F32))
        PS = [PSALL[:, i, :] for i in range(8)]

        class PSPool:
            def __init__(self, idx):
                self.idx = idx
                self.i = 0

            def next(self):
                b = self.idx[self.i % len(self.idx)]
                self.i += 1
                return PS[b], ("ps", b)

        def pspool(idx, name):
            return PSPool(idx)

        cmat = sb(top, "cmat", [128, 3, 128], F32)
        ones_f = sb(top, "ones_f", [128, 128], F32)
        msk = sb(top, "msk", [128, 2], F32)
        P.dma("sp", cmat[:, :, :], cmat_in.rearrange("c p n -> p c n"), writes=["cmat"])
        P.dma("sp", msk[:, :], msk_in, writes=["msk"])
        P.memset("pool", ones_f[:, :], 1.0, writes=["ones"])
        epsc = sb(top, "epsc", [128, 2], F32)
        P.memset("pool", epsc[:, 0:1], QK_EPS, writes=["epsc"])
        P.memset("pool", epsc[:, 1:2], LN_EPS, writes=["epsc"])
        ident = cmat[:, 0, :]
        bones = cmat[:, 1, :]
        rrot = cmat[:, 2, :]

        with ExitStack() as es:
            ab = sb(es, "ab", [128, 24 * 256], F32)
            tri = sb(es, "tri", [128, 2, 256], F32)
            P.dma("sp", ab[:, :], abias_in, writes=["ab"])
            P.dma("sp", tri[:, :, :], tri_in.rearrange("c p n -> p c n"), writes=["tri"])
            P.act(ab[:, :], ab[:, :], AF.Exp, reads=["ab"], writes=["ab"])
            for gh in range(24):
                sl = slice(gh * 256, (gh + 1) * 256)
                P.tt("dve", ab[:, sl], ab[:, sl], tri[:, 0, :], ALU.mult, reads=["ab", "tri"], writes=[("ab", gh)])
            P.dma("sp", Edram, ab[:, :], reads=[("ab", gh) for gh in range(24)])
            P.barrier()
            P.flush()

        for l in range(DEPTH):
            if debug and l > debug.get("layers", DEPTH) - 1:
                break
            xsrc_l = x_in if l == 0 else x1
            ydst_l = x1 if l < DEPTH - 1 else y_out
            with ExitStack() as lay:
                shiftT = sb(lay, f"shiftT{l}", [128, 8], F32)
                sc1T = sb(lay, f"sc1T{l}", [128, 8], F32)
                gate_b = sb(lay, f"gate_b{l}", [128, D], F32)
                lng_b = sb(lay, f"lng_b{l}", [128, D], F32)
                lnb_b = sb(lay, f"lnb_b{l}", [128, D], F32)
                bgT = sb(lay, f"bgT{l}", [128, 16], F32)
                qkg = sb(lay, f"qkg{l}", [128, 2], F32)

                with ExitStack() as es:
                    cT = sb(es, f"cT{l}", [128, 8], F32)
                    silc = sb(es, f"silc{l}", [128, 8], F32)
                    rows = sb(es, f"rows{l}", [1, 5 * D], F32)
                    brow = sb(es, f"brow{l}", [1, 3 * D], F32)
                    wst = [sb(es, f"wada{l}_{i}", [128, 8, 512], F32) for i in range(2)]
                    wp = TPool(wst, "wada")
                    psp = pspool([0, 1], "ps")
                    P.dma("sp", cT[:, :], cT_in, writes=["cT"])
                    P.dma("sp", brow[:, :], b_ada_in[l], writes=["brow"])
                    P.dma("sp", rows[:, 3 * D:4 * D], ln_g_in[l], writes=["rows_ln"])
                    P.dma("sp", rows[:, 4 * D:5 * D], ln_b_in[l], writes=["rows_ln"])
                    P.dma("sp", bgT[:, :], bgT_in[l], writes=["bgT"])
                    P.dma("sp", qkg[:, :], qkg_in[l], writes=["qkg"])
                    P.act(silc[:, :], cT[:, :], AF.Silu, reads=["cT"], writes=["silc"])
                    for n in range(6):
                        wt, kw = wp.next()
                        P.dma("sp", wt[:, :, :],
                              w_ada_in[l, :, n * 512:(n + 1) * 512].rearrange("(kc p) n -> p kc n", p=128),
                              writes=[kw])
                        ps, kp = psp.next()
                        for kc in range(8):
                            P.mm(ps[0:1, :], silc[:, kc:kc + 1], wt[:, kc, :], start=(kc == 0), stop=(kc == 7),
                                 reads=[kw, "silc"], writes=[kp])
                        P.tt("dve", rows[0:1, n * 512:(n + 1) * 512], ps[0:1, :], brow[0:1, n * 512:(n + 1) * 512],
                             ALU.add, reads=[kp, "brow"], writes=[("rows", n)])
                    ps, kp = psp.next()
                    for j in range(16):
                        P.mm(ps[:, j:j + 1], rows[0:1, j * 128:(j + 1) * 128], ones_f[0:1, 0:1],
                             reads=[("rows", j // 4), "ones"], writes=[kp])
                    P.copy("dve", shiftT[:, :], ps[:, 0:8], reads=[kp], writes=["mod"])
                    P.ts("dve", sc1T[:, :], ps[:, 8:16], 1.0, None, ALU.add, reads=[kp], writes=["mod2"])
                    for (dst, c0, rk) in ((gate_b, 2 * D, [("rows", 4), ("rows", 5)]), (lng_b, 3 * D, ["rows_ln"]),
                                          (lnb_b, 4 * D, ["rows_ln"])):
                        for nh in range(2):
                            ps, kp = psp.next()
                            P.mm(ps[:, :], ones_f[0:1, :], rows[0:1, c0 + nh * 512:c0 + (nh + 1) * 512],
                                 reads=rk + ["ones"], writes=[kp])
                            P.copy("dve", dst[:, nh * 512:(nh + 1) * 512], ps[:, :], reads=[kp], writes=["bc"])
                    P.barrier()
                    P.flush()

                with ExitStack() as es:
                  if not (debug and debug.get("skipP1")):
                        uT = sb(es, f"uT{l}", [128, 8, T], BF16)
                        with ExitStack() as es2:
                            xp = TPool([sb(es2, f"xt{l}_{i}", [128, 4, D], F32) for i in range(2)], "xt")
                            psp = pspool([0, 1, 2, 3], "ps")
                            for tb in range(8):
                                xt, kx = xp.next()
                                P.dma("sp", xt[:, :, :],
                                      xsrc_l[tb * 512:(tb + 1) * 512, :].rearrange("(t p) d -> p t d", p=128), writes=[kx])
                                for kc in range(8):
                                    ps, kp = psp.next()
                                    for t in range(4):
                                        P.tr(ps[:, t * 128:(t + 1) * 128], xt[:, t, kc * 128:(kc + 1) * 128], ident,
                                             reads=[kx], writes=[kp])
                                    P.act(uT[:, kc, tb * 512:(tb + 1) * 512], ps[:, :], AF.Identity,
                                          bias=shiftT[:, kc:kc + 1], scale=sc1T[:, kc:kc + 1], reads=[kp])
                            P.barrier()
                            P.flush()

                        uTp = sb(es, f"uTp{l}", [128, 8, T], BF16)
                        wstp = TPool([sb(es, f"wst{l}_{i}", [128, 8, 256], F32) for i in range(2)], "wst")
                        wbfp = TPool([sb(es, f"wbf{l}_{i}", [128, 8, 512], BF16) for i in range(2)], "wbf")
                        stgp = TPool([sb(es, f"stg{l}_{i}", [128, 512], BF16) for i in range(4)], "stg")
                        vstp = TPool([sb(es, f"vst{l}_{i}", [128, 8, 65], BF16) for i in range(3)], "vst")
                        f32p = TPool([sb(es, f"f32t{l}_{i}", [128, 512], F32) for i in range(7)], "f32t")
                        csp = TPool([sb(es, f"cst{l}_{i}", [128, 2, 512], F32) for i in range(2)], "cst")
                        psA = pspool([0, 1, 2, 3, 4], "ps")
                        psB = pspool([5, 6, 7], "ps")
                        for vt in vstp.tiles:
                            P.memset("pool", vt[:, :, :], 1.0)
                        P.barrier()
                        evac_rr = [0]

                        def evac_copy(out, in_, reads, writes):
                            evac_rr[0] += 1
                            if evac_rr[0] % 2:
                                return P.copy("act", out, in_, reads=reads, writes=writes)
                            return P.copy("dve", out, in_, reads=reads, writes=writes)

                        def load_strip(c0, W):
                            wb, kb = wbfp.next()
                            for w0 in range(0, W, 256):
                                wt, kw = wstp.next()
                                P.dma("sp", wt[:, :, :],
                                      w_in_in[l, :, c0 + w0:c0 + w0 + 256].rearrange("(kc p) n -> p kc n", p=128),
                                      writes=[kw])
                                P.copy("pool", wb[:, :, w0:w0 + 256], wt[:, :, :], reads=[kw], writes=[(kb, w0)])
                            return wb, [(kb, w0) for w0 in range(0, W, 256)]

                        def permute_uT(d):
                            engs = ("dve", "pool", "act")
                            for kc in range(8):
                                P.copy(engs[kc % 3], uTp[:, kc, :].rearrange("k (r p) -> k r p", r=d),
                                       uT[:, kc, :].rearrange("k (p r) -> k r p", r=d), writes=[("uTp", kc)])

                        def rhs_perm(g, kc, pb):
                            src = uT if g == 0 else uTp
                            return src[:, kc, pb * 512:(pb + 1) * 512]

                        def lhs_perm(g, kc, it):
                            src = uT if g == 0 else uTp
                            return src[:, kc, it * 128:(it + 1) * 128]

                        def fm_chunk(wb, kb, col0, rhs_fn, extra=()):
                            ps, kp = psA.next()
                            for kc in range(8):
                                rd_ = list(kb) + [e_ for e_ in extra if e_[1] == kc]
                                P.mm(ps[:, :], wb[:, kc, col0:col0 + 128], rhs_fn(kc), start=(kc == 0), stop=(kc == 7),
                                     reads=rd_, writes=[kp])
                            return ps, kp

                        def do_qk(which, g, wb, kb):
                            d = GROUPS[g][1]
                            ex_ = [("uTp", kc) for kc in range(8)] if g > 0 else []
                            for fcl in range(4):
                                for pb in range(8):
                                    ps, kp = fm_chunk(wb, kb, fcl * 128, lambda kc: rhs_perm(g, kc, pb), ex_)
                                    stg, ks = stgp.next()
                                    evac_copy(stg[:, :], ps[:, :], [kp], [ks])
                                    rows_ = slice(fcl * 128, (fcl + 1) * 128)
                                    if which == "q":
                                        P.dma("sp", aqT[g, rows_, pb * 512:(pb + 1) * 512], stg[:, :], reads=[ks])
                                    elif g == 0:
                                        P.dma("sp", akT[0][rows_, 0, 64 + pb * 512:64 + (pb + 1) * 512], stg[:, :],
                                              reads=[ks])
                                    elif g == 1:
                                        r, p0 = pb // 2, (pb % 2) * 512
                                        P.dma("sp", akT[1][rows_, r, 64 + p0:64 + p0 + 512], stg[:, :], reads=[ks])
                                    else:
                                        P.dma("sp", akT[2][rows_, 2 * pb:2 * pb + 2, 64:64 + 256],
                                              stg[:, :].rearrange("p (r c) -> p r c", r=2), reads=[ks])

                        def do_v(g, wb, kb):
                            d = GROUPS[g][1]
                            per = (T // d) // 128
                            for it in range(32):
                                ps, kp = psA.next()
                                for kc in range(8):
                                    rd_ = list(kb) + ([("uTp", kc)] if g > 0 else [])
                                    P.mm(ps[:, :], lhs_perm(g, kc, it), wb[:, kc, 0:512], start=(kc == 0), stop=(kc == 7),
                                         reads=rd_, writes=[kp])
                                vt, kv = vstp.next()
                                evac_copy(vt[:, :, 0:64], ps[:, :].rearrange("p (h e) -> p h e", h=8), [kp], [kv])
                                r, p0 = it // per, (it % per) * 128
                                P.dma("sp", avp[g][r, 64 + p0:64 + p0 + 128, :],
                                      vt[:, :, :].rearrange("p h c -> p (h c)"), reads=[kv])

                        def do_gate(dst, s_, wb, kb):
                            for fcl in range(4):
                                for tb in range(8):
                                    ps, kp = fm_chunk(wb, kb, fcl * 128, lambda kc: uT[:, kc, tb * 512:(tb + 1) * 512])
                                    stg, ks = stgp.next()
                                    j = s_ * 4 + fcl
                                    if dst is gT:
                                        P.act(stg[:, :], ps[:, :], AF.Sigmoid, bias=bgT[:, j:j + 1], reads=[kp],
                                              writes=[ks])
                                    else:
                                        P.act(stg[:, :], ps[:, :], AF.Silu, reads=[kp], writes=[ks])
                                    P.dma("sp", dst[j * 128:(j + 1) * 128, tb * 512:(tb + 1) * 512], stg[:, :],
                                          reads=[ks])

                        jobs = []
                        for g in range(3):
                            if g > 0:
                                jobs.append((None, 0, (lambda g_: (lambda wb, kb: permute_uT(GROUPS[g_][1])))(g)))
                            jobs.append((C_AQ + g * 512, 512, (lambda g_: (lambda wb, kb: do_qk("q", g_, wb, kb)))(g)))
                            jobs.append((C_AK + g * 512, 512, (lambda g_: (lambda wb, kb: do_qk("k", g_, wb, kb)))(g)))
                            jobs.append((C_AV + g * 512, 512, (lambda g_: (lambda wb, kb: do_v(g_, wb, kb)))(g)))
                        jobs.append((C_AZ, 512, lambda wb, kb: do_gate(sazT, 0, wb, kb)))
                        jobs.append((C_BZ, 512, lambda wb, kb: do_gate(sbzT, 0, wb, kb)))
                        for s_ in range(4):
                            jobs.append((C_GL + s_ * 512, 512, (lambda s2: (lambda wb, kb: do_gate(gT, s2, wb, kb)))(s_)))
                        strips = [j for j in jobs if j[0] is not None]
                        loaded = {}
                        nxt = [0]

                        def prefetch():
                            if nxt[0] < len(strips):
                                c0_, W_, _ = strips[nxt[0]]
                                loaded[nxt[0]] = load_strip(c0_, W_)
                                nxt[0] += 1

                        prefetch()
                        si = 0
                        for (c0_, W_, fn_) in jobs:
                            if c0_ is None:
                                fn_(None, None)
                                continue
                            wb, kb = loaded.pop(si)
                            si += 1
                            prefetch()
                            fn_(wb, kb)
                        xs_kb = xs("KB")
                        xs_vb = xs("VB").rearrange("r c -> (r c)").rearrange("(t f) -> t f", f=130)
                        wbq, kbq = load_strip(C_BQ, 512)
                        wbk, kbk = load_strip(C_BK, 256)
                        for fcl in range(5):
                            wb, kb, col0, gcol = (wbq, kbq, fcl * 128, 0) if fcl < 4 else (wbk, kbk, 0, 1)
                            for tb in range(8):
                                tsl = slice(tb * 512, (tb + 1) * 512)
                                cst, kc_ = csp.next()
                                P.dma("sp", cst[:, :, :], cstab_in[:, :, tsl].rearrange("c p t -> p c t"), writes=[kc_])
                                ps, kp = fm_chunk(wb, kb, col0, lambda kc: uT[:, kc, tsl])
                                sq, k1 = f32p.next()
                                P.act(sq[:, :], ps[:, :], AF.Square, reads=[kp], writes=[k1])
                                ps2, kp2 = psB.next()
                                P.mm(ps2[:, :], bones, sq[:, :], reads=[k1], writes=[kp2])
                                srt, k2a = f32p.next()
                                P.act(srt[:, :], ps2[:, :], AF.Sqrt, bias=epsc[:, 0:1], reads=[kp2], writes=[k2a])
                                rstd, k2 = f32p.next()
                                P.recip(rstd[:, :], srt[:, :], reads=[k2a], writes=[k2])
                                xn, k3 = f32p.next()
                                P.stt(xn[:, :], ps[:, :], qkg[:, gcol:gcol + 1], rstd[:, :], ALU.mult, ALU.mult,
                                      reads=[kp, k2], writes=[k3])
                                ps3, kp3 = psB.next()
                                P.mm(ps3[:, :], rrot, xn[:, :], reads=[k3], writes=[kp3])
                                ta, k4 = f32p.next()
                                P.tt("pool", ta[:, :], xn[:, :], cst[:, 0, :], ALU.mult, reads=[k3, kc_], writes=[k4])
                                tb_, k5 = f32p.next()
                                P.tt("dve", tb_[:, :], ps3[:, :], cst[:, 1, :], ALU.mult, reads=[kp3, kc_], writes=[k5])
                                stg, ks = stgp.next()
                                P.tt("pool", stg[:, :], ta[:, :], tb_[:, :], ALU.add, reads=[k4, k5], writes=[ks])
                                if fcl < 4:
                                    P.dma("sp", qbT[fcl * 128:(fcl + 1) * 128, tsl], stg[:, :], reads=[ks])
                                else:
                                    P.dma("sp", xs_kb[:, tsl], stg[:, :], reads=[ks])
                        for it in range(32):
                            ps, kp = psA.next()
                            for kc in range(8):
                                P.mm(ps[:, 0:128], uT[:, kc, it * 128:(it + 1) * 128], wbk[:, kc, 128:256],
                                     start=(kc == 0), stop=(kc == 7), reads=kbk, writes=[kp])
                            vt, kv = vstp.next()
                            evac_copy(vt[:, 0:2, 0:64], ps[:, 0:128].rearrange("p (h e) -> p h e", h=2), [kp], [kv])
                            P.dma("sp", xs_vb[it * 128:(it + 1) * 128, :],
                                  vt[:, 0:2, :].rearrange("p h c -> p (h c)"), reads=[kv])
                        P.barrier()
                        P.flush()

                if debug and debug.get("upto") == "P1":
                    break

                with ExitStack() as es:
                    vh = TPool([sb(es, f"vh{l}_{i}", [64, 16, 520], BF16) for i in range(2)], "vh")
                    for g, (_, d) in enumerate(GROUPS):
                        L = T // d
                        for (kk, c0) in (("AKF", 64), ("AKL", L)):
                            sec = xs((kk, g)).rearrange("r c -> (r c)").rearrange(
                                "(f r c) -> f r c", r=d, c=64)
                            for fb in range(4):
                                P.dma("sp", sec[fb * 128:(fb + 1) * 128], akT[g][fb * 128:(fb + 1) * 128, :, c0:c0 + 64])
                        for (kk, c0) in (("AVF", 64), ("AVL", L)):
                            nel = d * 64 * 520
                            sec = xs((kk, g)).rearrange("r c -> (r c)")[0:nel].rearrange(
                                "(r p f) -> r p f", p=64, f=520)
                            P.dma("sp", sec, avp[g][:, c0:c0 + 64, :])
                    P.barrier()
                    for u in range(len(XUNITS)):
                        P.collective((lambda a, b: (lambda e: e.collective_compute(
                            "AllGather", ALU.bypass, replica_groups=[[0, 1], [2, 3], [4, 5], [6, 7]],
                            ins=[a], outs=[b])))(xsrc_u[u], xdst_u[u]))
                    P.barrier()
                    for g, (_, d) in enumerate(GROUPS):
                        L = T // d
                        for (kk, rb, c0) in (("AKL", 0, 0), ("AKF", 1, 64 + L)):
                            sec = xd((kk, g), rb).rearrange("r c -> (r c)").rearrange(
                                "(f r c) -> f r c", r=d, c=64)
                            for fb in range(4):
                                P.dma("sp", akT[g][fb * 128:(fb + 1) * 128, :, c0:c0 + 64], sec[fb * 128:(fb + 1) * 128])
                        for (kk, rb, c0, mc) in (("AVL", 0, 0, 0), ("AVF", 1, 64 + L, 1)):
                            nel = d * 64 * 520
                            sec = xd((kk, g), rb).rearrange("r c -> (r c)")[0:nel].rearrange(
                                "(r p f) -> p r f", p=64, f=520)
                            t_, kt = vh.next()
                            P.dma("sp", t_[:, 0:d, :], sec, writes=[kt])
                            P.ts("dve", t_[:, 0:d, :], t_[:, 0:d, :], msk[0:64, mc:mc + 1], None, ALU.mult,
                                 reads=[kt, "msk"], writes=[kt])
                            P.dma("sp", avp[g][:, c0:c0 + 64, :].rearrange("r p f -> p r f"), t_[:, 0:d, :], reads=[kt])
                    P.barrier()
                    P.flush()

                if debug and debug.get("upto") == "X":
                    break

                with ExitStack() as es:
                    E = sb(es, f"E{l}", [128, 24, 256], F32)
                    acc = sb(es, f"acc{l}", [65, 4, T], F32)
                    vaug = sb(es, f"vaug{l}", [128, 48, 4, 65], BF16)
                    ktp = TPool([sb(es, f"kt{l}_{i}", [64, 6144], BF16) for i in range(2)], "kt")
                    qtp = TPool([sb(es, f"qt{l}_{i}", [64, T], BF16) for i in range(2)], "qt")
                    exp_ = TPool([sb(es, f"ex{l}_{i}", [128, 512], F32) for i in range(3)], "ex")
                    ptp = TPool([sb(es, f"pt{l}_{i}", [128, 512], BF16) for i in range(4)], "pt")
                    rdp = TPool([sb(es, f"rd{l}_{i}", [65, 512], F32) for i in range(2)], "rd")
                    nmp = TPool([sb(es, f"nm{l}_{i}", [64, 512], F32) for i in range(2)], "nm")
                    szp = TPool([sb(es, f"sz{l}_{i}", [64, 512], BF16) for i in range(2)], "sz")
                    ysp = TPool([sb(es, f"ys{l}_{i}", [64, 512], BF16) for i in range(2)], "ys")
                    psS = pspool([0, 1, 2, 3], "ps")
                    psO = pspool([4, 5, 6], "ps")
                    psN = pspool([7], "ps")
                    P.dma("sp", E[:, :, :], Edram.rearrange("p (g c) -> p g c", c=256), writes=["E"])
                    for hh in range(2):
                        for g, (_, d) in enumerate(GROUPS):
                            L = T // d
                            nch = L // 128 + 1
                            vsrc = avp[g].rearrange("r (m p) (h c) -> p (r m) h c", p=128, c=65)
                            for c0 in range(0, d * nch, 12):
                                c1 = min(d * nch, c0 + 12)
                                P.dma("sp", vaug[:, c0:c1, :, :], vsrc[:, c0:c1, hh * 4:(hh + 1) * 4, :],
                                      writes=[("vaug", c0)])
                            vkeys = [("vaug", c0) for c0 in range(0, d * nch, 12)]
                            for hl in range(4):
                                h = hh * 4 + hl
                                kt, kk = ktp.next()
                                P.dma("sp", kt[:, 0:d * (L + 128)],
                                      akT[g][h * 64:(h + 1) * 64, :, :].rearrange("p r c -> p (r c)"), writes=[kk])
                                qt, kq = qtp.next()
                                P.dma("sp", qt[:, :], aqT[g, h * 64:(h + 1) * 64, :], writes=[kq])
                                blocks = [(r, n2) for r in range(d) for n2 in range(L // 256)]
                                st = {}
                                e0 = E[:, g * 8 + h, :]
                                ebc = bass.AP(e0.tensor, e0.offset, [list(e0.ap[0]), [0, 2], list(e0.ap[1])])

                                def stageA(i):
                                    r, n2 = blocks[i]
                                    n = 2 * n2
                                    ps, kp = psS.next()
                                    kb0 = r * (L + 128) + 128 * n
                                    qb0 = r * L + 128 * n
                                    P.mm(ps[:, 0:128], kt[:, kb0:kb0 + 128], qt[:, qb0:qb0 + 128],
                                         reads=[kk, kq], writes=[kp])
                                    P.mm(ps[:, 128:384], kt[:, kb0 + 128:kb0 + 256], qt[:, qb0:qb0 + 256],
                                         reads=[kk, kq], writes=[kp])
                                    P.mm(ps[:, 384:512], kt[:, kb0 + 256:kb0 + 384], qt[:, qb0 + 128:qb0 + 256],
                                         reads=[kk, kq], writes=[kp])
                                    ex, ke = exp_.next()
                                    P.act(ex[:, :], ps[:, :], AF.Exp, scale=0.125, reads=[kp], writes=[ke])
                                    pt, kpt = ptp.next()
                                    P.tt("dve", pt[:, :].rearrange("p (a b) -> p a b", a=2),
                                         ex[:, :].rearrange("p (a b) -> p a b", a=2), ebc, ALU.mult,
                                         reads=[ke, "E"], writes=[kpt])
                                    st[i] = (pt, kpt)

                                def stageB(i):
                                    r, n2 = blocks[i]
                                    n = 2 * n2
                                    pt, kpt = st.pop(i)
                                    po, ko = psO.next()
                                    ch = r * nch + n
                                    vk = lambda c_: [("vaug", (c_ // 12) * 12)]
                                    P.mm(po[0:65, 0:256], vaug[:, ch + 1, hl, :], pt[:, 128:384], start=True, stop=False,
                                         reads=[kpt] + vk(ch + 1), writes=[ko])
                                    P.mm(po[0:65, 0:128], vaug[:, ch, hl, :], pt[:, 0:128], start=False, stop=False,
                                         reads=[kpt] + vk(ch), writes=[ko])
                                    P.mm(po[0:65, 128:256], vaug[:, ch + 2, hl, :], pt[:, 384:512], start=False, stop=True,
                                         reads=[kpt] + vk(ch + 2), writes=[ko])
                                    t0 = r + d * 128 * n
                                    av_ = acc[:, hl, t0:t0 + d * 255 + 1:d]
                                    if g == 0:
                                        P.copy("act", av_, po[0:65, 0:256], reads=[ko], writes=[("acc", hl)])
                                    else:
                                        P.tt("dve", av_, av_, po[0:65, 0:256], ALU.add, reads=[ko, ("acc", hl)],
                                             writes=[("acc", hl)])

                                nb = len(blocks)
                                LK = 2
                                for i in range(nb + LK):
                                    if i < nb:
                                        stageA(i)
                                    if i >= LK:
                                        stageB(i - LK)
                        for hl in range(4):
                            h = hh * 4 + hl
                            for tb in range(8):
                                tsl = slice(tb * 512, (tb + 1) * 512)
                                sz, ksz = szp.next()
                                P.dma("sp", sz[:, :], sazT[h * 64:(h + 1) * 64, tsl], writes=[ksz])
                                rd, krd = rdp.next()
                                P.recip(rd[64:65, :], acc[64:65, hl, tsl], reads=[("acc", hl)], writes=[krd])
                                pb_, kpb = psN.next()
                                P.mm(pb_[0:64, :], ones_f[64:65, 0:64], rd[64:65, :], reads=[krd, "ones"], writes=[kpb])
                                nm, knm = nmp.next()
                                P.tt("dve", nm[:, :], acc[0:64, hl, tsl], pb_[0:64, :], ALU.mult,
                                     reads=[("acc", hl), kpb], writes=[knm])
                                ys, kys = ysp.next()
                                P.tt("pool", ys[:, :], nm[:, :], sz[:, :], ALU.mult, reads=[knm, ksz], writes=[kys])
                                P.dma("pool", yaT[h * 64:(h + 1) * 64, tsl], ys[:, :], reads=[kys])
                    P.barrier()
                    P.flush()

                if debug and debug.get("upto") == "P2":
                    break

                with ExitStack() as es:
                    kTd = sb(es, f"kTd{l}", [128, 2, S], BF16)
                    vb = sb(es, f"vb{l}", [128, 64, 130], BF16)
                    qT = sb(es, f"qT{l}", [128, 4, T], BF16)
                    ptp = TPool([sb(es, f"pB{l}_{i}", [128, 1024], BF16) for i in range(3)], "pB")
                    evp = TPool([sb(es, f"evB{l}_{i}", [65, 512], F32) for i in range(3)], "evB")
                    rdp = TPool([sb(es, f"rdB{l}_{i}", [65, 512], F32) for i in range(2)], "rdB")
                    nm2 = TPool([sb(es, f"nmC{l}_{i}", [64, 512], F32) for i in range(2)], "nmC")
                    szp = TPool([sb(es, f"szB{l}_{i}", [64, 512], BF16) for i in range(3)], "szB")
                    ysp = TPool([sb(es, f"ysB{l}_{i}", [64, 512], BF16) for i in range(3)], "ysB")
                    psA2 = [(PSALL[:, 0:2, :].rearrange("p a b -> p (a b)"), [("ps", 0), ("ps", 1)]),
                            (PSALL[:, 2:4, :].rearrange("p a b -> p (a b)"), [("ps", 2), ("ps", 3)])]
                    psO = pspool([4, 5, 6], "ps")
                    psN = pspool([7], "ps")
                    for rk in range(2):
                        kb_ = xd("KB", rk)
                        for kvh in range(2):
                            for half in range(2):
                                P.dma("sp", kTd[half * 64:(half + 1) * 64, kvh, rk * T:(rk + 1) * T],
                                      kb_[kvh * 64:(kvh + 1) * 64, :], writes=[("kTd", rk, kvh, half)])
                        vsec = xd("VB", rk).rearrange("r c -> (r c)").rearrange("(k p f) -> p k f", p=128, f=130)
                        for k0 in range(0, 32, 8):
                            P.dma("sp", vb[:, rk * 32 + k0:rk * 32 + k0 + 8, :], vsec[:, k0:k0 + 8, :],
                                  writes=[("vb", rk, k0)])
                    for c_ in range(4):
                        P.dma("sp", qT[:, c_, :], qbT[c_ * 128:(c_ + 1) * 128, :], writes=[("qT", c_)])
                    steps = [(qb, hp, kc) for qb in range(8) for hp in range(4) for kc in range(64)]
                    st = {}
                    acc_o = {}

                    def stageA(i):
                        qb, hp, kc = steps[i]
                        kvh = hp // 2
                        rk = kc // 32
                        psa, keys = psA2[i % 2]
                        for hh_ in range(2):
                            pr = hh_ * 64
                            P.mm(psa[:, hh_ * 512:(hh_ + 1) * 512], kTd[pr:pr + 64, kvh, kc * 128:(kc + 1) * 128],
                                 qT[pr:pr + 64, hp, qb * 512:(qb + 1) * 512],
                                 reads=[("kTd", rk, kvh, hh_), ("qT", hp)], writes=keys)
                        pt, kpt = ptp.next()
                        P.act(pt[:, :], psa, AF.Exp, scale=0.125, reads=keys, writes=[kpt])
                        st[i] = (pt, kpt)

                    def stageB(i):
                        qb, hp, kc = steps[i]
                        kvh = hp // 2
                        pt, kpt = st.pop(i)
                        if kc == 0:
                            acc_o[(qb, hp)] = [psO.next(), psO.next()]
                        for hh_ in range(2):
                            po, ko = acc_o[(qb, hp)][hh_]
                            P.mm(po[0:65, :], vb[:, kc, kvh * 65:(kvh + 1) * 65], pt[:, hh_ * 512:(hh_ + 1) * 512],
                                 start=(kc == 0), stop=(kc == 63),
                                 reads=[kpt, ("vb", kc // 32, ((kc % 32) // 8) * 8)], writes=[ko])
                        if kc == 63:
                            tsl = slice(qb * 512, (qb + 1) * 512)
                            for hh_ in range(2):
                                h = 2 * hp + hh_
                                po, ko = acc_o[(qb, hp)][hh_]
                                ev, kev = evp.next()
                                P.copy("dve", ev[:, :], po[0:65, :], reads=[ko], writes=[kev])
                                sz, ksz = szp.next()
                                P.dma("sp", sz[:, :], sbzT[h * 64:(h + 1) * 64, tsl], writes=[ksz])
                                rd, krd = rdp.next()
                                P.recip(rd[64:65, :], ev[64:65, :], reads=[kev], writes=[krd])
                                pb_, kpb = psN.next()
                                P.mm(pb_[0:64, :], ones_f[64:65, 0:64], rd[64:65, :], reads=[krd, "ones"], writes=[kpb])
                                n2, kn2 = nm2.next()
                                P.tt("dve", n2[:, :], ev[0:64, :], pb_[0:64, :], ALU.mult, reads=[kev, kpb], writes=[kn2])
                                ys, kys = ysp.next()
                                P.tt("pool", ys[:, :], n2[:, :], sz[:, :], ALU.mult, reads=[kn2, ksz], writes=[kys])
                                P.dma("pool", ybT[h * 64:(h + 1) * 64, tsl], ys[:, :], reads=[kys])

                    LOOK = 1
                    ns = len(steps)
                    for i in range(ns + LOOK):
                        if i < ns:
                            stageA(i)
                        if i >= LOOK:
                            stageB(i - LOOK)
                    P.barrier()
                    P.flush()

                if debug and debug.get("upto") == "P3":
                    break

                with ExitStack() as es:
                    wpa = sb(es, f"wpa{l}", [128, 4, D], BF16)
                    wpb = sb(es, f"wpb{l}", [128, 4, D], BF16)
                    wo = sb(es, f"wo{l}", [128, 8, D], BF16)
                    wf = TPool([sb(es, f"wf{l}_{i}", [128, 8, 256], F32) for i in range(2)], "wf")
                    for (wdst, wsrc, nk_) in ((wpa, w_pa_in, 4), (wpb, w_pb_in, 4), (wo, w_o_in, 8)):
                        for nh in range(4):
                            wt, kw = wf.next()
                            P.dma("sp", wt[:, 0:nk_, :],
                                  wsrc[l, :, nh * 256:(nh + 1) * 256].rearrange("(k p) n -> p k n", p=128), writes=[kw])
                            P.copy("pool", wdst[:, :, nh * 256:(nh + 1) * 256], wt[:, 0:nk_, :], reads=[kw],
                                   writes=["wP4"])
                    yap = TPool([sb(es, f"ya{l}_{i}", [128, 4, 512], BF16) for i in range(2)], "ya")
                    ybp = TPool([sb(es, f"yb{l}_{i}", [128, 4, 512], BF16) for i in range(2)], "yb")
                    gp = TPool([sb(es, f"gg{l}_{i}", [128, 16, 512], BF16) for i in range(2)], "gg")
                    mtp = TPool([sb(es, f"mT{l}_{i}", [128, 8, 512], BF16) for i in range(2)], "mT")
                    t1p = TPool([sb(es, f"t1{l}_{i}", [128, 512], F32) for i in range(2)], "t1")
                    t2p = TPool([sb(es, f"t2{l}_{i}", [128, 512], F32) for i in range(2)], "t2")
                    xrp = TPool([sb(es, f"xr{l}_{i}", [128, D], F32) for i in range(3)], "xr")
                    hp = TPool([sb(es, f"hh{l}_{i}", [128, D], F32) for i in range(2)], "hh")
                    op_ = TPool([sb(es, f"oo{l}_{i}", [128, D], F32) for i in range(2)], "oo")
                    stp = TPool([sb(es, f"bst{l}_{i}", [128, 16], F32) for i in range(2)], "bst")
                    psA = pspool([0, 1, 2, 3], "ps")
                    psB = pspool([4, 5, 6, 7], "ps")
                    for tb in range(8):
                        tsl = slice(tb * 512, (tb + 1) * 512)
                        ya, kya = yap.next()
                        yb, kyb = ybp.next()
                        gg, kgg = gp.next()
                        P.dma("sp", ya[:, :, :], yaT[:, tsl].rearrange("(h p) t -> p h t", p=128), writes=[kya])
                        P.dma("sp", yb[:, :, :], ybT[:, tsl].rearrange("(h p) t -> p h t", p=128), writes=[kyb])
                        P.dma("sp", gg[:, :, :], gT[:, tsl].rearrange("(j p) t -> p j t", p=128), writes=[kgg])
                        mT, kmT = mtp.next()
                        for m in range(8):
                            pa, kpa = psA.next()
                            for h in range(4):
                                P.mm(pa[:, :], wpa[:, h, m * 128:(m + 1) * 128], ya[:, h, :], start=(h == 0),
                                     stop=(h == 3), reads=["wP4", kya], writes=[kpa])
                            pb_, kpb = psA.next()
                            for h in range(4):
                                P.mm(pb_[:, :], wpb[:, h, m * 128:(m + 1) * 128], yb[:, h, :], start=(h == 0),
                                     stop=(h == 3), reads=["wP4", kyb], writes=[kpb])
                            t1, k1 = t1p.next()
                            P.tt("dve", t1[:, :], pa[:, :], gg[:, m, :], ALU.mult, reads=[kpa, kgg], writes=[k1])
                            t2, k2 = t2p.next()
                            P.tt("dve", t2[:, :], pb_[:, :], gg[:, 8 + m, :], ALU.mult, reads=[kpb, kgg], writes=[k2])
                            P.tt("pool", mT[:, m, :], t1[:, :], t2[:, :], ALU.add, reads=[k1, k2], writes=[(kmT, m)])
                        for tt_ in range(4):
                            r0 = tb * 512 + tt_ * 128
                            xr, kxr = xrp.next()
                            P.dma("sp", xr[:, :], xsrc_l[r0:r0 + 128, :], writes=[kxr])
                            hb, khb = hp.next()
                            for nh in range(2):
                                po, ko = psB.next()
                                for kc in range(8):
                                    P.mm(po[:, :], mT[:, kc, tt_ * 128:(tt_ + 1) * 128], wo[:, kc, nh * 512:(nh + 1) * 512],
                                         start=(kc == 0), stop=(kc == 7), reads=["wP4"] + [(kmT, m) for m in range(8)],
                                         writes=[ko])
                                P.tt("dve", hb[:, nh * 512:(nh + 1) * 512], po[:, :], gate_b[:, nh * 512:(nh + 1) * 512],
                                     ALU.mult, reads=[ko], writes=[(khb, nh)])
                            P.stt(hb[:, :], xr[:, :], ALPHA, hb[:, :], ALU.mult, ALU.add,
                                  reads=[kxr, (khb, 0), (khb, 1)], writes=[(khb, 0), (khb, 1)])
                            bs, kbs = stp.next()
                            for nh in range(2):
                                P.op("dve", (lambda o, i_: (lambda e: e.bn_stats(o, i_)))(
                                    bs[:, nh * 6:(nh + 1) * 6], hb[:, nh * 512:(nh + 1) * 512]),
                                    reads=[(khb, 0), (khb, 1)], writes=[(kbs, nh)])
                            P.op("dve", (lambda o, i_: (lambda e: e.bn_aggr(o, i_)))(bs[:, 12:14], bs[:, 0:12]),
                                 reads=[(kbs, 0), (kbs, 1)], writes=[(kbs, 2)])
                            P.act(bs[:, 15:16], bs[:, 13:14], AF.Sqrt, bias=epsc[:, 1:2], reads=[(kbs, 2)],
                                  writes=[(kbs, 4)])
                            P.recip(bs[:, 14:15], bs[:, 15:16], reads=[(kbs, 4)], writes=[(kbs, 3)])
                            ob, kob = op_.next()
                            P.ts("dve", ob[:, :], hb[:, :], bs[:, 12:13], bs[:, 14:15], ALU.subtract, ALU.mult,
                                 reads=[(khb, 0), (khb, 1), (kbs, 2), (kbs, 3)], writes=[kob])
                            P.tt("pool", ob[:, :], ob[:, :], lng_b[:, :], ALU.mult, reads=[kob], writes=[kob])
                            P.tt("pool", ob[:, :], ob[:, :], lnb_b[:, :], ALU.add, reads=[kob], writes=[kob])
                            P.dma("pool", ydst_l[r0:r0 + 128, :], ob[:, :], reads=[kob])
                    P.barrier()
                    P.flush()
        P.barrier()
        P.flush()
    return nc, dbg_names


def _t5_bucket_np(rel):
    half, max_exact = 16, 8
    ret = np.where(rel > 0, half, 0)
    a = np.abs(rel)
    af = np.maximum(a, 1).astype(np.float32)
    large = max_exact + (np.log(af / np.float32(max_exact)) / np.float32(math.log(1024 / max_exact))
                         * np.float32(half - max_exact)).astype(np.int32)
    large = np.minimum(large, half - 1)
    return ret + np.where(a < max_exact, a, large)


def _host_consts(rel_table):
    ident = np.eye(128, dtype=np.float32)
    bones = np.zeros((128, 128), np.float32)
    bones[:64, :64] = 1.0 / 64
    bones[64:, 64:] = 1.0 / 64
    rrot = np.zeros((128, 128), np.float32)
    for base in range(0, 128, 32):
        for e in range(16):
            rrot[base + e + 16, base + e] = -1.0
            rrot[base + e, base + e + 16] = 1.0
    cmat = np.stack([ident, bones, rrot])
    i = np.arange(128)[:, None]
    j = np.arange(128)[None, :]
    rel0 = i - 64 - j
    rel1 = i + 64 - j
    tri0 = np.concatenate([(i >= j), (i <= j)], axis=1).astype(np.float32)
    tri = np.stack([tri0, (tri0 - 1.0) * 30000.0]).astype(np.float32)
    ab = np.zeros((128, 24, 256), np.float32)
    for g, (_, d) in enumerate(GROUPS):
        for c, rel in enumerate((rel0, rel1)):
            bk = _t5_bucket_np(np.clip(rel, -64, 64) * d)
            ab[:, g * 8:(g + 1) * 8, c * 128:(c + 1) * 128] = rel_table[bk][:, :, g * 8:(g + 1) * 8].transpose(0, 2, 1)
    return cmat, tri, ab.reshape(128, 24 * 256)


def _cs_tables(half):
    t = np.arange(half * T, (half + 1) * T)
    row = (t // 64).astype(np.float32)
    col = (t % 64).astype(np.float32)
    inv = (np.float32(10000.0) ** (-np.arange(0, 32, 2, dtype=np.float32) / np.float32(32))).astype(np.float32)
    ar = (row[:, None] * inv[None]).astype(np.float32)
    ac = (col[:, None] * inv[None]).astype(np.float32)
    ang = np.zeros((128, T), np.float32)
    for p in range(128):
        e = p % 64
        ang[p] = ar[:, e % 16] if e < 32 else ac[:, (e - 32) % 16]
    return np.stack([np.cos(ang), np.sin(ang)]).astype(np.float32)


def _in_maps(inputs):
    f = lambda a: np.ascontiguousarray(np.asarray(a, dtype=np.float32))
    x = f(inputs["x"]); c = f(inputs["c"])
    cmat, tri, ab = _host_consts(f(inputs["rel_table"]))
    bgT = np.ascontiguousarray(f(inputs["b_gate"]).reshape(DEPTH, 16, 128).transpose(0, 2, 1))
    qg = f(inputs["q_norm_g"]); kg = f(inputs["k_norm_g"])
    qkg = np.ascontiguousarray(np.stack([np.tile(qg, (1, 2)), np.tile(kg, (1, 2))], axis=-1))
    shared = {
        "ln_g": f(inputs["ln_g"]).reshape(DEPTH, 1, D), "ln_b": f(inputs["ln_b"]).reshape(DEPTH, 1, D),
        "w_ada": f(inputs["w_ada"]), "b_ada": f(inputs["b_ada"]).reshape(DEPTH, 1, 3 * D),
        "w_in": f(inputs["w_in"]), "bgT": bgT, "qkg": qkg, "w_pa": f(inputs["w_pa"]), "w_pb": f(inputs["w_pb"]),
        "w_o": f(inputs["w_o"]), "cmat": cmat, "abias": ab, "tri": tri,
    }
    cs = [_cs_tables(0), _cs_tables(1)]
    maps = []
    for core in range(8):
        b, hf = core // 2, core % 2
        m = dict(shared)
        m["x"] = np.ascontiguousarray(x[b, hf * T:(hf + 1) * T])
        m["cT"] = np.ascontiguousarray(c[b].reshape(8, 128).T)
        m["cstab"] = cs[hf]
        mk = np.ones((128, 2), np.float32)
        mk[:, 0] = 0.0 if hf == 0 else 1.0
        mk[:, 1] = 1.0 if hf == 0 else 0.0
        m["msk"] = mk
        maps.append(m)
    return maps


def kernel(**inputs):
    nc, _ = _build()
    maps = _in_maps(inputs)
    res = run_bass_kernel_spmd(nc, maps, core_ids=list(range(8)))
    out = np.empty((4, S, D), np.float32)
    for core in range(8):
        b, hf = core // 2, core % 2
        out[b, hf * T:(hf + 1) * T] = res.results[core]["y"]
    return out
```

```python
import math
from contextlib import ExitStack

import numpy as np
import concourse.bass as bass
import concourse.mybir as mybir
from concourse.bass_utils import run_bass_kernel_spmd

F32 = mybir.dt.float32
BF16 = mybir.dt.bfloat16
AF = mybir.ActivationFunctionType
ALU = mybir.AluOpType

D = 1024
T = 4096
S = 8192
DEPTH = 2
GROUPS = ((128, 1), (512, 4), (2048, 16))
ALPHA = float((2 * DEPTH) ** 0.25)
LN_EPS = 1e-5
QK_EPS = 1e-6
C_AQ, C_AK, C_AV, C_AZ, C_BQ, C_BK, C_BV, C_BZ, C_GL = 0, 1536, 3072, 4608, 5120, 5632, 5760, 5888, 6400
XCOLS = 4096


def _xlayout():
    units = []
    sec = {}

    def add_unit(items):
        u = len(units)
        r = 0
        for k, n in items:
            sec[k] = (u, r, n)
            r += n
        units.append(r)

    add_unit([("KB", 128)])
    add_unit([("VB", 130)])
    for g, (_, d) in enumerate(GROUPS):
        nk = 8 * d
        nv = -(-(d * 64 * 520) // XCOLS)
        items = [(("AKF", g), nk), (("AKL", g), nk), (("AVF", g), nv), (("AVL", g), nv)]
        if nk + nk + nv + nv <= 130:
            add_unit(items)
        else:
            for it in items:
                add_unit([it])
    return units, sec


XUNITS, XSEC = _xlayout()


class Prog:
    CE = ("pe", "act", "dve", "pool")
    ALLE = ("pe", "act", "dve", "pool", "sp")
    NDS = 8

    def __init__(self, nc, es):
        self.nc = nc
        self.sem = {}
        for e in self.CE:
            self.sem[("c", e)] = es.enter_context(nc.semaphore(f"c_{e}"))
        for q in ("sp", "act", "pool"):
            for i in range(self.NDS):
                self.sem[("d", q, i)] = es.enter_context(nc.semaphore(f"d_{q}{i}"))
        self.sem[("cc",)] = es.enter_context(nc.semaphore("ccsem"))
        self.cnt = {k: 0 for k in self.sem}
        self.dnext = {q: 0 for q in ("sp", "act", "pool")}
        self.ops = {e: [] for e in self.ALLE}
        self.waited = {e: {} for e in self.ALLE}
        self.lastw = {}
        self.readers = {}
        self.nops = 0

    def _deps(self, reads, writes):
        deps = set()
        for k in reads:
            if k in self.lastw:
                deps.add(self.lastw[k])
        for k in writes:
            if k in self.lastw:
                deps.add(self.lastw[k])
            deps.update(self.readers.get(k, ()))
        return deps

    def _commit(self, ticket, reads, writes):
        for k in reads:
            self.readers.setdefault(k, []).append(ticket)
        for k in writes:
            self.lastw[k] = ticket
            self.readers[k] = []

    def _waits(self, eng, deps):
        w = []
        for (sk, val) in sorted(deps, key=lambda t: (str(t[0]), t[1])):
            if eng == "pe" and sk == ("c", "pe"):
                continue
            if self.waited[eng].get(sk, 0) >= val:
                continue
            self.waited[eng][sk] = val
            w.append((self.sem[sk], val))
        return w

    def op(self, eng, fn, reads=(), writes=()):
        deps = self._deps(reads, writes)
        w = self._waits(eng, deps)
        sk = ("c", eng)
        self.cnt[sk] += 1
        t = (sk, self.cnt[sk])
        self.ops[eng].append((w, fn, self.sem[sk], 1))
        self._commit(t, reads, writes)
        self.nops += 1
        return t

    def dma(self, q, out, in_, reads=(), writes=()):
        deps = self._deps(reads, writes)
        slot = self.dnext[q] % self.NDS
        self.dnext[q] += 1
        sk = ("d", q, slot)
        if self.cnt[sk] > 0:
            deps.add((sk, self.cnt[sk]))
        w = self._waits(q, deps)
        self.cnt[sk] += 16
        t = (sk, self.cnt[sk])
        self.ops[q].append((w, lambda e: e.dma_start(out=out, in_=in_), self.sem[sk], 16))
        self._commit(t, reads, writes)
        self.nops += 1
        return t

    def collective(self, fn, reads=(), writes=()):
        deps = self._deps(reads, writes)
        w = self._waits("pool", deps)
        sk = ("cc",)
        self.cnt[sk] += 1
        t = (sk, self.cnt[sk])
        self.ops["pool"].append((w, fn, self.sem[sk], 1))
        self._commit(t, reads, writes)
        return t

    def barrier(self):
        tickets = set((sk, v) for sk, v in self.cnt.items() if v > 0)
        for e in self.ALLE:
            w = self._waits(e, tickets)
            if w:
                self.ops[e].append((w, None, None, 0))
        self.lastw = {}
        self.readers = {}

    def flush(self):
        nc = self.nc
        ops = self.ops

        def replay(lst, e):
            for (w, fn, sem, inc) in lst:
                for (s, v) in w:
                    e.wait_ge(s, v)
                if fn is not None:
                    ins = fn(e)
                    ins.then_inc(sem, inc)

        with nc.Block() as block:
            @block.tensor
            def _(e):
                replay(ops["pe"], e)

            @block.scalar
            def _(e):
                replay(ops["act"], e)

            @block.vector
            def _(e):
                replay(ops["dve"], e)

            @block.gpsimd
            def _(e):
                replay(ops["pool"], e)

            @block.sync
            def _(e):
                replay(ops["sp"], e)
        self.ops = {e: [] for e in self.ALLE}

    def mm(self, out, lhsT, rhs, start=True, stop=True, reads=(), writes=()):
        return self.op("pe", lambda e: e.matmul(out, lhsT, rhs, start=start, stop=stop), reads, writes)

    def tr(self, out, in_, ident, reads=(), writes=()):
        return self.op("pe", lambda e: e.transpose(out, in_, ident), reads, writes)

    def act(self, out, in_, func, bias=None, scale=None, reads=(), writes=(), eng="act"):
        kw = {}
        if bias is not None:
            kw["bias"] = bias
        if scale is not None:
            kw["scale"] = scale
        return self.op(eng, lambda e: e.activation(out, in_, func, **kw), reads, writes)

    def tt(self, eng, out, in0, in1, op, reads=(), writes=()):
        return self.op(eng, lambda e: e.tensor_tensor(out, in0, in1, op), reads, writes)

    def ts(self, eng, out, in0, s1, s2, op0, op1=None, reads=(), writes=()):
        if op1 is None:
            return self.op(eng, lambda e: e.tensor_scalar(out, in0, s1, None, op0), reads, writes)
        return self.op(eng, lambda e: e.tensor_scalar(out, in0, s1, s2, op0, op1), reads, writes)

    def stt(self, out, in0, scalar, in1, op0, op1, reads=(), writes=()):
        return self.op("dve", lambda e: e.scalar_tensor_tensor(out, in0, scalar, in1, op0, op1), reads, writes)

    def copy(self, eng, out, in_, reads=(), writes=()):
        if eng == "act":
            return self.op("act", lambda e: e.activation(out, in_, AF.Copy), reads, writes)
        return self.op(eng, lambda e: e.tensor_copy(out, in_), reads, writes)

    def memset(self, eng, ap, val, reads=(), writes=()):
        return self.op(eng, lambda e: e.memset(ap, val), reads, writes)

    def recip(self, out, in_, reads=(), writes=()):
        return self.op("dve", lambda e: e.reciprocal(out, in_), reads, writes)


class TPool:
    def __init__(self, tiles, name):
        self.tiles = tiles
        self.name = name
        self.i = 0

    def next(self):
        j = self.i % len(self.tiles)
        self.i += 1
        return self.tiles[j], (self.name, j)


def _build(debug=None):
    nc = bass.Bass("TRN2", target_bir_lowering=False)
    dbg_kind = "ExternalOutput" if debug else "Internal"

    def din(name, shape, dt=F32):
        return nc.dram_tensor(name, list(shape), dt, kind="ExternalInput").ap()

    def dscr(name, shape, dt=BF16, dbg=False):
        isdbg = bool(debug) and dbg and name in debug.get("outs", ())
        return nc.dram_tensor(name, list(shape), dt, kind=("ExternalOutput" if isdbg else "Internal")).ap()

    x_in = din("x", [T, D])
    cT_in = din("cT", [128, 8])
    ln_g_in = din("ln_g", [DEPTH, 1, D])
    ln_b_in = din("ln_b", [DEPTH, 1, D])
    w_ada_in = din("w_ada", [DEPTH, D, 3 * D])
    b_ada_in = din("b_ada", [DEPTH, 1, 3 * D])
    w_in_in = din("w_in", [DEPTH, D, 8448])
    bgT_in = din("bgT", [DEPTH, 128, 16])
    qkg_in = din("qkg", [DEPTH, 128, 2])
    w_pa_in = din("w_pa", [DEPTH, 512, D])
    w_pb_in = din("w_pb", [DEPTH, 512, D])
    w_o_in = din("w_o", [DEPTH, D, D])
    cmat_in = din("cmat", [3, 128, 128])
    cstab_in = din("cstab", [2, 128, T])
    abias_in = din("abias", [128, 24 * 256])
    tri_in = din("tri", [2, 128, 256])
    msk_in = din("msk", [128, 2])
    y_out = nc.dram_tensor("y", [T, D], F32, kind="ExternalOutput").ap()

    aqT = dscr("aqT", [3, 512, T], dbg=True)
    akT = [dscr(f"akT{g}", [512, d, T // d + 128], dbg=True) for g, (_, d) in enumerate(GROUPS)]
    avp = [dscr(f"avp{g}", [d, T // d + 128, 520], dbg=True) for g, (_, d) in enumerate(GROUPS)]
    sazT = dscr("sazT", [512, T], dbg=True)
    sbzT = dscr("sbzT", [512, T], dbg=True)
    qbT = dscr("qbT", [512, T], dbg=True)
    gT = dscr("gT", [2048, T], dbg=True)
    yaT = dscr("yaT", [512, T], dbg=True)
    ybT = dscr("ybT", [512, T], dbg=True)
    x1 = dscr("x1", [T, D], F32, dbg=True)
    Edram = dscr("Edram", [128, 24 * 256], F32)
    rdscr = dscr("rdscr", [8, 512], F32)
    rdslot = [0]
    xsrc_u = [dscr(f"xsrc{u}", [n, XCOLS]) for u, n in enumerate(XUNITS)]
    xdst_u = [dscr(f"xdst{u}", [2 * n, XCOLS]) for u, n in enumerate(XUNITS)]

    def xs(key):
        u, r0, n = XSEC[key]
        return xsrc_u[u][r0:r0 + n, :]

    def xd(key, rank):
        u, r0, n = XSEC[key]
        return xdst_u[u][rank * XUNITS[u] + r0:rank * XUNITS[u] + r0 + n, :]
    dbg_names = ["aqT", "akT0", "akT1", "akT2", "avp0", "avp1", "avp2", "sazT", "sbzT", "qbT", "gT",
                 "yaT", "ybT", "x1"]

    with ExitStack() as top:
        P = Prog(nc, top)

        def sb(es, name, shape, dt):
            return es.enter_context(nc.sbuf_tensor("s_" + name, list(shape), dt))

        PSALL = top.enter_context(nc.psum_tensor("psall", [128, 8, 512], F32))
        PS = [PSALL[:, i, :] for i in range(8)]

        class PSPool:
            def __init__(self, idx):
                self.idx = idx
                self.i = 0

            def next(self):
                b = self.idx[self.i % len(self.idx)]
                self.i += 1
                return PS[b], ("ps", b)

        def pspool(idx, name):
            return PSPool(idx)

        cmat = sb(top, "cmat", [128, 3, 128], F32)
        ones_f = sb(top, "ones_f", [128, 128], F32)
        msk = sb(top, "msk", [128, 2], F32)
        P.dma("sp", cmat[:, :, :], cmat_in.rearrange("c p n -> p c n"), writes=["cmat"])
        P.dma("sp", msk[:, :], msk_in, writes=["msk"])
        P.memset("pool", ones_f[:, :], 1.0, writes=["ones"])
        epsc = sb(top, "epsc", [128, 2], F32)
        P.memset("pool", epsc[:, 0:1], QK_EPS, writes=["epsc"])
        P.memset("pool", epsc[:, 1:2], LN_EPS, writes=["epsc"])
        ident = cmat[:, 0, :]
        bones = cmat[:, 1, :]
        rrot = cmat[:, 2, :]

        with ExitStack() as es:
            ab = sb(es, "ab", [128, 24 * 256], F32)
            tri = sb(es, "tri", [128, 2, 256], F32)
            P.dma("sp", ab[:, :], abias_in, writes=["ab"])
            P.dma("sp", tri[:, :, :], tri_in.rearrange("c p n -> p c n"), writes=["tri"])
            P.act(ab[:, :], ab[:, :], AF.Exp, reads=["ab"], writes=["ab"])
            for gh in range(24):
                sl = slice(gh * 256, (gh + 1) * 256)
                P.tt("dve", ab[:, sl], ab[:, sl], tri[:, 0, :], ALU.mult, reads=["ab", "tri"], writes=[("ab", gh)])
            P.dma("sp", Edram, ab[:, :], reads=[("ab", gh) for gh in range(24)])
            P.barrier()
            P.flush()

        for l in range(DEPTH):
            if debug and l > debug.get("layers", DEPTH) - 1:
                break
            xsrc_l = x_in if l == 0 else x1
            ydst_l = x1 if l < DEPTH - 1 else y_out
            with ExitStack() as lay:
                shiftT = sb(lay, f"shiftT{l}", [128, 8], F32)
                sc1T = sb(lay, f"sc1T{l}", [128, 8], F32)
                gate_b = sb(lay, f"gate_b{l}", [128, D], F32)
                lng_b = sb(lay, f"lng_b{l}", [128, D], F32)
                lnb_b = sb(lay, f"lnb_b{l}", [128, D], F32)
                bgT = sb(lay, f"bgT{l}", [128, 16], F32)
                qkg = sb(lay, f"qkg{l}", [128, 2], F32)

                with ExitStack() as es:
                    cT = sb(es, f"cT{l}", [128, 8], F32)
                    silc = sb(es, f"silc{l}", [128, 8], F32)
                    rows = sb(es, f"rows{l}", [1, 5 * D], F32)
                    brow = sb(es, f"brow{l}", [1, 3 * D], F32)
                    wst = [sb(es, f"wada{l}_{i}", [128, 8, 512], F32) for i in range(2)]
                    wp = TPool(wst, "wada")
                    psp = pspool([0, 1], "ps")
                    P.dma("sp", cT[:, :], cT_in, writes=["cT"])
                    P.dma("sp", brow[:, :], b_ada_in[l], writes=["brow"])
                    P.dma("sp", rows[:, 3 * D:4 * D], ln_g_in[l], writes=["rows_ln"])
                    P.dma("sp", rows[:, 4 * D:5 * D], ln_b_in[l], writes=["rows_ln"])
                    P.dma("sp", bgT[:, :], bgT_in[l], writes=["bgT"])
                    P.dma("sp", qkg[:, :], qkg_in[l], writes=["qkg"])
                    P.act(silc[:, :], cT[:, :], AF.Silu, reads=["cT"], writes=["silc"])
                    for n in range(6):
                        wt, kw = wp.next()
                        P.dma("sp", wt[:, :, :],
                              w_ada_in[l, :, n * 512:(n + 1) * 512].rearrange("(kc p) n -> p kc n", p=128),
                              writes=[kw])
                        ps, kp = psp.next()
                        for kc in range(8):
                            P.mm(ps[0:1, :], silc[:, kc:kc + 1], wt[:, kc, :], start=(kc == 0), stop=(kc == 7),
                                 reads=[kw, "silc"], writes=[kp])
                        P.tt("dve", rows[0:1, n * 512:(n + 1) * 512], ps[0:1, :], brow[0:1, n * 512:(n + 1) * 512],
                             ALU.add, reads=[kp, "brow"], writes=[("rows", n)])
                    ps, kp = psp.next()
                    for j in range(16):
                        P.mm(ps[:, j:j + 1], rows[0:1, j * 128:(j + 1) * 128], ones_f[0:1, 0:1],
                             reads=[("rows", j // 4), "ones"], writes=[kp])
                    P.copy("dve", shiftT[:, :], ps[:, 0:8], reads=[kp], writes=["mod"])
                    P.ts("dve", sc1T[:, :], ps[:, 8:16], 1.0, None, ALU.add, reads=[kp], writes=["mod2"])
                    for (dst, c0, rk) in ((gate_b, 2 * D, [("rows", 4), ("rows", 5)]), (lng_b, 3 * D, ["rows_ln"]),
                                          (lnb_b, 4 * D, ["rows_ln"])):
                        for nh in range(2):
                            ps, kp = psp.next()
                            P.mm(ps[:, :], ones_f[0:1, :], rows[0:1, c0 + nh * 512:c0 + (nh + 1) * 512],
                                 reads=rk + ["ones"], writes=[kp])
                            P.copy("dve", dst[:, nh * 512:(nh + 1) * 512], ps[:, :], reads=[kp], writes=["bc"])
                    P.barrier()
                    P.flush()

                with ExitStack() as es:
                  if not (debug and debug.get("skipP1")):
                        uT = sb(es, f"uT{l}", [128, 8, T], BF16)
                        with ExitStack() as es2:
                            xp = TPool([sb(es2, f"xt{l}_{i}", [128, 4, D], F32) for i in range(2)], "xt")
                            psp = pspool([0, 1, 2, 3], "ps")
                            for tb in range(8):
                                xt, kx = xp.next()
                                P.dma("sp", xt[:, :, :],
                                      xsrc_l[tb * 512:(tb + 1) * 512, :].rearrange("(t p) d -> p t d", p=128), writes=[kx])
                                for kc in range(8):
                                    ps, kp = psp.next()
                                    for t in range(4):
                                        P.tr(ps[:, t * 128:(t + 1) * 128], xt[:, t, kc * 128:(kc + 1) * 128], ident,
                                             reads=[kx], writes=[kp])
                                    P.act(uT[:, kc, tb * 512:(tb + 1) * 512], ps[:, :], AF.Identity,
                                          bias=shiftT[:, kc:kc + 1], scale=sc1T[:, kc:kc + 1], reads=[kp])
                            P.barrier()
                            P.flush()

                        uTp = sb(es, f"uTp{l}", [128, 8, T], BF16)
                        wstp = TPool([sb(es, f"wst{l}_{i}", [128, 8, 256], F32) for i in range(2)], "wst")
                        wbfp = TPool([sb(es, f"wbf{l}_{i}", [128, 8, 512], BF16) for i in range(2)], "wbf")
                        stgp = TPool([sb(es, f"stg{l}_{i}", [128, 512], BF16) for i in range(4)], "stg")
                        vstp = TPool([sb(es, f"vst{l}_{i}", [128, 8, 65], BF16) for i in range(3)], "vst")
                        f32p = TPool([sb(es, f"f32t{l}_{i}", [128, 512], F32) for i in range(7)], "f32t")
                        csp = TPool([sb(es, f"cst{l}_{i}", [128, 2, 512], F32) for i in range(2)], "cst")
                        psA = pspool([0, 1, 2, 3, 4], "ps")
                        psB = pspool([5, 6, 7], "ps")
                        for vt in vstp.tiles:
                            P.memset("pool", vt[:, :, :], 1.0)
                        P.barrier()
                        evac_rr = [0]

                        def evac_copy(out, in_, reads, writes):
                            evac_rr[0] += 1
                            if evac_rr[0] % 2:
                                return P.copy("act", out, in_, reads=reads, writes=writes)
                            return P.copy("dve", out, in_, reads=reads, writes=writes)

                        def load_strip(c0, W):
                            wb, kb = wbfp.next()
                            for w0 in range(0, W, 256):
                                wt, kw = wstp.next()
                                P.dma("sp", wt[:, :, :],
                                      w_in_in[l, :, c0 + w0:c0 + w0 + 256].rearrange("(kc p) n -> p kc n", p=128),
                                      writes=[kw])
                                P.copy("pool", wb[:, :, w0:w0 + 256], wt[:, :, :], reads=[kw], writes=[(kb, w0)])
                            return wb, [(kb, w0) for w0 in range(0, W, 256)]

                        def permute_uT(d):
                            engs = ("dve", "pool", "act")
                            for kc in range(8):
                                P.copy(engs[kc % 3], uTp[:, kc, :].rearrange("k (r p) -> k r p", r=d),
                                       uT[:, kc, :].rearrange("k (p r) -> k r p", r=d), writes=[("uTp", kc)])

                        def rhs_perm(g, kc, pb):
                            src = uT if g == 0 else uTp
                            return src[:, kc, pb * 512:(pb + 1) * 512]

                        def lhs_perm(g, kc, it):
                            src = uT if g == 0 else uTp
                            return src[:, kc, it * 128:(it + 1) * 128]

                        def fm_chunk(wb, kb, col0, rhs_fn, extra=()):
                            ps, kp = psA.next()
                            for kc in range(8):
                                rd_ = list(kb) + [e_ for e_ in extra if e_[1] == kc]
                                P.mm(ps[:, :], wb[:, kc, col0:col0 + 128], rhs_fn(kc), start=(kc == 0), stop=(kc == 7),
                                     reads=rd_, writes=[kp])
                            return ps, kp

                        def do_qk(which, g, wb, kb):
                            d = GROUPS[g][1]
                            ex_ = [("uTp", kc) for kc in range(8)] if g > 0 else []
                            for fcl in range(4):
                                for pb in range(8):
                                    ps, kp = fm_chunk(wb, kb, fcl * 128, lambda kc: rhs_perm(g, kc, pb), ex_)
                                    stg, ks = stgp.next()
                                    evac_copy(stg[:, :], ps[:, :], [kp], [ks])
                                    rows_ = slice(fcl * 128, (fcl + 1) * 128)
                                    if which == "q":
                                        P.dma("sp", aqT[g, rows_, pb * 512:(pb + 1) * 512], stg[:, :], reads=[ks])
                                    elif g == 0:
                                        P.dma("sp", akT[0][rows_, 0, 64 + pb * 512:64 + (pb + 1) * 512], stg[:, :],
                                              reads=[ks])
                                    elif g == 1:
                                        r, p0 = pb // 2, (pb % 2) * 512
                                        P.dma("sp", akT[1][rows_, r, 64 + p0:64 + p0 + 512], stg[:, :], reads=[ks])
                                    else:
                                        P.dma("sp", akT[2][rows_, 2 * pb:2 * pb + 2, 64:64 + 256],
                                              stg[:, :].rearrange("p (r c) -> p r c", r=2), reads=[ks])

                        def do_v(g, wb, kb):
                            d = GROUPS[g][1]
                            per = (T // d) // 128
                            for it in range(32):
                                ps, kp = psA.next()
                                for kc in range(8):
                                    rd_ = list(kb) + ([("uTp", kc)] if g > 0 else [])
                                    P.mm(ps[:, :], lhs_perm(g, kc, it), wb[:, kc, 0:512], start=(kc == 0), stop=(kc == 7),
                                         reads=rd_, writes=[kp])
                                vt, kv = vstp.next()
                                evac_copy(vt[:, :, 0:64], ps[:, :].rearrange("p (h e) -> p h e", h=8), [kp], [kv])
                                r, p0 = it // per, (it % per) * 128
                                P.dma("sp", avp[g][r, 64 + p0:64 + p0 + 128, :],
                                      vt[:, :, :].rearrange("p h c -> p (h c)"), reads=[kv])

                        def do_gate(dst, s_, wb, kb):
                            for fcl in range(4):
                                for tb in range(8):
                                    ps, kp = fm_chunk(wb, kb, fcl * 128, lambda kc: uT[:, kc, tb * 512:(tb + 1) * 512])
                                    stg, ks = stgp.next()
                                    j = s_ * 4 + fcl
                                    if dst is gT:
                                        P.act(stg[:, :], ps[:, :], AF.Sigmoid, bias=bgT[:, j:j + 1], reads=[kp],
                                              writes=[ks])
                                    else:
                                        P.act(stg[:, :], ps[:, :], AF.Silu, reads=[kp], writes=[ks])
                                    P.dma("sp", dst[j * 128:(j + 1) * 128, tb * 512:(tb + 1) * 512], stg[:, :],
                                          reads=[ks])

                        jobs = []
                        for g in range(3):
                            if g > 0:
                                jobs.append((None, 0, (lambda g_: (lambda wb, kb: permute_uT(GROUPS[g_][1])))(g)))
                            jobs.append((C_AQ + g * 512, 512, (lambda g_: (lambda wb, kb: do_qk("q", g_, wb, kb)))(g)))
                            jobs.append((C_AK + g * 512, 512, (lambda g_: (lambda wb, kb: do_qk("k", g_, wb, kb)))(g)))
                            jobs.append((C_AV + g * 512, 512, (lambda g_: (lambda wb, kb: do_v(g_, wb, kb)))(g)))
                        jobs.append((C_AZ, 512, lambda wb, kb: do_gate(sazT, 0, wb, kb)))
                        jobs.append((C_BZ, 512, lambda wb, kb: do_gate(sbzT, 0, wb, kb)))
                        for s_ in range(4):
                            jobs.append((C_GL + s_ * 512, 512, (lambda s2: (lambda wb, kb: do_gate(gT, s2, wb, kb)))(s_)))
                        strips = [j for j in jobs if j[0] is not None]
                        loaded = {}
                        nxt = [0]

                        def prefetch():
                            if nxt[0] < len(strips):
                                c0_, W_, _ = strips[nxt[0]]
                                loaded[nxt[0]] = load_strip(c0_, W_)
                                nxt[0] += 1

                        prefetch()
                        si = 0
                        for (c0_, W_, fn_) in jobs:
                            if c0_ is None:
                                fn_(None, None)
                                continue
                            wb, kb = loaded.pop(si)
                            si += 1
                            prefetch()
                            fn_(wb, kb)
                        xs_kb = xs("KB")
                        xs_vb = xs("VB").rearrange("r c -> (r c)").rearrange("(t f) -> t f", f=130)
                        wbq, kbq = load_strip(C_BQ, 512)
                        wbk, kbk = load_strip(C_BK, 256)
                        for fcl in range(5):
                            wb, kb, col0, gcol = (wbq, kbq, fcl * 128, 0) if fcl < 4 else (wbk, kbk, 0, 1)
                            for tb in range(8):
                                tsl = slice(tb * 512, (tb + 1) * 512)
                                cst, kc_ = csp.next()
                                P.dma("sp", cst[:, :, :], cstab_in[:, :, tsl].rearrange("c p t -> p c t"), writes=[kc_])
                                ps, kp = fm_chunk(wb, kb, col0, lambda kc: uT[:, kc, tsl])
                                sq, k1 = f32p.next()
                                P.act(sq[:, :], ps[:, :], AF.Square, reads=[kp], writes=[k1])
                                ps2, kp2 = psB.next()
                                P.mm(ps2[:, :], bones, sq[:, :], reads=[k1], writes=[kp2])
                                srt, k2a = f32p.next()
                                P.act(srt[:, :], ps2[:, :], AF.Sqrt, bias=epsc[:, 0:1], reads=[kp2], writes=[k2a])
                                rstd, k2 = f32p.next()
                                P.recip(rstd[:, :], srt[:, :], reads=[k2a], writes=[k2])
                                xn, k3 = f32p.next()
                                P.stt(xn[:, :], ps[:, :], qkg[:, gcol:gcol + 1], rstd[:, :], ALU.mult, ALU.mult,
                                      reads=[kp, k2], writes=[k3])
                                ps3, kp3 = psB.next()
                                P.mm(ps3[:, :], rrot, xn[:, :], reads=[k3], writes=[kp3])
                                ta, k4 = f32p.next()
                                P.tt("pool", ta[:, :], xn[:, :], cst[:, 0, :], ALU.mult, reads=[k3, kc_], writes=[k4])
                                tb_, k5 = f32p.next()
                                P.tt("dve", tb_[:, :], ps3[:, :], cst[:, 1, :], ALU.mult, reads=[kp3, kc_], writes=[k5])
                                stg, ks = stgp.next()
                                P.tt("pool", stg[:, :], ta[:, :], tb_[:, :], ALU.add, reads=[k4, k5], writes=[ks])
                                if fcl < 4:
                                    P.dma("sp", qbT[fcl * 128:(fcl + 1) * 128, tsl], stg[:, :], reads=[ks])
                                else:
                                    P.dma("sp", xs_kb[:, tsl], stg[:, :], reads=[ks])
                        for it in range(32):
                            ps, kp = psA.next()
                            for kc in range(8):
                                P.mm(ps[:, 0:128], uT[:, kc, it * 128:(it + 1) * 128], wbk[:, kc, 128:256],
                                     start=(kc == 0), stop=(kc == 7), reads=kbk, writes=[kp])
                            vt, kv = vstp.next()
                            evac_copy(vt[:, 0:2, 0:64], ps[:, 0:128].rearrange("p (h e) -> p h e", h=2), [kp], [kv])
                            P.dma("sp", xs_vb[it * 128:(it + 1) * 128, :],
                                  vt[:, 0:2, :].rearrange("p h c -> p (h c)"), reads=[kv])
                        P.barrier()
                        P.flush()

                if debug and debug.get("upto") == "P1":
                    break

                with ExitStack() as es:
                    vh = TPool([sb(es, f"vh{l}_{i}", [64, 16, 520], BF16) for i in range(2)], "vh")
                    for g, (_, d) in enumerate(GROUPS):
                        L = T // d
                        for (kk, c0) in (("AKF", 64), ("AKL", L)):
                            sec = xs((kk, g)).rearrange("r c -> (r c)").rearrange(
                                "(f r c) -> f r c", r=d, c=64)
                            for fb in range(4):
                                P.dma("sp", sec[fb * 128:(fb + 1) * 128], akT[g][fb * 128:(fb + 1) * 128, :, c0:c0 + 64])
                        for (kk, c0) in (("AVF", 64), ("AVL", L)):
                            nel = d * 64 * 520
                            sec = xs((kk, g)).rearrange("r c -> (r c)")[0:nel].rearrange(
                                "(r p f) -> r p f", p=64, f=520)
                            P.dma("sp", sec, avp[g][:, c0:c0 + 64, :])
                    P.barrier()
                    for u in range(len(XUNITS)):
                        P.collective((lambda a, b: (lambda e: e.collective_compute(
                            "AllGather", ALU.bypass, replica_groups=[[0, 1], [2, 3], [4, 5], [6, 7]],
                            ins=[a], outs=[b])))(xsrc_u[u], xdst_u[u]))
                    P.barrier()
                    for g, (_, d) in enumerate(GROUPS):
                        L = T // d
                        for (kk, rb, c0) in (("AKL", 0, 0), ("AKF", 1, 64 + L)):
                            sec = xd((kk, g), rb).rearrange("r c -> (r c)").rearrange(
                                "(f r c) -> f r c", r=d, c=64)
                            for fb in range(4):
                                P.dma("sp", akT[g][fb * 128:(fb + 1) * 128, :, c0:c0 + 64], sec[fb * 128:(fb + 1) * 128])
                        for (kk, rb, c0, mc) in (("AVL", 0, 0, 0), ("AVF", 1, 64 + L, 1)):
                            nel = d * 64 * 520
                            sec = xd((kk, g), rb).rearrange("r c -> (r c)")[0:nel].rearrange(
                                "(r p f) -> p r f", p=64, f=520)
                            t_, kt = vh.next()
                            P.dma("sp", t_[:, 0:d, :], sec, writes=[kt])
                            P.ts("dve", t_[:, 0:d, :], t_[:, 0:d, :], msk[0:64, mc:mc + 1], None, ALU.mult,
                                 reads=[kt, "msk"], writes=[kt])
                            P.dma("sp", avp[g][:, c0:c0 + 64, :].rearrange("r p f -> p r f"), t_[:, 0:d, :], reads=[kt])
                    P.barrier()
                    P.flush()

                if debug and debug.get("upto") == "X":
                    break

                with ExitStack() as es:
                    E = sb(es, f"E{l}", [128, 24, 256], F32)
                    acc = sb(es, f"acc{l}", [65, 4, T], F32)
                    vaug = sb(es, f"vaug{l}", [128, 48, 4, 65], BF16)
                    ktp = TPool([sb(es, f"kt{l}_{i}", [64, 6144], BF16) for i in range(2)], "kt")
                    qtp = TPool([sb(es, f"qt{l}_{i}", [64, T], BF16) for i in range(2)], "qt")
                    exp_ = TPool([sb(es, f"ex{l}_{i}", [128, 512], F32) for i in range(4)], "ex")
                    ptp = TPool([sb(es, f"pt{l}_{i}", [128, 512], BF16) for i in range(5)], "pt")
                    rdp = TPool([sb(es, f"rd{l}_{i}", [65, 512], F32) for i in range(2)], "rd")
                    nmp = TPool([sb(es, f"nm{l}_{i}", [64, 512], F32) for i in range(2)], "nm")
                    szp = TPool([sb(es, f"sz{l}_{i}", [64, 512], BF16) for i in range(2)], "sz")
                    ysp = TPool([sb(es, f"ys{l}_{i}", [64, 512], BF16) for i in range(2)], "ys")
                    psS = pspool([0, 1, 2, 3], "ps")
                    psO = pspool([4, 5, 6], "ps")
                    psN = pspool([7], "ps")
                    P.dma("sp", E[:, :, :], Edram.rearrange("p (g c) -> p g c", c=256), writes=["E"])
                    for hh in range(2):
                        for g, (_, d) in enumerate(GROUPS):
                            L = T // d
                            nch = L // 128 + 1
                            vsrc = avp[g].rearrange("r (m p) (h c) -> p (r m) h c", p=128, c=65)
                            for c0 in range(0, d * nch, 12):
                                c1 = min(d * nch, c0 + 12)
                                P.dma("sp", vaug[:, c0:c1, :, :], vsrc[:, c0:c1, hh * 4:(hh + 1) * 4, :],
                                      writes=[("vaug", c0)])
                            vkeys = [("vaug", c0) for c0 in range(0, d * nch, 12)]
                            for hl in range(4):
                                h = hh * 4 + hl
                                kt, kk = ktp.next()
                                P.dma("sp", kt[:, 0:d * (L + 128)],
                                      akT[g][h * 64:(h + 1) * 64, :, :].rearrange("p r c -> p (r c)"), writes=[kk])
                                qt, kq = qtp.next()
                                P.dma("sp", qt[:, :], aqT[g, h * 64:(h + 1) * 64, :], writes=[kq])
                                blocks = [(r, n2) for r in range(d) for n2 in range(L // 256)]
                                st = {}
                                e0 = E[:, g * 8 + h, :]
                                ebc = bass.AP(e0.tensor, e0.offset, [list(e0.ap[0]), [0, 2], list(e0.ap[1])])

                                def stageA(i):
                                    r, n2 = blocks[i]
                                    n = 2 * n2
                                    ps, kp = psS.next()
                                    kb0 = r * (L + 128) + 128 * n
                                    qb0 = r * L + 128 * n
                                    P.mm(ps[:, 0:128], kt[:, kb0:kb0 + 128], qt[:, qb0:qb0 + 128],
                                         reads=[kk, kq], writes=[kp])
                                    P.mm(ps[:, 128:384], kt[:, kb0 + 128:kb0 + 256], qt[:, qb0:qb0 + 256],
                                         reads=[kk, kq], writes=[kp])
                                    P.mm(ps[:, 384:512], kt[:, kb0 + 256:kb0 + 384], qt[:, qb0 + 128:qb0 + 256],
                                         reads=[kk, kq], writes=[kp])
                                    ex, ke = exp_.next()
                                    P.act(ex[:, :], ps[:, :], AF.Exp, scale=0.125, reads=[kp], writes=[ke])
                                    pt, kpt = ptp.next()
                                    P.tt("pool", pt[:, :].rearrange("p (a b) -> p a b", a=2),
                                         ex[:, :].rearrange("p (a b) -> p a b", a=2), ebc, ALU.mult,
                                         reads=[ke, "E"], writes=[kpt])
                                    st[i] = (pt, kpt)

                                def stageB(i):
                                    r, n2 = blocks[i]
                                    n = 2 * n2
                                    pt, kpt = st.pop(i)
                                    po, ko = psO.next()
                                    ch = r * nch + n
                                    vk = lambda c_: [("vaug", (c_ // 12) * 12)]
                                    P.mm(po[0:65, 0:256], vaug[:, ch + 1, hl, :], pt[:, 128:384], start=True, stop=False,
                                         reads=[kpt] + vk(ch + 1), writes=[ko])
                                    P.mm(po[0:65, 0:128], vaug[:, ch, hl, :], pt[:, 0:128], start=False, stop=False,
                                         reads=[kpt] + vk(ch), writes=[ko])
                                    P.mm(po[0:65, 128:256], vaug[:, ch + 2, hl, :], pt[:, 384:512], start=False, stop=True,
                                         reads=[kpt] + vk(ch + 2), writes=[ko])
                                    t0 = r + d * 128 * n
                                    av_ = acc[:, hl, t0:t0 + d * 255 + 1:d]
                                    if g == 0:
                                        P.copy("act", av_, po[0:65, 0:256], reads=[ko], writes=[("acc", hl)])
                                    else:
                                        P.tt("dve", av_, av_, po[0:65, 0:256], ALU.add, reads=[ko, ("acc", hl)],
                                             writes=[("acc", hl)])

                                nb = len(blocks)
                                LK = 3
                                for i in range(nb + LK):
                                    if i < nb:
                                        stageA(i)
                                    if i >= LK:
                                        stageB(i - LK)
                        for hl in range(4):
                            h = hh * 4 + hl
                            for tb in range(8):
                                tsl = slice(tb * 512, (tb + 1) * 512)
                                sz, ksz = szp.next()
                                P.dma("sp", sz[:, :], sazT[h * 64:(h + 1) * 64, tsl], writes=[ksz])
                                rd, krd = rdp.next()
                                P.recip(rd[64:65, :], acc[64:65, hl, tsl], reads=[("acc", hl)], writes=[krd])
                                pb_, kpb = psN.next()
                                P.mm(pb_[0:64, :], ones_f[64:65, 0:64], rd[64:65, :], reads=[krd, "ones"], writes=[kpb])
                                nm, knm = nmp.next()
                                P.tt("dve", nm[:, :], acc[0:64, hl, tsl], pb_[0:64, :], ALU.mult,
                                     reads=[("acc", hl), kpb], writes=[knm])
                                ys, kys = ysp.next()
                                P.tt("pool", ys[:, :], nm[:, :], sz[:, :], ALU.mult, reads=[knm, ksz], writes=[kys])
                                P.dma("pool", yaT[h * 64:(h + 1) * 64, tsl], ys[:, :], reads=[kys])
                    P.barrier()
                    P.flush()

                if debug and debug.get("upto") == "P2":
                    break

                with ExitStack() as es:
                    kTd = sb(es, f"kTd{l}", [128, 2, S], BF16)
                    vb = sb(es, f"vb{l}", [128, 64, 130], BF16)
                    qT = sb(es, f"qT{l}", [128, 4, T], BF16)
                    ptp = TPool([sb(es, f"pB{l}_{i}", [128, 1024], BF16) for i in range(4)], "pB")
                    bcp = TPool([sb(es, f"bcB{l}_{i}", [64, 512], F32) for i in range(3)], "bcB")
                    evp = TPool([sb(es, f"evB{l}_{i}", [65, 512], F32) for i in range(3)], "evB")
                    rdp = TPool([sb(es, f"rdB{l}_{i}", [65, 512], F32) for i in range(2)], "rdB")
                    nm2 = TPool([sb(es, f"nmC{l}_{i}", [64, 512], F32) for i in range(2)], "nmC")
                    szp = TPool([sb(es, f"szB{l}_{i}", [64, 512], BF16) for i in range(3)], "szB")
                    ysp = TPool([sb(es, f"ysB{l}_{i}", [64, 512], BF16) for i in range(3)], "ysB")
                    psA2 = [(PSALL[:, 2 * j_:2 * j_ + 2, :].rearrange("p a b -> p (a b)"),
                             [("ps", 2 * j_), ("ps", 2 * j_ + 1)]) for j_ in range(3)]
                    psO = pspool([6, 7], "ps")
                    for rk in range(2):
                        kb_ = xd("KB", rk)
                        for kvh in range(2):
                            for half in range(2):
                                P.dma("sp", kTd[half * 64:(half + 1) * 64, kvh, rk * T:(rk + 1) * T],
                                      kb_[kvh * 64:(kvh + 1) * 64, :], writes=[("kTd", rk, kvh, half)])
                        vsec = xd("VB", rk).rearrange("r c -> (r c)").rearrange("(k p f) -> p k f", p=128, f=130)
                        for k0 in range(0, 32, 8):
                            P.dma("sp", vb[:, rk * 32 + k0:rk * 32 + k0 + 8, :], vsec[:, k0:k0 + 8, :],
                                  writes=[("vb", rk, k0)])
                    for c_ in range(4):
                        P.dma("sp", qT[:, c_, :], qbT[c_ * 128:(c_ + 1) * 128, :], writes=[("qT", c_)])
                    steps = [(qb, hp, kc) for qb in range(8) for hp in range(4) for kc in range(64)]
                    st = {}
                    acc_o = {}

                    def stageA(i):
                        qb, hp, kc = steps[i]
                        kvh = hp // 2
                        rk = kc // 32
                        psa, keys = psA2[i % 3]
                        for hh_ in range(2):
                            pr = hh_ * 64
                            P.mm(psa[:, hh_ * 512:(hh_ + 1) * 512], kTd[pr:pr + 64, kvh, kc * 128:(kc + 1) * 128],
                                 qT[pr:pr + 64, hp, qb * 512:(qb + 1) * 512],
                                 reads=[("kTd", rk, kvh, hh_), ("qT", hp)], writes=keys)
                        pt, kpt = ptp.next()
                        P.act(pt[:, :], psa, AF.Exp, scale=0.125, reads=keys, writes=[kpt])
                        st[i] = (pt, kpt)

                    def stageB(i):
                        qb, hp, kc = steps[i]
                        kvh = hp // 2
                        pt, kpt = st.pop(i)
                        if kc == 0:
                            acc_o[(qb, hp)] = [psO.next(), psO.next()]
                        for hh_ in range(2):
                            po, ko = acc_o[(qb, hp)][hh_]
                            P.mm(po[0:65, :], vb[:, kc, kvh * 65:(kvh + 1) * 65], pt[:, hh_ * 512:(hh_ + 1) * 512],
                                 start=(kc == 0), stop=(kc == 63),
                                 reads=[kpt, ("vb", kc // 32, ((kc % 32) // 8) * 8)], writes=[ko])
                        if kc == 63:
                            tsl = slice(qb * 512, (qb + 1) * 512)
                            for hh_ in range(2):
                                h = 2 * hp + hh_
                                po, ko = acc_o[(qb, hp)][hh_]
                                ev, kev = evp.next()
                                P.copy("dve", ev[:, :], po[0:65, :], reads=[ko], writes=[kev])
                                sz, ksz = szp.next()
                                P.dma("sp", sz[:, :], sbzT[h * 64:(h + 1) * 64, tsl], writes=[ksz])
                                rd, krd = rdp.next()
                                P.recip(rd[64:65, :], ev[64:65, :], reads=[kev], writes=[krd])
                                bc, kbc = bcp.next()
                                sl_ = rdslot[0] % 8
                                rdslot[0] += 1
                                P.dma("sp", rdscr[sl_:sl_ + 1, :], rd[64:65, :], reads=[krd], writes=[("rdscr", sl_)])
                                P.dma("sp", bc[:, :], bass.AP(rdscr.tensor, sl_ * 512, [[0, 64], [1, 512]]),
                                      reads=[("rdscr", sl_)], writes=[kbc])
                                n2, kn2 = nm2.next()
                                P.tt("dve", n2[:, :], ev[0:64, :], bc[:, :], ALU.mult, reads=[kev, kbc], writes=[kn2])
                                ys, kys = ysp.next()
                                P.tt("pool", ys[:, :], n2[:, :], sz[:, :], ALU.mult, reads=[kn2, ksz], writes=[kys])
                                P.dma("pool", ybT[h * 64:(h + 1) * 64, tsl], ys[:, :], reads=[kys])

                    LOOK = 2
                    ns = len(steps)
                    for i in range(ns + LOOK):
                        if i < ns:
                            stageA(i)
                        if i >= LOOK:
                            stageB(i - LOOK)
                    P.barrier()
                    P.flush()

                if debug and debug.get("upto") == "P3":
                    break

                with ExitStack() as es:
                    wpa = sb(es, f"wpa{l}", [128, 4, D], BF16)
                    wpb = sb(es, f"wpb{l}", [128, 4, D], BF16)
                    wo = sb(es, f"wo{l}", [128, 8, D], BF16)
                    wf = TPool([sb(es, f"wf{l}_{i}", [128, 8, 256], F32) for i in range(2)], "wf")
                    for (wdst, wsrc, nk_) in ((wpa, w_pa_in, 4), (wpb, w_pb_in, 4), (wo, w_o_in, 8)):
                        for nh in range(4):
                            wt, kw = wf.next()
                            P.dma("sp", wt[:, 0:nk_, :],
                                  wsrc[l, :, nh * 256:(nh + 1) * 256].rearrange("(k p) n -> p k n", p=128), writes=[kw])
                            P.copy("pool", wdst[:, :, nh * 256:(nh + 1) * 256], wt[:, 0:nk_, :], reads=[kw],
                                   writes=["wP4"])
                    yap = TPool([sb(es, f"ya{l}_{i}", [128, 4, 512], BF16) for i in range(2)], "ya")
                    ybp = TPool([sb(es, f"yb{l}_{i}", [128, 4, 512], BF16) for i in range(2)], "yb")
                    gp = TPool([sb(es, f"gg{l}_{i}", [128, 16, 512], BF16) for i in range(2)], "gg")
                    mtp = TPool([sb(es, f"mT{l}_{i}", [128, 8, 512], BF16) for i in range(2)], "mT")
                    t1p = TPool([sb(es, f"t1{l}_{i}", [128, 512], F32) for i in range(2)], "t1")
                    t2p = TPool([sb(es, f"t2{l}_{i}", [128, 512], F32) for i in range(2)], "t2")
                    xrp = TPool([sb(es, f"xr{l}_{i}", [128, D], F32) for i in range(3)], "xr")
                    hp = TPool([sb(es, f"hh{l}_{i}", [128, D], F32) for i in range(2)], "hh")
                    op_ = TPool([sb(es, f"oo{l}_{i}", [128, D], F32) for i in range(2)], "oo")
                    stp = TPool([sb(es, f"bst{l}_{i}", [128, 16], F32) for i in range(2)], "bst")
                    psA = pspool([0, 1, 2, 3], "ps")
                    psB = pspool([4, 5, 6, 7], "ps")
                    for tb in range(8):
                        tsl = slice(tb * 512, (tb + 1) * 512)
                        ya, kya = yap.next()
                        yb, kyb = ybp.next()
                        gg, kgg = gp.next()
                        P.dma("sp", ya[:, :, :], yaT[:, tsl].rearrange("(h p) t -> p h t", p=128), writes=[kya])
                        P.dma("sp", yb[:, :, :], ybT[:, tsl].rearrange("(h p) t -> p h t", p=128), writes=[kyb])
                        P.dma("sp", gg[:, :, :], gT[:, tsl].rearrange("(j p) t -> p j t", p=128), writes=[kgg])
                        mT, kmT = mtp.next()
                        for m in range(8):
                            pa, kpa = psA.next()
                            for h in range(4):
                                P.mm(pa[:, :], wpa[:, h, m * 128:(m + 1) * 128], ya[:, h, :], start=(h == 0),
                                     stop=(h == 3), reads=["wP4", kya], writes=[kpa])
                            pb_, kpb = psA.next()
                            for h in range(4):
                                P.mm(pb_[:, :], wpb[:, h, m * 128:(m + 1) * 128], yb[:, h, :], start=(h == 0),
                                     stop=(h == 3), reads=["wP4", kyb], writes=[kpb])
                            t1, k1 = t1p.next()
                            P.tt("dve", t1[:, :], pa[:, :], gg[:, m, :], ALU.mult, reads=[kpa, kgg], writes=[k1])
                            t2, k2 = t2p.next()
                            P.tt("dve", t2[:, :], pb_[:, :], gg[:, 8 + m, :], ALU.mult, reads=[kpb, kgg], writes=[k2])
                            P.tt("pool", mT[:, m, :], t1[:, :], t2[:, :], ALU.add, reads=[k1, k2], writes=[(kmT, m)])
                        for tt_ in range(4):
                            r0 = tb * 512 + tt_ * 128
                            xr, kxr = xrp.next()
                            P.dma("sp", xr[:, :], xsrc_l[r0:r0 + 128, :], writes=[kxr])
                            hb, khb = hp.next()
                            for nh in range(2):
                                po, ko = psB.next()
                                for kc in range(8):
                                    P.mm(po[:, :], mT[:, kc, tt_ * 128:(tt_ + 1) * 128], wo[:, kc, nh * 512:(nh + 1) * 512],
                                         start=(kc == 0), stop=(kc == 7), reads=["wP4"] + [(kmT, m) for m in range(8)],
                                         writes=[ko])
                                P.tt("dve", hb[:, nh * 512:(nh + 1) * 512], po[:, :], gate_b[:, nh * 512:(nh + 1) * 512],
                                     ALU.mult, reads=[ko], writes=[(khb, nh)])
                            P.stt(hb[:, :], xr[:, :], ALPHA, hb[:, :], ALU.mult, ALU.add,
                                  reads=[kxr, (khb, 0), (khb, 1)], writes=[(khb, 0), (khb, 1)])
                            bs, kbs = stp.next()
                            for nh in range(2):
                                P.op("dve", (lambda o, i_: (lambda e: e.bn_stats(o, i_)))(
                                    bs[:, nh * 6:(nh + 1) * 6], hb[:, nh * 512:(nh + 1) * 512]),
                                    reads=[(khb, 0), (khb, 1)], writes=[(kbs, nh)])
                            P.op("dve", (lambda o, i_: (lambda e: e.bn_aggr(o, i_)))(bs[:, 12:14], bs[:, 0:12]),
                                 reads=[(kbs, 0), (kbs, 1)], writes=[(kbs, 2)])
                            P.act(bs[:, 15:16], bs[:, 13:14], AF.Sqrt, bias=epsc[:, 1:2], reads=[(kbs, 2)],
                                  writes=[(kbs, 4)])
                            P.recip(bs[:, 14:15], bs[:, 15:16], reads=[(kbs, 4)], writes=[(kbs, 3)])
                            ob, kob = op_.next()
                            P.ts("dve", ob[:, :], hb[:, :], bs[:, 12:13], bs[:, 14:15], ALU.subtract, ALU.mult,
                                 reads=[(khb, 0), (khb, 1), (kbs, 2), (kbs, 3)], writes=[kob])
                            P.tt("pool", ob[:, :], ob[:, :], lng_b[:, :], ALU.mult, reads=[kob], writes=[kob])
                            P.tt("pool", ob[:, :], ob[:, :], lnb_b[:, :], ALU.add, reads=[kob], writes=[kob])
                            P.dma("pool", ydst_l[r0:r0 + 128, :], ob[:, :], reads=[kob])
                    P.barrier()
                    P.flush()
        P.barrier()
        P.flush()
    return nc, dbg_names


def _t5_bucket_np(rel):
    half, max_exact = 16, 8
    ret = np.where(rel > 0, half, 0)
    a = np.abs(rel)
    af = np.maximum(a, 1).astype(np.float32)
    large = max_exact + (np.log(af / np.float32(max_exact)) / np.float32(math.log(1024 / max_exact))
                         * np.float32(half - max_exact)).astype(np.int32)
    large = np.minimum(large, half - 1)
    return ret + np.where(a < max_exact, a, large)


def _host_consts(rel_table):
    ident = np.eye(128, dtype=np.float32)
    bones = np.zeros((128, 128), np.float32)
    bones[:64, :64] = 1.0 / 64
    bones[64:, 64:] = 1.0 / 64
    rrot = np.zeros((128, 128), np.float32)
    for base in range(0, 128, 32):
        for e in range(16):
            rrot[base + e + 16, base + e] = -1.0
            rrot[base + e, base + e + 16] = 1.0
    cmat = np.stack([ident, bones, rrot])
    i = np.arange(128)[:, None]
    j = np.arange(128)[None, :]
    rel0 = i - 64 - j
    rel1 = i + 64 - j
    tri0 = np.concatenate([(i >= j), (i <= j)], axis=1).astype(np.float32)
    tri = np.stack([tri0, (tri0 - 1.0) * 30000.0]).astype(np.float32)
    ab = np.zeros((128, 24, 256), np.float32)
    for g, (_, d) in enumerate(GROUPS):
        for c, rel in enumerate((rel0, rel1)):
            bk = _t5_bucket_np(np.clip(rel, -64, 64) * d)
            ab[:, g * 8:(g + 1) * 8, c * 128:(c + 1) * 128] = rel_table[bk][:, :, g * 8:(g + 1) * 8].transpose(0, 2, 1)
    return cmat, tri, ab.reshape(128, 24 * 256)


def _cs_tables(half):
    t = np.arange(half * T, (half + 1) * T)
    row = (t // 64).astype(np.float32)
    col = (t % 64).astype(np.float32)
    inv = (np.float32(10000.0) ** (-np.arange(0, 32, 2, dtype=np.float32) / np.float32(32))).astype(np.float32)
    ar = (row[:, None] * inv[None]).astype(np.float32)
    ac = (col[:, None] * inv[None]).astype(np.float32)
    ang = np.zeros((128, T), np.float32)
    for p in range(128):
        e = p % 64
        ang[p] = ar[:, e % 16] if e < 32 else ac[:, (e - 32) % 16]
    return np.stack([np.cos(ang), np.sin(ang)]).astype(np.float32)


def _in_maps(inputs):
    f = lambda a: np.ascontiguousarray(np.asarray(a, dtype=np.float32))
    x = f(inputs["x"]); c = f(inputs["c"])
    cmat, tri, ab = _host_consts(f(inputs["rel_table"]))
    bgT = np.ascontiguousarray(f(inputs["b_gate"]).reshape(DEPTH, 16, 128).transpose(0, 2, 1))
    qg = f(inputs["q_norm_g"]); kg = f(inputs["k_norm_g"])
    qkg = np.ascontiguousarray(np.stack([np.tile(qg, (1, 2)), np.tile(kg, (1, 2))], axis=-1))
    shared = {
        "ln_g": f(inputs["ln_g"]).reshape(DEPTH, 1, D), "ln_b": f(inputs["ln_b"]).reshape(DEPTH, 1, D),
        "w_ada": f(inputs["w_ada"]), "b_ada": f(inputs["b_ada"]).reshape(DEPTH, 1, 3 * D),
        "w_in": f(inputs["w_in"]), "bgT": bgT, "qkg": qkg, "w_pa": f(inputs["w_pa"]), "w_pb": f(inputs["w_pb"]),
        "w_o": f(inputs["w_o"]), "cmat": cmat, "abias": ab, "tri": tri,
    }
    cs = [_cs_tables(0), _cs_tables(1)]
    maps = []
    for core in range(8):
        b, hf = core // 2, core % 2
        m = dict(shared)
        m["x"] = np.ascontiguousarray(x[b, hf * T:(hf + 1) * T])
        m["cT"] = np.ascontiguousarray(c[b].reshape(8, 128).T)
        m["cstab"] = cs[hf]
        mk = np.ones((128, 2), np.float32)
        mk[:, 0] = 0.0 if hf == 0 else 1.0
        mk[:, 1] = 1.0 if hf == 0 else 0.0
        m["msk"] = mk
        maps.append(m)
    return maps


def kernel(**inputs):
    nc, _ = _build()
    maps = _in_maps(inputs)
    res = run_bass_kernel_spmd(nc, maps, core_ids=list(range(8)))
    out = np.empty((4, S, D), np.float32)
    for core in range(8):
        b, hf = core // 2, core % 2
        out[b, hf * T:(hf + 1) * T] = res.results[core]["y"]
    return out
```

```python
import math
from contextlib import ExitStack

import numpy as np
import concourse.bass as bass
import concourse.mybir as mybir
from concourse.bass_utils import run_bass_kernel_spmd

F32 = mybir.dt.float32
BF16 = mybir.dt.bfloat16
AF = mybir.ActivationFunctionType
ALU = mybir.AluOpType

D = 1024
T = 4096
S = 8192
DEPTH = 2
GROUPS = ((128, 1), (512, 4), (2048, 16))
ALPHA = float((2 * DEPTH) ** 0.25)
LN_EPS = 1e-5
QK_EPS = 1e-6
C_AQ, C_AK, C_AV, C_AZ, C_BQ, C_BK, C_BV, C_BZ, C_GL = 0, 1536, 3072, 4608, 5120, 5632, 5760, 5888, 6400
XCOLS = 4096


def _xlayout():
    units = []
    sec = {}

    def add_unit(items):
        u = len(units)
        r = 0
        for k, n in items:
            sec[k] = (u, r, n)
            r += n
        units.append(r)

    add_unit([("KB", 128)])
    add_unit([("VB", 130)])
    for g, (_, d) in enumerate(GROUPS):
        nk = 8 * d
        nv = -(-(d * 64 * 520) // XCOLS)
        items = [(("AKF", g), nk), (("AKL", g), nk), (("AVF", g), nv), (("AVL", g), nv)]
        if nk + nk + nv + nv <= 130:
            add_unit(items)
        else:
            for it in items:
                add_unit([it])
    return units, sec


XUNITS, XSEC = _xlayout()


class Prog:
    CE = ("pe", "act", "dve", "pool")
    ALLE = ("pe", "act", "dve", "pool", "sp")
    NDS = 8

    def __init__(self, nc, es):
        self.nc = nc
        self.sem = {}
        for e in self.CE:
            self.sem[("c", e)] = es.enter_context(nc.semaphore(f"c_{e}"))
        for q in ("sp", "act", "pool"):
            for i in range(self.NDS):
                self.sem[("d", q, i)] = es.enter_context(nc.semaphore(f"d_{q}{i}"))
        self.sem[("cc",)] = es.enter_context(nc.semaphore("ccsem"))
        self.cnt = {k: 0 for k in self.sem}
        self.dnext = {q: 0 for q in ("sp", "act", "pool")}
        self.ops = {e: [] for e in self.ALLE}
        self.waited = {e: {} for e in self.ALLE}
        self.lastw = {}
        self.readers = {}
        self.nops = 0

    def _deps(self, reads, writes):
        deps = set()
        for k in reads:
            if k in self.lastw:
                deps.add(self.lastw[k])
        for k in writes:
            if k in self.lastw:
                deps.add(self.lastw[k])
            deps.update(self.readers.get(k, ()))
        return deps

    def _commit(self, ticket, reads, writes):
        for k in reads:
            self.readers.setdefault(k, []).append(ticket)
        for k in writes:
            self.lastw[k] = ticket
            self.readers[k] = []

    def _waits(self, eng, deps):
        w = []
        for (sk, val) in sorted(deps, key=lambda t: (str(t[0]), t[1])):
            if eng == "pe" and sk == ("c", "pe"):
                continue
            if self.waited[eng].get(sk, 0) >= val:
                continue
            self.waited[eng][sk] = val
            w.append((self.sem[sk], val))
        return w

    def op(self, eng, fn, reads=(), writes=()):
        deps = self._deps(reads, writes)
        w = self._waits(eng, deps)
        sk = ("c", eng)
        self.cnt[sk] += 1
        t = (sk, self.cnt[sk])
        self.ops[eng].append((w, fn, self.sem[sk], 1))
        self._commit(t, reads, writes)
        self.nops += 1
        return t

    def dma(self, q, out, in_, reads=(), writes=()):
        deps = self._deps(reads, writes)
        slot = self.dnext[q] % self.NDS
        self.dnext[q] += 1
        sk = ("d", q, slot)
        if self.cnt[sk] > 0:
            deps.add((sk, self.cnt[sk]))
        w = self._waits(q, deps)
        self.cnt[sk] += 16
        t = (sk, self.cnt[sk])
        self.ops[q].append((w, lambda e: e.dma_start(out=out, in_=in_), self.sem[sk], 16))
        self._commit(t, reads, writes)
        self.nops += 1
        return t

    def collective(self, fn, reads=(), writes=()):
        deps = self._deps(reads, writes)
        w = self._waits("pool", deps)
        sk = ("cc",)
        self.cnt[sk] += 1
        t = (sk, self.cnt[sk])
        self.ops["pool"].append((w, fn, self.sem[sk], 1))
        self._commit(t, reads, writes)
        return t

    def barrier(self):
        tickets = set((sk, v) for sk, v in self.cnt.items() if v > 0)
        for e in self.ALLE:
            w = self._waits(e, tickets)
            if w:
                self.ops[e].append((w, None, None, 0))
        self.lastw = {}
        self.readers = {}

    def flush(self):
        nc = self.nc
        ops = self.ops

        def replay(lst, e):
            for (w, fn, sem, inc) in lst:
                for (s, v) in w:
                    e.wait_ge(s, v)
                if fn is not None:
                    ins = fn(e)
                    ins.then_inc(sem, inc)

        with nc.Block() as block:
            @block.tensor
            def _(e):
                replay(ops["pe"], e)

            @block.scalar
            def _(e):
                replay(ops["act"], e)

            @block.vector
            def _(e):
                replay(ops["dve"], e)

            @block.gpsimd
            def _(e):
                replay(ops["pool"], e)

            @block.sync
            def _(e):
                replay(ops["sp"], e)
        self.ops = {e: [] for e in self.ALLE}

    def mm(self, out, lhsT, rhs, start=True, stop=True, reads=(), writes=()):
        return self.op("pe", lambda e: e.matmul(out, lhsT, rhs, start=start, stop=stop), reads, writes)

    def tr(self, out, in_, ident, reads=(), writes=()):
        return self.op("pe", lambda e: e.transpose(out, in_, ident), reads, writes)

    def act(self, out, in_, func, bias=None, scale=None, reads=(), writes=(), eng="act"):
        kw = {}
        if bias is not None:
            kw["bias"] = bias
        if scale is not None:
            kw["scale"] = scale
        return self.op(eng, lambda e: e.activation(out, in_, func, **kw), reads, writes)

    def tt(self, eng, out, in0, in1, op, reads=(), writes=()):
        return self.op(eng, lambda e: e.tensor_tensor(out, in0, in1, op), reads, writes)

    def ts(self, eng, out, in0, s1, s2, op0, op1=None, reads=(), writes=()):
        if op1 is None:
            return self.op(eng, lambda e: e.tensor_scalar(out, in0, s1, None, op0), reads, writes)
        return self.op(eng, lambda e: e.tensor_scalar(out, in0, s1, s2, op0, op1), reads, writes)

    def stt(self, out, in0, scalar, in1, op0, op1, reads=(), writes=()):
        return self.op("dve", lambda e: e.scalar_tensor_tensor(out, in0, scalar, in1, op0, op1), reads, writes)

    def copy(self, eng, out, in_, reads=(), writes=()):
        if eng == "act":
            return self.op("act", lambda e: e.activation(out, in_, AF.Copy), reads, writes)
        return self.op(eng, lambda e: e.tensor_copy(out, in_), reads, writes)

    def memset(self, eng, ap, val, reads=(), writes=()):
        return self.op(eng, lambda e: e.memset(ap, val), reads, writes)

    def recip(self, out, in_, reads=(), writes=()):
        return self.op("dve", lambda e: e.reciprocal(out, in_), reads, writes)


class TPool:
    def __init__(self, tiles, name):
        self.tiles = tiles
        self.name = name
        self.i = 0

    def next(self):
        j = self.i % len(self.tiles)
        self.i += 1
        return self.tiles[j], (self.name, j)


def _build(debug=None):
    nc = bass.Bass("TRN2", target_bir_lowering=False)
    dbg_kind = "ExternalOutput" if debug else "Internal"

    def din(name, shape, dt=F32):
        return nc.dram_tensor(name, list(shape), dt, kind="ExternalInput").ap()

    def dscr(name, shape, dt=BF16, dbg=False):
        isdbg = bool(debug) and dbg and name in debug.get("outs", ())
        return nc.dram_tensor(name, list(shape), dt, kind=("ExternalOutput" if isdbg else "Internal")).ap()

    x_in = din("x", [T, D])
    cT_in = din("cT", [128, 8])
    ln_g_in = din("ln_g", [DEPTH, 1, D])
    ln_b_in = din("ln_b", [DEPTH, 1, D])
    w_ada_in = din("w_ada", [DEPTH, D, 3 * D])
    b_ada_in = din("b_ada", [DEPTH, 1, 3 * D])
    w_in_in = din("w_in", [DEPTH, D, 8448])
    bgT_in = din("bgT", [DEPTH, 128, 16])
    qkg_in = din("qkg", [DEPTH, 128, 2])
    w_pa_in = din("w_pa", [DEPTH, 512, D])
    w_pb_in = din("w_pb", [DEPTH, 512, D])
    w_o_in = din("w_o", [DEPTH, D, D])
    cmat_in = din("cmat", [3, 128, 128])
    cstab_in = din("cstab", [2, 128, T])
    abias_in = din("abias", [128, 24 * 256])
    tri_in = din("tri", [2, 128, 256])
    msk_in = din("msk", [128, 2])
    y_out = nc.dram_tensor("y", [T, D], F32, kind="ExternalOutput").ap()

    aqT = dscr("aqT", [3, 512, T], dbg=True)
    akT = [dscr(f"akT{g}", [512, d, T // d + 128], dbg=True) for g, (_, d) in enumerate(GROUPS)]
    avp = [dscr(f"avp{g}", [d, T // d + 128, 520], dbg=True) for g, (_, d) in enumerate(GROUPS)]
    sazT = dscr("sazT", [512, T], dbg=True)
    sbzT = dscr("sbzT", [512, T], dbg=True)
    qbT = dscr("qbT", [512, T], dbg=True)
    gT = dscr("gT", [2048, T], dbg=True)
    yaT = dscr("yaT", [512, T], dbg=True)
    ybT = dscr("ybT", [512, T], dbg=True)
    x1 = dscr("x1", [T, D], F32, dbg=True)
    Edram = dscr("Edram", [128, 24 * 256], F32)
    rdscr = dscr("rdscr", [8, 512], F32)
    rdslot = [0]
    xsrc_u = [dscr(f"xsrc{u}", [n, XCOLS]) for u, n in enumerate(XUNITS)]
    xdst_u = [dscr(f"xdst{u}", [2 * n, XCOLS]) for u, n in enumerate(XUNITS)]

    def xs(key):
        u, r0, n = XSEC[key]
        return xsrc_u[u][r0:r0 + n, :]

    def xd(key, rank):
        u, r0, n = XSEC[key]
        return xdst_u[u][rank * XUNITS[u] + r0:rank * XUNITS[u] + r0 + n, :]
    dbg_names = ["aqT", "akT0", "akT1", "akT2", "avp0", "avp1", "avp2", "sazT", "sbzT", "qbT", "gT",
                 "yaT", "ybT", "x1"]

    with ExitStack() as top:
        P = Prog(nc, top)

        def sb(es, name, shape, dt):
            return es.enter_context(nc.sbuf_tensor("s_" + name, list(shape), dt))

        PSALL = top.enter_context(nc.psum_tensor("psall", [128, 8, 512], F32))
        PS = [PSALL[:, i, :] for i in range(8)]

        class PSPool:
            def __init__(self, idx):
                self.idx = idx
                self.i = 0

            def next(self):
                b = self.idx[self.i % len(self.idx)]
                self.i += 1
                return PS[b], ("ps", b)

        def pspool(idx, name):
            return PSPool(idx)

        cmat = sb(top, "cmat", [128, 3, 128], F32)
        ones_f = sb(top, "ones_f", [128, 128], F32)
        msk = sb(top, "msk", [128, 2], F32)
        P.dma("sp", cmat[:, :, :], cmat_in.rearrange("c p n -> p c n"), writes=["cmat"])
        P.dma("sp", msk[:, :], msk_in, writes=["msk"])
        P.memset("pool", ones_f[:, :], 1.0, writes=["ones"])
        cmatb = sb(top, "cmatb", [128, 2, 128], BF16)
        P.copy("dve", cmatb[:, :, :], cmat[:, 1:3, :], reads=["cmat"], writes=["cmatb"])
        epsc = sb(top, "epsc", [128, 2], F32)
        P.memset("pool", epsc[:, 0:1], QK_EPS, writes=["epsc"])
        P.memset("pool", epsc[:, 1:2], LN_EPS, writes=["epsc"])
        ident = cmat[:, 0, :]
        bones = cmat[:, 1, :]
        rrot = cmat[:, 2, :]

        with ExitStack() as es:
            ab = sb(es, "ab", [128, 24 * 256], F32)
            tri = sb(es, "tri", [128, 2, 256], F32)
            P.dma("sp", ab[:, :], abias_in, writes=["ab"])
            P.dma("sp", tri[:, :, :], tri_in.rearrange("c p n -> p c n"), writes=["tri"])
            P.act(ab[:, :], ab[:, :], AF.Exp, reads=["ab"], writes=["ab"])
            for gh in range(24):
                sl = slice(gh * 256, (gh + 1) * 256)
                P.tt("dve", ab[:, sl], ab[:, sl], tri[:, 0, :], ALU.mult, reads=["ab", "tri"], writes=[("ab", gh)])
            P.dma("sp", Edram, ab[:, :], reads=[("ab", gh) for gh in range(24)])
            P.barrier()
            P.flush()

        for l in range(DEPTH):
            if debug and l > debug.get("layers", DEPTH) - 1:
                break
            xsrc_l = x_in if l == 0 else x1
            ydst_l = x1 if l < DEPTH - 1 else y_out
            with ExitStack() as lay:
                shiftT = sb(lay, f"shiftT{l}", [128, 8], F32)
                sc1T = sb(lay, f"sc1T{l}", [128, 8], F32)
                gate_b = sb(lay, f"gate_b{l}", [128, D], F32)
                lng_b = sb(lay, f"lng_b{l}", [128, D], F32)
                lnb_b = sb(lay, f"lnb_b{l}", [128, D], F32)
                bgT = sb(lay, f"bgT{l}", [128, 16], F32)
                qkg = sb(lay, f"qkg{l}", [128, 2], F32)

                with ExitStack() as es:
                    cT = sb(es, f"cT{l}", [128, 8], F32)
                    silc = sb(es, f"silc{l}", [128, 8], F32)
                    rows = sb(es, f"rows{l}", [1, 5 * D], F32)
                    brow = sb(es, f"brow{l}", [1, 3 * D], F32)
                    wst = [sb(es, f"wada{l}_{i}", [128, 8, 512], F32) for i in range(2)]
                    wp = TPool(wst, "wada")
                    psp = pspool([0, 1], "ps")
                    P.dma("sp", cT[:, :], cT_in, writes=["cT"])
                    P.dma("sp", brow[:, :], b_ada_in[l], writes=["brow"])
                    P.dma("sp", rows[:, 3 * D:4 * D], ln_g_in[l], writes=["rows_ln"])
                    P.dma("sp", rows[:, 4 * D:5 * D], ln_b_in[l], writes=["rows_ln"])
                    P.dma("sp", bgT[:, :], bgT_in[l], writes=["bgT"])
                    P.dma("sp", qkg[:, :], qkg_in[l], writes=["qkg"])
                    P.act(silc[:, :], cT[:, :], AF.Silu, reads=["cT"], writes=["silc"])
                    for n in range(6):
                        wt, kw = wp.next()
                        P.dma("sp", wt[:, :, :],
                              w_ada_in[l, :, n * 512:(n + 1) * 512].rearrange("(kc p) n -> p kc n", p=128),
                              writes=[kw])
                        ps, kp = psp.next()
                        for kc in range(8):
                            P.mm(ps[0:1, :], silc[:, kc:kc + 1], wt[:, kc, :], start=(kc == 0), stop=(kc == 7),
                                 reads=[kw, "silc"], writes=[kp])
                        P.tt("dve", rows[0:1, n * 512:(n + 1) * 512], ps[0:1, :], brow[0:1, n * 512:(n + 1) * 512],
                             ALU.add, reads=[kp, "brow"], writes=[("rows", n)])
                    ps, kp = psp.next()
                    for j in range(16):
                        P.mm(ps[:, j:j + 1], rows[0:1, j * 128:(j + 1) * 128], ones_f[0:1, 0:1],
                             reads=[("rows", j // 4), "ones"], writes=[kp])
                    P.copy("dve", shiftT[:, :], ps[:, 0:8], reads=[kp], writes=["mod"])
                    P.ts("dve", sc1T[:, :], ps[:, 8:16], 1.0, None, ALU.add, reads=[kp], writes=["mod2"])
                    for (dst, c0, rk) in ((gate_b, 2 * D, [("rows", 4), ("rows", 5)]), (lng_b, 3 * D, ["rows_ln"]),
                                          (lnb_b, 4 * D, ["rows_ln"])):
                        for nh in range(2):
                            ps, kp = psp.next()
                            P.mm(ps[:, :], ones_f[0:1, :], rows[0:1, c0 + nh * 512:c0 + (nh + 1) * 512],
                                 reads=rk + ["ones"], writes=[kp])
                            P.copy("dve", dst[:, nh * 512:(nh + 1) * 512], ps[:, :], reads=[kp], writes=["bc"])
                    P.barrier()
                    P.flush()

                with ExitStack() as es:
                  if not (debug and debug.get("skipP1")):
                        uT = sb(es, f"uT{l}", [128, 8, T], BF16)
                        with ExitStack() as es2:
                            xp = TPool([sb(es2, f"xt{l}_{i}", [128, 4, D], F32) for i in range(2)], "xt")
                            psp = pspool([0, 1, 2, 3], "ps")
                            for tb in range(8):
                                xt, kx = xp.next()
                                P.dma("sp", xt[:, :, :],
                                      xsrc_l[tb * 512:(tb + 1) * 512, :].rearrange("(t p) d -> p t d", p=128), writes=[kx])
                                for kc in range(8):
                                    ps, kp = psp.next()
                                    for t in range(4):
                                        P.tr(ps[:, t * 128:(t + 1) * 128], xt[:, t, kc * 128:(kc + 1) * 128], ident,
                                             reads=[kx], writes=[kp])
                                    P.act(uT[:, kc, tb * 512:(tb + 1) * 512], ps[:, :], AF.Identity,
                                          bias=shiftT[:, kc:kc + 1], scale=sc1T[:, kc:kc + 1], reads=[kp])
                            P.barrier()
                            P.flush()

                        uTp = sb(es, f"uTp{l}", [128, 8, T], BF16)
                        wstp = TPool([sb(es, f"wst{l}_{i}", [128, 8, 256], F32) for i in range(2)], "wst")
                        wbfp = TPool([sb(es, f"wbf{l}_{i}", [128, 8, 512], BF16) for i in range(2)], "wbf")
                        stgp = TPool([sb(es, f"stg{l}_{i}", [128, 512], BF16) for i in range(4)], "stg")
                        vstp = TPool([sb(es, f"vst{l}_{i}", [128, 8, 65], BF16) for i in range(3)], "vst")
                        f32p = TPool([sb(es, f"f32t{l}_{i}", [128, 512], F32) for i in range(6)], "f32t")
                        b16p = TPool([sb(es, f"b16t{l}_{i}", [128, 512], BF16) for i in range(4)], "b16t")
                        csp = TPool([sb(es, f"cst{l}_{i}", [128, 2, 512], F32) for i in range(2)], "cst")
                        psA = pspool([0, 1, 2, 3, 4], "ps")
                        psB = pspool([5, 6, 7], "ps")
                        for vt in vstp.tiles:
                            P.memset("pool", vt[:, :, :], 1.0)
                        P.barrier()
                        evac_rr = [0]

                        def evac_copy(out, in_, reads, writes):
                            evac_rr[0] += 1
                            if evac_rr[0] % 2:
                                return P.copy("act", out, in_, reads=reads, writes=writes)
                            return P.copy("dve", out, in_, reads=reads, writes=writes)

                        def load_strip(c0, W):
                            wb, kb = wbfp.next()
                            for w0 in range(0, W, 256):
                                wt, kw = wstp.next()
                                P.dma("sp", wt[:, :, :],
                                      w_in_in[l, :, c0 + w0:c0 + w0 + 256].rearrange("(kc p) n -> p kc n", p=128),
                                      writes=[kw])
                                P.copy("pool", wb[:, :, w0:w0 + 256], wt[:, :, :], reads=[kw], writes=[(kb, w0)])
                            return wb, [(kb, w0) for w0 in range(0, W, 256)]

                        def permute_uT(d):
                            engs = ("dve", "pool", "act")
                            for kc in range(8):
                                P.copy(engs[kc % 3], uTp[:, kc, :].rearrange("k (r p) -> k r p", r=d),
                                       uT[:, kc, :].rearrange("k (p r) -> k r p", r=d), writes=[("uTp", kc)])

                        def rhs_perm(g, kc, pb):
                            src = uT if g == 0 else uTp
                            return src[:, kc, pb * 512:(pb + 1) * 512]

                        def lhs_perm(g, kc, it):
                            src = uT if g == 0 else uTp
                            return src[:, kc, it * 128:(it + 1) * 128]

                        def fm_chunk(wb, kb, col0, rhs_fn, extra=()):
                            ps, kp = psA.next()
                            for kc in range(8):
                                rd_ = list(kb) + [e_ for e_ in extra if e_[1] == kc]
                                P.mm(ps[:, :], wb[:, kc, col0:col0 + 128], rhs_fn(kc), start=(kc == 0), stop=(kc == 7),
                                     reads=rd_, writes=[kp])
                            return ps, kp

                        def do_qk(which, g, wb, kb):
                            d = GROUPS[g][1]
                            ex_ = [("uTp", kc) for kc in range(8)] if g > 0 else []
                            for fcl in range(4):
                                for pb in range(8):
                                    ps, kp = fm_chunk(wb, kb, fcl * 128, lambda kc: rhs_perm(g, kc, pb), ex_)
                                    stg, ks = stgp.next()
                                    evac_copy(stg[:, :], ps[:, :], [kp], [ks])
                                    rows_ = slice(fcl * 128, (fcl + 1) * 128)
                                    if which == "q":
                                        P.dma("sp", aqT[g, rows_, pb * 512:(pb + 1) * 512], stg[:, :], reads=[ks])
                                    elif g == 0:
                                        P.dma("sp", akT[0][rows_, 0, 64 + pb * 512:64 + (pb + 1) * 512], stg[:, :],
                                              reads=[ks])
                                    elif g == 1:
                                        r, p0 = pb // 2, (pb % 2) * 512
                                        P.dma("sp", akT[1][rows_, r, 64 + p0:64 + p0 + 512], stg[:, :], reads=[ks])
                                    else:
                                        P.dma("sp", akT[2][rows_, 2 * pb:2 * pb + 2, 64:64 + 256],
                                              stg[:, :].rearrange("p (r c) -> p r c", r=2), reads=[ks])

                        def do_v(g, wb, kb):
                            d = GROUPS[g][1]
                            per = (T // d) // 128
                            for it in range(32):
                                ps, kp = psA.next()
                                for kc in range(8):
                                    rd_ = list(kb) + ([("uTp", kc)] if g > 0 else [])
                                    P.mm(ps[:, :], lhs_perm(g, kc, it), wb[:, kc, 0:512], start=(kc == 0), stop=(kc == 7),
                                         reads=rd_, writes=[kp])
                                vt, kv = vstp.next()
                                evac_copy(vt[:, :, 0:64], ps[:, :].rearrange("p (h e) -> p h e", h=8), [kp], [kv])
                                r, p0 = it // per, (it % per) * 128
                                P.dma("sp", avp[g][r, 64 + p0:64 + p0 + 128, :],
                                      vt[:, :, :].rearrange("p h c -> p (h c)"), reads=[kv])

                        def do_gate(dst, s_, wb, kb):
                            for fcl in range(4):
                                for tb in range(8):
                                    ps, kp = fm_chunk(wb, kb, fcl * 128, lambda kc: uT[:, kc, tb * 512:(tb + 1) * 512])
                                    stg, ks = stgp.next()
                                    j = s_ * 4 + fcl
                                    if dst is gT:
                                        P.act(stg[:, :], ps[:, :], AF.Sigmoid, bias=bgT[:, j:j + 1], reads=[kp],
                                              writes=[ks])
                                    else:
                                        P.act(stg[:, :], ps[:, :], AF.Silu, reads=[kp], writes=[ks])
                                    P.dma("sp", dst[j * 128:(j + 1) * 128, tb * 512:(tb + 1) * 512], stg[:, :],
                                          reads=[ks])

                        jobs = []
                        for g in range(3):
                            if g > 0:
                                jobs.append((None, 0, (lambda g_: (lambda wb, kb: permute_uT(GROUPS[g_][1])))(g)))
                            jobs.append((C_AQ + g * 512, 512, (lambda g_: (lambda wb, kb: do_qk("q", g_, wb, kb)))(g)))
                            jobs.append((C_AK + g * 512, 512, (lambda g_: (lambda wb, kb: do_qk("k", g_, wb, kb)))(g)))
                            jobs.append((C_AV + g * 512, 512, (lambda g_: (lambda wb, kb: do_v(g_, wb, kb)))(g)))
                        jobs.append((C_AZ, 512, lambda wb, kb: do_gate(sazT, 0, wb, kb)))
                        jobs.append((C_BZ, 512, lambda wb, kb: do_gate(sbzT, 0, wb, kb)))
                        for s_ in range(4):
                            jobs.append((C_GL + s_ * 512, 512, (lambda s2: (lambda wb, kb: do_gate(gT, s2, wb, kb)))(s_)))
                        strips = [j for j in jobs if j[0] is not None]
                        loaded = {}
                        nxt = [0]

                        def prefetch():
                            if nxt[0] < len(strips):
                                c0_, W_, _ = strips[nxt[0]]
                                loaded[nxt[0]] = load_strip(c0_, W_)
                                nxt[0] += 1

                        prefetch()
                        si = 0
                        for (c0_, W_, fn_) in jobs:
                            if c0_ is None:
                                fn_(None, None)
                                continue
                            wb, kb = loaded.pop(si)
                            si += 1
                            prefetch()
                            fn_(wb, kb)
                        xs_kb = xs("KB")
                        xs_vb = xs("VB").rearrange("r c -> (r c)").rearrange("(t f) -> t f", f=130)
                        wbq, kbq = load_strip(C_BQ, 512)
                        wbk, kbk = load_strip(C_BK, 256)
                        for fcl in range(5):
                            wb, kb, col0, gcol = (wbq, kbq, fcl * 128, 0) if fcl < 4 else (wbk, kbk, 0, 1)
                            for tb in range(8):
                                tsl = slice(tb * 512, (tb + 1) * 512)
                                cst, kc_ = csp.next()
                                P.dma("sp", cst[:, :, :], cstab_in[:, :, tsl].rearrange("c p t -> p c t"), writes=[kc_])
                                ps, kp = fm_chunk(wb, kb, col0, lambda kc: uT[:, kc, tsl])
                                sq, k1 = b16p.next()
                                P.act(sq[:, :], ps[:, :], AF.Square, reads=[kp], writes=[k1])
                                ps2, kp2 = psB.next()
                                P.mm(ps2[:, :], cmatb[:, 0, :], sq[:, :], reads=[k1, "cmatb"], writes=[kp2])
                                srt, k2a = f32p.next()
                                P.act(srt[:, :], ps2[:, :], AF.Ln, bias=epsc[:, 0:1], reads=[kp2], writes=[k2a])
                                rstd, k2 = f32p.next()
                                P.act(rstd[:, :], srt[:, :], AF.Exp, scale=-0.5, reads=[k2a], writes=[k2])
                                xn, k3 = b16p.next()
                                P.stt(xn[:, :], ps[:, :], qkg[:, gcol:gcol + 1], rstd[:, :], ALU.mult, ALU.mult,
                                      reads=[kp, k2], writes=[k3])
                                ps3, kp3 = psB.next()
                                P.mm(ps3[:, :], cmatb[:, 1, :], xn[:, :], reads=[k3, "cmatb"], writes=[kp3])
                                ta, k4 = f32p.next()
                                P.tt("pool", ta[:, :], xn[:, :], cst[:, 0, :], ALU.mult, reads=[k3, kc_], writes=[k4])
                                tb_, k5 = f32p.next()
                                P.tt("dve", tb_[:, :], ps3[:, :], cst[:, 1, :], ALU.mult, reads=[kp3, kc_], writes=[k5])
                                stg, ks = stgp.next()
                                P.tt("pool", stg[:, :], ta[:, :], tb_[:, :], ALU.add, reads=[k4, k5], writes=[ks])
                                if fcl < 4:
                                    P.dma("sp", qbT[fcl * 128:(fcl + 1) * 128, tsl], stg[:, :], reads=[ks])
                                else:
                                    P.dma("sp", xs_kb[:, tsl], stg[:, :], reads=[ks])
                        for it in range(32):
                            ps, kp = psA.next()
                            for kc in range(8):
                                P.mm(ps[:, 0:128], uT[:, kc, it * 128:(it + 1) * 128], wbk[:, kc, 128:256],
                                     start=(kc == 0), stop=(kc == 7), reads=kbk, writes=[kp])
                            vt, kv = vstp.next()
                            evac_copy(vt[:, 0:2, 0:64], ps[:, 0:128].rearrange("p (h e) -> p h e", h=2), [kp], [kv])
                            P.dma("sp", xs_vb[it * 128:(it + 1) * 128, :],
                                  vt[:, 0:2, :].rearrange("p h c -> p (h c)"), reads=[kv])
                        P.barrier()
                        P.flush()

                if debug and debug.get("upto") == "P1":
                    break

                with ExitStack() as es:
                    vh = TPool([sb(es, f"vh{l}_{i}", [64, 16, 520], BF16) for i in range(2)], "vh")
                    for g, (_, d) in enumerate(GROUPS):
                        L = T // d
                        for (kk, c0) in (("AKF", 64), ("AKL", L)):
                            sec = xs((kk, g)).rearrange("r c -> (r c)").rearrange(
                                "(f r c) -> f r c", r=d, c=64)
                            for fb in range(4):
                                P.dma("sp", sec[fb * 128:(fb + 1) * 128], akT[g][fb * 128:(fb + 1) * 128, :, c0:c0 + 64])
                        for (kk, c0) in (("AVF", 64), ("AVL", L)):
                            nel = d * 64 * 520
                            sec = xs((kk, g)).rearrange("r c -> (r c)")[0:nel].rearrange(
                                "(r p f) -> r p f", p=64, f=520)
                            P.dma("sp", sec, avp[g][:, c0:c0 + 64, :])
                    P.barrier()
                    for u in range(len(XUNITS)):
                        P.collective((lambda a, b: (lambda e: e.collective_compute(
                            "AllGather", ALU.bypass, replica_groups=[[0, 1], [2, 3], [4, 5], [6, 7]],
                            ins=[a], outs=[b])))(xsrc_u[u], xdst_u[u]))
                    P.barrier()
                    for g, (_, d) in enumerate(GROUPS):
                        L = T // d
                        for (kk, rb, c0) in (("AKL", 0, 0), ("AKF", 1, 64 + L)):
                            sec = xd((kk, g), rb).rearrange("r c -> (r c)").rearrange(
                                "(f r c) -> f r c", r=d, c=64)
                            for fb in range(4):
                                P.dma("sp", akT[g][fb * 128:(fb + 1) * 128, :, c0:c0 + 64], sec[fb * 128:(fb + 1) * 128])
                        for (kk, rb, c0, mc) in (("AVL", 0, 0, 0), ("AVF", 1, 64 + L, 1)):
                            nel = d * 64 * 520
                            sec = xd((kk, g), rb).rearrange("r c -> (r c)")[0:nel].rearrange(
                                "(r p f) -> p r f", p=64, f=520)
                            t_, kt = vh.next()
                            P.dma("sp", t_[:, 0:d, :], sec, writes=[kt])
                            P.ts("dve", t_[:, 0:d, :], t_[:, 0:d, :], msk[0:64, mc:mc + 1], None, ALU.mult,
                                 reads=[kt, "msk"], writes=[kt])
                            P.dma("sp", avp[g][:, c0:c0 + 64, :].rearrange("r p f -> p r f"), t_[:, 0:d, :], reads=[kt])
                    P.barrier()
                    P.flush()

                if debug and debug.get("upto") == "X":
                    break

                with ExitStack() as es:
                    E = sb(es, f"E{l}", [128, 24, 256], F32)
                    acc = sb(es, f"acc{l}", [65, 4, T], F32)
                    vaug = sb(es, f"vaug{l}", [128, 48, 4, 65], BF16)
                    ktp = TPool([sb(es, f"kt{l}_{i}", [64, 6144], BF16) for i in range(2)], "kt")
                    qtp = TPool([sb(es, f"qt{l}_{i}", [64, T], BF16) for i in range(2)], "qt")
                    exp_ = TPool([sb(es, f"ex{l}_{i}", [128, 512], F32) for i in range(4)], "ex")
                    ptp = TPool([sb(es, f"pt{l}_{i}", [128, 512], BF16) for i in range(5)], "pt")
                    rdp = TPool([sb(es, f"rd{l}_{i}", [65, 512], F32) for i in range(2)], "rd")
                    nmp = TPool([sb(es, f"nm{l}_{i}", [64, 512], F32) for i in range(2)], "nm")
                    szp = TPool([sb(es, f"sz{l}_{i}", [64, 512], BF16) for i in range(2)], "sz")
                    ysp = TPool([sb(es, f"ys{l}_{i}", [64, 512], BF16) for i in range(2)], "ys")
                    psS = pspool([0, 1, 2, 3], "ps")
                    psO = pspool([4, 5, 6], "ps")
                    psN = pspool([7], "ps")
                    P.dma("sp", E[:, :, :], Edram.rearrange("p (g c) -> p g c", c=256), writes=["E"])
                    for hh in range(2):
                        for g, (_, d) in enumerate(GROUPS):
                            L = T // d
                            nch = L // 128 + 1
                            vsrc = avp[g].rearrange("r (m p) (h c) -> p (r m) h c", p=128, c=65)
                            for c0 in range(0, d * nch, 12):
                                c1 = min(d * nch, c0 + 12)
                                P.dma("sp", vaug[:, c0:c1, :, :], vsrc[:, c0:c1, hh * 4:(hh + 1) * 4, :],
                                      writes=[("vaug", c0)])
                            vkeys = [("vaug", c0) for c0 in range(0, d * nch, 12)]
                            for hl in range(4):
                                h = hh * 4 + hl
                                kt, kk = ktp.next()
                                P.dma("sp", kt[:, 0:d * (L + 128)],
                                      akT[g][h * 64:(h + 1) * 64, :, :].rearrange("p r c -> p (r c)"), writes=[kk])
                                qt, kq = qtp.next()
                                P.dma("sp", qt[:, :], aqT[g, h * 64:(h + 1) * 64, :], writes=[kq])
                                blocks = [(r, n2) for r in range(d) for n2 in range(L // 256)]
                                st = {}
                                e0 = E[:, g * 8 + h, :]
                                ebc = bass.AP(e0.tensor, e0.offset, [list(e0.ap[0]), [0, 2], list(e0.ap[1])])

                                def stageA(i):
                                    r, n2 = blocks[i]
                                    n = 2 * n2
                                    ps, kp = psS.next()
                                    kb0 = r * (L + 128) + 128 * n
                                    qb0 = r * L + 128 * n
                                    P.mm(ps[:, 0:128], kt[:, kb0:kb0 + 128], qt[:, qb0:qb0 + 128],
                                         reads=[kk, kq], writes=[kp])
                                    P.mm(ps[:, 128:384], kt[:, kb0 + 128:kb0 + 256], qt[:, qb0:qb0 + 256],
                                         reads=[kk, kq], writes=[kp])
                                    P.mm(ps[:, 384:512], kt[:, kb0 + 256:kb0 + 384], qt[:, qb0 + 128:qb0 + 256],
                                         reads=[kk, kq], writes=[kp])
                                    ex, ke = exp_.next()
                                    P.act(ex[:, :], ps[:, :], AF.Exp, scale=0.125, reads=[kp], writes=[ke])
                                    pt, kpt = ptp.next()
                                    P.tt("pool", pt[:, :].rearrange("p (a b) -> p a b", a=2),
                                         ex[:, :].rearrange("p (a b) -> p a b", a=2), ebc, ALU.mult,
                                         reads=[ke, "E"], writes=[kpt])
                                    st[i] = (pt, kpt)

                                def stageB(i):
                                    r, n2 = blocks[i]
                                    n = 2 * n2
                                    pt, kpt = st.pop(i)
                                    po, ko = psO.next()
                                    ch = r * nch + n
                                    vk = lambda c_: [("vaug", (c_ // 12) * 12)]
                                    P.mm(po[0:65, 0:256], vaug[:, ch + 1, hl, :], pt[:, 128:384], start=True, stop=False,
                                         reads=[kpt] + vk(ch + 1), writes=[ko])
                                    P.mm(po[0:65, 0:128], vaug[:, ch, hl, :], pt[:, 0:128], start=False, stop=False,
                                         reads=[kpt] + vk(ch), writes=[ko])
                                    P.mm(po[0:65, 128:256], vaug[:, ch + 2, hl, :], pt[:, 384:512], start=False, stop=True,
                                         reads=[kpt] + vk(ch + 2), writes=[ko])
                                    t0 = r + d * 128 * n
                                    av_ = acc[:, hl, t0:t0 + d * 255 + 1:d]
                                    if g == 0:
                                        P.copy("act", av_, po[0:65, 0:256], reads=[ko], writes=[("acc", hl)])
                                    else:
                                        P.tt("dve", av_, av_, po[0:65, 0:256], ALU.add, reads=[ko, ("acc", hl)],
                                             writes=[("acc", hl)])

                                nb = len(blocks)
                                LK = 3
                                for i in range(nb + LK):
                                    if i < nb:
                                        stageA(i)
                                    if i >= LK:
                                        stageB(i - LK)
                        for hl in range(4):
                            P.act(acc[64:65, hl, :], acc[64:65, hl, :], AF.Ln, reads=[("acc", hl)], writes=[("acc", hl)])
                        for hl in range(4):
                            P.act(acc[64:65, hl, :], acc[64:65, hl, :], AF.Exp, scale=-1.0, reads=[("acc", hl)],
                                  writes=[("acc", hl)])
                        for hl in range(4):
                            h = hh * 4 + hl
                            for tb in range(8):
                                tsl = slice(tb * 512, (tb + 1) * 512)
                                sz, ksz = szp.next()
                                P.dma("sp", sz[:, :], sazT[h * 64:(h + 1) * 64, tsl], writes=[ksz])
                                pb_, kpb = psN.next()
                                P.mm(pb_[0:64, :], ones_f[64:65, 0:64], acc[64:65, hl, tsl], reads=[("acc", hl), "ones"],
                                     writes=[kpb])
                                nm, knm = nmp.next()
                                P.tt("dve", nm[:, :], acc[0:64, hl, tsl], pb_[0:64, :], ALU.mult,
                                     reads=[("acc", hl), kpb], writes=[knm])
                                ys, kys = ysp.next()
                                P.tt("pool", ys[:, :], nm[:, :], sz[:, :], ALU.mult, reads=[knm, ksz], writes=[kys])
                                P.dma("pool", yaT[h * 64:(h + 1) * 64, tsl], ys[:, :], reads=[kys])
                    P.barrier()
                    P.flush()

                if debug and debug.get("upto") == "P2":
                    break

                with ExitStack() as es:
                    kTd = sb(es, f"kTd{l}", [128, 2, S], BF16)
                    vb = sb(es, f"vb{l}", [128, 64, 130], BF16)
                    qT = sb(es, f"qT{l}", [128, 4, T], BF16)
                    ptp = TPool([sb(es, f"pB{l}_{i}", [128, 1024], BF16) for i in range(4)], "pB")
                    bcp = TPool([sb(es, f"bcB{l}_{i}", [64, 512], F32) for i in range(3)], "bcB")
                    evp = TPool([sb(es, f"evB{l}_{i}", [65, 512], F32) for i in range(3)], "evB")
                    rdp = TPool([sb(es, f"rdB{l}_{i}", [65, 512], F32) for i in range(2)], "rdB")
                    nm2 = TPool([sb(es, f"nmC{l}_{i}", [64, 512], F32) for i in range(2)], "nmC")
                    szp = TPool([sb(es, f"szB{l}_{i}", [64, 512], BF16) for i in range(3)], "szB")
                    ysp = TPool([sb(es, f"ysB{l}_{i}", [64, 512], BF16) for i in range(3)], "ysB")
                    psA2 = [(PSALL[:, 2 * j_:2 * j_ + 2, :].rearrange("p a b -> p (a b)"),
                             [("ps", 2 * j_), ("ps", 2 * j_ + 1)]) for j_ in range(3)]
                    psO = pspool([6, 7], "ps")
                    for rk in range(2):
                        kb_ = xd("KB", rk)
                        for kvh in range(2):
                            for half in range(2):
                                P.dma("sp", kTd[half * 64:(half + 1) * 64, kvh, rk * T:(rk + 1) * T],
                                      kb_[kvh * 64:(kvh + 1) * 64, :], writes=[("kTd", rk, kvh, half)])
                        vsec = xd("VB", rk).rearrange("r c -> (r c)").rearrange("(k p f) -> p k f", p=128, f=130)
                        for k0 in range(0, 32, 8):
                            P.dma("sp", vb[:, rk * 32 + k0:rk * 32 + k0 + 8, :], vsec[:, k0:k0 + 8, :],
                                  writes=[("vb", rk, k0)])
                    for c_ in range(4):
                        P.dma("sp", qT[:, c_, :], qbT[c_ * 128:(c_ + 1) * 128, :], writes=[("qT", c_)])
                    steps = [(qb, hp, kc) for qb in range(8) for hp in range(4) for kc in range(64)]
                    st = {}
                    acc_o = {}

                    def stageA(i):
                        qb, hp, kc = steps[i]
                        kvh = hp // 2
                        rk = kc // 32
                        psa, keys = psA2[i % 3]
                        for hh_ in range(2):
                            pr = hh_ * 64
                            P.mm(psa[:, hh_ * 512:(hh_ + 1) * 512], kTd[pr:pr + 64, kvh, kc * 128:(kc + 1) * 128],
                                 qT[pr:pr + 64, hp, qb * 512:(qb + 1) * 512],
                                 reads=[("kTd", rk, kvh, hh_), ("qT", hp)], writes=keys)
                        pt, kpt = ptp.next()
                        P.act(pt[:, :], psa, AF.Exp, scale=0.125, reads=keys, writes=[kpt])
                        st[i] = (pt, kpt)

                    def stageB(i):
                        qb, hp, kc = steps[i]
                        kvh = hp // 2
                        pt, kpt = st.pop(i)
                        if kc == 0:
                            acc_o[(qb, hp)] = [psO.next(), psO.next()]
                        for hh_ in range(2):
                            po, ko = acc_o[(qb, hp)][hh_]
                            P.mm(po[0:65, :], vb[:, kc, kvh * 65:(kvh + 1) * 65], pt[:, hh_ * 512:(hh_ + 1) * 512],
                                 start=(kc == 0), stop=(kc == 63),
                                 reads=[kpt, ("vb", kc // 32, ((kc % 32) // 8) * 8)], writes=[ko])
                        if kc == 63:
                            tsl = slice(qb * 512, (qb + 1) * 512)
                            for hh_ in range(2):
                                h = 2 * hp + hh_
                                po, ko = acc_o[(qb, hp)][hh_]
                                ev, kev = evp.next()
                                P.copy("dve", ev[:, :], po[0:65, :], reads=[ko], writes=[kev])
                                sz, ksz = szp.next()
                                P.dma("sp", sz[:, :], sbzT[h * 64:(h + 1) * 64, tsl], writes=[ksz])
                                rd, krd = rdp.next()
                                P.recip(rd[64:65, :], ev[64:65, :], reads=[kev], writes=[krd])
                                bc, kbc = bcp.next()
                                sl_ = rdslot[0] % 8
                                rdslot[0] += 1
                                P.dma("sp", rdscr[sl_:sl_ + 1, :], rd[64:65, :], reads=[krd], writes=[("rdscr", sl_)])
                                P.dma("sp", bc[:, :], bass.AP(rdscr.tensor, sl_ * 512, [[0, 64], [1, 512]]),
                                      reads=[("rdscr", sl_)], writes=[kbc])
                                n2, kn2 = nm2.next()
                                P.tt("dve", n2[:, :], ev[0:64, :], bc[:, :], ALU.mult, reads=[kev, kbc], writes=[kn2])
                                ys, kys = ysp.next()
                                P.tt("pool", ys[:, :], n2[:, :], sz[:, :], ALU.mult, reads=[kn2, ksz], writes=[kys])
                                P.dma("pool", ybT[h * 64:(h + 1) * 64, tsl], ys[:, :], reads=[kys])

                    LOOK = 2
                    ns = len(steps)
                    for i in range(ns + LOOK):
                        if i < ns:
                            stageA(i)
                        if i >= LOOK:
                            stageB(i - LOOK)
                    P.barrier()
                    P.flush()

                if debug and debug.get("upto") == "P3":
                    break

                with ExitStack() as es:
                    wpa = sb(es, f"wpa{l}", [128, 4, D], BF16)
                    wpb = sb(es, f"wpb{l}", [128, 4, D], BF16)
                    wo = sb(es, f"wo{l}", [128, 8, D], BF16)
                    wf = TPool([sb(es, f"wf{l}_{i}", [128, 8, 256], F32) for i in range(2)], "wf")
                    for (wdst, wsrc, nk_) in ((wpa, w_pa_in, 4), (wpb, w_pb_in, 4), (wo, w_o_in, 8)):
                        for nh in range(4):
                            wt, kw = wf.next()
                            P.dma("sp", wt[:, 0:nk_, :],
                                  wsrc[l, :, nh * 256:(nh + 1) * 256].rearrange("(k p) n -> p k n", p=128), writes=[kw])
                            P.copy("pool", wdst[:, :, nh * 256:(nh + 1) * 256], wt[:, 0:nk_, :], reads=[kw],
                                   writes=["wP4"])
                    yap = TPool([sb(es, f"ya{l}_{i}", [128, 4, 512], BF16) for i in range(2)], "ya")
                    ybp = TPool([sb(es, f"yb{l}_{i}", [128, 4, 512], BF16) for i in range(2)], "yb")
                    gp = TPool([sb(es, f"gg{l}_{i}", [128, 16, 512], BF16) for i in range(2)], "gg")
                    mtp = TPool([sb(es, f"mT{l}_{i}", [128, 8, 512], BF16) for i in range(2)], "mT")
                    t1p = TPool([sb(es, f"t1{l}_{i}", [128, 512], F32) for i in range(2)], "t1")
                    t2p = TPool([sb(es, f"t2{l}_{i}", [128, 512], F32) for i in range(2)], "t2")
                    xrp = TPool([sb(es, f"xr{l}_{i}", [128, D], F32) for i in range(3)], "xr")
                    hp = TPool([sb(es, f"hh{l}_{i}", [128, D], F32) for i in range(2)], "hh")
                    op_ = TPool([sb(es, f"oo{l}_{i}", [128, D], F32) for i in range(2)], "oo")
                    stp = TPool([sb(es, f"bst{l}_{i}", [128, 16], F32) for i in range(2)], "bst")
                    psA = pspool([0, 1, 2, 3], "ps")
                    psB = pspool([4, 5, 6, 7], "ps")
                    for tb in range(8):
                        tsl = slice(tb * 512, (tb + 1) * 512)
                        ya, kya = yap.next()
                        yb, kyb = ybp.next()
                        gg, kgg = gp.next()
                        P.dma("sp", ya[:, :, :], yaT[:, tsl].rearrange("(h p) t -> p h t", p=128), writes=[kya])
                        P.dma("sp", yb[:, :, :], ybT[:, tsl].rearrange("(h p) t -> p h t", p=128), writes=[kyb])
                        P.dma("sp", gg[:, :, :], gT[:, tsl].rearrange("(j p) t -> p j t", p=128), writes=[kgg])
                        mT, kmT = mtp.next()
                        for m in range(8):
                            pa, kpa = psA.next()
                            for h in range(4):
                                P.mm(pa[:, :], wpa[:, h, m * 128:(m + 1) * 128], ya[:, h, :], start=(h == 0),
                                     stop=(h == 3), reads=["wP4", kya], writes=[kpa])
                            pb_, kpb = psA.next()
                            for h in range(4):
                                P.mm(pb_[:, :], wpb[:, h, m * 128:(m + 1) * 128], yb[:, h, :], start=(h == 0),
                                     stop=(h == 3), reads=["wP4", kyb], writes=[kpb])
                            t1, k1 = t1p.next()
                            P.tt("dve", t1[:, :], pa[:, :], gg[:, m, :], ALU.mult, reads=[kpa, kgg], writes=[k1])
                            t2, k2 = t2p.next()
                            P.tt("dve", t2[:, :], pb_[:, :], gg[:, 8 + m, :], ALU.mult, reads=[kpb, kgg], writes=[k2])
                            P.tt("pool", mT[:, m, :], t1[:, :], t2[:, :], ALU.add, reads=[k1, k2], writes=[(kmT, m)])
                        for tt_ in range(4):
                            r0 = tb * 512 + tt_ * 128
                            xr, kxr = xrp.next()
                            P.dma("sp", xr[:, :], xsrc_l[r0:r0 + 128, :], writes=[kxr])
                            hb, khb = hp.next()
                            for nh in range(2):
                                po, ko = psB.next()
                                for kc in range(8):
                                    P.mm(po[:, :], mT[:, kc, tt_ * 128:(tt_ + 1) * 128], wo[:, kc, nh * 512:(nh + 1) * 512],
                                         start=(kc == 0), stop=(kc == 7), reads=["wP4"] + [(kmT, m) for m in range(8)],
                                         writes=[ko])
                                P.tt("dve", hb[:, nh * 512:(nh + 1) * 512], po[:, :], gate_b[:, nh * 512:(nh + 1) * 512],
                                     ALU.mult, reads=[ko], writes=[(khb, nh)])
                            P.stt(hb[:, :], xr[:, :], ALPHA, hb[:, :], ALU.mult, ALU.add,
                                  reads=[kxr, (khb, 0), (khb, 1)], writes=[(khb, 0), (khb, 1)])
                            bs, kbs = stp.next()
                            for nh in range(2):
                                P.op("dve", (lambda o, i_: (lambda e: e.bn_stats(o, i_)))(
                                    bs[:, nh * 6:(nh + 1) * 6], hb[:, nh * 512:(nh + 1) * 512]),
                                    reads=[(khb, 0), (khb, 1)], writes=[(kbs, nh)])
                            P.op("dve", (lambda o, i_: (lambda e: e.bn_aggr(o, i_)))(bs[:, 12:14], bs[:, 0:12]),
                                 reads=[(kbs, 0), (kbs, 1)], writes=[(kbs, 2)])
                            P.act(bs[:, 15:16], bs[:, 13:14], AF.Sqrt, bias=epsc[:, 1:2], reads=[(kbs, 2)],
                                  writes=[(kbs, 4)])
                            P.recip(bs[:, 14:15], bs[:, 15:16], reads=[(kbs, 4)], writes=[(kbs, 3)])
                            ob, kob = op_.next()
                            P.ts("dve", ob[:, :], hb[:, :], bs[:, 12:13], bs[:, 14:15], ALU.subtract, ALU.mult,
                                 reads=[(khb, 0), (khb, 1), (kbs, 2), (kbs, 3)], writes=[kob])
                            P.tt("pool", ob[:, :], ob[:, :], lng_b[:, :], ALU.mult, reads=[kob], writes=[kob])
                            P.tt("pool", ob[:, :], ob[:, :], lnb_b[:, :], ALU.add, reads=[kob], writes=[kob])
                            P.dma("pool", ydst_l[r0:r0 + 128, :], ob[:, :], reads=[kob])
                    P.barrier()
                    P.flush()
        P.barrier()
        P.flush()
    return nc, dbg_names


def _t5_bucket_np(rel):
    half, max_exact = 16, 8
    ret = np.where(rel > 0, half, 0)
    a = np.abs(rel)
    af = np.maximum(a, 1).astype(np.float32)
    large = max_exact + (np.log(af / np.float32(max_exact)) / np.float32(math.log(1024 / max_exact))
                         * np.float32(half - max_exact)).astype(np.int32)
    large = np.minimum(large, half - 1)
    return ret + np.where(a < max_exact, a, large)


def _host_consts(rel_table):
    ident = np.eye(128, dtype=np.float32)
    bones = np.zeros((128, 128), np.float32)
    bones[:64, :64] = 1.0 / 64
    bones[64:, 64:] = 1.0 / 64
    rrot = np.zeros((128, 128), np.float32)
    for base in range(0, 128, 32):
        for e in range(16):
            rrot[base + e + 16, base + e] = -1.0
            rrot[base + e, base + e + 16] = 1.0
    cmat = np.stack([ident, bones, rrot])
    i = np.arange(128)[:, None]
    j = np.arange(128)[None, :]
    rel0 = i - 64 - j
    rel1 = i + 64 - j
    tri0 = np.concatenate([(i >= j), (i <= j)], axis=1).astype(np.float32)
    tri = np.stack([tri0, (tri0 - 1.0) * 30000.0]).astype(np.float32)
    ab = np.zeros((128, 24, 256), np.float32)
    for g, (_, d) in enumerate(GROUPS):
        for c, rel in enumerate((rel0, rel1)):
            bk = _t5_bucket_np(np.clip(rel, -64, 64) * d)
            ab[:, g * 8:(g + 1) * 8, c * 128:(c + 1) * 128] = rel_table[bk][:, :, g * 8:(g + 1) * 8].transpose(0, 2, 1)
    return cmat, tri, ab.reshape(128, 24 * 256)


def _cs_tables(half):
    t = np.arange(half * T, (half + 1) * T)
    row = (t // 64).astype(np.float32)
    col = (t % 64).astype(np.float32)
    inv = (np.float32(10000.0) ** (-np.arange(0, 32, 2, dtype=np.float32) / np.float32(32))).astype(np.float32)
    ar = (row[:, None] * inv[None]).astype(np.float32)
    ac = (col[:, None] * inv[None]).astype(np.float32)
    ang = np.zeros((128, T), np.float32)
    for p in range(128):
        e = p % 64
        ang[p] = ar[:, e % 16] if e < 32 else ac[:, (e - 32) % 16]
    return np.stack([np.cos(ang), np.sin(ang)]).astype(np.float32)


def _in_maps(inputs):
    f = lambda a: np.ascontiguousarray(np.asarray(a, dtype=np.float32))
    x = f(inputs["x"]); c = f(inputs["c"])
    cmat, tri, ab = _host_consts(f(inputs["rel_table"]))
    bgT = np.ascontiguousarray(f(inputs["b_gate"]).reshape(DEPTH, 16, 128).transpose(0, 2, 1))
    qg = f(inputs["q_norm_g"]); kg = f(inputs["k_norm_g"])
    qkg = np.ascontiguousarray(np.stack([np.tile(qg, (1, 2)), np.tile(kg, (1, 2))], axis=-1))
    shared = {
        "ln_g": f(inputs["ln_g"]).reshape(DEPTH, 1, D), "ln_b": f(inputs["ln_b"]).reshape(DEPTH, 1, D),
        "w_ada": f(inputs["w_ada"]), "b_ada": f(inputs["b_ada"]).reshape(DEPTH, 1, 3 * D),
        "w_in": f(inputs["w_in"]), "bgT": bgT, "qkg": qkg, "w_pa": f(inputs["w_pa"]), "w_pb": f(inputs["w_pb"]),
        "w_o": f(inputs["w_o"]), "cmat": cmat, "abias": ab, "tri": tri,
    }
    cs = [_cs_tables(0), _cs_tables(1)]
    maps = []
    for core in range(8):
        b, hf = core // 2, core % 2
        m = dict(shared)
        m["x"] = np.ascontiguousarray(x[b, hf * T:(hf + 1) * T])
        m["cT"] = np.ascontiguousarray(c[b].reshape(8, 128).T)
        m["cstab"] = cs[hf]
        mk = np.ones((128, 2), np.float32)
        mk[:, 0] = 0.0 if hf == 0 else 1.0
        mk[:, 1] = 1.0 if hf == 0 else 0.0
        m["msk"] = mk
        maps.append(m)
    return maps


def kernel(**inputs):
    nc, _ = _build()
    maps = _in_maps(inputs)
    res = run_bass_kernel_spmd(nc, maps, core_ids=list(range(8)))
    out = np.empty((4, S, D), np.float32)
    for core in range(8):
        b, hf = core // 2, core % 2
        out[b, hf * T:(hf + 1) * T] = res.results[core]["y"]
    return out
```

```python
import math
from contextlib import ExitStack

import numpy as np
import concourse.bass as bass
import concourse.mybir as mybir
from concourse.bass_utils import run_bass_kernel_spmd

F32 = mybir.dt.float32
BF16 = mybir.dt.bfloat16
AF = mybir.ActivationFunctionType
ALU = mybir.AluOpType

D = 1024
T = 4096
S = 8192
DEPTH = 2
GROUPS = ((128, 1), (512, 4), (2048, 16))
ALPHA = float((2 * DEPTH) ** 0.25)
LN_EPS = 1e-5
QK_EPS = 1e-6
C_AQ, C_AK, C_AV, C_AZ, C_BQ, C_BK, C_BV, C_BZ, C_GL = 0, 1536, 3072, 4608, 5120, 5632, 5760, 5888, 6400
XCOLS = 4096


def _xlayout():
    units = []
    sec = {}

    def add_unit(items):
        u = len(units)
        r = 0
        for k, n in items:
            sec[k] = (u, r, n)
            r += n
        units.append(r)

    add_unit([("KB", 128)])
    add_unit([("VB", 130)])
    for g, (_, d) in enumerate(GROUPS):
        nk = 8 * d
        nv = -(-(d * 64 * 520) // XCOLS)
        items = [(("AKF", g), nk), (("AKL", g), nk), (("AVF", g), nv), (("AVL", g), nv)]
        if nk + nk + nv + nv <= 130:
            add_unit(items)
        else:
            for it in items:
                add_unit([it])
    return units, sec


XUNITS, XSEC = _xlayout()


class Prog:
    CE = ("pe", "act", "dve", "pool")
    ALLE = ("pe", "act", "dve", "pool", "sp")
    NDS = 8

    def __init__(self, nc, es):
        self.nc = nc
        self.sem = {}
        for e in self.CE:
            self.sem[("c", e)] = es.enter_context(nc.semaphore(f"c_{e}"))
        for q in ("sp", "act", "pool"):
            for i in range(self.NDS):
                self.sem[("d", q, i)] = es.enter_context(nc.semaphore(f"d_{q}{i}"))
        self.sem[("cc",)] = es.enter_context(nc.semaphore("ccsem"))
        self.cnt = {k: 0 for k in self.sem}
        self.dnext = {q: 0 for q in ("sp", "act", "pool")}
        self.ops = {e: [] for e in self.ALLE}
        self.waited = {e: {} for e in self.ALLE}
        self.lastw = {}
        self.readers = {}
        self.nops = 0

    def _deps(self, reads, writes):
        deps = set()
        for k in reads:
            if k in self.lastw:
                deps.add(self.lastw[k])
        for k in writes:
            if k in self.lastw:
                deps.add(self.lastw[k])
            deps.update(self.readers.get(k, ()))
        return deps

    def _commit(self, ticket, reads, writes):
        for k in reads:
            self.readers.setdefault(k, []).append(ticket)
        for k in writes:
            self.lastw[k] = ticket
            self.readers[k] = []

    def _waits(self, eng, deps):
        w = []
        for (sk, val) in sorted(deps, key=lambda t: (str(t[0]), t[1])):
            if eng == "pe" and sk == ("c", "pe"):
                continue
            if self.waited[eng].get(sk, 0) >= val:
                continue
            self.waited[eng][sk] = val
            w.append((self.sem[sk], val))
        return w

    def op(self, eng, fn, reads=(), writes=()):
        deps = self._deps(reads, writes)
        w = self._waits(eng, deps)
        sk = ("c", eng)
        self.cnt[sk] += 1
        t = (sk, self.cnt[sk])
        self.ops[eng].append((w, fn, self.sem[sk], 1))
        self._commit(t, reads, writes)
        self.nops += 1
        return t

    def dma(self, q, out, in_, reads=(), writes=()):
        deps = self._deps(reads, writes)
        slot = self.dnext[q] % self.NDS
        self.dnext[q] += 1
        sk = ("d", q, slot)
        if self.cnt[sk] > 0:
            deps.add((sk, self.cnt[sk]))
        w = self._waits(q, deps)
        self.cnt[sk] += 16
        t = (sk, self.cnt[sk])
        self.ops[q].append((w, lambda e: e.dma_start(out=out, in_=in_), self.sem[sk], 16))
        self._commit(t, reads, writes)
        self.nops += 1
        return t

    def collective(self, fn, reads=(), writes=()):
        deps = self._deps(reads, writes)
        w = self._waits("pool", deps)
        sk = ("cc",)
        self.cnt[sk] += 1
        t = (sk, self.cnt[sk])
        self.ops["pool"].append((w, fn, self.sem[sk], 1))
        self._commit(t, reads, writes)
        return t

    def barrier(self):
        tickets = set((sk, v) for sk, v in self.cnt.items() if v > 0)
        for e in self.ALLE:
            w = self._waits(e, tickets)
            if w:
                self.ops[e].append((w, None, None, 0))
        self.lastw = {}
        self.readers = {}

    def flush(self):
        nc = self.nc
        ops = self.ops

        def replay(lst, e):
            for (w, fn, sem, inc) in lst:
                for (s, v) in w:
                    e.wait_ge(s, v)
                if fn is not None:
                    ins = fn(e)
                    ins.then_inc(sem, inc)

        with nc.Block() as block:
            @block.tensor
            def _(e):
                replay(ops["pe"], e)

            @block.scalar
            def _(e):
                replay(ops["act"], e)

            @block.vector
            def _(e):
                replay(ops["dve"], e)

            @block.gpsimd
            def _(e):
                replay(ops["pool"], e)

            @block.sync
            def _(e):
                replay(ops["sp"], e)
        self.ops = {e: [] for e in self.ALLE}

    def mm(self, out, lhsT, rhs, start=True, stop=True, reads=(), writes=()):
        return self.op("pe", lambda e: e.matmul(out, lhsT, rhs, start=start, stop=stop), reads, writes)

    def tr(self, out, in_, ident, reads=(), writes=()):
        return self.op("pe", lambda e: e.transpose(out, in_, ident), reads, writes)

    def act(self, out, in_, func, bias=None, scale=None, reads=(), writes=(), eng="act"):
        kw = {}
        if bias is not None:
            kw["bias"] = bias
        if scale is not None:
            kw["scale"] = scale
        return self.op(eng, lambda e: e.activation(out, in_, func, **kw), reads, writes)

    def tt(self, eng, out, in0, in1, op, reads=(), writes=()):
        return self.op(eng, lambda e: e.tensor_tensor(out, in0, in1, op), reads, writes)

    def ts(self, eng, out, in0, s1, s2, op0, op1=None, reads=(), writes=()):
        if op1 is None:
            return self.op(eng, lambda e: e.tensor_scalar(out, in0, s1, None, op0), reads, writes)
        return self.op(eng, lambda e: e.tensor_scalar(out, in0, s1, s2, op0, op1), reads, writes)

    def stt(self, out, in0, scalar, in1, op0, op1, reads=(), writes=()):
        return self.op("dve", lambda e: e.scalar_tensor_tensor(out, in0, scalar, in1, op0, op1), reads, writes)

    def copy(self, eng, out, in_, reads=(), writes=()):
        if eng == "act":
            return self.op("act", lambda e: e.activation(out, in_, AF.Copy), reads, writes)
        return self.op(eng, lambda e: e.tensor_copy(out, in_), reads, writes)

    def memset(self, eng, ap, val, reads=(), writes=()):
        return self.op(eng, lambda e: e.memset(ap, val), reads, writes)

    def recip(self, out, in_, reads=(), writes=()):
        return self.op("dve", lambda e: e.reciprocal(out, in_), reads, writes)


class TPool:
    def __init__(self, tiles, name):
        self.tiles = tiles
        self.name = name
        self.i = 0

    def next(self):
        j = self.i % len(self.tiles)
        self.i += 1
        return self.tiles[j], (self.name, j)


def _build(debug=None):
    nc = bass.Bass("TRN2", target_bir_lowering=False)
    dbg_kind = "ExternalOutput" if debug else "Internal"

    def din(name, shape, dt=F32):
        return nc.dram_tensor(name, list(shape), dt, kind="ExternalInput").ap()

    def dscr(name, shape, dt=BF16, dbg=False):
        isdbg = bool(debug) and dbg and name in debug.get("outs", ())
        return nc.dram_tensor(name, list(shape), dt, kind=("ExternalOutput" if isdbg else "Internal")).ap()

    x_in = din("x", [T, D])
    cT_in = din("cT", [128, 8])
    ln_g_in = din("ln_g", [DEPTH, 1, D])
    ln_b_in = din("ln_b", [DEPTH, 1, D])
    w_ada_in = din("w_ada", [DEPTH, D, 3 * D])
    b_ada_in = din("b_ada", [DEPTH, 1, 3 * D])
    w_in_in = din("w_in", [DEPTH, D, 8448])
    bgT_in = din("bgT", [DEPTH, 128, 16])
    qkg_in = din("qkg", [DEPTH, 128, 2])
    w_pa_in = din("w_pa", [DEPTH, 512, D])
    w_pb_in = din("w_pb", [DEPTH, 512, D])
    w_o_in = din("w_o", [DEPTH, D, D])
    cmat_in = din("cmat", [3, 128, 128])
    cstab_in = din("cstab", [2, 128, T])
    abias_in = din("abias", [128, 24 * 256])
    tri_in = din("tri", [2, 128, 256])
    msk_in = din("msk", [128, 2])
    y_out = nc.dram_tensor("y", [T, D], F32, kind="ExternalOutput").ap()

    aqT = dscr("aqT", [3, 512, T], dbg=True)
    akT = [dscr(f"akT{g}", [512, d, T // d + 128], dbg=True) for g, (_, d) in enumerate(GROUPS)]
    avp = [dscr(f"avp{g}", [d, T // d + 128, 520], dbg=True) for g, (_, d) in enumerate(GROUPS)]
    sazT = dscr("sazT", [512, T], dbg=True)
    sbzT = dscr("sbzT", [512, T], dbg=True)
    qbT = dscr("qbT", [512, T], dbg=True)
    gT = dscr("gT", [2048, T], dbg=True)
    yaT = dscr("yaT", [512, T], dbg=True)
    ybT = dscr("ybT", [512, T], dbg=True)
    x1 = dscr("x1", [T, D], F32, dbg=True)
    Edram = dscr("Edram", [128, 24 * 256], F32)
    rdscr = dscr("rdscr", [8, 512], F32)
    rdslot = [0]
    xsrc_u = [dscr(f"xsrc{u}", [n, XCOLS]) for u, n in enumerate(XUNITS)]
    xdst_u = [dscr(f"xdst{u}", [2 * n, XCOLS]) for u, n in enumerate(XUNITS)]

    def xs(key):
        u, r0, n = XSEC[key]
        return xsrc_u[u][r0:r0 + n, :]

    def xd(key, rank):
        u, r0, n = XSEC[key]
        return xdst_u[u][rank * XUNITS[u] + r0:rank * XUNITS[u] + r0 + n, :]
    dbg_names = ["aqT", "akT0", "akT1", "akT2", "avp0", "avp1", "avp2", "sazT", "sbzT", "qbT", "gT",
                 "yaT", "ybT", "x1"]

    with ExitStack() as top:
        P = Prog(nc, top)

        def sb(es, name, shape, dt):
            return es.enter_context(nc.sbuf_tensor("s_" + name, list(shape), dt))

        PSALL = top.enter_context(nc.psum_tensor("psall", [128, 8, 512], F32))
        PS = [PSALL[:, i, :] for i in range(8)]

        class PSPool:
            def __init__(self, idx):
                self.idx = idx
                self.i = 0

            def next(self):
                b = self.idx[self.i % len(self.idx)]
                self.i += 1
                return PS[b], ("ps", b)

        def pspool(idx, name):
            return PSPool(idx)

        cmat = sb(top, "cmat", [128, 3, 128], F32)
        ones_f = sb(top, "ones_f", [128, 128], F32)
        msk = sb(top, "msk", [128, 2], F32)
        P.dma("sp", cmat[:, :, :], cmat_in.rearrange("c p n -> p c n"), writes=["cmat"])
        P.dma("sp", msk[:, :], msk_in, writes=["msk"])
        P.memset("pool", ones_f[:, :], 1.0, writes=["ones"])
        cmatb = sb(top, "cmatb", [128, 2, 128], BF16)
        P.copy("dve", cmatb[:, :, :], cmat[:, 1:3, :], reads=["cmat"], writes=["cmatb"])
        epsc = sb(top, "epsc", [128, 2], F32)
        P.memset("pool", epsc[:, 0:1], QK_EPS, writes=["epsc"])
        P.memset("pool", epsc[:, 1:2], LN_EPS, writes=["epsc"])
        ident = cmat[:, 0, :]
        bones = cmat[:, 1, :]
        rrot = cmat[:, 2, :]

        with ExitStack() as es:
            ab = sb(es, "ab", [128, 24 * 256], F32)
            tri = sb(es, "tri", [128, 2, 256], F32)
            P.dma("sp", ab[:, :], abias_in, writes=["ab"])
            P.dma("sp", tri[:, :, :], tri_in.rearrange("c p n -> p c n"), writes=["tri"])
            P.act(ab[:, :], ab[:, :], AF.Exp, reads=["ab"], writes=["ab"])
            for gh in range(24):
                sl = slice(gh * 256, (gh + 1) * 256)
                P.tt("dve", ab[:, sl], ab[:, sl], tri[:, 0, :], ALU.mult, reads=["ab", "tri"], writes=[("ab", gh)])
            P.dma("sp", Edram, ab[:, :], reads=[("ab", gh) for gh in range(24)])
            P.barrier()
            P.flush()

        for l in range(DEPTH):
            if debug and l > debug.get("layers", DEPTH) - 1:
                break
            xsrc_l = x_in if l == 0 else x1
            ydst_l = x1 if l < DEPTH - 1 else y_out
            with ExitStack() as lay:
                shiftT = sb(lay, f"shiftT{l}", [128, 8], F32)
                sc1T = sb(lay, f"sc1T{l}", [128, 8], F32)
                gate_b = sb(lay, f"gate_b{l}", [128, D], F32)
                lng_b = sb(lay, f"lng_b{l}", [128, D], F32)
                lnb_b = sb(lay, f"lnb_b{l}", [128, D], F32)
                bgT = sb(lay, f"bgT{l}", [128, 16], F32)
                qkg = sb(lay, f"qkg{l}", [128, 2], F32)

                with ExitStack() as es:
                    cT = sb(es, f"cT{l}", [128, 8], F32)
                    silc = sb(es, f"silc{l}", [128, 8], F32)
                    rows = sb(es, f"rows{l}", [1, 5 * D], F32)
                    brow = sb(es, f"brow{l}", [1, 3 * D], F32)
                    wst = [sb(es, f"wada{l}_{i}", [128, 8, 512], F32) for i in range(2)]
                    wp = TPool(wst, "wada")
                    psp = pspool([0, 1], "ps")
                    P.dma("sp", cT[:, :], cT_in, writes=["cT"])
                    P.dma("sp", brow[:, :], b_ada_in[l], writes=["brow"])
                    P.dma("sp", rows[:, 3 * D:4 * D], ln_g_in[l], writes=["rows_ln"])
                    P.dma("sp", rows[:, 4 * D:5 * D], ln_b_in[l], writes=["rows_ln"])
                    P.dma("sp", bgT[:, :], bgT_in[l], writes=["bgT"])
                    P.dma("sp", qkg[:, :], qkg_in[l], writes=["qkg"])
                    P.act(silc[:, :], cT[:, :], AF.Silu, reads=["cT"], writes=["silc"])
                    for n in range(6):
                        wt, kw = wp.next()
                        P.dma("sp", wt[:, :, :],
                              w_ada_in[l, :, n * 512:(n + 1) * 512].rearrange("(kc p) n -> p kc n", p=128),
                              writes=[kw])
                        ps, kp = psp.next()
                        for kc in range(8):
                            P.mm(ps[0:1, :], silc[:, kc:kc + 1], wt[:, kc, :], start=(kc == 0), stop=(kc == 7),
                                 reads=[kw, "silc"], writes=[kp])
                        P.tt("dve", rows[0:1, n * 512:(n + 1) * 512], ps[0:1, :], brow[0:1, n * 512:(n + 1) * 512],
                             ALU.add, reads=[kp, "brow"], writes=[("rows", n)])
                    ps, kp = psp.next()
                    for j in range(16):
                        P.mm(ps[:, j:j + 1], rows[0:1, j * 128:(j + 1) * 128], ones_f[0:1, 0:1],
                             reads=[("rows", j // 4), "ones"], writes=[kp])
                    P.copy("dve", shiftT[:, :], ps[:, 0:8], reads=[kp], writes=["mod"])
                    P.ts("dve", sc1T[:, :], ps[:, 8:16], 1.0, None, ALU.add, reads=[kp], writes=["mod2"])
                    for (dst, c0, rk) in ((gate_b, 2 * D, [("rows", 4), ("rows", 5)]), (lng_b, 3 * D, ["rows_ln"]),
                                          (lnb_b, 4 * D, ["rows_ln"])):
                        for nh in range(2):
                            ps, kp = psp.next()
                            P.mm(ps[:, :], ones_f[0:1, :], rows[0:1, c0 + nh * 512:c0 + (nh + 1) * 512],
                                 reads=rk + ["ones"], writes=[kp])
                            P.copy("dve", dst[:, nh * 512:(nh + 1) * 512], ps[:, :], reads=[kp], writes=["bc"])
                    P.barrier()
                    P.flush()

                with ExitStack() as es:
                  if not (debug and debug.get("skipP1")):
                        uT = sb(es, f"uT{l}", [128, 8, T], BF16)
                        with ExitStack() as es2:
                            xp = TPool([sb(es2, f"xt{l}_{i}", [128, 4, D], F32) for i in range(2)], "xt")
                            psp = pspool([0, 1, 2, 3], "ps")
                            for tb in range(8):
                                xt, kx = xp.next()
                                P.dma("sp", xt[:, :, :],
                                      xsrc_l[tb * 512:(tb + 1) * 512, :].rearrange("(t p) d -> p t d", p=128), writes=[kx])
                                for kc in range(8):
                                    ps, kp = psp.next()
                                    for t in range(4):
                                        P.tr(ps[:, t * 128:(t + 1) * 128], xt[:, t, kc * 128:(kc + 1) * 128], ident,
                                             reads=[kx], writes=[kp])
                                    P.act(uT[:, kc, tb * 512:(tb + 1) * 512], ps[:, :], AF.Identity,
                                          bias=shiftT[:, kc:kc + 1], scale=sc1T[:, kc:kc + 1], reads=[kp])
                            P.barrier()
                            P.flush()

                        uTp = sb(es, f"uTp{l}", [128, 8, T], BF16)
                        wstp = TPool([sb(es, f"wst{l}_{i}", [128, 8, 256], F32) for i in range(2)], "wst")
                        wbfp = TPool([sb(es, f"wbf{l}_{i}", [128, 8, 512], BF16) for i in range(2)], "wbf")
                        stgp = TPool([sb(es, f"stg{l}_{i}", [128, 512], BF16) for i in range(4)], "stg")
                        vstp = TPool([sb(es, f"vst{l}_{i}", [128, 8, 65], BF16) for i in range(3)], "vst")
                        f32p = TPool([sb(es, f"f32t{l}_{i}", [128, 512], F32) for i in range(6)], "f32t")
                        b16p = TPool([sb(es, f"b16t{l}_{i}", [128, 512], BF16) for i in range(5)], "b16t")
                        csp = TPool([sb(es, f"cst{l}_{i}", [128, 2, 512], F32) for i in range(2)], "cst")
                        psA = pspool([0, 1, 2, 3, 4], "ps")
                        psB = pspool([5, 6, 7], "ps")
                        for vt in vstp.tiles:
                            P.memset("pool", vt[:, :, :], 1.0)
                        P.barrier()
                        evac_rr = [0]

                        def evac_copy(out, in_, reads, writes):
                            evac_rr[0] += 1
                            if evac_rr[0] % 2:
                                return P.copy("act", out, in_, reads=reads, writes=writes)
                            return P.copy("dve", out, in_, reads=reads, writes=writes)

                        def load_strip(c0, W):
                            wb, kb = wbfp.next()
                            for w0 in range(0, W, 256):
                                wt, kw = wstp.next()
                                P.dma("sp", wt[:, :, :],
                                      w_in_in[l, :, c0 + w0:c0 + w0 + 256].rearrange("(kc p) n -> p kc n", p=128),
                                      writes=[kw])
                                P.copy("pool", wb[:, :, w0:w0 + 256], wt[:, :, :], reads=[kw], writes=[(kb, w0)])
                            return wb, [(kb, w0) for w0 in range(0, W, 256)]

                        def permute_uT(d):
                            engs = ("dve", "pool", "act")
                            for kc in range(8):
                                P.copy(engs[kc % 3], uTp[:, kc, :].rearrange("k (r p) -> k r p", r=d),
                                       uT[:, kc, :].rearrange("k (p r) -> k r p", r=d), writes=[("uTp", kc)])

                        def rhs_perm(g, kc, pb):
                            src = uT if g == 0 else uTp
                            return src[:, kc, pb * 512:(pb + 1) * 512]

                        def lhs_perm(g, kc, it):
                            src = uT if g == 0 else uTp
                            return src[:, kc, it * 128:(it + 1) * 128]

                        def fm_chunk(wb, kb, col0, rhs_fn, extra=()):
                            ps, kp = psA.next()
                            for kc in range(8):
                                rd_ = list(kb) + [e_ for e_ in extra if e_[1] == kc]
                                P.mm(ps[:, :], wb[:, kc, col0:col0 + 128], rhs_fn(kc), start=(kc == 0), stop=(kc == 7),
                                     reads=rd_, writes=[kp])
                            return ps, kp

                        def do_qk(which, g, wb, kb):
                            d = GROUPS[g][1]
                            ex_ = [("uTp", kc) for kc in range(8)] if g > 0 else []
                            for fcl in range(4):
                                for pb in range(8):
                                    ps, kp = fm_chunk(wb, kb, fcl * 128, lambda kc: rhs_perm(g, kc, pb), ex_)
                                    stg, ks = stgp.next()
                                    evac_copy(stg[:, :], ps[:, :], [kp], [ks])
                                    rows_ = slice(fcl * 128, (fcl + 1) * 128)
                                    if which == "q":
                                        P.dma("sp", aqT[g, rows_, pb * 512:(pb + 1) * 512], stg[:, :], reads=[ks])
                                    elif g == 0:
                                        P.dma("sp", akT[0][rows_, 0, 64 + pb * 512:64 + (pb + 1) * 512], stg[:, :],
                                              reads=[ks])
                                    elif g == 1:
                                        r, p0 = pb // 2, (pb % 2) * 512
                                        P.dma("sp", akT[1][rows_, r, 64 + p0:64 + p0 + 512], stg[:, :], reads=[ks])
                                    else:
                                        P.dma("sp", akT[2][rows_, 2 * pb:2 * pb + 2, 64:64 + 256],
                                              stg[:, :].rearrange("p (r c) -> p r c", r=2), reads=[ks])

                        def do_v(g, wb, kb):
                            d = GROUPS[g][1]
                            per = (T // d) // 128
                            for it in range(32):
                                ps, kp = psA.next()
                                for kc in range(8):
                                    rd_ = list(kb) + ([("uTp", kc)] if g > 0 else [])
                                    P.mm(ps[:, :], lhs_perm(g, kc, it), wb[:, kc, 0:512], start=(kc == 0), stop=(kc == 7),
                                         reads=rd_, writes=[kp])
                                vt, kv = vstp.next()
                                evac_copy(vt[:, :, 0:64], ps[:, :].rearrange("p (h e) -> p h e", h=8), [kp], [kv])
                                r, p0 = it // per, (it % per) * 128
                                P.dma("sp", avp[g][r, 64 + p0:64 + p0 + 128, :],
                                      vt[:, :, :].rearrange("p h c -> p (h c)"), reads=[kv])

                        def do_gate(dst, s_, wb, kb):
                            for fcl in range(4):
                                for tb in range(8):
                                    ps, kp = fm_chunk(wb, kb, fcl * 128, lambda kc: uT[:, kc, tb * 512:(tb + 1) * 512])
                                    stg, ks = stgp.next()
                                    j = s_ * 4 + fcl
                                    if dst is gT:
                                        P.act(stg[:, :], ps[:, :], AF.Sigmoid, bias=bgT[:, j:j + 1], reads=[kp],
                                              writes=[ks])
                                    else:
                                        P.act(stg[:, :], ps[:, :], AF.Silu, reads=[kp], writes=[ks])
                                    P.dma("sp", dst[j * 128:(j + 1) * 128, tb * 512:(tb + 1) * 512], stg[:, :],
                                          reads=[ks])

                        jobs = []
                        for g in range(3):
                            if g > 0:
                                jobs.append((None, 0, (lambda g_: (lambda wb, kb: permute_uT(GROUPS[g_][1])))(g)))
                            jobs.append((C_AQ + g * 512, 512, (lambda g_: (lambda wb, kb: do_qk("q", g_, wb, kb)))(g)))
                            jobs.append((C_AK + g * 512, 512, (lambda g_: (lambda wb, kb: do_qk("k", g_, wb, kb)))(g)))
                            jobs.append((C_AV + g * 512, 512, (lambda g_: (lambda wb, kb: do_v(g_, wb, kb)))(g)))
                        jobs.append((C_AZ, 512, lambda wb, kb: do_gate(sazT, 0, wb, kb)))
                        jobs.append((C_BZ, 512, lambda wb, kb: do_gate(sbzT, 0, wb, kb)))
                        for s_ in range(4):
                            jobs.append((C_GL + s_ * 512, 512, (lambda s2: (lambda wb, kb: do_gate(gT, s2, wb, kb)))(s_)))
                        strips = [j for j in jobs if j[0] is not None]
                        loaded = {}
                        nxt = [0]

                        def prefetch():
                            if nxt[0] < len(strips):
                                c0_, W_, _ = strips[nxt[0]]
                                loaded[nxt[0]] = load_strip(c0_, W_)
                                nxt[0] += 1

                        prefetch()
                        si = 0
                        for (c0_, W_, fn_) in jobs:
                            if c0_ is None:
                                fn_(None, None)
                                continue
                            wb, kb = loaded.pop(si)
                            si += 1
                            prefetch()
                            fn_(wb, kb)
                        xs_kb = xs("KB")
                        xs_vb = xs("VB").rearrange("r c -> (r c)").rearrange("(t f) -> t f", f=130)
                        wbq, kbq = load_strip(C_BQ, 512)
                        wbk, kbk = load_strip(C_BK, 256)
                        units = [(fcl, tb) for fcl in range(5) for tb in range(8)]
                        ust = {}

                        def ropeA(i):
                            fcl, tb = units[i]
                            wb, kb, col0, gcol = (wbq, kbq, fcl * 128, 0) if fcl < 4 else (wbk, kbk, 0, 1)
                            tsl = slice(tb * 512, (tb + 1) * 512)
                            cst, kc_ = csp.next()
                            P.dma("sp", cst[:, :, :], cstab_in[:, :, tsl].rearrange("c p t -> p c t"), writes=[kc_])
                            ps, kp = fm_chunk(wb, kb, col0, lambda kc: uT[:, kc, tsl])
                            sq, k1 = b16p.next()
                            P.act(sq[:, :], ps[:, :], AF.Square, reads=[kp], writes=[k1])
                            ust[i] = dict(fcl=fcl, tsl=tsl, gcol=gcol, cst=cst, kc_=kc_, ps=ps, kp=kp, sq=sq, k1=k1)

                        def ropeB(i):
                            u = ust[i]
                            ps2, kp2 = psB.next()
                            P.mm(ps2[:, :], cmatb[:, 0, :], u["sq"][:, :], reads=[u["k1"], "cmatb"], writes=[kp2])
                            srt, k2a = f32p.next()
                            P.act(srt[:, :], ps2[:, :], AF.Ln, bias=epsc[:, 0:1], reads=[kp2], writes=[k2a])
                            rstd, k2 = f32p.next()
                            P.act(rstd[:, :], srt[:, :], AF.Exp, scale=-0.5, reads=[k2a], writes=[k2])
                            xn, k3 = b16p.next()
                            P.stt(xn[:, :], u["ps"][:, :], qkg[:, u["gcol"]:u["gcol"] + 1], rstd[:, :], ALU.mult, ALU.mult,
                                  reads=[u["kp"], k2], writes=[k3])
                            u["xn"], u["k3"] = xn, k3

                        def ropeC(i):
                            u = ust.pop(i)
                            xn, k3, cst, kc_ = u["xn"], u["k3"], u["cst"], u["kc_"]
                            ps3, kp3 = psB.next()
                            P.mm(ps3[:, :], cmatb[:, 1, :], xn[:, :], reads=[k3, "cmatb"], writes=[kp3])
                            ta, k4 = f32p.next()
                            P.tt("pool", ta[:, :], xn[:, :], cst[:, 0, :], ALU.mult, reads=[k3, kc_], writes=[k4])
                            tb_, k5 = f32p.next()
                            P.tt("dve", tb_[:, :], ps3[:, :], cst[:, 1, :], ALU.mult, reads=[kp3, kc_], writes=[k5])
                            stg, ks = stgp.next()
                            P.tt("pool", stg[:, :], ta[:, :], tb_[:, :], ALU.add, reads=[k4, k5], writes=[ks])
                            if u["fcl"] < 4:
                                P.dma("sp", qbT[u["fcl"] * 128:(u["fcl"] + 1) * 128, u["tsl"]], stg[:, :], reads=[ks])
                            else:
                                P.dma("sp", xs_kb[:, u["tsl"]], stg[:, :], reads=[ks])

                        nu = len(units)
                        for i in range(nu + 2):
                            if i >= 2:
                                ropeC(i - 2)
                            if 1 <= i <= nu:
                                ropeB(i - 1)
                            if i < nu:
                                ropeA(i)
                        for it in range(32):
                            ps, kp = psA.next()
                            for kc in range(8):
                                P.mm(ps[:, 0:128], uT[:, kc, it * 128:(it + 1) * 128], wbk[:, kc, 128:256],
                                     start=(kc == 0), stop=(kc == 7), reads=kbk, writes=[kp])
                            vt, kv = vstp.next()
                            evac_copy(vt[:, 0:2, 0:64], ps[:, 0:128].rearrange("p (h e) -> p h e", h=2), [kp], [kv])
                            P.dma("sp", xs_vb[it * 128:(it + 1) * 128, :],
                                  vt[:, 0:2, :].rearrange("p h c -> p (h c)"), reads=[kv])
                        P.barrier()
                        P.flush()

                if debug and debug.get("upto") == "P1":
                    break

                with ExitStack() as es:
                    vh = TPool([sb(es, f"vh{l}_{i}", [64, 16, 520], BF16) for i in range(2)], "vh")
                    for g, (_, d) in enumerate(GROUPS):
                        L = T // d
                        for (kk, c0) in (("AKF", 64), ("AKL", L)):
                            sec = xs((kk, g)).rearrange("r c -> (r c)").rearrange(
                                "(f r c) -> f r c", r=d, c=64)
                            for fb in range(4):
                                P.dma("sp", sec[fb * 128:(fb + 1) * 128], akT[g][fb * 128:(fb + 1) * 128, :, c0:c0 + 64])
                        for (kk, c0) in (("AVF", 64), ("AVL", L)):
                            nel = d * 64 * 520
                            sec = xs((kk, g)).rearrange("r c -> (r c)")[0:nel].rearrange(
                                "(r p f) -> r p f", p=64, f=520)
                            P.dma("sp", sec, avp[g][:, c0:c0 + 64, :])
                    P.barrier()
                    for u in range(len(XUNITS)):
                        P.collective((lambda a, b: (lambda e: e.collective_compute(
                            "AllGather", ALU.bypass, replica_groups=[[0, 1], [2, 3], [4, 5], [6, 7]],
                            ins=[a], outs=[b])))(xsrc_u[u], xdst_u[u]))
                    P.barrier()
                    for g, (_, d) in enumerate(GROUPS):
                        L = T // d
                        for (kk, rb, c0) in (("AKL", 0, 0), ("AKF", 1, 64 + L)):
                            sec = xd((kk, g), rb).rearrange("r c -> (r c)").rearrange(
                                "(f r c) -> f r c", r=d, c=64)
                            for fb in range(4):
                                P.dma("sp", akT[g][fb * 128:(fb + 1) * 128, :, c0:c0 + 64], sec[fb * 128:(fb + 1) * 128])
                        for (kk, rb, c0, mc) in (("AVL", 0, 0, 0), ("AVF", 1, 64 + L, 1)):
                            nel = d * 64 * 520
                            sec = xd((kk, g), rb).rearrange("r c -> (r c)")[0:nel].rearrange(
                                "(r p f) -> p r f", p=64, f=520)
                            t_, kt = vh.next()
                            P.dma("sp", t_[:, 0:d, :], sec, writes=[kt])
                            P.ts("dve", t_[:, 0:d, :], t_[:, 0:d, :], msk[0:64, mc:mc + 1], None, ALU.mult,
                                 reads=[kt, "msk"], writes=[kt])
                            P.dma("sp", avp[g][:, c0:c0 + 64, :].rearrange("r p f -> p r f"), t_[:, 0:d, :], reads=[kt])
                    P.barrier()
                    P.flush()

                if debug and debug.get("upto") == "X":
                    break

                with ExitStack() as es:
                    E = sb(es, f"E{l}", [128, 24, 256], F32)
                    acc = sb(es, f"acc{l}", [65, 4, T], F32)
                    vaug = sb(es, f"vaug{l}", [128, 48, 4, 65], BF16)
                    ktp = TPool([sb(es, f"kt{l}_{i}", [64, 6144], BF16) for i in range(2)], "kt")
                    qtp = TPool([sb(es, f"qt{l}_{i}", [64, T], BF16) for i in range(2)], "qt")
                    exp_ = TPool([sb(es, f"ex{l}_{i}", [128, 512], F32) for i in range(4)], "ex")
                    ptp = TPool([sb(es, f"pt{l}_{i}", [128, 512], BF16) for i in range(5)], "pt")
                    rdp = TPool([sb(es, f"rd{l}_{i}", [65, 512], F32) for i in range(2)], "rd")
                    nmp = TPool([sb(es, f"nm{l}_{i}", [64, 512], F32) for i in range(2)], "nm")
                    szp = TPool([sb(es, f"sz{l}_{i}", [64, 512], BF16) for i in range(2)], "sz")
                    ysp = TPool([sb(es, f"ys{l}_{i}", [64, 512], BF16) for i in range(2)], "ys")
                    psS = pspool([0, 1, 2, 3], "ps")
                    psO = pspool([4, 5, 6], "ps")
                    psN = pspool([7], "ps")
                    P.dma("sp", E[:, :, :], Edram.rearrange("p (g c) -> p g c", c=256), writes=["E"])
                    for hh in range(2):
                        for g, (_, d) in enumerate(GROUPS):
                            L = T // d
                            nch = L // 128 + 1
                            vsrc = avp[g].rearrange("r (m p) (h c) -> p (r m) h c", p=128, c=65)
                            for c0 in range(0, d * nch, 12):
                                c1 = min(d * nch, c0 + 12)
                                P.dma("sp", vaug[:, c0:c1, :, :], vsrc[:, c0:c1, hh * 4:(hh + 1) * 4, :],
                                      writes=[("vaug", c0)])
                            vkeys = [("vaug", c0) for c0 in range(0, d * nch, 12)]
                            for hl in range(4):
                                h = hh * 4 + hl
                                kt, kk = ktp.next()
                                P.dma("sp", kt[:, 0:d * (L + 128)],
                                      akT[g][h * 64:(h + 1) * 64, :, :].rearrange("p r c -> p (r c)"), writes=[kk])
                                qt, kq = qtp.next()
                                P.dma("sp", qt[:, :], aqT[g, h * 64:(h + 1) * 64, :], writes=[kq])
                                blocks = [(r, n2) for r in range(d) for n2 in range(L // 256)]
                                st = {}
                                e0 = E[:, g * 8 + h, :]
                                ebc = bass.AP(e0.tensor, e0.offset, [list(e0.ap[0]), [0, 2], list(e0.ap[1])])

                                def stageA(i):
                                    r, n2 = blocks[i]
                                    n = 2 * n2
                                    ps, kp = psS.next()
                                    kb0 = r * (L + 128) + 128 * n
                                    qb0 = r * L + 128 * n
                                    P.mm(ps[:, 0:128], kt[:, kb0:kb0 + 128], qt[:, qb0:qb0 + 128],
                                         reads=[kk, kq], writes=[kp])
                                    P.mm(ps[:, 128:384], kt[:, kb0 + 128:kb0 + 256], qt[:, qb0:qb0 + 256],
                                         reads=[kk, kq], writes=[kp])
                                    P.mm(ps[:, 384:512], kt[:, kb0 + 256:kb0 + 384], qt[:, qb0 + 128:qb0 + 256],
                                         reads=[kk, kq], writes=[kp])
                                    ex, ke = exp_.next()
                                    P.act(ex[:, :], ps[:, :], AF.Exp, scale=0.125, reads=[kp], writes=[ke])
                                    pt, kpt = ptp.next()
                                    P.tt("dve" if i % 3 == 2 else "pool", pt[:, :].rearrange("p (a b) -> p a b", a=2),
                                         ex[:, :].rearrange("p (a b) -> p a b", a=2), ebc, ALU.mult,
                                         reads=[ke, "E"], writes=[kpt])
                                    st[i] = (pt, kpt)

                                def stageB(i):
                                    r, n2 = blocks[i]
                                    n = 2 * n2
                                    pt, kpt = st.pop(i)
                                    po, ko = psO.next()
                                    ch = r * nch + n
                                    vk = lambda c_: [("vaug", (c_ // 12) * 12)]
                                    P.mm(po[0:65, 0:256], vaug[:, ch + 1, hl, :], pt[:, 128:384], start=True, stop=False,
                                         reads=[kpt] + vk(ch + 1), writes=[ko])
                                    P.mm(po[0:65, 0:128], vaug[:, ch, hl, :], pt[:, 0:128], start=False, stop=False,
                                         reads=[kpt] + vk(ch), writes=[ko])
                                    P.mm(po[0:65, 128:256], vaug[:, ch + 2, hl, :], pt[:, 384:512], start=False, stop=True,
                                         reads=[kpt] + vk(ch + 2), writes=[ko])
                                    t0 = r + d * 128 * n
                                    av_ = acc[:, hl, t0:t0 + d * 255 + 1:d]
                                    if g == 0:
                                        P.copy("act", av_, po[0:65, 0:256], reads=[ko], writes=[("acc", hl)])
                                    else:
                                        P.tt("dve", av_, av_, po[0:65, 0:256], ALU.add, reads=[ko, ("acc", hl)],
                                             writes=[("acc", hl)])

                                nb = len(blocks)
                                LK = 3
                                for i in range(nb + LK):
                                    if i < nb:
                                        stageA(i)
                                    if i >= LK:
                                        stageB(i - LK)
                        for hl in range(4):
                            P.act(acc[64:65, hl, :], acc[64:65, hl, :], AF.Ln, reads=[("acc", hl)], writes=[("acc", hl)])
                        for hl in range(4):
                            P.act(acc[64:65, hl, :], acc[64:65, hl, :], AF.Exp, scale=-1.0, reads=[("acc", hl)],
                                  writes=[("acc", hl)])
                        for hl in range(4):
                            h = hh * 4 + hl
                            for tb in range(8):
                                tsl = slice(tb * 512, (tb + 1) * 512)
                                sz, ksz = szp.next()
                                P.dma("sp", sz[:, :], sazT[h * 64:(h + 1) * 64, tsl], writes=[ksz])
                                pb_, kpb = psN.next()
                                P.mm(pb_[0:64, :], ones_f[64:65, 0:64], acc[64:65, hl, tsl], reads=[("acc", hl), "ones"],
                                     writes=[kpb])
                                nm, knm = nmp.next()
                                P.tt("dve", nm[:, :], acc[0:64, hl, tsl], pb_[0:64, :], ALU.mult,
                                     reads=[("acc", hl), kpb], writes=[knm])
                                ys, kys = ysp.next()
                                P.tt("pool", ys[:, :], nm[:, :], sz[:, :], ALU.mult, reads=[knm, ksz], writes=[kys])
                                P.dma("pool", yaT[h * 64:(h + 1) * 64, tsl], ys[:, :], reads=[kys])
                    P.barrier()
                    P.flush()

                if debug and debug.get("upto") == "P2":
                    break

                with ExitStack() as es:
                    kTd = sb(es, f"kTd{l}", [128, 2, S], BF16)
                    vb = sb(es, f"vb{l}", [128, 64, 130], BF16)
                    qT = sb(es, f"qT{l}", [128, 4, T], BF16)
                    ptp = TPool([sb(es, f"pB{l}_{i}", [128, 1024], BF16) for i in range(4)], "pB")
                    bcp = TPool([sb(es, f"bcB{l}_{i}", [64, 512], F32) for i in range(3)], "bcB")
                    evp = TPool([sb(es, f"evB{l}_{i}", [65, 512], F32) for i in range(3)], "evB")
                    rdp = TPool([sb(es, f"rdB{l}_{i}", [65, 512], F32) for i in range(2)], "rdB")
                    nm2 = TPool([sb(es, f"nmC{l}_{i}", [64, 512], F32) for i in range(2)], "nmC")
                    szp = TPool([sb(es, f"szB{l}_{i}", [64, 512], BF16) for i in range(3)], "szB")
                    ysp = TPool([sb(es, f"ysB{l}_{i}", [64, 512], BF16) for i in range(3)], "ysB")
                    psA2 = [(PSALL[:, 2 * j_:2 * j_ + 2, :].rearrange("p a b -> p (a b)"),
                             [("ps", 2 * j_), ("ps", 2 * j_ + 1)]) for j_ in range(3)]
                    psO = pspool([6, 7], "ps")
                    for rk in range(2):
                        kb_ = xd("KB", rk)
                        for kvh in range(2):
                            for half in range(2):
                                P.dma("sp", kTd[half * 64:(half + 1) * 64, kvh, rk * T:(rk + 1) * T],
                                      kb_[kvh * 64:(kvh + 1) * 64, :], writes=[("kTd", rk, kvh, half)])
                        vsec = xd("VB", rk).rearrange("r c -> (r c)").rearrange("(k p f) -> p k f", p=128, f=130)
                        for k0 in range(0, 32, 8):
                            P.dma("sp", vb[:, rk * 32 + k0:rk * 32 + k0 + 8, :], vsec[:, k0:k0 + 8, :],
                                  writes=[("vb", rk, k0)])
                    for c_ in range(4):
                        P.dma("sp", qT[:, c_, :], qbT[c_ * 128:(c_ + 1) * 128, :], writes=[("qT", c_)])
                    steps = [(qb, hp, kc) for qb in range(8) for hp in range(4) for kc in range(64)]
                    st = {}
                    acc_o = {}

                    def stageA(i):
                        qb, hp, kc = steps[i]
                        kvh = hp // 2
                        rk = kc // 32
                        psa, keys = psA2[i % 3]
                        for hh_ in range(2):
                            pr = hh_ * 64
                            P.mm(psa[:, hh_ * 512:(hh_ + 1) * 512], kTd[pr:pr + 64, kvh, kc * 128:(kc + 1) * 128],
                                 qT[pr:pr + 64, hp, qb * 512:(qb + 1) * 512],
                                 reads=[("kTd", rk, kvh, hh_), ("qT", hp)], writes=keys)
                        pt, kpt = ptp.next()
                        P.act(pt[:, :], psa, AF.Exp, scale=0.125, reads=keys, writes=[kpt])
                        st[i] = (pt, kpt)

                    def stageB(i):
                        qb, hp, kc = steps[i]
                        kvh = hp // 2
                        pt, kpt = st.pop(i)
                        if kc == 0:
                            acc_o[(qb, hp)] = [psO.next(), psO.next()]
                        for hh_ in range(2):
                            po, ko = acc_o[(qb, hp)][hh_]
                            P.mm(po[0:65, :], vb[:, kc, kvh * 65:(kvh + 1) * 65], pt[:, hh_ * 512:(hh_ + 1) * 512],
                                 start=(kc == 0), stop=(kc == 63),
                                 reads=[kpt, ("vb", kc // 32, ((kc % 32) // 8) * 8)], writes=[ko])
                        if kc == 63:
                            tsl = slice(qb * 512, (qb + 1) * 512)
                            for hh_ in range(2):
                                h = 2 * hp + hh_
                                po, ko = acc_o[(qb, hp)][hh_]
                                ev, kev = evp.next()
                                P.copy("dve", ev[:, :], po[0:65, :], reads=[ko], writes=[kev])
                                sz, ksz = szp.next()
                                P.dma("sp", sz[:, :], sbzT[h * 64:(h + 1) * 64, tsl], writes=[ksz])
                                rd, krd = rdp.next()
                                P.recip(rd[64:65, :], ev[64:65, :], reads=[kev], writes=[krd])
                                bc, kbc = bcp.next()
                                sl_ = rdslot[0] % 8
                                rdslot[0] += 1
                                P.dma("sp", rdscr[sl_:sl_ + 1, :], rd[64:65, :], reads=[krd], writes=[("rdscr", sl_)])
                                P.dma("sp", bc[:, :], bass.AP(rdscr.tensor, sl_ * 512, [[0, 64], [1, 512]]),
                                      reads=[("rdscr", sl_)], writes=[kbc])
                                n2, kn2 = nm2.next()
                                P.tt("dve", n2[:, :], ev[0:64, :], bc[:, :], ALU.mult, reads=[kev, kbc], writes=[kn2])
                                ys, kys = ysp.next()
                                P.tt("pool", ys[:, :], n2[:, :], sz[:, :], ALU.mult, reads=[kn2, ksz], writes=[kys])
                                P.dma("pool", ybT[h * 64:(h + 1) * 64, tsl], ys[:, :], reads=[kys])

                    LOOK = 2
                    ns = len(steps)
                    for i in range(ns + LOOK):
                        if i < ns:
                            stageA(i)
                        if i >= LOOK:
                            stageB(i - LOOK)
                    P.barrier()
                    P.flush()

                if debug and debug.get("upto") == "P3":
                    break

                with ExitStack() as es:
                    wpa = sb(es, f"wpa{l}", [128, 4, D], BF16)
                    wpb = sb(es, f"wpb{l}", [128, 4, D], BF16)
                    wo = sb(es, f"wo{l}", [128, 8, D], BF16)
                    wf = TPool([sb(es, f"wf{l}_{i}", [128, 8, 256], F32) for i in range(2)], "wf")
                    for (wdst, wsrc, nk_) in ((wpa, w_pa_in, 4), (wpb, w_pb_in, 4), (wo, w_o_in, 8)):
                        for nh in range(4):
                            wt, kw = wf.next()
                            P.dma("sp", wt[:, 0:nk_, :],
                                  wsrc[l, :, nh * 256:(nh + 1) * 256].rearrange("(k p) n -> p k n", p=128), writes=[kw])
                            P.copy("pool", wdst[:, :, nh * 256:(nh + 1) * 256], wt[:, 0:nk_, :], reads=[kw],
                                   writes=["wP4"])
                    yap = TPool([sb(es, f"ya{l}_{i}", [128, 4, 512], BF16) for i in range(2)], "ya")
                    ybp = TPool([sb(es, f"yb{l}_{i}", [128, 4, 512], BF16) for i in range(2)], "yb")
                    gp = TPool([sb(es, f"gg{l}_{i}", [128, 16, 512], BF16) for i in range(2)], "gg")
                    mtp = TPool([sb(es, f"mT{l}_{i}", [128, 8, 512], BF16) for i in range(2)], "mT")
                    t1p = TPool([sb(es, f"t1{l}_{i}", [128, 512], F32) for i in range(2)], "t1")
                    t2p = TPool([sb(es, f"t2{l}_{i}", [128, 512], F32) for i in range(2)], "t2")
                    xrp = TPool([sb(es, f"xr{l}_{i}", [128, D], F32) for i in range(3)], "xr")
                    hp = TPool([sb(es, f"hh{l}_{i}", [128, D], F32) for i in range(2)], "hh")
                    op_ = TPool([sb(es, f"oo{l}_{i}", [128, D], F32) for i in range(2)], "oo")
                    stp = TPool([sb(es, f"bst{l}_{i}", [128, 16], F32) for i in range(2)], "bst")
                    psA = pspool([0, 1, 2, 3], "ps")
                    psB = pspool([4, 5, 6, 7], "ps")
                    for tb in range(8):
                        tsl = slice(tb * 512, (tb + 1) * 512)
                        ya, kya = yap.next()
                        yb, kyb = ybp.next()
                        gg, kgg = gp.next()
                        P.dma("sp", ya[:, :, :], yaT[:, tsl].rearrange("(h p) t -> p h t", p=128), writes=[kya])
                        P.dma("sp", yb[:, :, :], ybT[:, tsl].rearrange("(h p) t -> p h t", p=128), writes=[kyb])
                        P.dma("sp", gg[:, :, :], gT[:, tsl].rearrange("(j p) t -> p j t", p=128), writes=[kgg])
                        mT, kmT = mtp.next()
                        for m in range(8):
                            pa, kpa = psA.next()
                            for h in range(4):
                                P.mm(pa[:, :], wpa[:, h, m * 128:(m + 1) * 128], ya[:, h, :], start=(h == 0),
                                     stop=(h == 3), reads=["wP4", kya], writes=[kpa])
                            pb_, kpb = psA.next()
                            for h in range(4):
                                P.mm(pb_[:, :], wpb[:, h, m * 128:(m + 1) * 128], yb[:, h, :], start=(h == 0),
                                     stop=(h == 3), reads=["wP4", kyb], writes=[kpb])
                            t1, k1 = t1p.next()
                            P.tt("dve", t1[:, :], pa[:, :], gg[:, m, :], ALU.mult, reads=[kpa, kgg], writes=[k1])
                            t2, k2 = t2p.next()
                            P.tt("dve", t2[:, :], pb_[:, :], gg[:, 8 + m, :], ALU.mult, reads=[kpb, kgg], writes=[k2])
                            P.tt("pool", mT[:, m, :], t1[:, :], t2[:, :], ALU.add, reads=[k1, k2], writes=[(kmT, m)])
                        for tt_ in range(4):
                            r0 = tb * 512 + tt_ * 128
                            xr, kxr = xrp.next()
                            P.dma("sp", xr[:, :], xsrc_l[r0:r0 + 128, :], writes=[kxr])
                            hb, khb = hp.next()
                            for nh in range(2):
                                po, ko = psB.next()
                                for kc in range(8):
                                    P.mm(po[:, :], mT[:, kc, tt_ * 128:(tt_ + 1) * 128], wo[:, kc, nh * 512:(nh + 1) * 512],
                                         start=(kc == 0), stop=(kc == 7), reads=["wP4"] + [(kmT, m) for m in range(8)],
                                         writes=[ko])
                                P.tt("dve", hb[:, nh * 512:(nh + 1) * 512], po[:, :], gate_b[:, nh * 512:(nh + 1) * 512],
                                     ALU.mult, reads=[ko], writes=[(khb, nh)])
                            P.stt(hb[:, :], xr[:, :], ALPHA, hb[:, :], ALU.mult, ALU.add,
                                  reads=[kxr, (khb, 0), (khb, 1)], writes=[(khb, 0), (khb, 1)])
                            bs, kbs = stp.next()
                            for nh in range(2):
                                P.op("dve", (lambda o, i_: (lambda e: e.bn_stats(o, i_)))(
                                    bs[:, nh * 6:(nh + 1) * 6], hb[:, nh * 512:(nh + 1) * 512]),
                                    reads=[(khb, 0), (khb, 1)], writes=[(kbs, nh)])
                            P.op("dve", (lambda o, i_: (lambda e: e.bn_aggr(o, i_)))(bs[:, 12:14], bs[:, 0:12]),
                                 reads=[(kbs, 0), (kbs, 1)], writes=[(kbs, 2)])
                            P.act(bs[:, 15:16], bs[:, 13:14], AF.Sqrt, bias=epsc[:, 1:2], reads=[(kbs, 2)],
                                  writes=[(kbs, 4)])
                            P.recip(bs[:, 14:15], bs[:, 15:16], reads=[(kbs, 4)], writes=[(kbs, 3)])
                            ob, kob = op_.next()
                            P.ts("dve", ob[:, :], hb[:, :], bs[:, 12:13], bs[:, 14:15], ALU.subtract, ALU.mult,
                                 reads=[(khb, 0), (khb, 1), (kbs, 2), (kbs, 3)], writes=[kob])
                            P.tt("pool", ob[:, :], ob[:, :], lng_b[:, :], ALU.mult, reads=[kob], writes=[kob])
                            P.tt("pool", ob[:, :], ob[:, :], lnb_b[:, :], ALU.add, reads=[kob], writes=[kob])
                            P.dma("pool", ydst_l[r0:r0 + 128, :], ob[:, :], reads=[kob])
                    P.barrier()
                    P.flush()
        P.barrier()
        P.flush()
    return nc, dbg_names


def _t5_bucket_np(rel):
    half, max_exact = 16, 8
    ret = np.where(rel > 0, half, 0)
    a = np.abs(rel)
    af = np.maximum(a, 1).astype(np.float32)
    large = max_exact + (np.log(af / np.float32(max_exact)) / np.float32(math.log(1024 / max_exact))
                         * np.float32(half - max_exact)).astype(np.int32)
    large = np.minimum(large, half - 1)
    return ret + np.where(a < max_exact, a, large)


def _host_consts(rel_table):
    ident = np.eye(128, dtype=np.float32)
    bones = np.zeros((128, 128), np.float32)
    bones[:64, :64] = 1.0 / 64
    bones[64:, 64:] = 1.0 / 64
    rrot = np.zeros((128, 128), np.float32)
    for base in range(0, 128, 32):
        for e in range(16):
            rrot[base + e + 16, base + e] = -1.0
            rrot[base + e, base + e + 16] = 1.0
    cmat = np.stack([ident, bones, rrot])
    i = np.arange(128)[:, None]
    j = np.arange(128)[None, :]
    rel0 = i - 64 - j
    rel1 = i + 64 - j
    tri0 = np.concatenate([(i >= j), (i <= j)], axis=1).astype(np.float32)
    tri = np.stack([tri0, (tri0 - 1.0) * 30000.0]).astype(np.float32)
    ab = np.zeros((128, 24, 256), np.float32)
    for g, (_, d) in enumerate(GROUPS):
        for c, rel in enumerate((rel0, rel1)):
            bk = _t5_bucket_np(np.clip(rel, -64, 64) * d)
            ab[:, g * 8:(g + 1) * 8, c * 128:(c + 1) * 128] = rel_table[bk][:, :, g * 8:(g + 1) * 8].transpose(0, 2, 1)
    return cmat, tri, ab.reshape(128, 24 * 256)


def _cs_tables(half):
    t = np.arange(half * T, (half + 1) * T)
    row = (t // 64).astype(np.float32)
    col = (t % 64).astype(np.float32)
    inv = (np.float32(10000.0) ** (-np.arange(0, 32, 2, dtype=np.float32) / np.float32(32))).astype(np.float32)
    ar = (row[:, None] * inv[None]).astype(np.float32)
    ac = (col[:, None] * inv[None]).astype(np.float32)
    ang = np.zeros((128, T), np.float32)
    for p in range(128):
        e = p % 64
        ang[p] = ar[:, e % 16] if e < 32 else ac[:, (e - 32) % 16]
    return np.stack([np.cos(ang), np.sin(ang)]).astype(np.float32)


def _in_maps(inputs):
    f = lambda a: np.ascontiguousarray(np.asarray(a, dtype=np.float32))
    x = f(inputs["x"]); c = f(inputs["c"])
    cmat, tri, ab = _host_consts(f(inputs["rel_table"]))
    bgT = np.ascontiguousarray(f(inputs["b_gate"]).reshape(DEPTH, 16, 128).transpose(0, 2, 1))
    qg = f(inputs["q_norm_g"]); kg = f(inputs["k_norm_g"])
    qkg = np.ascontiguousarray(np.stack([np.tile(qg, (1, 2)), np.tile(kg, (1, 2))], axis=-1))
    shared = {
        "ln_g": f(inputs["ln_g"]).reshape(DEPTH, 1, D), "ln_b": f(inputs["ln_b"]).reshape(DEPTH, 1, D),
        "w_ada": f(inputs["w_ada"]), "b_ada": f(inputs["b_ada"]).reshape(DEPTH, 1, 3 * D),
        "w_in": f(inputs["w_in"]), "bgT": bgT, "qkg": qkg, "w_pa": f(inputs["w_pa"]), "w_pb": f(inputs["w_pb"]),
        "w_o": f(inputs["w_o"]), "cmat": cmat, "abias": ab, "tri": tri,
    }
    cs = [_cs_tables(0), _cs_tables(1)]
    maps = []
    for core in range(8):
        b, hf = core // 2, core % 2
        m = dict(shared)
        m["x"] = np.ascontiguousarray(x[b, hf * T:(hf + 1) * T])
        m["cT"] = np.ascontiguousarray(c[b].reshape(8, 128).T)
        m["cstab"] = cs[hf]
        mk = np.ones((128, 2), np.float32)
        mk[:, 0] = 0.0 if hf == 0 else 1.0
        mk[:, 1] = 1.0 if hf == 0 else 0.0
        m["msk"] = mk
        maps.append(m)
    return maps


def kernel(**inputs):
    nc, _ = _build()
    maps = _in_maps(inputs)
    res = run_bass_kernel_spmd(nc, maps, core_ids=list(range(8)))
    out = np.empty((4, S, D), np.float32)
    for core in range(8):
        b, hf = core // 2, core % 2
        out[b, hf * T:(hf + 1) * T] = res.results[core]["y"]
    return out
```

```python
import math
from contextlib import ExitStack

import numpy as np
import concourse.bass as bass
import concourse.mybir as mybir
from concourse.bass_utils import run_bass_kernel_spmd

F32 = mybir.dt.float32
BF16 = mybir.dt.bfloat16
AF = mybir.ActivationFunctionType
ALU = mybir.AluOpType

D = 1024
T = 4096
S = 8192
DEPTH = 2
GROUPS = ((128, 1), (512, 4), (2048, 16))
ALPHA = float((2 * DEPTH) ** 0.25)
LN_EPS = 1e-5
QK_EPS = 1e-6
C_AQ, C_AK, C_AV, C_AZ, C_BQ, C_BK, C_BV, C_BZ, C_GL = 0, 1536, 3072, 4608, 5120, 5632, 5760, 5888, 6400
XCOLS = 4096


def _xlayout():
    units = []
    sec = {}

    def add_unit(items):
        u = len(units)
        r = 0
        for k, n in items:
            sec[k] = (u, r, n)
            r += n
        units.append(r)

    add_unit([("KB", 128)])
    add_unit([("VB", 130)])
    for g, (_, d) in enumerate(GROUPS):
        nk = 8 * d
        nv = -(-(d * 64 * 520) // XCOLS)
        items = [(("AKF", g), nk), (("AKL", g), nk), (("AVF", g), nv), (("AVL", g), nv)]
        if nk + nk + nv + nv <= 130:
            add_unit(items)
        else:
            for it in items:
                add_unit([it])
    return units, sec


XUNITS, XSEC = _xlayout()


class Prog:
    CE = ("pe", "act", "dve", "pool")
    ALLE = ("pe", "act", "dve", "pool", "sp")
    NDS = 8

    def __init__(self, nc, es):
        self.nc = nc
        self.sem = {}
        for e in self.CE:
            self.sem[("c", e)] = es.enter_context(nc.semaphore(f"c_{e}"))
        for q in ("sp", "act", "pool"):
            for i in range(self.NDS):
                self.sem[("d", q, i)] = es.enter_context(nc.semaphore(f"d_{q}{i}"))
        self.sem[("cc",)] = es.enter_context(nc.semaphore("ccsem"))
        self.cnt = {k: 0 for k in self.sem}
        self.dnext = {q: 0 for q in ("sp", "act", "pool")}
        self.ops = {e: [] for e in self.ALLE}
        self.waited = {e: {} for e in self.ALLE}
        self.lastw = {}
        self.readers = {}
        self.nops = 0

    def _deps(self, reads, writes):
        deps = set()
        for k in reads:
            if k in self.lastw:
                deps.add(self.lastw[k])
        for k in writes:
            if k in self.lastw:
                deps.add(self.lastw[k])
            deps.update(self.readers.get(k, ()))
        return deps

    def _commit(self, ticket, reads, writes):
        for k in reads:
            self.readers.setdefault(k, []).append(ticket)
        for k in writes:
            self.lastw[k] = ticket
            self.readers[k] = []

    def _waits(self, eng, deps):
        w = []
        for (sk, val) in sorted(deps, key=lambda t: (str(t[0]), t[1])):
            if eng == "pe" and sk == ("c", "pe"):
                continue
            if self.waited[eng].get(sk, 0) >= val:
                continue
            self.waited[eng][sk] = val
            w.append((self.sem[sk], val))
        return w

    def op(self, eng, fn, reads=(), writes=()):
        deps = self._deps(reads, writes)
        w = self._waits(eng, deps)
        sk = ("c", eng)
        self.cnt[sk] += 1
        t = (sk, self.cnt[sk])
        self.ops[eng].append((w, fn, self.sem[sk], 1))
        self._commit(t, reads, writes)
        self.nops += 1
        return t

    def dma(self, q, out, in_, reads=(), writes=()):
        deps = self._deps(reads, writes)
        slot = self.dnext[q] % self.NDS
        self.dnext[q] += 1
        sk = ("d", q, slot)
        if self.cnt[sk] > 0:
            deps.add((sk, self.cnt[sk]))
        w = self._waits(q, deps)
        self.cnt[sk] += 16
        t = (sk, self.cnt[sk])
        self.ops[q].append((w, lambda e: e.dma_start(out=out, in_=in_), self.sem[sk], 16))
        self._commit(t, reads, writes)
        self.nops += 1
        return t

    def collective(self, fn, reads=(), writes=()):
        deps = self._deps(reads, writes)
        w = self._waits("pool", deps)
        sk = ("cc",)
        self.cnt[sk] += 1
        t = (sk, self.cnt[sk])
        self.ops["pool"].append((w, fn, self.sem[sk], 1))
        self._commit(t, reads, writes)
        return t

    def barrier(self):
        tickets = set((sk, v) for sk, v in self.cnt.items() if v > 0)
        for e in self.ALLE:
            w = self._waits(e, tickets)
            if w:
                self.ops[e].append((w, None, None, 0))
        self.lastw = {}
        self.readers = {}

    def flush(self):
        nc = self.nc
        ops = self.ops

        def replay(lst, e):
            for (w, fn, sem, inc) in lst:
                for (s, v) in w:
                    e.wait_ge(s, v)
                if fn is not None:
                    ins = fn(e)
                    ins.then_inc(sem, inc)

        with nc.Block() as block:
            @block.tensor
            def _(e):
                replay(ops["pe"], e)

            @block.scalar
            def _(e):
                replay(ops["act"], e)

            @block.vector
            def _(e):
                replay(ops["dve"], e)

            @block.gpsimd
            def _(e):
                replay(ops["pool"], e)

            @block.sync
            def _(e):
                replay(ops["sp"], e)
        self.ops = {e: [] for e in self.ALLE}

    def mm(self, out, lhsT, rhs, start=True, stop=True, reads=(), writes=()):
        return self.op("pe", lambda e: e.matmul(out, lhsT, rhs, start=start, stop=stop), reads, writes)

    def tr(self, out, in_, ident, reads=(), writes=()):
        return self.op("pe", lambda e: e.transpose(out, in_, ident), reads, writes)

    def act(self, out, in_, func, bias=None, scale=None, reads=(), writes=(), eng="act"):
        kw = {}
        if bias is not None:
            kw["bias"] = bias
        if scale is not None:
            kw["scale"] = scale
        return self.op(eng, lambda e: e.activation(out, in_, func, **kw), reads, writes)

    def tt(self, eng, out, in0, in1, op, reads=(), writes=()):
        return self.op(eng, lambda e: e.tensor_tensor(out, in0, in1, op), reads, writes)

    def ts(self, eng, out, in0, s1, s2, op0, op1=None, reads=(), writes=()):
        if op1 is None:
            return self.op(eng, lambda e: e.tensor_scalar(out, in0, s1, None, op0), reads, writes)
        return self.op(eng, lambda e: e.tensor_scalar(out, in0, s1, s2, op0, op1), reads, writes)

    def stt(self, out, in0, scalar, in1, op0, op1, reads=(), writes=()):
        return self.op("dve", lambda e: e.scalar_tensor_tensor(out, in0, scalar, in1, op0, op1), reads, writes)

    def copy(self, eng, out, in_, reads=(), writes=()):
        if eng == "act":
            return self.op("act", lambda e: e.activation(out, in_, AF.Copy), reads, writes)
        return self.op(eng, lambda e: e.tensor_copy(out, in_), reads, writes)

    def memset(self, eng, ap, val, reads=(), writes=()):
        return self.op(eng, lambda e: e.memset(ap, val), reads, writes)

    def recip(self, out, in_, reads=(), writes=()):
        return self.op("dve", lambda e: e.reciprocal(out, in_), reads, writes)


class TPool:
    def __init__(self, tiles, name):
        self.tiles = tiles
        self.name = name
        self.i = 0

    def next(self):
        j = self.i % len(self.tiles)
        self.i += 1
        return self.tiles[j], (self.name, j)


def _build(debug=None):
    nc = bass.Bass("TRN2", target_bir_lowering=False)
    dbg_kind = "ExternalOutput" if debug else "Internal"

    def din(name, shape, dt=F32):
        return nc.dram_tensor(name, list(shape), dt, kind="ExternalInput").ap()

    def dscr(name, shape, dt=BF16, dbg=False):
        isdbg = bool(debug) and dbg and name in debug.get("outs", ())
        return nc.dram_tensor(name, list(shape), dt, kind=("ExternalOutput" if isdbg else "Internal")).ap()

    x_in = din("x", [T, D])
    cT_in = din("cT", [128, 8])
    ln_g_in = din("ln_g", [DEPTH, 1, D])
    ln_b_in = din("ln_b", [DEPTH, 1, D])
    w_ada_in = din("w_ada", [DEPTH, D, 3 * D])
    b_ada_in = din("b_ada", [DEPTH, 1, 3 * D])
    w_in_in = din("w_in", [DEPTH, D, 8448])
    bgT_in = din("bgT", [DEPTH, 128, 16])
    qkg_in = din("qkg", [DEPTH, 128, 2])
    w_pa_in = din("w_pa", [DEPTH, 512, D])
    w_pb_in = din("w_pb", [DEPTH, 512, D])
    w_o_in = din("w_o", [DEPTH, D, D])
    cmat_in = din("cmat", [3, 128, 128])
    cstab_in = din("cstab", [2, 128, T])
    abias_in = din("abias", [128, 24 * 256])
    tri_in = din("tri", [2, 128, 256])
    msk_in = din("msk", [128, 2])
    y_out = nc.dram_tensor("y", [T, D], F32, kind="ExternalOutput").ap()

    aqT = dscr("aqT", [3, 512, T], dbg=True)
    akT = [dscr(f"akT{g}", [512, d, T // d + 128], dbg=True) for g, (_, d) in enumerate(GROUPS)]
    avp = [dscr(f"avp{g}", [d, T // d + 128, 520], dbg=True) for g, (_, d) in enumerate(GROUPS)]
    sazT = dscr("sazT", [512, T], dbg=True)
    sbzT = dscr("sbzT", [512, T], dbg=True)
    qbT = dscr("qbT", [512, T], dbg=True)
    gT = dscr("gT", [2048, T], dbg=True)
    yaT = dscr("yaT", [512, T], dbg=True)
    ybT = dscr("ybT", [512, T], dbg=True)
    x1 = dscr("x1", [T, D], F32, dbg=True)
    Edram = dscr("Edram", [128, 24 * 256], F32)
    rdscr = dscr("rdscr", [8, 512], F32)
    rdslot = [0]
    xsrc_u = [dscr(f"xsrc{u}", [n, XCOLS]) for u, n in enumerate(XUNITS)]
    xdst_u = [dscr(f"xdst{u}", [2 * n, XCOLS]) for u, n in enumerate(XUNITS)]

    def xs(key):
        u, r0, n = XSEC[key]
        return xsrc_u[u][r0:r0 + n, :]

    def xd(key, rank):
        u, r0, n = XSEC[key]
        return xdst_u[u][rank * XUNITS[u] + r0:rank * XUNITS[u] + r0 + n, :]
    dbg_names = ["aqT", "akT0", "akT1", "akT2", "avp0", "avp1", "avp2", "sazT", "sbzT", "qbT", "gT",
                 "yaT", "ybT", "x1"]

    with ExitStack() as top:
        P = Prog(nc, top)

        def sb(es, name, shape, dt):
            return es.enter_context(nc.sbuf_tensor("s_" + name, list(shape), dt))

        PSALL = top.enter_context(nc.psum_tensor("psall", [128, 8, 512], F32))
        PS = [PSALL[:, i, :] for i in range(8)]

        class PSPool:
            def __init__(self, idx):
                self.idx = idx
                self.i = 0

            def next(self):
                b = self.idx[self.i % len(self.idx)]
                self.i += 1
                return PS[b], ("ps", b)

        def pspool(idx, name):
            return PSPool(idx)

        cmat = sb(top, "cmat", [128, 3, 128], F32)
        ones_f = sb(top, "ones_f", [128, 128], F32)
        msk = sb(top, "msk", [128, 2], F32)
        P.dma("sp", cmat[:, :, :], cmat_in.rearrange("c p n -> p c n"), writes=["cmat"])
        P.dma("sp", msk[:, :], msk_in, writes=["msk"])
        P.memset("pool", ones_f[:, :], 1.0, writes=["ones"])
        cmatb = sb(top, "cmatb", [128, 2, 128], BF16)
        P.copy("dve", cmatb[:, :, :], cmat[:, 1:3, :], reads=["cmat"], writes=["cmatb"])
        epsc = sb(top, "epsc", [128, 2], F32)
        P.memset("pool", epsc[:, 0:1], QK_EPS, writes=["epsc"])
        P.memset("pool", epsc[:, 1:2], LN_EPS, writes=["epsc"])
        ident = cmat[:, 0, :]
        bones = cmat[:, 1, :]
        rrot = cmat[:, 2, :]

        zrow = sb(top, "zrow", [128, 32], BF16)
        P.memset("pool", zrow[:, :], 0.0, writes=["zrow"])
        for g, (_, d) in enumerate(GROUPS):
            nel = d * 64 * 520
            pad = (-nel) % XCOLS
            if pad:
                for kk in ("AVF", "AVL"):
                    fl = xs((kk, g)).rearrange("r c -> (r c)")
                    P.dma("sp", fl[nel:nel + pad].rearrange("(a n) -> a n", a=128), zrow[:, 0:pad // 128], reads=["zrow"])
        with ExitStack() as es:
            ab = sb(es, "ab", [128, 24 * 256], F32)
            tri = sb(es, "tri", [128, 2, 256], F32)
            P.dma("sp", ab[:, :], abias_in, writes=["ab"])
            P.dma("sp", tri[:, :, :], tri_in.rearrange("c p n -> p c n"), writes=["tri"])
            P.act(ab[:, :], ab[:, :], AF.Exp, reads=["ab"], writes=["ab"])
            for gh in range(24):
                sl = slice(gh * 256, (gh + 1) * 256)
                P.tt("dve", ab[:, sl], ab[:, sl], tri[:, 0, :], ALU.mult, reads=["ab", "tri"], writes=[("ab", gh)])
            P.dma("sp", Edram, ab[:, :], reads=[("ab", gh) for gh in range(24)])
            P.barrier()
            P.flush()

        for l in range(DEPTH):
            if debug and l > debug.get("layers", DEPTH) - 1:
                break
            xsrc_l = x_in if l == 0 else x1
            ydst_l = x1 if l < DEPTH - 1 else y_out
            with ExitStack() as lay:
                shiftT = sb(lay, f"shiftT{l}", [128, 8], F32)
                sc1T = sb(lay, f"sc1T{l}", [128, 8], F32)
                gate_b = sb(lay, f"gate_b{l}", [128, D], F32)
                lng_b = sb(lay, f"lng_b{l}", [128, D], F32)
                lnb_b = sb(lay, f"lnb_b{l}", [128, D], F32)
                bgT = sb(lay, f"bgT{l}", [128, 16], F32)
                qkg = sb(lay, f"qkg{l}", [128, 2], F32)

                with ExitStack() as es:
                    cT = sb(es, f"cT{l}", [128, 8], F32)
                    silc = sb(es, f"silc{l}", [128, 8], F32)
                    rows = sb(es, f"rows{l}", [1, 5 * D], F32)
                    brow = sb(es, f"brow{l}", [1, 3 * D], F32)
                    wst = [sb(es, f"wada{l}_{i}", [128, 8, 512], F32) for i in range(2)]
                    wp = TPool(wst, "wada")
                    psp = pspool([0, 1], "ps")
                    P.dma("sp", cT[:, :], cT_in, writes=["cT"])
                    P.dma("sp", brow[:, :], b_ada_in[l], writes=["brow"])
                    P.dma("sp", rows[:, 3 * D:4 * D], ln_g_in[l], writes=["rows_ln"])
                    P.dma("sp", rows[:, 4 * D:5 * D], ln_b_in[l], writes=["rows_ln"])
                    P.dma("sp", bgT[:, :], bgT_in[l], writes=["bgT"])
                    P.dma("sp", qkg[:, :], qkg_in[l], writes=["qkg"])
                    P.act(silc[:, :], cT[:, :], AF.Silu, reads=["cT"], writes=["silc"])
                    for n in range(6):
                        wt, kw = wp.next()
                        P.dma("sp", wt[:, :, :],
                              w_ada_in[l, :, n * 512:(n + 1) * 512].rearrange("(kc p) n -> p kc n", p=128),
                              writes=[kw])
                        ps, kp = psp.next()
                        for kc in range(8):
                            P.mm(ps[0:1, :], silc[:, kc:kc + 1], wt[:, kc, :], start=(kc == 0), stop=(kc == 7),
                                 reads=[kw, "silc"], writes=[kp])
                        P.tt("dve", rows[0:1, n * 512:(n + 1) * 512], ps[0:1, :], brow[0:1, n * 512:(n + 1) * 512],
                             ALU.add, reads=[kp, "brow"], writes=[("rows", n)])
                    ps, kp = psp.next()
                    for j in range(16):
                        P.mm(ps[:, j:j + 1], rows[0:1, j * 128:(j + 1) * 128], ones_f[0:1, 0:1],
                             reads=[("rows", j // 4), "ones"], writes=[kp])
                    P.copy("dve", shiftT[:, :], ps[:, 0:8], reads=[kp], writes=["mod"])
                    P.ts("dve", sc1T[:, :], ps[:, 8:16], 1.0, None, ALU.add, reads=[kp], writes=["mod2"])
                    for (dst, c0, rk) in ((gate_b, 2 * D, [("rows", 4), ("rows", 5)]), (lng_b, 3 * D, ["rows_ln"]),
                                          (lnb_b, 4 * D, ["rows_ln"])):
                        for nh in range(2):
                            ps, kp = psp.next()
                            P.mm(ps[:, :], ones_f[0:1, :], rows[0:1, c0 + nh * 512:c0 + (nh + 1) * 512],
                                 reads=rk + ["ones"], writes=[kp])
                            P.copy("dve", dst[:, nh * 512:(nh + 1) * 512], ps[:, :], reads=[kp], writes=["bc"])
                    P.barrier()
                    P.flush()

                with ExitStack() as es:
                  if not (debug and debug.get("skipP1")):
                        uT = sb(es, f"uT{l}", [128, 8, T], BF16)
                        with ExitStack() as es2:
                            xp = TPool([sb(es2, f"xt{l}_{i}", [128, 4, D], F32) for i in range(2)], "xt")
                            psp = pspool([0, 1, 2, 3], "ps")
                            for tb in range(8):
                                xt, kx = xp.next()
                                P.dma("sp", xt[:, :, :],
                                      xsrc_l[tb * 512:(tb + 1) * 512, :].rearrange("(t p) d -> p t d", p=128), writes=[kx])
                                for kc in range(8):
                                    ps, kp = psp.next()
                                    for t in range(4):
                                        P.tr(ps[:, t * 128:(t + 1) * 128], xt[:, t, kc * 128:(kc + 1) * 128], ident,
                                             reads=[kx], writes=[kp])
                                    P.act(uT[:, kc, tb * 512:(tb + 1) * 512], ps[:, :], AF.Identity,
                                          bias=shiftT[:, kc:kc + 1], scale=sc1T[:, kc:kc + 1], reads=[kp])
                            P.barrier()
                            P.flush()

                        uTp = sb(es, f"uTp{l}", [128, 8, T], BF16)
                        wstp = TPool([sb(es, f"wst{l}_{i}", [128, 8, 256], F32) for i in range(2)], "wst")
                        wbfp = TPool([sb(es, f"wbf{l}_{i}", [128, 8, 512], BF16) for i in range(2)], "wbf")
                        stgp = TPool([sb(es, f"stg{l}_{i}", [128, 512], BF16) for i in range(4)], "stg")
                        vstp = TPool([sb(es, f"vst{l}_{i}", [128, 8, 65], BF16) for i in range(3)], "vst")
                        f32p = TPool([sb(es, f"f32t{l}_{i}", [128, 512], F32) for i in range(6)], "f32t")
                        b16p = TPool([sb(es, f"b16t{l}_{i}", [128, 512], BF16) for i in range(5)], "b16t")
                        csp = TPool([sb(es, f"cst{l}_{i}", [128, 2, 512], F32) for i in range(2)], "cst")
                        psA = pspool([0, 1, 2, 3, 4], "ps")
                        psB = pspool([5, 6, 7], "ps")
                        for vt in vstp.tiles:
                            P.memset("pool", vt[:, :, :], 1.0)
                        P.barrier()
                        evac_rr = [0]

                        hkeys = {0: [], 1: [], 2: []}

                        def hkey(g):
                            k_ = ("hst", g, len(hkeys[g]))
                            hkeys[g].append(k_)
                            return k_

                        def evac_copy(out, in_, reads, writes):
                            evac_rr[0] += 1
                            if evac_rr[0] % 2:
                                return P.copy("act", out, in_, reads=reads, writes=writes)
                            return P.copy("dve", out, in_, reads=reads, writes=writes)

                        def load_strip(c0, W):
                            wb, kb = wbfp.next()
                            for w0 in range(0, W, 256):
                                wt, kw = wstp.next()
                                P.dma("sp", wt[:, :, :],
                                      w_in_in[l, :, c0 + w0:c0 + w0 + 256].rearrange("(kc p) n -> p kc n", p=128),
                                      writes=[kw])
                                P.copy("pool", wb[:, :, w0:w0 + 256], wt[:, :, :], reads=[kw], writes=[(kb, w0)])
                            return wb, [(kb, w0) for w0 in range(0, W, 256)]

                        def permute_uT(d):
                            engs = ("dve", "pool", "act")
                            for kc in range(8):
                                P.copy(engs[kc % 3], uTp[:, kc, :].rearrange("k (r p) -> k r p", r=d),
                                       uT[:, kc, :].rearrange("k (p r) -> k r p", r=d), writes=[("uTp", kc)])

                        def rhs_perm(g, kc, pb):
                            src = uT if g == 0 else uTp
                            return src[:, kc, pb * 512:(pb + 1) * 512]

                        def lhs_perm(g, kc, it):
                            src = uT if g == 0 else uTp
                            return src[:, kc, it * 128:(it + 1) * 128]

                        def fm_chunk(wb, kb, col0, rhs_fn, extra=()):
                            ps, kp = psA.next()
                            for kc in range(8):
                                rd_ = list(kb) + [e_ for e_ in extra if e_[1] == kc]
                                P.mm(ps[:, :], wb[:, kc, col0:col0 + 128], rhs_fn(kc), start=(kc == 0), stop=(kc == 7),
                                     reads=rd_, writes=[kp])
                            return ps, kp

                        def do_qk(which, g, wb, kb):
                            d = GROUPS[g][1]
                            ex_ = [("uTp", kc) for kc in range(8)] if g > 0 else []
                            for fcl in range(4):
                                for pb in range(8):
                                    ps, kp = fm_chunk(wb, kb, fcl * 128, lambda kc: rhs_perm(g, kc, pb), ex_)
                                    stg, ks = stgp.next()
                                    evac_copy(stg[:, :], ps[:, :], [kp], [ks])
                                    rows_ = slice(fcl * 128, (fcl + 1) * 128)
                                    if which == "q":
                                        P.dma("sp", aqT[g, rows_, pb * 512:(pb + 1) * 512], stg[:, :], reads=[ks])
                                    elif g == 0:
                                        P.dma("sp", akT[0][rows_, 0, 64 + pb * 512:64 + (pb + 1) * 512], stg[:, :],
                                              reads=[ks], writes=[hkey(g)])
                                    elif g == 1:
                                        r, p0 = pb // 2, (pb % 2) * 512
                                        P.dma("sp", akT[1][rows_, r, 64 + p0:64 + p0 + 512], stg[:, :], reads=[ks],
                                              writes=[hkey(g)])
                                    else:
                                        P.dma("sp", akT[2][rows_, 2 * pb:2 * pb + 2, 64:64 + 256],
                                              stg[:, :].rearrange("p (r c) -> p r c", r=2), reads=[ks],
                                              writes=[hkey(g)])

                        def do_v(g, wb, kb):
                            d = GROUPS[g][1]
                            per = (T // d) // 128
                            for it in range(32):
                                ps, kp = psA.next()
                                for kc in range(8):
                                    rd_ = list(kb) + ([("uTp", kc)] if g > 0 else [])
                                    P.mm(ps[:, :], lhs_perm(g, kc, it), wb[:, kc, 0:512], start=(kc == 0), stop=(kc == 7),
                                         reads=rd_, writes=[kp])
                                vt, kv = vstp.next()
                                evac_copy(vt[:, :, 0:64], ps[:, :].rearrange("p (h e) -> p h e", h=8), [kp], [kv])
                                r, p0 = it // per, (it % per) * 128
                                P.dma("sp", avp[g][r, 64 + p0:64 + p0 + 128, :],
                                      vt[:, :, :].rearrange("p h c -> p (h c)"), reads=[kv], writes=[hkey(g)])

                        def do_pack(g):
                            d = GROUPS[g][1]
                            L = T // d
                            for (kk, c0) in (("AKF", 64), ("AKL", L)):
                                sec = xs((kk, g)).rearrange("r c -> (r c)").rearrange("(f r c) -> f r c", r=d, c=64)
                                for fb in range(4):
                                    P.dma("sp", sec[fb * 128:(fb + 1) * 128], akT[g][fb * 128:(fb + 1) * 128, :, c0:c0 + 64],
                                          reads=hkeys[g], writes=[("xs", kk, g, fb)])
                            for (kk, c0) in (("AVF", 64), ("AVL", L)):
                                nel = d * 64 * 520
                                sec = xs((kk, g)).rearrange("r c -> (r c)")[0:nel].rearrange(
                                    "(r p f) -> r p f", p=64, f=520)
                                P.dma("sp", sec, avp[g][:, c0:c0 + 64, :], reads=hkeys[g], writes=[("xs", kk, g, 0)])

                        def do_gather(g):
                            us = sorted(set(XSEC[(kk, g)][0] for kk in ("AKF", "AKL", "AVF", "AVL")))
                            rk_ = [("xs", kk, g, fb) for kk in ("AKF", "AKL") for fb in range(4)] + \
                                  [("xs", kk, g, 0) for kk in ("AVF", "AVL")]
                            for u in us:
                                P.collective((lambda a, b: (lambda e: e.collective_compute(
                                    "AllGather", ALU.bypass, replica_groups=[[0, 1], [2, 3], [4, 5], [6, 7]],
                                    ins=[a], outs=[b])))(xsrc_u[u], xdst_u[u]), reads=rk_)

                        def do_gate(dst, s_, wb, kb):
                            for fcl in range(4):
                                for tb in range(8):
                                    ps, kp = fm_chunk(wb, kb, fcl * 128, lambda kc: uT[:, kc, tb * 512:(tb + 1) * 512])
                                    stg, ks = stgp.next()
                                    j = s_ * 4 + fcl
                                    if dst is gT:
                                        P.act(stg[:, :], ps[:, :], AF.Sigmoid, bias=bgT[:, j:j + 1], reads=[kp],
                                              writes=[ks])
                                    else:
                                        P.act(stg[:, :], ps[:, :], AF.Silu, reads=[kp], writes=[ks])
                                    P.dma("sp", dst[j * 128:(j + 1) * 128, tb * 512:(tb + 1) * 512], stg[:, :],
                                          reads=[ks])

                        jobs = []
                        for g in range(3):
                            if g > 0:
                                jobs.append((None, 0, (lambda g_: (lambda wb, kb: permute_uT(GROUPS[g_][1])))(g)))
                            jobs.append((C_AQ + g * 512, 512, (lambda g_: (lambda wb, kb: do_qk("q", g_, wb, kb)))(g)))
                            jobs.append((C_AK + g * 512, 512, (lambda g_: (lambda wb, kb: do_qk("k", g_, wb, kb)))(g)))
                            jobs.append((C_AV + g * 512, 512, (lambda g_: (lambda wb, kb: do_v(g_, wb, kb)))(g)))
                            jobs.append((None, 0, (lambda g_: (lambda wb, kb: do_pack(g_)))(g)))
                            if g > 0:
                                jobs.append((None, 0, (lambda g_: (lambda wb, kb: do_gather(g_)))(g - 1)))
                        jobs.append((C_AZ, 512, lambda wb, kb: do_gate(sazT, 0, wb, kb)))
                        jobs.append((None, 0, lambda wb, kb: do_gather(2)))
                        jobs.append((C_BZ, 512, lambda wb, kb: do_gate(sbzT, 0, wb, kb)))
                        for s_ in range(4):
                            jobs.append((C_GL + s_ * 512, 512, (lambda s2: (lambda wb, kb: do_gate(gT, s2, wb, kb)))(s_)))
                        strips = [j for j in jobs if j[0] is not None]
                        loaded = {}
                        nxt = [0]

                        def prefetch():
                            if nxt[0] < len(strips):
                                c0_, W_, _ = strips[nxt[0]]
                                loaded[nxt[0]] = load_strip(c0_, W_)
                                nxt[0] += 1

                        prefetch()
                        si = 0
                        for (c0_, W_, fn_) in jobs:
                            if c0_ is None:
                                fn_(None, None)
                                continue
                            wb, kb = loaded.pop(si)
                            si += 1
                            prefetch()
                            fn_(wb, kb)
                        xs_kb = xs("KB")
                        xs_vb = xs("VB").rearrange("r c -> (r c)").rearrange("(t f) -> t f", f=130)
                        wbq, kbq = load_strip(C_BQ, 512)
                        wbk, kbk = load_strip(C_BK, 256)
                        units = [(fcl, tb) for fcl in range(5) for tb in range(8)]
                        ust = {}

                        def ropeA(i):
                            fcl, tb = units[i]
                            wb, kb, col0, gcol = (wbq, kbq, fcl * 128, 0) if fcl < 4 else (wbk, kbk, 0, 1)
                            tsl = slice(tb * 512, (tb + 1) * 512)
                            cst, kc_ = csp.next()
                            P.dma("sp", cst[:, :, :], cstab_in[:, :, tsl].rearrange("c p t -> p c t"), writes=[kc_])
                            ps, kp = fm_chunk(wb, kb, col0, lambda kc: uT[:, kc, tsl])
                            sq, k1 = b16p.next()
                            P.act(sq[:, :], ps[:, :], AF.Square, reads=[kp], writes=[k1])
                            ust[i] = dict(fcl=fcl, tsl=tsl, gcol=gcol, cst=cst, kc_=kc_, ps=ps, kp=kp, sq=sq, k1=k1)

                        def ropeB(i):
                            u = ust[i]
                            ps2, kp2 = psB.next()
                            P.mm(ps2[:, :], cmatb[:, 0, :], u["sq"][:, :], reads=[u["k1"], "cmatb"], writes=[kp2])
                            srt, k2a = f32p.next()
                            P.act(srt[:, :], ps2[:, :], AF.Ln, bias=epsc[:, 0:1], reads=[kp2], writes=[k2a])
                            rstd, k2 = f32p.next()
                            P.act(rstd[:, :], srt[:, :], AF.Exp, scale=-0.5, reads=[k2a], writes=[k2])
                            xn, k3 = b16p.next()
                            P.stt(xn[:, :], u["ps"][:, :], qkg[:, u["gcol"]:u["gcol"] + 1], rstd[:, :], ALU.mult, ALU.mult,
                                  reads=[u["kp"], k2], writes=[k3])
                            u["xn"], u["k3"] = xn, k3

                        def ropeC(i):
                            u = ust.pop(i)
                            xn, k3, cst, kc_ = u["xn"], u["k3"], u["cst"], u["kc_"]
                            ps3, kp3 = psB.next()
                            P.mm(ps3[:, :], cmatb[:, 1, :], xn[:, :], reads=[k3, "cmatb"], writes=[kp3])
                            ta, k4 = f32p.next()
                            P.tt("pool", ta[:, :], xn[:, :], cst[:, 0, :], ALU.mult, reads=[k3, kc_], writes=[k4])
                            tb_, k5 = f32p.next()
                            P.tt("dve", tb_[:, :], ps3[:, :], cst[:, 1, :], ALU.mult, reads=[kp3, kc_], writes=[k5])
                            stg, ks = stgp.next()
                            P.tt("pool", stg[:, :], ta[:, :], tb_[:, :], ALU.add, reads=[k4, k5], writes=[ks])
                            if u["fcl"] < 4:
                                P.dma("sp", qbT[u["fcl"] * 128:(u["fcl"] + 1) * 128, u["tsl"]], stg[:, :], reads=[ks])
                            else:
                                P.dma("sp", xs_kb[:, u["tsl"]], stg[:, :], reads=[ks])

                        nu = len(units)
                        for i in range(nu + 2):
                            if i >= 2:
                                ropeC(i - 2)
                            if 1 <= i <= nu:
                                ropeB(i - 1)
                            if i < nu:
                                ropeA(i)
                        for it in range(32):
                            ps, kp = psA.next()
                            for kc in range(8):
                                P.mm(ps[:, 0:128], uT[:, kc, it * 128:(it + 1) * 128], wbk[:, kc, 128:256],
                                     start=(kc == 0), stop=(kc == 7), reads=kbk, writes=[kp])
                            vt, kv = vstp.next()
                            evac_copy(vt[:, 0:2, 0:64], ps[:, 0:128].rearrange("p (h e) -> p h e", h=2), [kp], [kv])
                            P.dma("sp", xs_vb[it * 128:(it + 1) * 128, :],
                                  vt[:, 0:2, :].rearrange("p h c -> p (h c)"), reads=[kv])
                        P.barrier()
                        P.flush()

                if debug and debug.get("upto") == "P1":
                    break

                with ExitStack() as es:
                    vh = TPool([sb(es, f"vh{l}_{i}", [64, 16, 520], BF16) for i in range(2)], "vh")
                    for u in (XSEC["KB"][0], XSEC["VB"][0]):
                        P.collective((lambda a, b: (lambda e: e.collective_compute(
                            "AllGather", ALU.bypass, replica_groups=[[0, 1], [2, 3], [4, 5], [6, 7]],
                            ins=[a], outs=[b])))(xsrc_u[u], xdst_u[u]))
                    P.barrier()
                    for g, (_, d) in enumerate(GROUPS):
                        L = T // d
                        for (kk, rb, c0) in (("AKL", 0, 0), ("AKF", 1, 64 + L)):
                            sec = xd((kk, g), rb).rearrange("r c -> (r c)").rearrange(
                                "(f r c) -> f r c", r=d, c=64)
                            for fb in range(4):
                                P.dma("sp", akT[g][fb * 128:(fb + 1) * 128, :, c0:c0 + 64], sec[fb * 128:(fb + 1) * 128])
                        for (kk, rb, c0, mc) in (("AVL", 0, 0, 0), ("AVF", 1, 64 + L, 1)):
                            nel = d * 64 * 520
                            sec = xd((kk, g), rb).rearrange("r c -> (r c)")[0:nel].rearrange(
                                "(r p f) -> p r f", p=64, f=520)
                            t_, kt = vh.next()
                            P.dma("sp", t_[:, 0:d, :], sec, writes=[kt])
                            P.ts("dve", t_[:, 0:d, :], t_[:, 0:d, :], msk[0:64, mc:mc + 1], None, ALU.mult,
                                 reads=[kt, "msk"], writes=[kt])
                            P.dma("sp", avp[g][:, c0:c0 + 64, :].rearrange("r p f -> p r f"), t_[:, 0:d, :], reads=[kt])
                    P.barrier()
                    P.flush()

                if debug and debug.get("upto") == "X":
                    break

                with ExitStack() as es:
                    E = sb(es, f"E{l}", [128, 24, 256], F32)
                    acc = sb(es, f"acc{l}", [65, 4, T], F32)
                    vaug = sb(es, f"vaug{l}", [128, 48, 4, 65], BF16)
                    ktp = TPool([sb(es, f"kt{l}_{i}", [64, 6144], BF16) for i in range(2)], "kt")
                    qtp = TPool([sb(es, f"qt{l}_{i}", [64, T], BF16) for i in range(2)], "qt")
                    exp_ = TPool([sb(es, f"ex{l}_{i}", [128, 512], F32) for i in range(4)], "ex")
                    ptp = TPool([sb(es, f"pt{l}_{i}", [128, 512], BF16) for i in range(5)], "pt")
                    rdp = TPool([sb(es, f"rd{l}_{i}", [65, 512], F32) for i in range(2)], "rd")
                    nmp = TPool([sb(es, f"nm{l}_{i}", [64, 512], F32) for i in range(2)], "nm")
                    szp = TPool([sb(es, f"sz{l}_{i}", [64, 512], BF16) for i in range(2)], "sz")
                    ysp = TPool([sb(es, f"ys{l}_{i}", [64, 512], BF16) for i in range(2)], "ys")
                    psS = pspool([0, 1, 2, 3], "ps")
                    psO = pspool([4, 5, 6], "ps")
                    psN = pspool([7], "ps")
                    P.dma("sp", E[:, :, :], Edram.rearrange("p (g c) -> p g c", c=256), writes=["E"])
                    for hh in range(2):
                        for g, (_, d) in enumerate(GROUPS):
                            L = T // d
                            nch = L // 128 + 1
                            vsrc = avp[g].rearrange("r (m p) (h c) -> p (r m) h c", p=128, c=65)
                            for c0 in range(0, d * nch, 12):
                                c1 = min(d * nch, c0 + 12)
                                P.dma("sp", vaug[:, c0:c1, :, :], vsrc[:, c0:c1, hh * 4:(hh + 1) * 4, :],
                                      writes=[("vaug", c0)])
                            vkeys = [("vaug", c0) for c0 in range(0, d * nch, 12)]
                            for hl in range(4):
                                h = hh * 4 + hl
                                kt, kk = ktp.next()
                                P.dma("sp", kt[:, 0:d * (L + 128)],
                                      akT[g][h * 64:(h + 1) * 64, :, :].rearrange("p r c -> p (r c)"), writes=[kk])
                                qt, kq = qtp.next()
                                P.dma("sp", qt[:, :], aqT[g, h * 64:(h + 1) * 64, :], writes=[kq])
                                blocks = [(r, n2) for r in range(d) for n2 in range(L // 256)]
                                st = {}
                                e0 = E[:, g * 8 + h, :]
                                ebc = bass.AP(e0.tensor, e0.offset, [list(e0.ap[0]), [0, 2], list(e0.ap[1])])

                                def stageA(i):
                                    r, n2 = blocks[i]
                                    n = 2 * n2
                                    ps, kp = psS.next()
                                    kb0 = r * (L + 128) + 128 * n
                                    qb0 = r * L + 128 * n
                                    P.mm(ps[:, 0:128], kt[:, kb0:kb0 + 128], qt[:, qb0:qb0 + 128],
                                         reads=[kk, kq], writes=[kp])
                                    P.mm(ps[:, 128:384], kt[:, kb0 + 128:kb0 + 256], qt[:, qb0:qb0 + 256],
                                         reads=[kk, kq], writes=[kp])
                                    P.mm(ps[:, 384:512], kt[:, kb0 + 256:kb0 + 384], qt[:, qb0 + 128:qb0 + 256],
                                         reads=[kk, kq], writes=[kp])
                                    ex, ke = exp_.next()
                                    P.act(ex[:, :], ps[:, :], AF.Exp, scale=0.125, reads=[kp], writes=[ke])
                                    pt, kpt = ptp.next()
                                    P.tt("dve" if i % 3 == 2 else "pool", pt[:, :].rearrange("p (a b) -> p a b", a=2),
                                         ex[:, :].rearrange("p (a b) -> p a b", a=2), ebc, ALU.mult,
                                         reads=[ke, "E"], writes=[kpt])
                                    st[i] = (pt, kpt)

                                def stageB(i):
                                    r, n2 = blocks[i]
                                    n = 2 * n2
                                    pt, kpt = st.pop(i)
                                    po, ko = psO.next()
                                    ch = r * nch + n
                                    vk = lambda c_: [("vaug", (c_ // 12) * 12)]
                                    P.mm(po[0:65, 0:256], vaug[:, ch + 1, hl, :], pt[:, 128:384], start=True, stop=False,
                                         reads=[kpt] + vk(ch + 1), writes=[ko])
                                    P.mm(po[0:65, 0:128], vaug[:, ch, hl, :], pt[:, 0:128], start=False, stop=False,
                                         reads=[kpt] + vk(ch), writes=[ko])
                                    P.mm(po[0:65, 128:256], vaug[:, ch + 2, hl, :], pt[:, 384:512], start=False, stop=True,
                                         reads=[kpt] + vk(ch + 2), writes=[ko])
                                    t0 = r + d * 128 * n
                                    av_ = acc[:, hl, t0:t0 + d * 255 + 1:d]
                                    if g == 0:
                                        P.copy("act", av_, po[0:65, 0:256], reads=[ko], writes=[("acc", hl)])
                                    else:
                                        P.tt("dve", av_, av_, po[0:65, 0:256], ALU.add, reads=[ko, ("acc", hl)],
                                             writes=[("acc", hl)])

                                nb = len(blocks)
                                LK = 3
                                for i in range(nb + LK):
                                    if i < nb:
                                        stageA(i)
                                    if i >= LK:
                                        stageB(i - LK)
                        for hl in range(4):
                            P.act(acc[64:65, hl, :], acc[64:65, hl, :], AF.Ln, reads=[("acc", hl)], writes=[("acc", hl)])
                        for hl in range(4):
                            P.act(acc[64:65, hl, :], acc[64:65, hl, :], AF.Exp, scale=-1.0, reads=[("acc", hl)],
                                  writes=[("acc", hl)])
                        for hl in range(4):
                            h = hh * 4 + hl
                            for tb in range(8):
                                tsl = slice(tb * 512, (tb + 1) * 512)
                                sz, ksz = szp.next()
                                P.dma("sp", sz[:, :], sazT[h * 64:(h + 1) * 64, tsl], writes=[ksz])
                                pb_, kpb = psN.next()
                                P.mm(pb_[0:64, :], ones_f[64:65, 0:64], acc[64:65, hl, tsl], reads=[("acc", hl), "ones"],
                                     writes=[kpb])
                                nm, knm = nmp.next()
                                P.tt("dve", nm[:, :], acc[0:64, hl, tsl], pb_[0:64, :], ALU.mult,
                                     reads=[("acc", hl), kpb], writes=[knm])
                                ys, kys = ysp.next()
                                P.tt("pool", ys[:, :], nm[:, :], sz[:, :], ALU.mult, reads=[knm, ksz], writes=[kys])
                                P.dma("pool", yaT[h * 64:(h + 1) * 64, tsl], ys[:, :], reads=[kys])
                    P.barrier()
                    P.flush()

                if debug and debug.get("upto") == "P2":
                    break

                with ExitStack() as es:
                    kTd = sb(es, f"kTd{l}", [128, 2, S], BF16)
                    vb = sb(es, f"vb{l}", [128, 64, 130], BF16)
                    qT = sb(es, f"qT{l}", [128, 4, T], BF16)
                    ptp = TPool([sb(es, f"pB{l}_{i}", [128, 1024], BF16) for i in range(4)], "pB")
                    bcp = TPool([sb(es, f"bcB{l}_{i}", [64, 512], F32) for i in range(3)], "bcB")
                    evp = TPool([sb(es, f"evB{l}_{i}", [65, 512], F32) for i in range(3)], "evB")
                    rdp = TPool([sb(es, f"rdB{l}_{i}", [65, 512], F32) for i in range(2)], "rdB")
                    nm2 = TPool([sb(es, f"nmC{l}_{i}", [64, 512], F32) for i in range(2)], "nmC")
                    szp = TPool([sb(es, f"szB{l}_{i}", [64, 512], BF16) for i in range(3)], "szB")
                    ysp = TPool([sb(es, f"ysB{l}_{i}", [64, 512], BF16) for i in range(3)], "ysB")
                    psA2 = [(PSALL[:, 2 * j_:2 * j_ + 2, :].rearrange("p a b -> p (a b)"),
                             [("ps", 2 * j_), ("ps", 2 * j_ + 1)]) for j_ in range(3)]
                    psO = pspool([6, 7], "ps")
                    for rk in range(2):
                        kb_ = xd("KB", rk)
                        for kvh in range(2):
                            for half in range(2):
                                P.dma("sp", kTd[half * 64:(half + 1) * 64, kvh, rk * T:(rk + 1) * T],
                                      kb_[kvh * 64:(kvh + 1) * 64, :], writes=[("kTd", rk, kvh, half)])
                        vsec = xd("VB", rk).rearrange("r c -> (r c)").rearrange("(k p f) -> p k f", p=128, f=130)
                        for k0 in range(0, 32, 8):
                            P.dma("sp", vb[:, rk * 32 + k0:rk * 32 + k0 + 8, :], vsec[:, k0:k0 + 8, :],
                                  writes=[("vb", rk, k0)])
                    for c_ in range(4):
                        P.dma("sp", qT[:, c_, :], qbT[c_ * 128:(c_ + 1) * 128, :], writes=[("qT", c_)])
                    steps = [(qb, hp, kc) for qb in range(8) for hp in range(4) for kc in range(64)]
                    st = {}
                    acc_o = {}

                    def stageA(i):
                        qb, hp, kc = steps[i]
                        kvh = hp // 2
                        rk = kc // 32
                        psa, keys = psA2[i % 3]
                        for hh_ in range(2):
                            pr = hh_ * 64
                            P.mm(psa[:, hh_ * 512:(hh_ + 1) * 512], kTd[pr:pr + 64, kvh, kc * 128:(kc + 1) * 128],
                                 qT[pr:pr + 64, hp, qb * 512:(qb + 1) * 512],
                                 reads=[("kTd", rk, kvh, hh_), ("qT", hp)], writes=keys)
                        pt, kpt = ptp.next()
                        P.act(pt[:, :], psa, AF.Exp, scale=0.125, reads=keys, writes=[kpt])
                        st[i] = (pt, kpt)

                    def stageB(i):
                        qb, hp, kc = steps[i]
                        kvh = hp // 2
                        pt, kpt = st.pop(i)
                        if kc == 0:
                            acc_o[(qb, hp)] = [psO.next(), psO.next()]
                        for hh_ in range(2):
                            po, ko = acc_o[(qb, hp)][hh_]
                            P.mm(po[0:65, :], vb[:, kc, kvh * 65:(kvh + 1) * 65], pt[:, hh_ * 512:(hh_ + 1) * 512],
                                 start=(kc == 0), stop=(kc == 63),
                                 reads=[kpt, ("vb", kc // 32, ((kc % 32) // 8) * 8)], writes=[ko])
                        if kc == 63:
                            tsl = slice(qb * 512, (qb + 1) * 512)
                            for hh_ in range(2):
                                h = 2 * hp + hh_
                                po, ko = acc_o[(qb, hp)][hh_]
                                ev, kev = evp.next()
                                P.copy("dve", ev[:, :], po[0:65, :], reads=[ko], writes=[kev])
                                sz, ksz = szp.next()
                                P.dma("sp", sz[:, :], sbzT[h * 64:(h + 1) * 64, tsl], writes=[ksz])
                                rd, krd = rdp.next()
                                P.recip(rd[64:65, :], ev[64:65, :], reads=[kev], writes=[krd])
                                bc, kbc = bcp.next()
                                sl_ = rdslot[0] % 8
                                rdslot[0] += 1
                                P.dma("sp", rdscr[sl_:sl_ + 1, :], rd[64:65, :], reads=[krd], writes=[("rdscr", sl_)])
                                P.dma("sp", bc[:, :], bass.AP(rdscr.tensor, sl_ * 512, [[0, 64], [1, 512]]),
                                      reads=[("rdscr", sl_)], writes=[kbc])
                                n2, kn2 = nm2.next()
                                P.tt("dve", n2[:, :], ev[0:64, :], bc[:, :], ALU.mult, reads=[kev, kbc], writes=[kn2])
                                ys, kys = ysp.next()
                                P.tt("pool", ys[:, :], n2[:, :], sz[:, :], ALU.mult, reads=[kn2, ksz], writes=[kys])
                                P.dma("pool", ybT[h * 64:(h + 1) * 64, tsl], ys[:, :], reads=[kys])

                    LOOK = 2
                    ns = len(steps)
                    for i in range(ns + LOOK):
                        if i < ns:
                            stageA(i)
                        if i >= LOOK:
                            stageB(i - LOOK)
                    P.barrier()
                    P.flush()

                if debug and debug.get("upto") == "P3":
                    break

                with ExitStack() as es:
                    wpa = sb(es, f"wpa{l}", [128, 4, D], BF16)
                    wpb = sb(es, f"wpb{l}", [128, 4, D], BF16)
                    wo = sb(es, f"wo{l}", [128, 8, D], BF16)
                    wf = TPool([sb(es, f"wf{l}_{i}", [128, 8, 256], F32) for i in range(2)], "wf")
                    for (wdst, wsrc, nk_) in ((wpa, w_pa_in, 4), (wpb, w_pb_in, 4), (wo, w_o_in, 8)):
                        for nh in range(4):
                            wt, kw = wf.next()
                            P.dma("sp", wt[:, 0:nk_, :],
                                  wsrc[l, :, nh * 256:(nh + 1) * 256].rearrange("(k p) n -> p k n", p=128), writes=[kw])
                            P.copy("pool", wdst[:, :, nh * 256:(nh + 1) * 256], wt[:, 0:nk_, :], reads=[kw],
                                   writes=["wP4"])
                    yap = TPool([sb(es, f"ya{l}_{i}", [128, 4, 512], BF16) for i in range(2)], "ya")
                    ybp = TPool([sb(es, f"yb{l}_{i}", [128, 4, 512], BF16) for i in range(2)], "yb")
                    gp = TPool([sb(es, f"gg{l}_{i}", [128, 16, 512], BF16) for i in range(2)], "gg")
                    mtp = TPool([sb(es, f"mT{l}_{i}", [128, 8, 512], BF16) for i in range(2)], "mT")
                    t1p = TPool([sb(es, f"t1{l}_{i}", [128, 512], F32) for i in range(2)], "t1")
                    t2p = TPool([sb(es, f"t2{l}_{i}", [128, 512], F32) for i in range(2)], "t2")
                    xrp = TPool([sb(es, f"xr{l}_{i}", [128, D], F32) for i in range(3)], "xr")
                    hp = TPool([sb(es, f"hh{l}_{i}", [128, D], F32) for i in range(2)], "hh")
                    op_ = TPool([sb(es, f"oo{l}_{i}", [128, D], F32) for i in range(2)], "oo")
                    stp = TPool([sb(es, f"bst{l}_{i}", [128, 16], F32) for i in range(2)], "bst")
                    psA = pspool([0, 1, 2, 3], "ps")
                    psB = pspool([4, 5, 6, 7], "ps")
                    for tb in range(8):
                        tsl = slice(tb * 512, (tb + 1) * 512)
                        ya, kya = yap.next()
                        yb, kyb = ybp.next()
                        gg, kgg = gp.next()
                        P.dma("sp", ya[:, :, :], yaT[:, tsl].rearrange("(h p) t -> p h t", p=128), writes=[kya])
                        P.dma("sp", yb[:, :, :], ybT[:, tsl].rearrange("(h p) t -> p h t", p=128), writes=[kyb])
                        P.dma("sp", gg[:, :, :], gT[:, tsl].rearrange("(j p) t -> p j t", p=128), writes=[kgg])
                        mT, kmT = mtp.next()
                        for m in range(8):
                            pa, kpa = psA.next()
                            for h in range(4):
                                P.mm(pa[:, :], wpa[:, h, m * 128:(m + 1) * 128], ya[:, h, :], start=(h == 0),
                                     stop=(h == 3), reads=["wP4", kya], writes=[kpa])
                            pb_, kpb = psA.next()
                            for h in range(4):
                                P.mm(pb_[:, :], wpb[:, h, m * 128:(m + 1) * 128], yb[:, h, :], start=(h == 0),
                                     stop=(h == 3), reads=["wP4", kyb], writes=[kpb])
                            t1, k1 = t1p.next()
                            P.tt("dve", t1[:, :], pa[:, :], gg[:, m, :], ALU.mult, reads=[kpa, kgg], writes=[k1])
                            t2, k2 = t2p.next()
                            P.tt("dve", t2[:, :], pb_[:, :], gg[:, 8 + m, :], ALU.mult, reads=[kpb, kgg], writes=[k2])
                            P.tt("pool", mT[:, m, :], t1[:, :], t2[:, :], ALU.add, reads=[k1, k2], writes=[(kmT, m)])
                        for tt_ in range(4):
                            r0 = tb * 512 + tt_ * 128
                            xr, kxr = xrp.next()
                            P.dma("sp", xr[:, :], xsrc_l[r0:r0 + 128, :], writes=[kxr])
                            hb, khb = hp.next()
                            for nh in range(2):
                                po, ko = psB.next()
                                for kc in range(8):
                                    P.mm(po[:, :], mT[:, kc, tt_ * 128:(tt_ + 1) * 128], wo[:, kc, nh * 512:(nh + 1) * 512],
                                         start=(kc == 0), stop=(kc == 7), reads=["wP4"] + [(kmT, m) for m in range(8)],
                                         writes=[ko])
                                P.tt("dve", hb[:, nh * 512:(nh + 1) * 512], po[:, :], gate_b[:, nh * 512:(nh + 1) * 512],
                                     ALU.mult, reads=[ko], writes=[(khb, nh)])
                            P.stt(hb[:, :], xr[:, :], ALPHA, hb[:, :], ALU.mult, ALU.add,
                                  reads=[kxr, (khb, 0), (khb, 1)], writes=[(khb, 0), (khb, 1)])
                            bs, kbs = stp.next()
                            for nh in range(2):
                                P.op("dve", (lambda o, i_: (lambda e: e.bn_stats(o, i_)))(
                                    bs[:, nh * 6:(nh + 1) * 6], hb[:, nh * 512:(nh + 1) * 512]),
                                    reads=[(khb, 0), (khb, 1)], writes=[(kbs, nh)])
                            P.op("dve", (lambda o, i_: (lambda e: e.bn_aggr(o, i_)))(bs[:, 12:14], bs[:, 0:12]),
                                 reads=[(kbs, 0), (kbs, 1)], writes=[(kbs, 2)])
                            P.act(bs[:, 15:16], bs[:, 13:14], AF.Sqrt, bias=epsc[:, 1:2], reads=[(kbs, 2)],
                                  writes=[(kbs, 4)])
                            P.recip(bs[:, 14:15], bs[:, 15:16], reads=[(kbs, 4)], writes=[(kbs, 3)])
                            ob, kob = op_.next()
                            P.ts("dve", ob[:, :], hb[:, :], bs[:, 12:13], bs[:, 14:15], ALU.subtract, ALU.mult,
                                 reads=[(khb, 0), (khb, 1), (kbs, 2), (kbs, 3)], writes=[kob])
                            P.tt("pool", ob[:, :], ob[:, :], lng_b[:, :], ALU.mult, reads=[kob], writes=[kob])
                            P.tt("pool", ob[:, :], ob[:, :], lnb_b[:, :], ALU.add, reads=[kob], writes=[kob])
                            P.dma("pool", ydst_l[r0:r0 + 128, :], ob[:, :], reads=[kob])
                    P.barrier()
                    P.flush()
        P.barrier()
        P.flush()
    return nc, dbg_names


def _t5_bucket_np(rel):
    half, max_exact = 16, 8
    ret = np.where(rel > 0, half, 0)
    a = np.abs(rel)
    af = np.maximum(a, 1).astype(np.float32)
    large = max_exact + (np.log(af / np.float32(max_exact)) / np.float32(math.log(1024 / max_exact))
                         * np.float32(half - max_exact)).astype(np.int32)
    large = np.minimum(large, half - 1)
    return ret + np.where(a < max_exact, a, large)


def _host_consts(rel_table):
    ident = np.eye(128, dtype=np.float32)
    bones = np.zeros((128, 128), np.float32)
    bones[:64, :64] = 1.0 / 64
    bones[64:, 64:] = 1.0 / 64
    rrot = np.zeros((128, 128), np.float32)
    for base in range(0, 128, 32):
        for e in range(16):
            rrot[base + e + 16, base + e] = -1.0
            rrot[base + e, base + e + 16] = 1.0
    cmat = np.stack([ident, bones, rrot])
    i = np.arange(128)[:, None]
    j = np.arange(128)[None, :]
    rel0 = i - 64 - j
    rel1 = i + 64 - j
    tri0 = np.concatenate([(i >= j), (i <= j)], axis=1).astype(np.float32)
    tri = np.stack([tri0, (tri0 - 1.0) * 30000.0]).astype(np.float32)
    ab = np.zeros((128, 24, 256), np.float32)
    for g, (_, d) in enumerate(GROUPS):
        for c, rel in enumerate((rel0, rel1)):
            bk = _t5_bucket_np(np.clip(rel, -64, 64) * d)
            ab[:, g * 8:(g + 1) * 8, c * 128:(c + 1) * 128] = rel_table[bk][:, :, g * 8:(g + 1) * 8].transpose(0, 2, 1)
    return cmat, tri, ab.reshape(128, 24 * 256)


def _cs_tables(half):
    t = np.arange(half * T, (half + 1) * T)
    row = (t // 64).astype(np.float32)
    col = (t % 64).astype(np.float32)
    inv = (np.float32(10000.0) ** (-np.arange(0, 32, 2, dtype=np.float32) / np.float32(32))).astype(np.float32)
    ar = (row[:, None] * inv[None]).astype(np.float32)
    ac = (col[:, None] * inv[None]).astype(np.float32)
    ang = np.zeros((128, T), np.float32)
    for p in range(128):
        e = p % 64
        ang[p] = ar[:, e % 16] if e < 32 else ac[:, (e - 32) % 16]
    return np.stack([np.cos(ang), np.sin(ang)]).astype(np.float32)


def _in_maps(inputs):
    f = lambda a: np.ascontiguousarray(np.asarray(a, dtype=np.float32))
    x = f(inputs["x"]); c = f(inputs["c"])
    cmat, tri, ab = _host_consts(f(inputs["rel_table"]))
    bgT = np.ascontiguousarray(f(inputs["b_gate"]).reshape(DEPTH, 16, 128).transpose(0, 2, 1))
    qg = f(inputs["q_norm_g"]); kg = f(inputs["k_norm_g"])
    qkg = np.ascontiguousarray(np.stack([np.tile(qg, (1, 2)), np.tile(kg, (1, 2))], axis=-1))
    shared = {
        "ln_g": f(inputs["ln_g"]).reshape(DEPTH, 1, D), "ln_b": f(inputs["ln_b"]).reshape(DEPTH, 1, D),
        "w_ada": f(inputs["w_ada"]), "b_ada": f(inputs["b_ada"]).reshape(DEPTH, 1, 3 * D),
        "w_in": f(inputs["w_in"]), "bgT": bgT, "qkg": qkg, "w_pa": f(inputs["w_pa"]), "w_pb": f(inputs["w_pb"]),
        "w_o": f(inputs["w_o"]), "cmat": cmat, "abias": ab, "tri": tri,
    }
    cs = [_cs_tables(0), _cs_tables(1)]
    maps = []
    for core in range(8):
        b, hf = core // 2, core % 2
        m = dict(shared)
        m["x"] = np.ascontiguousarray(x[b, hf * T:(hf + 1) * T])
        m["cT"] = np.ascontiguousarray(c[b].reshape(8, 128).T)
        m["cstab"] = cs[hf]
        mk = np.ones((128, 2), np.float32)
        mk[:, 0] = 0.0 if hf == 0 else 1.0
        mk[:, 1] = 1.0 if hf == 0 else 0.0
        m["msk"] = mk
        maps.append(m)
    return maps


def kernel(**inputs):
    nc, _ = _build()
    maps = _in_maps(inputs)
    res = run_bass_kernel_spmd(nc, maps, core_ids=list(range(8)))
    out = np.empty((4, S, D), np.float32)
    for core in range(8):
        b, hf = core // 2, core % 2
        out[b, hf * T:(hf + 1) * T] = res.results[core]["y"]
    return out
```

```python
import math
from contextlib import ExitStack

import numpy as np
import concourse.bass as bass
import concourse.mybir as mybir
from concourse.bass_utils import run_bass_kernel_spmd

F32 = mybir.dt.float32
BF16 = mybir.dt.bfloat16
AF = mybir.ActivationFunctionType
ALU = mybir.AluOpType

D = 1024
T = 4096
S = 8192
DEPTH = 2
GROUPS = ((128, 1), (512, 4), (2048, 16))
ALPHA = float((2 * DEPTH) ** 0.25)
LN_EPS = 1e-5
QK_EPS = 1e-6
C_AQ, C_AK, C_AV, C_AZ, C_BQ, C_BK, C_BV, C_BZ, C_GL = 0, 1536, 3072, 4608, 5120, 5632, 5760, 5888, 6400
XCOLS = 4096


def _xlayout():
    units = []
    sec = {}

    def add_unit(items):
        u = len(units)
        r = 0
        for k, n in items:
            sec[k] = (u, r, n)
            r += n
        units.append(r)

    add_unit([("KB", 128)])
    add_unit([("VB", 130)])
    for g, (_, d) in enumerate(GROUPS):
        nk = 8 * d
        nv = -(-(d * 64 * 520) // XCOLS)
        items = [(("AKF", g), nk), (("AKL", g), nk), (("AVF", g), nv), (("AVL", g), nv)]
        if nk + nk + nv + nv <= 130:
            add_unit(items)
        else:
            for it in items:
                add_unit([it])
    return units, sec


XUNITS, XSEC = _xlayout()


class Prog:
    CE = ("pe", "act", "dve", "pool")
    ALLE = ("pe", "act", "dve", "pool", "sp")
    NDS = 8

    def __init__(self, nc, es):
        self.nc = nc
        self.sem = {}
        for e in self.CE:
            self.sem[("c", e)] = es.enter_context(nc.semaphore(f"c_{e}"))
        for q in ("sp", "act", "pool"):
            for i in range(self.NDS):
                self.sem[("d", q, i)] = es.enter_context(nc.semaphore(f"d_{q}{i}"))
        self.sem[("cc",)] = es.enter_context(nc.semaphore("ccsem"))
        self.cnt = {k: 0 for k in self.sem}
        self.dnext = {q: 0 for q in ("sp", "act", "pool")}
        self.ops = {e: [] for e in self.ALLE}
        self.waited = {e: {} for e in self.ALLE}
        self.lastw = {}
        self.readers = {}
        self.nops = 0

    def _deps(self, reads, writes):
        deps = set()
        for k in reads:
            if k in self.lastw:
                deps.add(self.lastw[k])
        for k in writes:
            if k in self.lastw:
                deps.add(self.lastw[k])
            deps.update(self.readers.get(k, ()))
        return deps

    def _commit(self, ticket, reads, writes):
        for k in reads:
            self.readers.setdefault(k, []).append(ticket)
        for k in writes:
            self.lastw[k] = ticket
            self.readers[k] = []

    def _waits(self, eng, deps):
        w = []
        for (sk, val) in sorted(deps, key=lambda t: (str(t[0]), t[1])):
            if eng == "pe" and sk == ("c", "pe"):
                continue
            if self.waited[eng].get(sk, 0) >= val:
                continue
            self.waited[eng][sk] = val
            w.append((self.sem[sk], val))
        return w

    def op(self, eng, fn, reads=(), writes=()):
        deps = self._deps(reads, writes)
        w = self._waits(eng, deps)
        sk = ("c", eng)
        self.cnt[sk] += 1
        t = (sk, self.cnt[sk])
        self.ops[eng].append((w, fn, self.sem[sk], 1))
        self._commit(t, reads, writes)
        self.nops += 1
        return t

    def dma(self, q, out, in_, reads=(), writes=()):
        deps = self._deps(reads, writes)
        slot = self.dnext[q] % self.NDS
        self.dnext[q] += 1
        sk = ("d", q, slot)
        if self.cnt[sk] > 0:
            deps.add((sk, self.cnt[sk]))
        w = self._waits(q, deps)
        self.cnt[sk] += 16
        t = (sk, self.cnt[sk])
        self.ops[q].append((w, lambda e: e.dma_start(out=out, in_=in_), self.sem[sk], 16))
        self._commit(t, reads, writes)
        self.nops += 1
        return t

    def collective(self, fn, reads=(), writes=()):
        deps = self._deps(reads, writes)
        w = self._waits("pool", deps)
        sk = ("cc",)
        self.cnt[sk] += 1
        t = (sk, self.cnt[sk])
        self.ops["pool"].append((w, fn, self.sem[sk], 1))
        self._commit(t, reads, writes)
        return t

    def barrier(self):
        tickets = set((sk, v) for sk, v in self.cnt.items() if v > 0)
        for e in self.ALLE:
            w = self._waits(e, tickets)
            if w:
                self.ops[e].append((w, None, None, 0))
        self.lastw = {}
        self.readers = {}

    def flush(self):
        nc = self.nc
        ops = self.ops

        def replay(lst, e):
            for (w, fn, sem, inc) in lst:
                for (s, v) in w:
                    e.wait_ge(s, v)
                if fn is not None:
                    ins = fn(e)
                    ins.then_inc(sem, inc)

        with nc.Block() as block:
            @block.tensor
            def _(e):
                replay(ops["pe"], e)

            @block.scalar
            def _(e):
                replay(ops["act"], e)

            @block.vector
            def _(e):
                replay(ops["dve"], e)

            @block.gpsimd
            def _(e):
                replay(ops["pool"], e)

            @block.sync
            def _(e):
                replay(ops["sp"], e)
        self.ops = {e: [] for e in self.ALLE}

    def mm(self, out, lhsT, rhs, start=True, stop=True, reads=(), writes=()):
        return self.op("pe", lambda e: e.matmul(out, lhsT, rhs, start=start, stop=stop), reads, writes)

    def tr(self, out, in_, ident, reads=(), writes=()):
        return self.op("pe", lambda e: e.transpose(out, in_, ident), reads, writes)

    def act(self, out, in_, func, bias=None, scale=None, reads=(), writes=(), eng="act"):
        kw = {}
        if bias is not None:
            kw["bias"] = bias
        if scale is not None:
            kw["scale"] = scale
        return self.op(eng, lambda e: e.activation(out, in_, func, **kw), reads, writes)

    def tt(self, eng, out, in0, in1, op, reads=(), writes=()):
        return self.op(eng, lambda e: e.tensor_tensor(out, in0, in1, op), reads, writes)

    def ts(self, eng, out, in0, s1, s2, op0, op1=None, reads=(), writes=()):
        if op1 is None:
            return self.op(eng, lambda e: e.tensor_scalar(out, in0, s1, None, op0), reads, writes)
        return self.op(eng, lambda e: e.tensor_scalar(out, in0, s1, s2, op0, op1), reads, writes)

    def stt(self, out, in0, scalar, in1, op0, op1, reads=(), writes=()):
        return self.op("dve", lambda e: e.scalar_tensor_tensor(out, in0, scalar, in1, op0, op1), reads, writes)

    def copy(self, eng, out, in_, reads=(), writes=()):
        if eng == "act":
            return self.op("act", lambda e: e.activation(out, in_, AF.Copy), reads, writes)
        return self.op(eng, lambda e: e.tensor_copy(out, in_), reads, writes)

    def memset(self, eng, ap, val, reads=(), writes=()):
        return self.op(eng, lambda e: e.memset(ap, val), reads, writes)

    def recip(self, out, in_, reads=(), writes=()):
        return self.op("dve", lambda e: e.reciprocal(out, in_), reads, writes)


class TPool:
    def __init__(self, tiles, name):
        self.tiles = tiles
        self.name = name
        self.i = 0

    def next(self):
        j = self.i % len(self.tiles)
        self.i += 1
        return self.tiles[j], (self.name, j)


def _build(debug=None):
    nc = bass.Bass("TRN2", target_bir_lowering=False)
    dbg_kind = "ExternalOutput" if debug else "Internal"

    def din(name, shape, dt=F32):
        return nc.dram_tensor(name, list(shape), dt, kind="ExternalInput").ap()

    def dscr(name, shape, dt=BF16, dbg=False):
        isdbg = bool(debug) and dbg and name in debug.get("outs", ())
        return nc.dram_tensor(name, list(shape), dt, kind=("ExternalOutput" if isdbg else "Internal")).ap()

    x_in = din("x", [T, D])
    cT_in = din("cT", [128, 8])
    ln_g_in = din("ln_g", [DEPTH, 1, D])
    ln_b_in = din("ln_b", [DEPTH, 1, D])
    w_ada_in = din("w_ada", [DEPTH, D, 3 * D])
    b_ada_in = din("b_ada", [DEPTH, 1, 3 * D])
    w_in_in = din("w_in", [DEPTH, D, 8448])
    bgT_in = din("bgT", [DEPTH, 128, 16])
    qkg_in = din("qkg", [DEPTH, 128, 2])
    w_pa_in = din("w_pa", [DEPTH, 512, D])
    w_pb_in = din("w_pb", [DEPTH, 512, D])
    w_o_in = din("w_o", [DEPTH, D, D])
    cmat_in = din("cmat", [3, 128, 128])
    cstab_in = din("cstab", [2, 128, T])
    abias_in = din("abias", [128, 24 * 256])
    tri_in = din("tri", [2, 128, 256])
    msk_in = din("msk", [128, 2])
    y_out = nc.dram_tensor("y", [T, D], F32, kind="ExternalOutput").ap()

    aqT = dscr("aqT", [3, 512, T], dbg=True)
    akT = [dscr(f"akT{g}", [512, d, T // d + 128], dbg=True) for g, (_, d) in enumerate(GROUPS)]
    avp = [dscr(f"avp{g}", [d, T // d + 128, 520], dbg=True) for g, (_, d) in enumerate(GROUPS)]
    sazT = dscr("sazT", [512, T], dbg=True)
    sbzT = dscr("sbzT", [512, T], dbg=True)
    qbT = dscr("qbT", [512, T], dbg=True)
    gT = dscr("gT", [2048, T], dbg=True)
    yaT = dscr("yaT", [512, T], dbg=True)
    ybT = dscr("ybT", [512, T], dbg=True)
    x1 = dscr("x1", [T, D], F32, dbg=True)
    Edram = dscr("Edram", [128, 24 * 256], F32)
    rdscr = dscr("rdscr", [8, 512], F32)
    rdslot = [0]
    xsrc_u = [dscr(f"xsrc{u}", [n, XCOLS]) for u, n in enumerate(XUNITS)]
    xdst_u = [dscr(f"xdst{u}", [2 * n, XCOLS]) for u, n in enumerate(XUNITS)]

    def xs(key):
        u, r0, n = XSEC[key]
        return xsrc_u[u][r0:r0 + n, :]

    def xd(key, rank):
        u, r0, n = XSEC[key]
        return xdst_u[u][rank * XUNITS[u] + r0:rank * XUNITS[u] + r0 + n, :]
    dbg_names = ["aqT", "akT0", "akT1", "akT2", "avp0", "avp1", "avp2", "sazT", "sbzT", "qbT", "gT",
                 "yaT", "ybT", "x1"]

    with ExitStack() as top:
        P = Prog(nc, top)

        def sb(es, name, shape, dt):
            return es.enter_context(nc.sbuf_tensor("s_" + name, list(shape), dt))

        PSALL = top.enter_context(nc.psum_tensor("psall", [128, 8, 512], F32))
        PS = [PSALL[:, i, :] for i in range(8)]

        class PSPool:
            def __init__(self, idx):
                self.idx = idx
                self.i = 0

            def next(self):
                b = self.idx[self.i % len(self.idx)]
                self.i += 1
                return PS[b], ("ps", b)

        def pspool(idx, name):
            return PSPool(idx)

        cmat = sb(top, "cmat", [128, 3, 128], F32)
        ones_f = sb(top, "ones_f", [128, 128], F32)
        msk = sb(top, "msk", [128, 2], F32)
        P.dma("sp", cmat[:, :, :], cmat_in.rearrange("c p n -> p c n"), writes=["cmat"])
        P.dma("sp", msk[:, :], msk_in, writes=["msk"])
        P.memset("pool", ones_f[:, :], 1.0, writes=["ones"])
        cmatb = sb(top, "cmatb", [128, 2, 128], BF16)
        P.copy("dve", cmatb[:, :, :], cmat[:, 1:3, :], reads=["cmat"], writes=["cmatb"])
        epsc = sb(top, "epsc", [128, 2], F32)
        P.memset("pool", epsc[:, 0:1], QK_EPS, writes=["epsc"])
        P.memset("pool", epsc[:, 1:2], LN_EPS, writes=["epsc"])
        ident = cmat[:, 0, :]
        bones = cmat[:, 1, :]
        rrot = cmat[:, 2, :]

        zrow = sb(top, "zrow", [128, 32], BF16)
        P.memset("pool", zrow[:, :], 0.0, writes=["zrow"])
        for g, (_, d) in enumerate(GROUPS):
            nel = d * 64 * 520
            pad = (-nel) % XCOLS
            if pad:
                for kk in ("AVF", "AVL"):
                    fl = xs((kk, g)).rearrange("r c -> (r c)")
                    P.dma("sp", fl[nel:nel + pad].rearrange("(a n) -> a n", a=128), zrow[:, 0:pad // 128], reads=["zrow"])
        with ExitStack() as es:
            ab = sb(es, "ab", [128, 24 * 256], F32)
            tri = sb(es, "tri", [128, 2, 256], F32)
            P.dma("sp", ab[:, :], abias_in, writes=["ab"])
            P.dma("sp", tri[:, :, :], tri_in.rearrange("c p n -> p c n"), writes=["tri"])
            P.act(ab[:, :], ab[:, :], AF.Exp, reads=["ab"], writes=["ab"])
            for gh in range(24):
                sl = slice(gh * 256, (gh + 1) * 256)
                P.tt("dve", ab[:, sl], ab[:, sl], tri[:, 0, :], ALU.mult, reads=["ab", "tri"], writes=[("ab", gh)])
            P.dma("sp", Edram, ab[:, :], reads=[("ab", gh) for gh in range(24)])
            P.barrier()
            P.flush()

        for l in range(DEPTH):
            if debug and l > debug.get("layers", DEPTH) - 1:
                break
            xsrc_l = x_in if l == 0 else x1
            ydst_l = x1 if l < DEPTH - 1 else y_out
            with ExitStack() as lay:
                shiftT = sb(lay, f"shiftT{l}", [128, 8], F32)
                sc1T = sb(lay, f"sc1T{l}", [128, 8], F32)
                gate_b = sb(lay, f"gate_b{l}", [128, D], F32)
                lng_b = sb(lay, f"lng_b{l}", [128, D], F32)
                lnb_b = sb(lay, f"lnb_b{l}", [128, D], F32)
                bgT = sb(lay, f"bgT{l}", [128, 16], F32)
                qkg = sb(lay, f"qkg{l}", [128, 2], F32)

                with ExitStack() as es:
                    cT = sb(es, f"cT{l}", [128, 8], F32)
                    silc = sb(es, f"silc{l}", [128, 8], F32)
                    rows = sb(es, f"rows{l}", [1, 5 * D], F32)
                    brow = sb(es, f"brow{l}", [1, 3 * D], F32)
                    wst = [sb(es, f"wada{l}_{i}", [128, 8, 512], F32) for i in range(2)]
                    wp = TPool(wst, "wada")
                    psp = pspool([0, 1], "ps")
                    P.dma("sp", cT[:, :], cT_in, writes=["cT"])
                    P.dma("sp", brow[:, :], b_ada_in[l], writes=["brow"])
                    P.dma("sp", rows[:, 3 * D:4 * D], ln_g_in[l], writes=["rows_ln"])
                    P.dma("sp", rows[:, 4 * D:5 * D], ln_b_in[l], writes=["rows_ln"])
                    P.dma("sp", bgT[:, :], bgT_in[l], writes=["bgT"])
                    P.dma("sp", qkg[:, :], qkg_in[l], writes=["qkg"])
                    P.act(silc[:, :], cT[:, :], AF.Silu, reads=["cT"], writes=["silc"])
                    for n in range(6):
                        wt, kw = wp.next()
                        P.dma("sp", wt[:, :, :],
                              w_ada_in[l, :, n * 512:(n + 1) * 512].rearrange("(kc p) n -> p kc n", p=128),
                              writes=[kw])
                        ps, kp = psp.next()
                        for kc in range(8):
                            P.mm(ps[0:1, :], silc[:, kc:kc + 1], wt[:, kc, :], start=(kc == 0), stop=(kc == 7),
                                 reads=[kw, "silc"], writes=[kp])
                        P.tt("dve", rows[0:1, n * 512:(n + 1) * 512], ps[0:1, :], brow[0:1, n * 512:(n + 1) * 512],
                             ALU.add, reads=[kp, "brow"], writes=[("rows", n)])
                    ps, kp = psp.next()
                    for j in range(16):
                        P.mm(ps[:, j:j + 1], rows[0:1, j * 128:(j + 1) * 128], ones_f[0:1, 0:1],
                             reads=[("rows", j // 4), "ones"], writes=[kp])
                    P.copy("dve", shiftT[:, :], ps[:, 0:8], reads=[kp], writes=["mod"])
                    P.ts("dve", sc1T[:, :], ps[:, 8:16], 1.0, None, ALU.add, reads=[kp], writes=["mod2"])
                    for (dst, c0, rk) in ((gate_b, 2 * D, [("rows", 4), ("rows", 5)]), (lng_b, 3 * D, ["rows_ln"]),
                                          (lnb_b, 4 * D, ["rows_ln"])):
                        for nh in range(2):
                            ps, kp = psp.next()
                            P.mm(ps[:, :], ones_f[0:1, :], rows[0:1, c0 + nh * 512:c0 + (nh + 1) * 512],
                                 reads=rk + ["ones"], writes=[kp])
                            P.copy("dve", dst[:, nh * 512:(nh + 1) * 512], ps[:, :], reads=[kp], writes=["bc"])
                    P.barrier()
                    P.flush()

                with ExitStack() as es:
                  if not (debug and debug.get("skipP1")):
                        uT = sb(es, f"uT{l}", [128, 8, T], BF16)
                        with ExitStack() as es2:
                            xp = TPool([sb(es2, f"xt{l}_{i}", [128, 4, D], F32) for i in range(2)], "xt")
                            psp = pspool([0, 1, 2, 3], "ps")
                            for tb in range(8):
                                xt, kx = xp.next()
                                P.dma("sp", xt[:, :, :],
                                      xsrc_l[tb * 512:(tb + 1) * 512, :].rearrange("(t p) d -> p t d", p=128), writes=[kx])
                                for kc in range(8):
                                    ps, kp = psp.next()
                                    for t in range(4):
                                        P.tr(ps[:, t * 128:(t + 1) * 128], xt[:, t, kc * 128:(kc + 1) * 128], ident,
                                             reads=[kx], writes=[kp])
                                    P.act(uT[:, kc, tb * 512:(tb + 1) * 512], ps[:, :], AF.Identity,
                                          bias=shiftT[:, kc:kc + 1], scale=sc1T[:, kc:kc + 1], reads=[kp])
                            P.barrier()
                            P.flush()

                        uTp = sb(es, f"uTp{l}", [128, 8, T], BF16)
                        wstp = TPool([sb(es, f"wst{l}_{i}", [128, 8, 256], F32) for i in range(2)], "wst")
                        wbfp = TPool([sb(es, f"wbf{l}_{i}", [128, 8, 512], BF16) for i in range(2)], "wbf")
                        stgp = TPool([sb(es, f"stg{l}_{i}", [128, 512], BF16) for i in range(4)], "stg")
                        vstp = TPool([sb(es, f"vst{l}_{i}", [128, 8, 65], BF16) for i in range(3)], "vst")
                        f32p = TPool([sb(es, f"f32t{l}_{i}", [128, 512], F32) for i in range(6)], "f32t")
                        b16p = TPool([sb(es, f"b16t{l}_{i}", [128, 512], BF16) for i in range(5)], "b16t")
                        csp = TPool([sb(es, f"cst{l}_{i}", [128, 2, 512], F32) for i in range(2)], "cst")
                        psA = pspool([0, 1, 2, 3, 4], "ps")
                        psB = pspool([5, 6, 7], "ps")
                        for vt in vstp.tiles:
                            P.memset("pool", vt[:, :, :], 1.0)
                        P.barrier()
                        evac_rr = [0]

                        hkeys = {0: [], 1: [], 2: []}

                        def hkey(g):
                            k_ = ("hst", g, len(hkeys[g]))
                            hkeys[g].append(k_)
                            return k_

                        def evac_copy(out, in_, reads, writes):
                            evac_rr[0] += 1
                            if evac_rr[0] % 2:
                                return P.copy("act", out, in_, reads=reads, writes=writes)
                            return P.copy("dve", out, in_, reads=reads, writes=writes)

                        def load_strip(c0, W):
                            wb, kb = wbfp.next()
                            for w0 in range(0, W, 256):
                                wt, kw = wstp.next()
                                P.dma("sp", wt[:, :, :],
                                      w_in_in[l, :, c0 + w0:c0 + w0 + 256].rearrange("(kc p) n -> p kc n", p=128),
                                      writes=[kw])
                                P.copy("pool", wb[:, :, w0:w0 + 256], wt[:, :, :], reads=[kw], writes=[(kb, w0)])
                            return wb, [(kb, w0) for w0 in range(0, W, 256)]

                        def permute_uT(d):
                            engs = ("dve", "pool", "act")
                            for kc in range(8):
                                P.copy(engs[kc % 3], uTp[:, kc, :].rearrange("k (r p) -> k r p", r=d),
                                       uT[:, kc, :].rearrange("k (p r) -> k r p", r=d), writes=[("uTp", kc)])

                        def rhs_perm(g, kc, pb):
                            src = uT if g == 0 else uTp
                            return src[:, kc, pb * 512:(pb + 1) * 512]

                        def lhs_perm(g, kc, it):
                            src = uT if g == 0 else uTp
                            return src[:, kc, it * 128:(it + 1) * 128]

                        def fm_chunk(wb, kb, col0, rhs_fn, extra=()):
                            ps, kp = psA.next()
                            for kc in range(8):
                                rd_ = list(kb) + [e_ for e_ in extra if e_[1] == kc]
                                P.mm(ps[:, :], wb[:, kc, col0:col0 + 128], rhs_fn(kc), start=(kc == 0), stop=(kc == 7),
                                     reads=rd_, writes=[kp])
                            return ps, kp

                        def do_qk(which, g, wb, kb):
                            d = GROUPS[g][1]
                            ex_ = [("uTp", kc) for kc in range(8)] if g > 0 else []
                            for fcl in range(4):
                                for pb in range(8):
                                    ps, kp = fm_chunk(wb, kb, fcl * 128, lambda kc: rhs_perm(g, kc, pb), ex_)
                                    stg, ks = stgp.next()
                                    evac_copy(stg[:, :], ps[:, :], [kp], [ks])
                                    rows_ = slice(fcl * 128, (fcl + 1) * 128)
                                    if which == "q":
                                        P.dma("sp", aqT[g, rows_, pb * 512:(pb + 1) * 512], stg[:, :], reads=[ks])
                                    elif g == 0:
                                        P.dma("sp", akT[0][rows_, 0, 64 + pb * 512:64 + (pb + 1) * 512], stg[:, :],
                                              reads=[ks], writes=[hkey(g)])
                                    elif g == 1:
                                        r, p0 = pb // 2, (pb % 2) * 512
                                        P.dma("sp", akT[1][rows_, r, 64 + p0:64 + p0 + 512], stg[:, :], reads=[ks],
                                              writes=[hkey(g)])
                                    else:
                                        P.dma("sp", akT[2][rows_, 2 * pb:2 * pb + 2, 64:64 + 256],
                                              stg[:, :].rearrange("p (r c) -> p r c", r=2), reads=[ks],
                                              writes=[hkey(g)])

                        def do_v(g, wb, kb):
                            d = GROUPS[g][1]
                            per = (T // d) // 128
                            for it in range(32):
                                ps, kp = psA.next()
                                for kc in range(8):
                                    rd_ = list(kb) + ([("uTp", kc)] if g > 0 else [])
                                    P.mm(ps[:, :], lhs_perm(g, kc, it), wb[:, kc, 0:512], start=(kc == 0), stop=(kc == 7),
                                         reads=rd_, writes=[kp])
                                vt, kv = vstp.next()
                                evac_copy(vt[:, :, 0:64], ps[:, :].rearrange("p (h e) -> p h e", h=8), [kp], [kv])
                                r, p0 = it // per, (it % per) * 128
                                P.dma("sp", avp[g][r, 64 + p0:64 + p0 + 128, :],
                                      vt[:, :, :].rearrange("p h c -> p (h c)"), reads=[kv], writes=[hkey(g)])

                        def do_pack(g):
                            d = GROUPS[g][1]
                            L = T // d
                            for (kk, c0) in (("AKF", 64), ("AKL", L)):
                                sec = xs((kk, g)).rearrange("r c -> (r c)").rearrange("(f r c) -> f r c", r=d, c=64)
                                for fb in range(4):
                                    P.dma("sp", sec[fb * 128:(fb + 1) * 128], akT[g][fb * 128:(fb + 1) * 128, :, c0:c0 + 64],
                                          reads=hkeys[g], writes=[("xs", kk, g, fb)])
                            for (kk, c0) in (("AVF", 64), ("AVL", L)):
                                nel = d * 64 * 520
                                sec = xs((kk, g)).rearrange("r c -> (r c)")[0:nel].rearrange(
                                    "(r p f) -> r p f", p=64, f=520)
                                P.dma("sp", sec, avp[g][:, c0:c0 + 64, :], reads=hkeys[g], writes=[("xs", kk, g, 0)])

                        def do_gather(g):
                            us = sorted(set(XSEC[(kk, g)][0] for kk in ("AKF", "AKL", "AVF", "AVL")))
                            rk_ = [("xs", kk, g, fb) for kk in ("AKF", "AKL") for fb in range(4)] + \
                                  [("xs", kk, g, 0) for kk in ("AVF", "AVL")]
                            for u in us:
                                P.collective((lambda a, b: (lambda e: e.collective_compute(
                                    "AllGather", ALU.bypass, replica_groups=[[0, 1], [2, 3], [4, 5], [6, 7]],
                                    ins=[a], outs=[b])))(xsrc_u[u], xdst_u[u]), reads=rk_)

                        def do_gate(dst, s_, wb, kb):
                            for fcl in range(4):
                                for tb in range(8):
                                    ps, kp = fm_chunk(wb, kb, fcl * 128, lambda kc: uT[:, kc, tb * 512:(tb + 1) * 512])
                                    stg, ks = stgp.next()
                                    j = s_ * 4 + fcl
                                    if dst is gT:
                                        P.act(stg[:, :], ps[:, :], AF.Sigmoid, bias=bgT[:, j:j + 1], reads=[kp],
                                              writes=[ks])
                                    else:
                                        P.act(stg[:, :], ps[:, :], AF.Silu, reads=[kp], writes=[ks])
                                    P.dma("sp", dst[j * 128:(j + 1) * 128, tb * 512:(tb + 1) * 512], stg[:, :],
                                          reads=[ks])

                        jobs = []
                        for g in range(3):
                            if g > 0:
                                jobs.append((None, 0, (lambda g_: (lambda wb, kb: permute_uT(GROUPS[g_][1])))(g)))
                            jobs.append((C_AQ + g * 512, 512, (lambda g_: (lambda wb, kb: do_qk("q", g_, wb, kb)))(g)))
                            jobs.append((C_AK + g * 512, 512, (lambda g_: (lambda wb, kb: do_qk("k", g_, wb, kb)))(g)))
                            jobs.append((C_AV + g * 512, 512, (lambda g_: (lambda wb, kb: do_v(g_, wb, kb)))(g)))
                            jobs.append((None, 0, (lambda g_: (lambda wb, kb: do_pack(g_)))(g)))
                        jobs.append((C_AZ, 512, lambda wb, kb: do_gate(sazT, 0, wb, kb)))
                        jobs.append((C_BZ, 512, lambda wb, kb: do_gate(sbzT, 0, wb, kb)))
                        for s_ in range(4):
                            jobs.append((C_GL + s_ * 512, 512, (lambda s2: (lambda wb, kb: do_gate(gT, s2, wb, kb)))(s_)))
                        strips = [j for j in jobs if j[0] is not None]
                        loaded = {}
                        nxt = [0]

                        def prefetch():
                            if nxt[0] < len(strips):
                                c0_, W_, _ = strips[nxt[0]]
                                loaded[nxt[0]] = load_strip(c0_, W_)
                                nxt[0] += 1

                        prefetch()
                        si = 0
                        for (c0_, W_, fn_) in jobs:
                            if c0_ is None:
                                fn_(None, None)
                                continue
                            wb, kb = loaded.pop(si)
                            si += 1
                            prefetch()
                            fn_(wb, kb)
                        xs_kb = xs("KB")
                        xs_vb = xs("VB").rearrange("r c -> (r c)").rearrange("(t f) -> t f", f=130)
                        wbq, kbq = load_strip(C_BQ, 512)
                        wbk, kbk = load_strip(C_BK, 256)
                        units = [(fcl, tb) for fcl in range(5) for tb in range(8)]
                        ust = {}

                        def ropeA(i):
                            fcl, tb = units[i]
                            wb, kb, col0, gcol = (wbq, kbq, fcl * 128, 0) if fcl < 4 else (wbk, kbk, 0, 1)
                            tsl = slice(tb * 512, (tb + 1) * 512)
                            cst, kc_ = csp.next()
                            P.dma("sp", cst[:, :, :], cstab_in[:, :, tsl].rearrange("c p t -> p c t"), writes=[kc_])
                            ps, kp = fm_chunk(wb, kb, col0, lambda kc: uT[:, kc, tsl])
                            sq, k1 = b16p.next()
                            P.act(sq[:, :], ps[:, :], AF.Square, reads=[kp], writes=[k1])
                            ust[i] = dict(fcl=fcl, tsl=tsl, gcol=gcol, cst=cst, kc_=kc_, ps=ps, kp=kp, sq=sq, k1=k1)

                        def ropeB(i):
                            u = ust[i]
                            ps2, kp2 = psB.next()
                            P.mm(ps2[:, :], cmatb[:, 0, :], u["sq"][:, :], reads=[u["k1"], "cmatb"], writes=[kp2])
                            srt, k2a = f32p.next()
                            P.act(srt[:, :], ps2[:, :], AF.Ln, bias=epsc[:, 0:1], reads=[kp2], writes=[k2a])
                            rstd, k2 = f32p.next()
                            P.act(rstd[:, :], srt[:, :], AF.Exp, scale=-0.5, reads=[k2a], writes=[k2])
                            xn, k3 = b16p.next()
                            P.stt(xn[:, :], u["ps"][:, :], qkg[:, u["gcol"]:u["gcol"] + 1], rstd[:, :], ALU.mult, ALU.mult,
                                  reads=[u["kp"], k2], writes=[k3])
                            u["xn"], u["k3"] = xn, k3

                        def ropeC(i):
                            u = ust.pop(i)
                            xn, k3, cst, kc_ = u["xn"], u["k3"], u["cst"], u["kc_"]
                            ps3, kp3 = psB.next()
                            P.mm(ps3[:, :], cmatb[:, 1, :], xn[:, :], reads=[k3, "cmatb"], writes=[kp3])
                            ta, k4 = f32p.next()
                            P.tt("pool", ta[:, :], xn[:, :], cst[:, 0, :], ALU.mult, reads=[k3, kc_], writes=[k4])
                            tb_, k5 = f32p.next()
                            P.tt("dve", tb_[:, :], ps3[:, :], cst[:, 1, :], ALU.mult, reads=[kp3, kc_], writes=[k5])
                            stg, ks = stgp.next()
                            P.tt("pool", stg[:, :], ta[:, :], tb_[:, :], ALU.add, reads=[k4, k5], writes=[ks])
                            if u["fcl"] < 4:
                                P.dma("sp", qbT[u["fcl"] * 128:(u["fcl"] + 1) * 128, u["tsl"]], stg[:, :], reads=[ks])
                            else:
                                P.dma("sp", xs_kb[:, u["tsl"]], stg[:, :], reads=[ks])

                        nu = len(units)
                        for i in range(nu + 2):
                            if i >= 2:
                                ropeC(i - 2)
                            if 1 <= i <= nu:
                                ropeB(i - 1)
                            if i < nu:
                                ropeA(i)
                        for it in range(32):
                            ps, kp = psA.next()
                            for kc in range(8):
                                P.mm(ps[:, 0:128], uT[:, kc, it * 128:(it + 1) * 128], wbk[:, kc, 128:256],
                                     start=(kc == 0), stop=(kc == 7), reads=kbk, writes=[kp])
                            vt, kv = vstp.next()
                            evac_copy(vt[:, 0:2, 0:64], ps[:, 0:128].rearrange("p (h e) -> p h e", h=2), [kp], [kv])
                            P.dma("sp", xs_vb[it * 128:(it + 1) * 128, :],
                                  vt[:, 0:2, :].rearrange("p h c -> p (h c)"), reads=[kv])
                        P.barrier()
                        P.flush()

                if debug and debug.get("upto") == "P1":
                    break

                with ExitStack() as es:
                    vh = TPool([sb(es, f"vh{l}_{i}", [64, 16, 520], BF16) for i in range(2)], "vh")
                    uorder = [u for u in range(len(XUNITS)) if u not in (XSEC["KB"][0], XSEC["VB"][0])] + \
                             [XSEC["KB"][0], XSEC["VB"][0]]
                    for u in uorder:
                        P.collective((lambda a, b: (lambda e: e.collective_compute(
                            "AllGather", ALU.bypass, replica_groups=[[0, 1], [2, 3], [4, 5], [6, 7]],
                            ins=[a], outs=[b])))(xsrc_u[u], xdst_u[u]), writes=[("xd", u)])
                    for g, (_, d) in enumerate(GROUPS):
                        L = T // d
                        for (kk, rb, c0) in (("AKL", 0, 0), ("AKF", 1, 64 + L)):
                            sec = xd((kk, g), rb).rearrange("r c -> (r c)").rearrange(
                                "(f r c) -> f r c", r=d, c=64)
                            for fb in range(4):
                                P.dma("sp", akT[g][fb * 128:(fb + 1) * 128, :, c0:c0 + 64], sec[fb * 128:(fb + 1) * 128],
                                      reads=[("xd", XSEC[(kk, g)][0])])
                        for (kk, rb, c0, mc) in (("AVL", 0, 0, 0), ("AVF", 1, 64 + L, 1)):
                            nel = d * 64 * 520
                            sec = xd((kk, g), rb).rearrange("r c -> (r c)")[0:nel].rearrange(
                                "(r p f) -> p r f", p=64, f=520)
                            t_, kt = vh.next()
                            P.dma("sp", t_[:, 0:d, :], sec, reads=[("xd", XSEC[(kk, g)][0])], writes=[kt])
                            P.ts("dve", t_[:, 0:d, :], t_[:, 0:d, :], msk[0:64, mc:mc + 1], None, ALU.mult,
                                 reads=[kt, "msk"], writes=[kt])
                            P.dma("sp", avp[g][:, c0:c0 + 64, :].rearrange("r p f -> p r f"), t_[:, 0:d, :], reads=[kt])
                    P.barrier()
                    P.flush()

                if debug and debug.get("upto") == "X":
                    break

                with ExitStack() as es:
                    E = sb(es, f"E{l}", [128, 24, 256], F32)
                    acc = sb(es, f"acc{l}", [65, 4, T], F32)
                    vaug = sb(es, f"vaug{l}", [128, 48, 4, 65], BF16)
                    ktp = TPool([sb(es, f"kt{l}_{i}", [64, 6144], BF16) for i in range(2)], "kt")
                    qtp = TPool([sb(es, f"qt{l}_{i}", [64, T], BF16) for i in range(2)], "qt")
                    exp_ = TPool([sb(es, f"ex{l}_{i}", [128, 512], F32) for i in range(4)], "ex")
                    ptp = TPool([sb(es, f"pt{l}_{i}", [128, 512], BF16) for i in range(5)], "pt")
                    rdp = TPool([sb(es, f"rd{l}_{i}", [65, 512], F32) for i in range(2)], "rd")
                    nmp = TPool([sb(es, f"nm{l}_{i}", [64, 512], F32) for i in range(2)], "nm")
                    szp = TPool([sb(es, f"sz{l}_{i}", [64, 512], BF16) for i in range(2)], "sz")
                    ysp = TPool([sb(es, f"ys{l}_{i}", [64, 512], BF16) for i in range(2)], "ys")
                    psS = pspool([0, 1, 2], "ps")
                    psO = pspool([3, 4, 5], "ps")
                    psN = pspool([6, 7], "ps")
                    P.dma("sp", E[:, :, :], Edram.rearrange("p (g c) -> p g c", c=256), writes=["E"])
                    for hh in range(2):
                        for g, (_, d) in enumerate(GROUPS):
                            L = T // d
                            nch = L // 128 + 1
                            vsrc = avp[g].rearrange("r (m p) (h c) -> p (r m) h c", p=128, c=65)
                            for c0 in range(0, d * nch, 12):
                                c1 = min(d * nch, c0 + 12)
                                P.dma("sp", vaug[:, c0:c1, :, :], vsrc[:, c0:c1, hh * 4:(hh + 1) * 4, :],
                                      writes=[("vaug", c0)])
                            vkeys = [("vaug", c0) for c0 in range(0, d * nch, 12)]
                            for hl in range(4):
                                h = hh * 4 + hl
                                kt, kk = ktp.next()
                                P.dma("sp", kt[:, 0:d * (L + 128)],
                                      akT[g][h * 64:(h + 1) * 64, :, :].rearrange("p r c -> p (r c)"), writes=[kk])
                                qt, kq = qtp.next()
                                P.dma("sp", qt[:, :], aqT[g, h * 64:(h + 1) * 64, :], writes=[kq])
                                blocks = [(r, n2) for r in range(d) for n2 in range(L // 256)]
                                st = {}
                                e0 = E[:, g * 8 + h, :]
                                ebc = bass.AP(e0.tensor, e0.offset, [list(e0.ap[0]), [0, 2], list(e0.ap[1])])

                                def stageA(i):
                                    r, n2 = blocks[i]
                                    n = 2 * n2
                                    ps, kp = psS.next()
                                    kb0 = r * (L + 128) + 128 * n
                                    qb0 = r * L + 128 * n
                                    P.mm(ps[:, 0:128], kt[:, kb0:kb0 + 128], qt[:, qb0:qb0 + 128],
                                         reads=[kk, kq], writes=[kp])
                                    P.mm(ps[:, 128:384], kt[:, kb0 + 128:kb0 + 256], qt[:, qb0:qb0 + 256],
                                         reads=[kk, kq], writes=[kp])
                                    P.mm(ps[:, 384:512], kt[:, kb0 + 256:kb0 + 384], qt[:, qb0 + 128:qb0 + 256],
                                         reads=[kk, kq], writes=[kp])
                                    ex, ke = exp_.next()
                                    P.act(ex[:, :], ps[:, :], AF.Exp, scale=0.125, reads=[kp], writes=[ke])
                                    pt, kpt = ptp.next()
                                    P.tt("dve" if i % 3 == 2 else "pool", pt[:, :].rearrange("p (a b) -> p a b", a=2),
                                         ex[:, :].rearrange("p (a b) -> p a b", a=2), ebc, ALU.mult,
                                         reads=[ke, "E"], writes=[kpt])
                                    st[i] = (pt, kpt)

                                def stageB(i):
                                    r, n2 = blocks[i]
                                    n = 2 * n2
                                    pt, kpt = st.pop(i)
                                    po, ko = psO.next()
                                    ch = r * nch + n
                                    vk = lambda c_: [("vaug", (c_ // 12) * 12)]
                                    P.mm(po[0:65, 0:256], vaug[:, ch + 1, hl, :], pt[:, 128:384], start=True, stop=False,
                                         reads=[kpt] + vk(ch + 1), writes=[ko])
                                    P.mm(po[0:65, 0:128], vaug[:, ch, hl, :], pt[:, 0:128], start=False, stop=False,
                                         reads=[kpt] + vk(ch), writes=[ko])
                                    P.mm(po[0:65, 128:256], vaug[:, ch + 2, hl, :], pt[:, 384:512], start=False, stop=True,
                                         reads=[kpt] + vk(ch + 2), writes=[ko])
                                    t0 = r + d * 128 * n
                                    av_ = acc[:, hl, t0:t0 + d * 255 + 1:d]
                                    if g == 0:
                                        P.copy("act", av_, po[0:65, 0:256], reads=[ko], writes=[("acc", hl)])
                                    else:
                                        P.tt("dve", av_, av_, po[0:65, 0:256], ALU.add, reads=[ko, ("acc", hl)],
                                             writes=[("acc", hl)])

                                nb = len(blocks)
                                LK = 3
                                for i in range(nb + LK):
                                    if i < nb:
                                        stageA(i)
                                    if i >= LK:
                                        stageB(i - LK)
                        for hl in range(4):
                            P.act(acc[64:65, hl, :], acc[64:65, hl, :], AF.Ln, reads=[("acc", hl)], writes=[("acc", hl)])
                        for hl in range(4):
                            P.act(acc[64:65, hl, :], acc[64:65, hl, :], AF.Exp, scale=-1.0, reads=[("acc", hl)],
                                  writes=[("acc", hl)])
                        for hl in range(4):
                            h = hh * 4 + hl
                            for tb in range(8):
                                tsl = slice(tb * 512, (tb + 1) * 512)
                                sz, ksz = szp.next()
                                P.dma("sp", sz[:, :], sazT[h * 64:(h + 1) * 64, tsl], writes=[ksz])
                                pb_, kpb = psN.next()
                                P.mm(pb_[0:64, :], ones_f[64:65, 0:64], acc[64:65, hl, tsl], reads=[("acc", hl), "ones"],
                                     writes=[kpb])
                                nm, knm = nmp.next()
                                P.tt("dve", nm[:, :], acc[0:64, hl, tsl], pb_[0:64, :], ALU.mult,
                                     reads=[("acc", hl), kpb], writes=[knm])
                                ys, kys = ysp.next()
                                P.tt("pool", ys[:, :], nm[:, :], sz[:, :], ALU.mult, reads=[knm, ksz], writes=[kys])
                                P.dma("pool", yaT[h * 64:(h + 1) * 64, tsl], ys[:, :], reads=[kys])
                    P.barrier()
                    P.flush()

                if debug and debug.get("upto") == "P2":
                    break

                with ExitStack() as es:
                    kTd = sb(es, f"kTd{l}", [128, 2, S], BF16)
                    vb = sb(es, f"vb{l}", [128, 64, 130], BF16)
                    qT = sb(es, f"qT{l}", [128, 4, T], BF16)
                    ptp = TPool([sb(es, f"pB{l}_{i}", [128, 1024], BF16) for i in range(4)], "pB")
                    bcp = TPool([sb(es, f"bcB{l}_{i}", [64, 512], F32) for i in range(3)], "bcB")
                    evp = TPool([sb(es, f"evB{l}_{i}", [65, 512], F32) for i in range(3)], "evB")
                    rdp = TPool([sb(es, f"rdB{l}_{i}", [65, 512], F32) for i in range(2)], "rdB")
                    nm2 = TPool([sb(es, f"nmC{l}_{i}", [64, 512], F32) for i in range(2)], "nmC")
                    szp = TPool([sb(es, f"szB{l}_{i}", [64, 512], BF16) for i in range(3)], "szB")
                    ysp = TPool([sb(es, f"ysB{l}_{i}", [64, 512], BF16) for i in range(3)], "ysB")
                    psA2 = [(PSALL[:, 2 * j_:2 * j_ + 2, :].rearrange("p a b -> p (a b)"),
                             [("ps", 2 * j_), ("ps", 2 * j_ + 1)]) for j_ in range(3)]
                    psO = pspool([6, 7], "ps")
                    P.dma("sp", qT[:, 0, :], qbT[0:128, :], writes=[("qT", 0)])
                    for kvh in range(2):
                        for rk in range(2):
                            kb_ = xd("KB", rk)
                            for half in range(2):
                                P.dma("sp", kTd[half * 64:(half + 1) * 64, kvh, rk * T:(rk + 1) * T],
                                      kb_[kvh * 64:(kvh + 1) * 64, :], writes=[("kTd", rk, kvh, half)])
                            if kvh == 0:
                                vsec = xd("VB", rk).rearrange("r c -> (r c)").rearrange("(k p f) -> p k f", p=128, f=130)
                                for k0 in range(0, 32, 8):
                                    P.dma("sp", vb[:, rk * 32 + k0:rk * 32 + k0 + 8, :], vsec[:, k0:k0 + 8, :],
                                          writes=[("vb", rk, k0)])
                        if kvh == 0:
                            P.dma("sp", qT[:, 1, :], qbT[128:256, :], writes=[("qT", 1)])
                    for c_ in range(2, 4):
                        P.dma("sp", qT[:, c_, :], qbT[c_ * 128:(c_ + 1) * 128, :], writes=[("qT", c_)])
                    steps = [(qb, hp, kc) for qb in range(8) for hp in range(4) for kc in range(64)]
                    st = {}
                    acc_o = {}

                    def stageA(i):
                        qb, hp, kc = steps[i]
                        kvh = hp // 2
                        rk = kc // 32
                        psa, keys = psA2[i % 3]
                        for hh_ in range(2):
                            pr = hh_ * 64
                            P.mm(psa[:, hh_ * 512:(hh_ + 1) * 512], kTd[pr:pr + 64, kvh, kc * 128:(kc + 1) * 128],
                                 qT[pr:pr + 64, hp, qb * 512:(qb + 1) * 512],
                                 reads=[("kTd", rk, kvh, hh_), ("qT", hp)], writes=keys)
                        pt, kpt = ptp.next()
                        P.act(pt[:, :], psa, AF.Exp, scale=0.125, reads=keys, writes=[kpt])
                        st[i] = (pt, kpt)

                    def stageB(i):
                        qb, hp, kc = steps[i]
                        kvh = hp // 2
                        pt, kpt = st.pop(i)
                        if kc == 0:
                            acc_o[(qb, hp)] = [psO.next(), psO.next()]
                        for hh_ in range(2):
                            po, ko = acc_o[(qb, hp)][hh_]
                            P.mm(po[0:65, :], vb[:, kc, kvh * 65:(kvh + 1) * 65], pt[:, hh_ * 512:(hh_ + 1) * 512],
                                 start=(kc == 0), stop=(kc == 63),
                                 reads=[kpt, ("vb", kc // 32, ((kc % 32) // 8) * 8)], writes=[ko])
                        if kc == 63:
                            tsl = slice(qb * 512, (qb + 1) * 512)
                            for hh_ in range(2):
                                h = 2 * hp + hh_
                                po, ko = acc_o[(qb, hp)][hh_]
                                ev, kev = evp.next()
                                P.copy("dve", ev[:, :], po[0:65, :], reads=[ko], writes=[kev])
                                sz, ksz = szp.next()
                                P.dma("sp", sz[:, :], sbzT[h * 64:(h + 1) * 64, tsl], writes=[ksz])
                                rd, krd = rdp.next()
                                P.recip(rd[64:65, :], ev[64:65, :], reads=[kev], writes=[krd])
                                bc, kbc = bcp.next()
                                sl_ = rdslot[0] % 8
                                rdslot[0] += 1
                                P.dma("sp", rdscr[sl_:sl_ + 1, :], rd[64:65, :], reads=[krd], writes=[("rdscr", sl_)])
                                P.dma("sp", bc[:, :], bass.AP(rdscr.tensor, sl_ * 512, [[0, 64], [1, 512]]),
                                      reads=[("rdscr", sl_)], writes=[kbc])
                                n2, kn2 = nm2.next()
                                P.tt("dve", n2[:, :], ev[0:64, :], bc[:, :], ALU.mult, reads=[kev, kbc], writes=[kn2])
                                ys, kys = ysp.next()
                                P.tt("pool", ys[:, :], n2[:, :], sz[:, :], ALU.mult, reads=[kn2, ksz], writes=[kys])
                                P.dma("pool", ybT[h * 64:(h + 1) * 64, tsl], ys[:, :], reads=[kys])

                    LOOK = 2
                    ns = len(steps)
                    for i in range(ns + LOOK):
                        if i < ns:
                            stageA(i)
                        if i >= LOOK:
                            stageB(i - LOOK)
                    P.barrier()
                    P.flush()

                if debug and debug.get("upto") == "P3":
                    break

                with ExitStack() as es:
                    wpa = sb(es, f"wpa{l}", [128, 4, D], BF16)
                    wpb = sb(es, f"wpb{l}", [128, 4, D], BF16)
                    wo = sb(es, f"wo{l}", [128, 8, D], BF16)
                    wf = TPool([sb(es, f"wf{l}_{i}", [128, 8, 256], F32) for i in range(2)], "wf")
                    for (wdst, wsrc, nk_) in ((wpa, w_pa_in, 4), (wpb, w_pb_in, 4), (wo, w_o_in, 8)):
                        for nh in range(4):
                            wt, kw = wf.next()
                            P.dma("sp", wt[:, 0:nk_, :],
                                  wsrc[l, :, nh * 256:(nh + 1) * 256].rearrange("(k p) n -> p k n", p=128), writes=[kw])
                            if wdst is wo:
                                g0_ = gate_b[:, nh * 256:(nh + 1) * 256]
                                gbc_ = bass.AP(g0_.tensor, g0_.offset, [list(g0_.ap[0]), [0, 8], list(g0_.ap[1])])
                                P.tt("pool", wdst[:, :, nh * 256:(nh + 1) * 256], wt[:, 0:nk_, :], gbc_, ALU.mult,
                                     reads=[kw], writes=["wP4"])
                            else:
                                P.copy("pool", wdst[:, :, nh * 256:(nh + 1) * 256], wt[:, 0:nk_, :], reads=[kw],
                                       writes=["wP4"])
                    yap = TPool([sb(es, f"ya{l}_{i}", [128, 4, 512], BF16) for i in range(2)], "ya")
                    ybp = TPool([sb(es, f"yb{l}_{i}", [128, 4, 512], BF16) for i in range(2)], "yb")
                    gp = TPool([sb(es, f"gg{l}_{i}", [128, 16, 512], BF16) for i in range(2)], "gg")
                    mtp = TPool([sb(es, f"mT{l}_{i}", [128, 8, 512], BF16) for i in range(2)], "mT")
                    t1p = TPool([sb(es, f"t1{l}_{i}", [128, 512], F32) for i in range(2)], "t1")
                    t2p = TPool([sb(es, f"t2{l}_{i}", [128, 512], F32) for i in range(2)], "t2")
                    xrp = TPool([sb(es, f"xr{l}_{i}", [128, D], F32) for i in range(3)], "xr")
                    hp = TPool([sb(es, f"hh{l}_{i}", [128, D], F32) for i in range(2)], "hh")
                    op_ = TPool([sb(es, f"oo{l}_{i}", [128, D], F32) for i in range(2)], "oo")
                    stp = TPool([sb(es, f"bst{l}_{i}", [128, 16], F32) for i in range(2)], "bst")
                    psA = pspool([0, 1, 2, 3], "ps")
                    psB = pspool([4, 5, 6, 7], "ps")
                    for tb in range(8):
                        tsl = slice(tb * 512, (tb + 1) * 512)
                        ya, kya = yap.next()
                        yb, kyb = ybp.next()
                        gg, kgg = gp.next()
                        P.dma("sp", ya[:, :, :], yaT[:, tsl].rearrange("(h p) t -> p h t", p=128), writes=[kya])
                        P.dma("sp", yb[:, :, :], ybT[:, tsl].rearrange("(h p) t -> p h t", p=128), writes=[kyb])
                        P.dma("sp", gg[:, :, :], gT[:, tsl].rearrange("(j p) t -> p j t", p=128), writes=[kgg])
                        mT, kmT = mtp.next()
                        for m in range(8):
                            pa, kpa = psA.next()
                            for h in range(4):
                                P.mm(pa[:, :], wpa[:, h, m * 128:(m + 1) * 128], ya[:, h, :], start=(h == 0),
                                     stop=(h == 3), reads=["wP4", kya], writes=[kpa])
                            pb_, kpb = psA.next()
                            for h in range(4):
                                P.mm(pb_[:, :], wpb[:, h, m * 128:(m + 1) * 128], yb[:, h, :], start=(h == 0),
                                     stop=(h == 3), reads=["wP4", kyb], writes=[kpb])
                            t1, k1 = t1p.next()
                            P.tt("dve", t1[:, :], pa[:, :], gg[:, m, :], ALU.mult, reads=[kpa, kgg], writes=[k1])
                            t2, k2 = t2p.next()
                            P.tt("dve", t2[:, :], pb_[:, :], gg[:, 8 + m, :], ALU.mult, reads=[kpb, kgg], writes=[k2])
                            P.tt("pool", mT[:, m, :], t1[:, :], t2[:, :], ALU.add, reads=[k1, k2], writes=[(kmT, m)])
                        for tt_ in range(4):
                            r0 = tb * 512 + tt_ * 128
                            xr, kxr = xrp.next()
                            P.dma("sp", xr[:, :], xsrc_l[r0:r0 + 128, :], writes=[kxr])
                            hb, khb = hp.next()
                            for nh in range(2):
                                po, ko = psB.next()
                                for kc in range(8):
                                    P.mm(po[:, :], mT[:, kc, tt_ * 128:(tt_ + 1) * 128], wo[:, kc, nh * 512:(nh + 1) * 512],
                                         start=(kc == 0), stop=(kc == 7), reads=["wP4"] + [(kmT, m) for m in range(8)],
                                         writes=[ko])
                                P.stt(hb[:, nh * 512:(nh + 1) * 512], xr[:, nh * 512:(nh + 1) * 512], ALPHA, po[:, :],
                                      ALU.mult, ALU.add, reads=[ko, kxr], writes=[(khb, nh)])
                            bs, kbs = stp.next()
                            for nh in range(2):
                                P.op("dve", (lambda o, i_: (lambda e: e.bn_stats(o, i_)))(
                                    bs[:, nh * 6:(nh + 1) * 6], hb[:, nh * 512:(nh + 1) * 512]),
                                    reads=[(khb, 0), (khb, 1)], writes=[(kbs, nh)])
                            P.op("dve", (lambda o, i_: (lambda e: e.bn_aggr(o, i_)))(bs[:, 12:14], bs[:, 0:12]),
                                 reads=[(kbs, 0), (kbs, 1)], writes=[(kbs, 2)])
                            P.act(bs[:, 15:16], bs[:, 13:14], AF.Sqrt, bias=epsc[:, 1:2], reads=[(kbs, 2)],
                                  writes=[(kbs, 4)])
                            P.recip(bs[:, 14:15], bs[:, 15:16], reads=[(kbs, 4)], writes=[(kbs, 3)])
                            ob, kob = op_.next()
                            P.stt(bs[:, 15:16], bs[:, 12:13], -1.0, bs[:, 14:15], ALU.mult, ALU.mult,
                                  reads=[(kbs, 2), (kbs, 3)], writes=[(kbs, 5)])
                            P.act(ob[:, :], hb[:, :], AF.Identity, bias=bs[:, 15:16], scale=bs[:, 14:15],
                                  reads=[(khb, 0), (khb, 1), (kbs, 3), (kbs, 5)], writes=[kob])
                            P.tt("pool", ob[:, :], ob[:, :], lng_b[:, :], ALU.mult, reads=[kob], writes=[kob])
                            P.tt("dve" if tt_ % 2 else "pool", ob[:, :], ob[:, :], lnb_b[:, :], ALU.add, reads=[kob],
                                 writes=[kob])
                            P.dma("pool", ydst_l[r0:r0 + 128, :], ob[:, :], reads=[kob])
                    P.barrier()
                    P.flush()
        P.barrier()
        P.flush()
    return nc, dbg_names


def _t5_bucket_np(rel):
    half, max_exact = 16, 8
    ret = np.where(rel > 0, half, 0)
    a = np.abs(rel)
    af = np.maximum(a, 1).astype(np.float32)
    large = max_exact + (np.log(af / np.float32(max_exact)) / np.float32(math.log(1024 / max_exact))
                         * np.float32(half - max_exact)).astype(np.int32)
    large = np.minimum(large, half - 1)
    return ret + np.where(a < max_exact, a, large)


def _host_consts(rel_table):
    ident = np.eye(128, dtype=np.float32)
    bones = np.zeros((128, 128), np.float32)
    bones[:64, :64] = 1.0 / 64
    bones[64:, 64:] = 1.0 / 64
    rrot = np.zeros((128, 128), np.float32)
    for base in range(0, 128, 32):
        for e in range(16):
            rrot[base + e + 16, base + e] = -1.0
            rrot[base + e, base + e + 16] = 1.0
    cmat = np.stack([ident, bones, rrot])
    i = np.arange(128)[:, None]
    j = np.arange(128)[None, :]
    rel0 = i - 64 - j
    rel1 = i + 64 - j
    tri0 = np.concatenate([(i >= j), (i <= j)], axis=1).astype(np.float32)
    tri = np.stack([tri0, (tri0 - 1.0) * 30000.0]).astype(np.float32)
    ab = np.zeros((128, 24, 256), np.float32)
    for g, (_, d) in enumerate(GROUPS):
        for c, rel in enumerate((rel0, rel1)):
            bk = _t5_bucket_np(np.clip(rel, -64, 64) * d)
            ab[:, g * 8:(g + 1) * 8, c * 128:(c + 1) * 128] = rel_table[bk][:, :, g * 8:(g + 1) * 8].transpose(0, 2, 1)
    return cmat, tri, ab.reshape(128, 24 * 256)


def _cs_tables(half):
    t = np.arange(half * T, (half + 1) * T)
    row = (t // 64).astype(np.float32)
    col = (t % 64).astype(np.float32)
    inv = (np.float32(10000.0) ** (-np.arange(0, 32, 2, dtype=np.float32) / np.float32(32))).astype(np.float32)
    ar = (row[:, None] * inv[None]).astype(np.float32)
    ac = (col[:, None] * inv[None]).astype(np.float32)
    ang = np.zeros((128, T), np.float32)
    for p in range(128):
        e = p % 64
        ang[p] = ar[:, e % 16] if e < 32 else ac[:, (e - 32) % 16]
    return np.stack([np.cos(ang), np.sin(ang)]).astype(np.float32)


def _in_maps(inputs):
    f = lambda a: np.ascontiguousarray(np.asarray(a, dtype=np.float32))
    x = f(inputs["x"]); c = f(inputs["c"])
    cmat, tri, ab = _host_consts(f(inputs["rel_table"]))
    bgT = np.ascontiguousarray(f(inputs["b_gate"]).reshape(DEPTH, 16, 128).transpose(0, 2, 1))
    qg = f(inputs["q_norm_g"]); kg = f(inputs["k_norm_g"])
    qkg = np.ascontiguousarray(np.stack([np.tile(qg, (1, 2)), np.tile(kg, (1, 2))], axis=-1))
    shared = {
        "ln_g": f(inputs["ln_g"]).reshape(DEPTH, 1, D), "ln_b": f(inputs["ln_b"]).reshape(DEPTH, 1, D),
        "w_ada": f(inputs["w_ada"]), "b_ada": f(inputs["b_ada"]).reshape(DEPTH, 1, 3 * D),
        "w_in": f(inputs["w_in"]), "bgT": bgT, "qkg": qkg, "w_pa": f(inputs["w_pa"]), "w_pb": f(inputs["w_pb"]),
        "w_o": f(inputs["w_o"]), "cmat": cmat, "abias": ab, "tri": tri,
    }
    cs = [_cs_tables(0), _cs_tables(1)]
    maps = []
    for core in range(8):
        b, hf = core // 2, core % 2
        m = dict(shared)
        m["x"] = np.ascontiguousarray(x[b, hf * T:(hf + 1) * T])
        m["cT"] = np.ascontiguousarray(c[b].reshape(8, 128).T)
        m["cstab"] = cs[hf]
        mk = np.ones((128, 2), np.float32)
        mk[:, 0] = 0.0 if hf == 0 else 1.0
        mk[:, 1] = 1.0 if hf == 0 else 0.0
        m["msk"] = mk
        maps.append(m)
    return maps


def kernel(**inputs):
    nc, _ = _build()
    maps = _in_maps(inputs)
    res = run_bass_kernel_spmd(nc, maps, core_ids=list(range(8)))
    out = np.empty((4, S, D), np.float32)
    for core in range(8):
        b, hf = core // 2, core % 2
        out[b, hf * T:(hf + 1) * T] = res.results[core]["y"]
    return out
```
